# Optimizing a Trainium2 kernel written in Bass

```python
import jax
import jax.numpy as jnp
from jax import lax
import numpy as np

D_MODEL = 1024
BATCH = 8
SEQ = 4096
DEPTH = 2

GRID_W = 64
CTX_LEN = 256
HEAD_DIM = 64
MIX_HALF = D_MODEL // 2
N_MOD = 9
D_FF = ((8 * D_MODEL // 3 + 127) // 128) * 128
EPS = 1e-6
ROPE_THETA = 10000.0
NEG = -1e30
N_EVEN = (DEPTH + 1) // 2
N_ODD = DEPTH // 2
A_HEADS = MIX_HALF // HEAD_DIM
A_KV_HEADS = A_HEADS // 4
WINDOW = 128
A_BLOCK = 128
A_W = A_HEADS * HEAD_DIM
A_KV = A_KV_HEADS * HEAD_DIM
B_GROUPS = MIX_HALF // HEAD_DIM
B_CHUNK = 128
B_W = B_GROUPS * HEAD_DIM
EVEN_SPLIT = [A_W, A_W + A_KV, A_W + 2 * A_KV, A_W + 2 * A_KV + B_W]
EVEN_IN = A_W + 2 * A_KV + 2 * B_W
EVEN_MIX = A_W + B_W
POOL_WINDOWS = (2, 4, 8, 16)
C_GROUPS = len(POOL_WINDOWS)
C_W = MIX_HALF
C_GROUP_W = C_W // C_GROUPS
D_HEADS = MIX_HALF // HEAD_DIM
D_W = D_HEADS * HEAD_DIM
NA_ROWS = 8
NA_COLS = 16
ODD_SPLIT = [C_W, C_W + D_W, C_W + 2 * D_W]
ODD_IN = C_W + 3 * D_W
ODD_MIX = C_W + D_W

kernel_name = 'hybrid_dit_prefix_trunk'


def rms_norm(x, g):
    xf = x.astype(jnp.float32)
    y = xf * lax.rsqrt(jnp.mean(xf * xf, axis=-1, keepdims=True) + EPS)
    return (y * g.astype(jnp.float32)).astype(x.dtype)


def norm_mod(h, g, shift, scale):
    return rms_norm(h, g) * (1.0 + scale) + shift


def swiglu(x, w_gu, w_down):
    gt, up = jnp.split(x @ w_gu, 2, axis=-1)
    return (jax.nn.silu(gt) * up) @ w_down


def half_ffn(h, g, shift, scale, gate, w_gu, w_down):
    return h + 0.5 * gate * swiglu(norm_mod(h, g, shift, scale), w_gu, w_down)


def axial_rope_angles(s):
    t = jnp.arange(s)
    row = (t // GRID_W).astype(jnp.float32)
    col = (t % GRID_W).astype(jnp.float32)
    m = HEAD_DIM // 4
    inv = 1.0 / (ROPE_THETA ** (jnp.arange(m, dtype=jnp.float32) / m))
    return row[:, None] * inv[None, :], col[:, None] * inv[None, :]


def rotate(x, ang):
    x1, x2 = jnp.split(x, 2, axis=-1)
    cos = jnp.cos(ang)[None, :, None, :]
    sin = jnp.sin(ang)[None, :, None, :]
    return jnp.concatenate([x1 * cos - x2 * sin, x2 * cos + x1 * sin], axis=-1)


def apply_rope_2d(x, ang_r, ang_c):
    half = HEAD_DIM // 2
    return jnp.concatenate([rotate(x[..., :half], ang_r), rotate(x[..., half:], ang_c)], axis=-1).astype(x.dtype)


def ctx_attention(q, k, v, sink):
    b, l, nq, dh = q.shape
    nkv = k.shape[2]
    g = nq // nkv
    qg = q.reshape(b, l, nkv, g, dh)
    sc = jnp.einsum('bqkgd,bskd->bkgqs', qg, k).astype(jnp.float32) * dh ** -0.5
    if sink is not None:
        s_sink = jnp.broadcast_to(sink.astype(jnp.float32).reshape(nkv, g)[None, :, :, None, None], sc.shape[:-1] + (1,))
        sc = jnp.concatenate([sc, s_sink], axis=-1)
    p = jax.nn.softmax(sc, axis=-1)[..., :l].astype(v.dtype)
    o = jnp.einsum('bkgqs,bskd->bqkgd', p, v)
    return o.reshape(b, l, nq * dh)


def window_gqa(q, k, v, k_ctx, v_ctx, sink):
    b, s, nq, dh = q.shape
    nkv = k.shape[2]
    g = nq // nkv
    nb = s // A_BLOCK
    lc = k_ctx.shape[1]
    nloc = 3 * A_BLOCK
    qb = q.reshape(b, nb, A_BLOCK, nkv, g, dh)

    def band(t):
        tb = t.reshape(b, nb, A_BLOCK, nkv, dh)
        pad = jnp.zeros_like(tb[:, :1])
        prev = jnp.concatenate([pad, tb[:, :-1]], axis=1)
        nxt = jnp.concatenate([tb[:, 1:], pad], axis=1)
        return jnp.concatenate([prev, tb, nxt], axis=2)

    kb, vb = band(k), band(v)
    scale = dh ** -0.5
    qpos = jnp.arange(nb)[:, None, None] * A_BLOCK + jnp.arange(A_BLOCK)[None, :, None]
    kpos = (jnp.arange(nb)[:, None, None] - 1) * A_BLOCK + jnp.arange(nloc)[None, None, :]
    valid = (jnp.abs(qpos - kpos) <= WINDOW) & (kpos >= 0) & (kpos < s)
    s_loc = jnp.einsum('bnqkgd,bnskd->bnkgqs', qb, kb).astype(jnp.float32) * scale
    s_loc = jnp.where(valid[None, :, None, None], s_loc, NEG)
    s_ctx = jnp.einsum('bnqkgd,bckd->bnkgqc', qb, k_ctx).astype(jnp.float32) * scale
    s_sink = jnp.broadcast_to(sink.astype(jnp.float32).reshape(nkv, g)[None, None, :, :, None, None], s_loc.shape[:-1] + (1,))
    p = jax.nn.softmax(jnp.concatenate([s_loc, s_ctx, s_sink], axis=-1), axis=-1).astype(v.dtype)
    o = (jnp.einsum('bnkgqs,bnskd->bnqkgd', p[..., :nloc], vb)
         + jnp.einsum('bnkgqc,bckd->bnqkgd', p[..., nloc:nloc + lc], v_ctx))
    return o.reshape(b, s, nq * dh)


def chunk_gmlp(bu, bv, v_gain, ws, bias):
    b, s, _ = bu.shape
    u = jax.nn.gelu(bu)
    vv = jax.nn.gelu(bv).reshape(b, s // B_CHUNK, B_CHUNK, B_GROUPS, HEAD_DIM)
    vv = rms_norm(vv, v_gain.reshape(B_GROUPS, HEAD_DIM))
    mixed = jnp.einsum('gij,bnjgd->bnigd', ws, vv) + bias.T[None, None, :, :, None]
    return u * mixed.reshape(b, s, B_W)


def multiscale_pool(xp, w_pool, c_scale):
    b, s, _ = xp.shape
    xf = xp.astype(jnp.float32)
    cs = jnp.concatenate([jnp.zeros_like(xf[:, :1]), jnp.cumsum(xf, axis=1)], axis=1)
    t = jnp.arange(s)
    outs = []
    for gi, w in enumerate(POOL_WINDOWS):
        lo = jnp.clip(t - w // 2, 0, s)
        hi = jnp.clip(t + w - w // 2, 0, s)
        csg = cs[..., gi * C_GROUP_W:(gi + 1) * C_GROUP_W]
        mean = (csg[:, hi] - csg[:, lo]) / (hi - lo).astype(jnp.float32)[None, :, None]
        outs.append(mean - xf[..., gi * C_GROUP_W:(gi + 1) * C_GROUP_W])
    pooled = jnp.stack(outs, axis=2).astype(xp.dtype)
    y = jnp.einsum('bsgc,gcd->bsgd', pooled, w_pool).reshape(b, s, C_W)
    return y * c_scale


def neighbourhood_attn(q, k, v, k_ctx, v_ctx, rpb):
    b, s, nh, dh = q.shape
    rows = s // GRID_W
    kr = min(NA_ROWS, rows)
    nqb = GRID_W // NA_COLS
    kbw = 2 * NA_COLS
    q5 = q.reshape(b, rows, GRID_W, nh, dh)
    k5 = k.reshape(b, rows, GRID_W, nh, dh)
    v5 = v.reshape(b, rows, GRID_W, nh, dh)
    qcol = np.arange(GRID_W).reshape(nqb, NA_COLS)
    qstart = np.clip(qcol - NA_COLS // 2, 0, GRID_W - NA_COLS)
    kstart = np.clip(np.arange(nqb) * NA_COLS - NA_COLS // 2, 0, GRID_W - kbw)
    kcol = kstart[:, None] + np.arange(kbw)[None, :]
    col_valid = (kcol[:, None, :] >= qstart[:, :, None]) & (kcol[:, None, :] < qstart[:, :, None] + NA_COLS)
    col_idx = np.clip(kcol[:, None, :] - qcol[:, :, None], 1 - NA_COLS, NA_COLS - 1) + NA_COLS - 1
    scale = dh ** -0.5
    nloc = kr * kbw

    def row_block(r):
        r0 = jnp.clip(r - kr // 2, 0, rows - kr)
        qr = lax.dynamic_index_in_dim(q5, r, axis=1, keepdims=False).reshape(b, nqb, NA_COLS, nh, dh)
        kblk = lax.dynamic_slice_in_dim(k5, r0, kr, axis=1)[:, :, kcol]
        vblk = lax.dynamic_slice_in_dim(v5, r0, kr, axis=1)[:, :, kcol]
        row_idx = r0 + jnp.arange(kr) - r + NA_ROWS - 1
        bias = jnp.transpose(rpb[:, row_idx][:, :, col_idx], (0, 2, 3, 1, 4))
        s_loc = jnp.einsum('bmqhd,brmkhd->bhmqrk', qr, kblk).astype(jnp.float32) * scale + bias[None].astype(jnp.float32)
        s_loc = jnp.where(col_valid[None, None, :, :, None, :], s_loc, NEG)
        s_ctx = jnp.einsum('bmqhd,bchd->bhmqc', qr, k_ctx).astype(jnp.float32) * scale
        p = jax.nn.softmax(jnp.concatenate([s_loc.reshape(b, nh, nqb, NA_COLS, nloc), s_ctx], axis=-1), axis=-1).astype(v.dtype)
        o = (jnp.einsum('bhmqrk,brmkhd->bmqhd', p[..., :nloc].reshape(s_loc.shape), vblk)
             + jnp.einsum('bhmqc,bchd->bmqhd', p[..., nloc:], v_ctx))
        return o.reshape(b, GRID_W, nh, dh)

    out = lax.map(row_block, jnp.arange(rows))
    return jnp.moveaxis(out, 0, 1).reshape(b, s, nh * dh)


def even_mixer(z, zc, w_in, w_out, q_gain, k_gain, sink, v_gain, ws, bias, ang_r, ang_c, need_ctx):
    b, s, _ = z.shape
    l = zc.shape[1]
    q, k, v, bu, bv = jnp.split(z @ w_in, EVEN_SPLIT, axis=-1)
    q = apply_rope_2d(rms_norm(q.reshape(b, s, A_HEADS, HEAD_DIM), q_gain), ang_r, ang_c)
    k = apply_rope_2d(rms_norm(k.reshape(b, s, A_KV_HEADS, HEAD_DIM), k_gain), ang_r, ang_c)
    v = v.reshape(b, s, A_KV_HEADS, HEAD_DIM)
    if need_ctx:
        qc, kc, vc, buc, bvc = jnp.split(zc @ w_in, EVEN_SPLIT, axis=-1)
    else:
        kc, vc = jnp.split(zc @ w_in[:, A_W:A_W + 2 * A_KV], 2, axis=-1)
    kc = rms_norm(kc.reshape(b, l, A_KV_HEADS, HEAD_DIM), k_gain)
    vc = vc.reshape(b, l, A_KV_HEADS, HEAD_DIM)
    y = jnp.concatenate([window_gqa(q, k, v, kc, vc, sink), chunk_gmlp(bu, bv, v_gain, ws, bias)], axis=-1) @ w_out
    yc = None
    if need_ctx:
        qc = rms_norm(qc.reshape(b, l, A_HEADS, HEAD_DIM), q_gain)
        yc = jnp.concatenate([ctx_attention(qc, kc, vc, sink), chunk_gmlp(buc, bvc, v_gain, ws, bias)], axis=-1) @ w_out
    return y, yc


def odd_mixer(z, zc, w_in, w_out, w_pool, c_scale, q_gain, k_gain, rpb, need_ctx):
    b, s, _ = z.shape
    l = zc.shape[1]
    xp, q, k, v = jnp.split(z @ w_in, ODD_SPLIT, axis=-1)
    q = rms_norm(q.reshape(b, s, D_HEADS, HEAD_DIM), q_gain)
    k = rms_norm(k.reshape(b, s, D_HEADS, HEAD_DIM), k_gain)
    v = v.reshape(b, s, D_HEADS, HEAD_DIM)
    if need_ctx:
        xpc, qc, kc, vc = jnp.split(zc @ w_in, ODD_SPLIT, axis=-1)
    else:
        kc, vc = jnp.split(zc @ w_in[:, C_W + D_W:], 2, axis=-1)
    kc = rms_norm(kc.reshape(b, l, D_HEADS, HEAD_DIM), k_gain)
    vc = vc.reshape(b, l, D_HEADS, HEAD_DIM)
    y = jnp.concatenate([multiscale_pool(xp, w_pool, c_scale), neighbourhood_attn(q, k, v, kc, vc, rpb)], axis=-1) @ w_out
    yc = None
    if need_ctx:
        qc = rms_norm(qc.reshape(b, l, D_HEADS, HEAD_DIM), q_gain)
        yc = jnp.concatenate([multiscale_pool(xpc, w_pool, c_scale), ctx_attention(qc, kc, vc, None)], axis=-1) @ w_out
    return y, yc


def setup_inputs(seed: int = 0) -> dict:
    key = jax.random.key(seed)
    ks = jax.random.split(key, 24)
    f32 = jnp.float32

    def nrm(k, shape, sd):
        return jax.random.normal(k, shape, f32) * sd

    d = D_MODEL
    return {
        'x': nrm(ks[0], (BATCH, SEQ, d), 1.0),
        'c': nrm(ks[1], (BATCH, d), 1.0),
        'ctx': nrm(ks[2], (BATCH, CTX_LEN, d), 1.0),
        'c_ctx': nrm(ks[3], (d,), 1.0),
        'ada_w': nrm(ks[4], (DEPTH, d, N_MOD * d), 0.5 * d ** -0.5),
        'ada_b': nrm(ks[5], (DEPTH, N_MOD * d), 0.02),
        'norm_g': 1.0 + nrm(ks[6], (DEPTH, 3, d), 0.05),
        'ffn_w_gu': nrm(ks[7], (DEPTH, 2, d, 2 * D_FF), d ** -0.5),
        'ffn_w_down': nrm(ks[8], (DEPTH, 2, D_FF, d), D_FF ** -0.5),
        'ev_w_in': nrm(ks[9], (N_EVEN, d, EVEN_IN), d ** -0.5),
        'ev_w_out': nrm(ks[10], (N_EVEN, EVEN_MIX, d), EVEN_MIX ** -0.5),
        'a_q_gain': 1.0 + nrm(ks[11], (N_EVEN, HEAD_DIM), 0.05),
        'a_k_gain': 1.0 + nrm(ks[12], (N_EVEN, HEAD_DIM), 0.05),
        'a_sink': nrm(ks[13], (N_EVEN, A_HEADS), 0.5),
        'b_v_gain': 1.0 + nrm(ks[14], (N_EVEN, B_W), 0.05),
        'b_ws': nrm(ks[15], (N_EVEN, B_GROUPS, B_CHUNK, B_CHUNK), B_CHUNK ** -0.5),
        'b_bias': 1.0 + nrm(ks[16], (N_EVEN, B_GROUPS, B_CHUNK), 0.1),
        'od_w_in': nrm(ks[17], (N_ODD, d, ODD_IN), d ** -0.5),
        'od_w_out': nrm(ks[18], (N_ODD, ODD_MIX, d), ODD_MIX ** -0.5),
        'c_w_pool': nrm(ks[19], (N_ODD, C_GROUPS, C_GROUP_W, C_GROUP_W), C_GROUP_W ** -0.5),
        'c_scale': 1.0 + nrm(ks[20], (N_ODD, C_W), 0.1),
        'd_q_gain': 1.0 + nrm(ks[21], (N_ODD, HEAD_DIM), 0.05),
        'd_k_gain': 1.0 + nrm(ks[22], (N_ODD, HEAD_DIM), 0.05),
        'd_rpb': nrm(ks[23], (N_ODD, D_HEADS, 2 * NA_ROWS - 1, 2 * NA_COLS - 1), 0.5),
    }


def reference(x, c, ctx, c_ctx, ada_w, ada_b, norm_g, ffn_w_gu, ffn_w_down,
              ev_w_in, ev_w_out, a_q_gain, a_k_gain, a_sink, b_v_gain, b_ws, b_bias,
              od_w_in, od_w_out, c_w_pool, c_scale, d_q_gain, d_k_gain, d_rpb):
    ang_r, ang_c = axial_rope_angles(x.shape[1])
    sc = jax.nn.silu(c)
    scc = jax.nn.silu(c_ctx)
    h, hc = x, ctx
    for li in range(DEPTH):
        need_ctx = li < DEPTH - 1
        m = jnp.split((sc @ ada_w[li] + ada_b[li])[:, None, :], N_MOD, axis=-1)
        mc = jnp.split(scc @ ada_w[li] + ada_b[li], N_MOD, axis=-1)
        h = half_ffn(h, norm_g[li, 0], m[0], m[1], m[2], ffn_w_gu[li, 0], ffn_w_down[li, 0])
        hc = half_ffn(hc, norm_g[li, 0], mc[0], mc[1], mc[2], ffn_w_gu[li, 0], ffn_w_down[li, 0])
        z = norm_mod(h, norm_g[li, 1], m[3], m[4])
        zc = norm_mod(hc, norm_g[li, 1], mc[3], mc[4])
        if li % 2 == 0:
            e = li // 2
            y, yc = even_mixer(z, zc, ev_w_in[e], ev_w_out[e], a_q_gain[e], a_k_gain[e], a_sink[e],
                               b_v_gain[e], b_ws[e], b_bias[e], ang_r, ang_c, need_ctx)
        else:
            o = li // 2
            y, yc = odd_mixer(z, zc, od_w_in[o], od_w_out[o], c_w_pool[o], c_scale[o],
                              d_q_gain[o], d_k_gain[o], d_rpb[o], need_ctx)
        h = h + m[5] * y
        h = half_ffn(h, norm_g[li, 2], m[6], m[7], m[8], ffn_w_gu[li, 1], ffn_w_down[li, 1])
        if need_ctx:
            hc = hc + mc[5] * yc
            hc = half_ffn(hc, norm_g[li, 2], mc[6], mc[7], mc[8], ffn_w_gu[li, 1], ffn_w_down[li, 1])
    return h
```

```python
import contextlib
import numpy as np
import concourse.bass as bass
import concourse.mybir as mybir
from concourse.bass_utils import run_bass_kernel_spmd

F32 = mybir.dt.float32
BF16 = mybir.dt.bfloat16
AF = mybir.ActivationFunctionType
ALU = mybir.AluOpType
AX = mybir.AxisListType

D = 1024
S_LAT = 4096
S_CTX = 256
NTL = 32
NT = 34
DFF = 2816
NFF = 22
EPS = 1e-6
NEGB = -30000.0

ENGS = ("sp", "act", "pool", "dve", "pe")
NDMA_SEM = 8


class Res:
    __slots__ = ("name", "last_w", "readers", "gen")

    def __init__(self, name):
        self.name = name
        self.last_w = None
        self.readers = {}
        self.gen = 0

    def bump(self):
        self.gen += 1
        return Ref(self, self.gen)


class Ref:
    __slots__ = ("phys", "gen")

    def __init__(self, phys, gen):
        self.phys = phys
        self.gen = gen


def _norm_res(lst):
    out = []
    for r in lst:
        if isinstance(r, Ref):
            assert r.gen == r.phys.gen, f"stale buffer reference {r.phys.name}"
            r = r.phys
        out.append(r)
    return out


def pipeline(make_gen, items, drain_before=None):
    active = []
    for it in items:
        if drain_before is not None and drain_before(it):
            while active:
                nxt = []
                for g in active:
                    try:
                        next(g)
                        nxt.append(g)
                    except StopIteration:
                        pass
                active = nxt
        nxt = []
        for g in active:
            try:
                next(g)
                nxt.append(g)
            except StopIteration:
                pass
        active = nxt
        g = make_gen(it)
        try:
            next(g)
            active.append(g)
        except StopIteration:
            pass
    while active:
        nxt = []
        for g in active:
            try:
                next(g)
                nxt.append(g)
            except StopIteration:
                pass
        active = nxt


class Op:
    __slots__ = ("eng", "fn", "deps", "dma", "signal", "sem", "val", "prewait")

    def __init__(self, eng, fn, dma):
        self.eng = eng
        self.fn = fn
        self.dma = dma
        self.deps = []
        self.signal = False
        self.sem = None
        self.val = 0
        self.prewait = None


class Sched:
    def __init__(self, nc):
        self.nc = nc
        self.ops = {e: [] for e in ENGS}
        self.bar = {}
        self.nres = 0

    def res(self, name=None):
        self.nres += 1
        return Res(name or f"r{self.nres}")

    def op(self, eng, fn, reads=(), writes=(), dma=False):
        reads = _norm_res(reads)
        writes = _norm_res(writes)
        o = Op(eng, fn, dma)
        deps = {}
        for r in reads:
            if r.last_w is not None:
                deps[id(r.last_w)] = r.last_w
        for r in writes:
            if r.last_w is not None:
                deps[id(r.last_w)] = r.last_w
            for rd in r.readers.values():
                if isinstance(rd, list):
                    for x in rd:
                        deps[id(x)] = x
                else:
                    deps[id(rd)] = rd
        for r in reads:
            if dma:
                r.readers.setdefault(("dma", eng), []).append(o)
            else:
                r.readers[eng] = o
        for r in writes:
            r.last_w = o
            r.readers = {}
        b = self.bar.pop(eng, None)
        if b:
            for x in b:
                deps[id(x)] = x
        dl = []
        for d in deps.values():
            if d is o:
                continue
            if (not dma) and (not d.dma) and d.eng == "pe" and eng == "pe":
                continue
            dl.append(d)
        o.deps = dl
        self.ops[eng].append(o)
        return o

    def dma(self, q, out, in_, reads=(), writes=(), **kw):
        return self.op(q, lambda e: e.dma_start(out=out, in_=in_, **kw), reads, writes, dma=True)

    def barrier(self):
        tails = []
        for e in ENGS:
            ops = self.ops[e]
            for o in reversed(ops):
                if not o.dma:
                    tails.append(o)
                    break
            cnt = 0
            for o in reversed(ops):
                if o.dma:
                    tails.append(o)
                    cnt += 1
                    if cnt >= NDMA_SEM:
                        break
        self.bar = {e: list(tails) for e in ENGS}

    def emit(self):
        nc = self.nc
        with contextlib.ExitStack() as st:
            csem = {e: st.enter_context(nc.semaphore(f"c_{e}")) for e in ENGS}
            dsem = {e: [st.enter_context(nc.semaphore(f"d_{e}{i}")) for i in range(NDMA_SEM)]
                    for e in ("sp", "act", "pool")}
            for e in ENGS:
                for o in self.ops[e]:
                    for d in o.deps:
                        d.signal = True
            self.stats = {}
            for e in ENGS:
                cnt = 0
                nd = 0
                for o in self.ops[e]:
                    if o.dma:
                        slot = nd % NDMA_SEM
                        o.sem = dsem[e][slot]
                        o.val = 16 * (nd // NDMA_SEM + 1)
                        if nd >= NDMA_SEM:
                            o.prewait = (dsem[e][slot], 16 * (nd // NDMA_SEM))
                        nd += 1
                    elif o.signal:
                        cnt += 1
                        o.sem = csem[e]
                        o.val = cnt
                self.stats[e] = (len(self.ops[e]), cnt, nd)
            block = st.enter_context(nc.Block())

            def run(eng_name, eng):
                seen = {}
                lastdma = {}
                for o in self.ops[eng_name]:
                    waits = []
                    if o.prewait is not None:
                        waits.append(o.prewait)
                    for d in o.deps:
                        waits.append((d.sem, d.val))
                    for sem, val in waits:
                        k = id(sem)
                        if seen.get(k, 0) >= val:
                            continue
                        seen[k] = val
                        eng.wait_ge(sem, val)
                    inst = o.fn(eng)
                    if o.dma:
                        inst.then_inc(o.sem, 16)
                        lastdma[id(o.sem)] = (o.sem, o.val)
                    elif o.signal:
                        inst.then_inc(o.sem, 1)
                for sem, val in lastdma.values():
                    if seen.get(id(sem), 0) < val:
                        eng.wait_ge(sem, val)

            @block.sync
            def _(e):
                run("sp", e)

            @block.scalar
            def _(e):
                run("act", e)

            @block.gpsimd
            def _(e):
                run("pool", e)

            @block.vector
            def _(e):
                run("dve", e)

            @block.tensor
            def _(e):
                run("pe", e)


class Arena:
    def __init__(self, nc, st, nbytes):
        self.nbytes = nbytes
        self.t = st.enter_context(nc.sbuf_tensor("arena", [128, nbytes // 2], BF16))
        self.off = 0
        self.base = 0

    def alloc(self, free, dtype, parts=128):
        free = list(free)
        n = int(np.prod(free))
        sz = n * (4 if dtype == F32 else 2)
        off = (self.off + 63) // 64 * 64
        assert off + sz <= self.nbytes, f"arena overflow {off + sz} > {self.nbytes}"
        self.off = off + sz
        ap = self.t[0:parts, off // 2: (off + sz) // 2]
        if dtype == F32:
            ap = ap.bitcast(F32)
        if len(free) == 2:
            ap = ap.rearrange("p (a b) -> p a b", a=free[0])
        elif len(free) == 3:
            ap = ap.rearrange("p (a b c) -> p a b c", a=free[0], b=free[1])
        elif len(free) == 4:
            ap = ap.rearrange("p (a b c d) -> p a b c d", a=free[0], b=free[1], c=free[2])
        return ap

    def mark_persistent(self):
        self.base = self.off

    def reset(self):
        self.off = self.base


class Ring:
    def __init__(self, B, n, free, dtype, parts=128):
        self.bufs = [(B.A.alloc(free, dtype, parts), B.S.res()) for _ in range(n)]
        self.i = 0

    def next(self):
        r = self.bufs[self.i % len(self.bufs)]
        self.i += 1
        return r[0], r[1].bump()


class Builder:
    def __init__(self, nc, st, dbg):
        self.nc = nc
        self.st = st
        self.S = Sched(nc)
        self.A = Arena(nc, st, 206 * 1024)
        self.banks = []
        for i in range(8):
            t = st.enter_context(nc.psum_tensor(f"bank{i}", [128, 512], F32))
            self.banks.append((t, self.S.res(f"bank{i}")))
        self.bi = 0
        self.nrr = 8
        self.dbg = dbg

    def bank(self):
        r = self.banks[self.bi % self.nrr]
        self.bi += 1
        return r[0][:], r[1].bump()

    def bank_fixed(self, idx):
        r = self.banks[idx]
        return r[0][:], r[1].bump()

    def dma(self, q, out, in_, reads=(), writes=(), **kw):
        return self.S.dma(q, out, in_, reads, writes, **kw)

    def mm(self, out, lhsT, rhs, start, stop, reads, writes):
        return self.S.op("pe", lambda e: e.matmul(out, lhsT=lhsT, rhs=rhs, start=start, stop=stop), reads, writes)

    def tr(self, out, in_, reads, writes):
        idn = self.ident
        return self.S.op("pe", lambda e: e.transpose(out, in_, idn), list(reads) + [self.r_ident], writes)

    def act(self, out, in_, func, reads, writes, scale=None, bias=None, accum=None):
        kw = {}
        if scale is not None:
            kw["scale"] = scale
        if bias is not None:
            kw["bias"] = bias
        if accum is not None:
            kw["accum_out"] = accum
        return self.S.op("act", lambda e: e.activation(out=out, in_=in_, func=func, **kw), reads, writes)

    def tt(self, eng, out, in0, in1, op, reads, writes):
        return self.S.op(eng, lambda e: e.tensor_tensor(out=out, in0=in0, in1=in1, op=op), reads, writes)

    def ts(self, eng, out, in0, s1, s2, op0, op1, reads, writes):
        if op1 is None:
            return self.S.op(eng, lambda e: e.tensor_scalar(out=out, in0=in0, scalar1=s1, scalar2=None, op0=op0), reads, writes)
        return self.S.op(eng, lambda e: e.tensor_scalar(out=out, in0=in0, scalar1=s1, scalar2=s2, op0=op0, op1=op1), reads, writes)

    def stt(self, eng, out, in0, scalar, in1, op0, op1, reads, writes):
        return self.S.op(eng, lambda e: e.scalar_tensor_tensor(out=out, in0=in0, scalar=scalar, in1=in1, op0=op0, op1=op1), reads, writes)

    def cp(self, eng, out, in_, reads, writes):
        if eng == "act":
            return self.S.op("act", lambda e: e.activation(out=out, in_=in_, func=AF.Copy), reads, writes)
        return self.S.op(eng, lambda e: e.tensor_copy(out=out, in_=in_), reads, writes)

    def recip(self, out, in_, reads, writes):
        return self.S.op("dve", lambda e: e.reciprocal(out=out, in_=in_), reads, writes)

    def memset(self, eng, out, val, writes):
        return self.S.op(eng, lambda e: e.memset(out, val), [], writes)

    def reduce_sum(self, out, in_, reads, writes):
        return self.S.op("dve", lambda e: e.tensor_reduce(out=out, in_=in_, axis=AX.X, op=ALU.add), reads, writes)

    def rstd(self, st, r_st, w, inv_n):
        self.ts("dve", st[:, w:2 * w], st[:, 0:w], inv_n, EPS, ALU.mult, ALU.add, [r_st], [r_st])
        self.act(st[:, w:2 * w], st[:, w:2 * w], AF.Sqrt, [r_st], [r_st])
        self.recip(st[:, 2 * w:3 * w], st[:, w:2 * w], [r_st], [r_st])
        return st[:, 2 * w:3 * w]

    def phase_begin(self):
        self.S.barrier()
        self.A.reset()

    def load_bc(self, q, src_1xn, n, name=None):
        t = self.A.alloc([n], F32)
        r = self.S.res(name)
        self.dma(q, t, src_1xn.partition_broadcast(128), [], [r])
        return t, r


def build_program(dbg_phases=None, dbg=False):
    nc = bass.Bass("TRN2", target_bir_lowering=False)
    I = {}

    def inp(name, shape, dt=F32):
        I[name] = nc.dram_tensor(name, list(shape), dt, kind="ExternalInput").ap()
        return I[name]

    inp("x", [S_LAT, D]); inp("ctx", [S_CTX, D]); inp("c", [1, D]); inp("c_ctx", [1, D])
    inp("ada_w", [2, D, 9 * D]); inp("ada_b", [2, 9 * D]); inp("norm_g", [2, 3, D])
    inp("ffn_w_gu", [2, 2, D, 2 * DFF]); inp("ffn_w_down", [2, 2, DFF, D])
    inp("ev_w_in", [D, 1792]); inp("ev_w_out", [D, D])
    inp("a_q_gain", [1, 64]); inp("a_k_gain", [1, 64]); inp("a_sink", [1, 8])
    inp("b_v_gain", [1, 512]); inp("b_ws", [8, 128, 128]); inp("b_bias", [8, 128])
    inp("od_w_in", [D, 2048]); inp("od_w_out", [D, D])
    inp("c_w_pool", [4, 128, 128]); inp("c_scale", [1, 512])
    inp("d_q_gain", [1, 64]); inp("d_k_gain", [1, 64])
    inp("k_ident", [128, 128]); inp("k_rope", [S_LAT, 2, 64]); inp("k_amask", [128, 2, 128])
    inp("k_band", [128, 4, 5, 128]); inp("k_dbias", [5, 128, 8, 5, 128])
    out = nc.dram_tensor("out", [S_LAT, D], F32, kind="ExternalOutput").ap()
    skind = "ExternalOutput" if dbg else "Internal"

    def scr(name, shape, dt):
        return nc.dram_tensor(name, list(shape), dt, kind=skind).ap()

    hA = scr("hA", [NT * 128, D], F32)
    mod_d = scr("mod_d", [2, 2, 9 * D], F32)
    qT_d = scr("qT_d", [8, 64, NT * 128], BF16)
    kT_d = scr("kT_d", [8, 64, NT * 128], BF16)
    v_d = scr("v_d", [NT * 128, 512], BF16)
    u_d = scr("u_d", [NT * 128, 512], F32)
    vv_d = scr("vv_d", [NT * 128, 512], BF16)
    xp_d = scr("xp_d", [S_LAT, 512], BF16)

    st = contextlib.ExitStack()
    with st:
        B = Builder(nc, st, dbg)
        S, A = B.S, B.A
        B.ident = A.alloc([128], BF16)
        B.r_ident = S.res("ident")
        B.dma("pool", B.ident, I["k_ident"][:, :], [], [B.r_ident])
        B.identf = A.alloc([128], F32)
        B.dma("sp", B.identf, I["k_ident"][:, :], [], [B.r_ident])
        A.mark_persistent()

        def src0(gt):
            if gt < NTL:
                return I["x"][gt * 128:(gt + 1) * 128, :]
            return I["ctx"][(gt - NTL) * 128:(gt - NTL + 1) * 128, :]

        def hsrc(gt):
            return hA[gt * 128:(gt + 1) * 128, :]

        def osrc(gt):
            return out[gt * 128:(gt + 1) * 128, :]

        phases = []

        def phase_mod(li):
            B.phase_begin()
            cc = A.alloc([2, 128], F32, parts=8)
            r_cc = S.res()
            B.dma("sp", cc[:, 0, :], I["c"][0, :].rearrange("(k p) -> k p", p=128), [], [r_cc])
            B.dma("sp", cc[:, 1, :], I["c_ctx"][0, :].rearrange("(k p) -> k p", p=128), [], [r_cc])
            ccb = A.alloc([2, 128], BF16, parts=8)
            r_ccb = S.res()
            B.act(ccb, cc, AF.Silu, [r_cc], [r_ccb])
            cs = A.alloc([8, 2], BF16)
            r_cs = S.res()
            bk, r_bk = B.bank()
            bkb = bk.bitcast(BF16)
            for j in range(2):
                B.S.op("pe", lambda e, j=j: e.transpose(bkb[:, j * 8:(j + 1) * 8], ccb[:, j, :], B.ident[0:8, 0:8]), [r_ccb, B.r_ident], [r_bk])
            B.cp("dve", cs, bkb[:, 0:16].rearrange("p (j k) -> p k j", j=2), [], [r_bk, r_cs])
            adab = A.alloc([9 * D], F32, parts=2)
            r_adab = S.res()
            B.dma("sp", adab, I["ada_b"][li:li + 1, :].partition_broadcast(2), [], [r_adab])
            msb = A.alloc([9 * D], F32, parts=2)
            r_msb = S.res()
            wr = Ring(B, 3, [8, 512], BF16)
            for nb in range(18):
                w, r_w = wr.next()
                B.dma("pool", w, I["ada_w"][li, :, nb * 512:(nb + 1) * 512].rearrange("(k p) n -> p k n", p=128), [], [r_w])
                bk, r_bk = B.bank()
                for k in range(8):
                    B.mm(bk[0:2, :], cs[:, k, :], w[:, k, :], k == 0, k == 7, [r_cs, r_w], [r_bk])
                B.tt("dve", msb[:, nb * 512:(nb + 1) * 512], bk[0:2, :], adab[:, nb * 512:(nb + 1) * 512], ALU.add,
                     [r_adab], [r_bk, r_msb])
            B.dma("sp", mod_d[li], msb, [r_msb], [])

        def load_mod_vecs(li, j_shift, j_scale, j_gate, gi, gate_mul, ntypes=2):
            outl = []
            gbc, r_g = (None, None)
            if j_scale is not None:
                gbc, r_g = B.load_bc("sp", I["norm_g"][li, gi:gi + 1, :], D)
            for ty in range(ntypes):
                r = S.res()
                sh = Gm = gt_ = None
                if j_shift is not None:
                    sh = A.alloc([D], F32)
                    B.dma("sp", sh, mod_d[li, ty:ty + 1, j_shift * D:(j_shift + 1) * D].partition_broadcast(128), [], [r])
                if j_scale is not None:
                    Gm = A.alloc([D], F32)
                    B.dma("sp", Gm, mod_d[li, ty:ty + 1, j_scale * D:(j_scale + 1) * D].partition_broadcast(128), [], [r])
                    B.stt("dve", Gm, Gm, 1.0, gbc, ALU.add, ALU.mult, [r_g, r], [r])
                if j_gate is not None:
                    gt_ = A.alloc([D], F32)
                    B.dma("sp", gt_, mod_d[li, ty:ty + 1, j_gate * D:(j_gate + 1) * D].partition_broadcast(128), [], [r])
                    if gate_mul != 1.0:
                        B.ts("dve", gt_, gt_, gate_mul, None, ALU.mult, None, [r], [r])
                outl.append((sh, Gm, gt_, r))
            return outl

        def norm_tile(hin, r_hin, mv, rings):
            sh, Gm, _, r_mv = mv
            sqj, r_sqj = rings["sqj"]
            st_, r_st = rings["st"].next()
            B.memset("dve", st_[:, 0:1], 0.0, [r_st])
            B.act(sqj, hin, AF.Square, [r_hin, r_st], [r_sqj, r_st], accum=st_[:, 0:1])
            rs = B.rstd(st_, r_st, 1, 1.0 / D)
            z1, r_z1 = rings["z1"].next()
            B.stt("dve", z1, hin, rs, Gm, ALU.mult, ALU.mult, [r_hin, r_st, r_mv], [r_z1])
            zt, r_zt = rings["ztok"].next()
            B.tt("pool", zt, z1, sh, ALU.add, [r_z1, r_mv], [r_zt])
            return zt, r_zt

        def transpose8(src, r_src, dst3, r_dst, eng="act"):
            bk, r_bk = B.bank()
            bkb = bk.bitcast(BF16)
            for k in range(8):
                B.tr(bkb[:, k * 128:(k + 1) * 128], src[:, k * 128:(k + 1) * 128], [r_src], [r_bk])
            B.cp(eng, dst3, bkb.rearrange("p (k t) -> p k t", k=8), [], [r_bk, r_dst])

        def phase_ffn(li, which, ntiles, srcf, dstf):
            B.phase_begin()
            j0 = 0 if which == 0 else 6
            gi = 0 if which == 0 else 2
            mvs = load_mod_vecs(li, j0, j0 + 1, j0 + 2, gi, 0.5, ntypes=2 if ntiles > NTL else 1)
            wd = A.alloc([NFF, D], BF16)
            r_wd = S.res()
            for c4 in range(0, NFF, 2):
                B.dma("pool", wd[:, c4:c4 + 2, :],
                      I["ffn_w_down"][li, which, c4 * 128:(c4 + 2) * 128, :].rearrange("(c p) n -> p c n", p=128), [], [r_wd])
            if ntiles == NT:
                groups = [list(range(0, 7)), list(range(7, 14)), list(range(14, 21)), list(range(21, 28)), list(range(28, 34))]
            else:
                groups = [list(range(g * 8, g * 8 + 8)) for g in range(4)]
            GM = max(len(g) for g in groups)
            zTs = [A.alloc([8, GM * 128], BF16) for _ in range(2)]
            zress = [[S.res() for _ in range(GM)] for _ in range(2)]
            actT = A.alloc([NFF, GM * 128], BF16)
            wgr = Ring(B, 2, [8, 2, 256], BF16)
            hinr = Ring(B, 2, [D], F32)
            rings = {"sqj": (A.alloc([D], BF16), S.res()), "st": Ring(B, 2, [3], F32),
                     "z1": Ring(B, 1, [D], F32), "ztok": Ring(B, 2, [D], BF16)}
            stmpr = Ring(B, 2, [512], F32)
            etmpr = Ring(B, 1, [D], F32)
            houtr = Ring(B, 2, [D], F32)
            wgu = I["ffn_w_gu"][li, which]

            def stage1_tile(gidx, lt, gt):
                ty = 0 if gt < NTL else 1
                hin, r_hin = hinr.next()
                B.dma("sp", hin, srcf(gt), [], [r_hin])
                zt, r_zt = norm_tile(hin, r_hin, mvs[ty], rings)
                transpose8(zt, r_zt, zTs[gidx % 2][:, :, lt * 128:(lt + 1) * 128], zress[gidx % 2][lt])

            for lt, gt in enumerate(groups[0]):
                stage1_tile(0, lt, gt)
            for gidx, grp in enumerate(groups):
                zT = zTs[gidx % 2]
                zres = zress[gidx % 2]
                T = len(grp) * 128
                nblk = (T + 511) // 512
                bs = T // nblk
                blocks = [(i * bs, (i + 1) * bs if i < nblk - 1 else T) for i in range(nblk)]
                ares = [S.res() for _ in blocks]
                pending = list(enumerate(groups[gidx + 1])) if gidx + 1 < len(groups) else []
                npend = len(pending)
                nunits = NFF * nblk
                unit = 0
                emitted = 0
                for cp_ in range(NFF // 2):
                    wg, r_wg = wgr.next()
                    B.dma("pool", wg[:, :, 0, :], wgu[:, cp_ * 256:(cp_ + 1) * 256].rearrange("(k p) n -> p k n", p=128), [], [r_wg])
                    B.dma("pool", wg[:, :, 1, :], wgu[:, DFF + cp_ * 256:DFF + (cp_ + 1) * 256].rearrange("(k p) n -> p k n", p=128), [], [r_wg])
                    for ci in range(2):
                        c = cp_ * 2 + ci
                        for bi_, (a, b_) in enumerate(blocks):
                            n = b_ - a
                            zr = [zres[t] for t in range(a // 128, (b_ - 1) // 128 + 1)]
                            bg, r_bg = B.bank()
                            bu, r_bu = B.bank()
                            for k in range(8):
                                B.mm(bg[:, 0:n], wg[:, k, 0, ci * 128:(ci + 1) * 128], zT[:, k, a:b_], k == 0, k == 7, [r_wg] + zr, [r_bg])
                            for k in range(8):
                                B.mm(bu[:, 0:n], wg[:, k, 1, ci * 128:(ci + 1) * 128], zT[:, k, a:b_], k == 0, k == 7, [r_wg] + zr, [r_bu])
                            stp, r_stp = stmpr.next()
                            B.act(stp[:, 0:n], bg[:, 0:n], AF.Silu, [], [r_bg, r_stp])
                            B.tt("dve", actT[:, c, a:b_], stp[:, 0:n], bu[:, 0:n], ALU.mult, [r_stp], [r_bu, ares[bi_]])
                            unit += 1
                            while emitted < npend and unit * npend >= (emitted + 1) * int(nunits * 0.7):
                                lt2, gt2 = pending[emitted]
                                stage1_tile(gidx + 1, lt2, gt2)
                                emitted += 1
                while emitted < npend:
                    lt2, gt2 = pending[emitted]
                    stage1_tile(gidx + 1, lt2, gt2)
                    emitted += 1
                for lt, gt in enumerate(grp):
                    ty = 0 if gt < NTL else 1
                    gate = mvs[ty][2]
                    r_mv = mvs[ty][3]
                    ar = [ares[i] for i, (a, b_) in enumerate(blocks) if a < (lt + 1) * 128 and b_ > lt * 128]
                    b0, r_b0 = B.bank()
                    b1, r_b1 = B.bank()
                    bb = [(b0, r_b0), (b1, r_b1)]
                    for c in range(NFF):
                        for hf in range(2):
                            B.mm(bb[hf][0], actT[:, c, lt * 128:(lt + 1) * 128], wd[:, c, hf * 512:(hf + 1) * 512],
                                 c == 0, c == NFF - 1, ar + [r_wd], [bb[hf][1]])
                    hin, r_hin = hinr.next()
                    B.dma("sp", hin, srcf(gt), [], [r_hin])
                    et, r_et = etmpr.next()
                    for hf in range(2):
                        B.tt("dve", et[:, hf * 512:(hf + 1) * 512], bb[hf][0], gate[:, hf * 512:(hf + 1) * 512], ALU.mult,
                             [r_mv], [bb[hf][1], r_et])
                    ho, r_ho = houtr.next()
                    B.tt("pool", ho, et, hin, ALU.add, [r_et, r_hin], [r_ho])
                    B.dma("sp", dstf(gt), ho, [r_ho], [])

        def head_norm(bank_ap, r_bank, nh, gain_bc, r_gain, rings, name):
            sq, r_sq = rings["sq"].next()
            w = nh * 64
            B.act(sq[:, 0:w], bank_ap, AF.Square, [], [r_bank, r_sq])
            st_, r_st = rings["st"].next()
            B.reduce_sum(st_[:, 0:nh], sq[:, 0:w].rearrange("p (h d) -> p h d", h=nh), [r_sq], [r_st])
            rs = B.rstd(st_, r_st, nh, 1.0 / 64)
            qn, r_qn = rings[name].next()
            qn3 = qn[:, 0:w].rearrange("p (h d) -> p h d", h=nh)
            B.tt("dve", qn3, bank_ap.rearrange("p (h d) -> p h d", h=nh), rs.unsqueeze(2).to_broadcast([128, nh, 64]), ALU.mult,
                 [r_st], [r_bank, r_qn])
            B.tt("pool", qn3, qn3, gain_bc.unsqueeze(1).to_broadcast([128, nh, 64]), ALU.mult, [r_gain, r_qn], [r_qn])
            return qn, r_qn

        def rope(qn, r_qn, nh, ropt, r_ropt, outb, r_out, rings):
            w = nh * 64
            a_, r_a = rings["ra"].next()
            b_, r_b = rings["rb"].next()
            q3 = qn[:, 0:w].rearrange("p (h d) -> p h d", h=nh)
            a3 = a_[:, 0:w].rearrange("p (h d) -> p h d", h=nh)
            B.tt("pool", a3, q3, ropt[:, 0, :].unsqueeze(1).to_broadcast([128, nh, 64]), ALU.mult, [r_qn, r_ropt], [r_a])
            q5 = qn[:, 0:w].rearrange("p (h a s d) -> p h a s d", h=nh, a=2, s=2)
            b5 = b_[:, 0:w].rearrange("p (h a s d) -> p h a s d", h=nh, a=2, s=2)
            s4 = ropt[:, 1, :].rearrange("p (a s d) -> p a s d", a=2, s=2)
            for ax in range(2):
                for s in range(2):
                    B.tt("dve", b5[:, :, ax, s, :], q5[:, :, ax, 1 - s, :],
                         s4[:, ax, s, :].unsqueeze(1).to_broadcast([128, nh, 16]), ALU.mult, [r_qn, r_ropt], [r_b])
            B.tt("dve", outb[:, 0:w], a_[:, 0:w], b_[:, 0:w], ALU.add, [r_a, r_b], [r_out])

        def head_transposes(src, r_src, nh, dst_dram, gt, rings):
            bk, r_bk = B.bank()
            bkb = bk.bitcast(BF16)
            for h in range(nh):
                B.tr(bkb[0:64, h * 128:(h + 1) * 128], src[:, h * 64:(h + 1) * 64], [r_src], [r_bk])
            ts_, r_ts = rings["hT"].next()
            B.cp("act", ts_[:, 0:nh, :], bkb[0:64, 0:nh * 128].rearrange("p (h t) -> p h t", h=nh), [], [r_bk, r_ts])
            B.dma("sp", dst_dram.rearrange("h d t -> d h t")[:, 0:nh, gt * 128:(gt + 1) * 128], ts_[:, 0:nh, :], [r_ts], [])

        def phase_even_prep(li):
            B.phase_begin()
            mvs = load_mod_vecs(li, 3, 4, None, 1, 1.0)
            win = A.alloc([8, 1792], BF16)
            r_win = S.res()
            B.dma("pool", win, I["ev_w_in"].rearrange("(k p) n -> p k n", p=128), [], [r_win])
            qg_bc, r_qg = B.load_bc("sp", I["a_q_gain"][0:1, :], 64)
            kg_bc, r_kg = B.load_bc("sp", I["a_k_gain"][0:1, :], 64)
            vg_bc, r_vg = B.load_bc("sp", I["b_v_gain"][0:1, :], 512)
            hinr = Ring(B, 3, [D], F32)
            rings = {"sqj": (A.alloc([D], BF16), S.res()), "st": Ring(B, 6, [30], F32),
                     "z1": Ring(B, 2, [D], F32), "ztok": Ring(B, 4, [D], BF16),
                     "sq": Ring(B, 3, [512], F32), "qn": Ring(B, 2, [512], F32), "kn": Ring(B, 2, [128], F32),
                     "gv": Ring(B, 2, [512], F32),
                     "ra": Ring(B, 2, [512], F32), "rb": Ring(B, 2, [512], F32), "hT": Ring(B, 3, [8, 128], BF16, parts=64)}
            zTr = Ring(B, 3, [8, 128], BF16)
            ropr = Ring(B, 5, [2, 64], F32)
            qbr = Ring(B, 4, [512], BF16)
            kbr = Ring(B, 4, [128], BF16)
            vbr = Ring(B, 3, [128], BF16)
            ur = Ring(B, 3, [512], F32)
            vvr = Ring(B, 3, [512], BF16)
            nsl = [(0, 512), (512, 768), (768, 1280), (1280, 1792)]

            def tile_gen(gt):
                ty = 0 if gt < NTL else 1
                hin, r_hin = hinr.next()
                B.dma("sp", hin, hsrc(gt), [], [r_hin])
                if ty == 0:
                    rt, r_rt = ropr.next()
                    B.dma("sp", rt, I["k_rope"][gt * 128:(gt + 1) * 128, :, :], [], [r_rt])
                zt, r_zt = norm_tile(hin, r_hin, mvs[ty], rings)
                yield
                zT, r_zT = zTr.next()
                transpose8(zt, r_zt, zT, r_zT)
                bks = [B.bank() for _ in range(4)]
                for k in range(8):
                    for i, (n0, n1) in enumerate(nsl):
                        B.mm(bks[i][0][:, 0:n1 - n0], zT[:, k, :], win[:, k, n0:n1], k == 0, k == 7, [r_zT, r_win], [bks[i][1]])
                (bq, r_bq), (bkv, r_bkv), (bbu, r_bbu), (bbv, r_bbv) = bks
                yield
                qn, r_qn = head_norm(bq, r_bq, 8, qg_bc, r_qg, rings, "qn")
                kn, r_kn = head_norm(bkv[:, 0:128], r_bkv, 2, kg_bc, r_kg, rings, "kn")
                qb, r_qb = qbr.next()
                kb, r_kb = kbr.next()
                if ty == 0:
                    rope(qn, r_qn, 8, rt, r_rt, qb, r_qb, rings)
                    rope(kn, r_kn, 2, rt, r_rt, kb, r_kb, rings)
                else:
                    B.cp("dve", qb, qn, [r_qn], [r_qb])
                    B.cp("dve", kb, kn[:, 0:128], [r_kn], [r_kb])
                vb, r_vb = vbr.next()
                B.cp("act", vb, bkv[:, 128:256], [], [r_bkv, r_vb])
                B.dma("sp", v_d[gt * 128:(gt + 1) * 128, 0:128], vb, [r_vb], [])
                u_, r_u = ur.next()
                B.act(u_, bbu, AF.Gelu_apprx_tanh, [], [r_bbu, r_u])
                B.dma("sp", u_d[gt * 128:(gt + 1) * 128, :], u_, [r_u], [])
                gv, r_gv = rings["gv"].next()
                B.act(gv, bbv, AF.Gelu_apprx_tanh, [], [r_bbv, r_gv])
                sq, r_sq = rings["sq"].next()
                B.act(sq, gv, AF.Square, [r_gv], [r_sq])
                st_, r_st = rings["st"].next()
                B.reduce_sum(st_[:, 0:8], sq.rearrange("p (h d) -> p h d", h=8), [r_sq], [r_st])
                rs = B.rstd(st_, r_st, 8, 1.0 / 64)
                gv3 = gv.rearrange("p (h d) -> p h d", h=8)
                B.tt("dve", gv3, gv3, rs.unsqueeze(2).to_broadcast([128, 8, 64]), ALU.mult, [r_st, r_gv], [r_gv])
                vv, r_vv = vvr.next()
                B.tt("pool", vv, gv, vg_bc, ALU.mult, [r_gv, r_vg], [r_vv])
                B.dma("sp", vv_d[gt * 128:(gt + 1) * 128, :], vv, [r_vv], [])
                yield
                head_transposes(qb, r_qb, 8, qT_d, gt, rings)
                head_transposes(kb, r_kb, 2, kT_d, gt, rings)

            pipeline(tile_gen, range(NT))

        def outproj_residual(mix, r_mix, wout, r_wout, gate, r_gate, gt, rings):
            mT, r_mT = rings["mixT"].next()
            transpose8(mix, r_mix, mT, r_mT)
            b0, r_b0 = B.bank()
            b1, r_b1 = B.bank()
            bb = [(b0, r_b0), (b1, r_b1)]
            for k in range(8):
                for hf in range(2):
                    B.mm(bb[hf][0], mT[:, k, :], wout[:, k, hf * 512:(hf + 1) * 512], k == 0, k == 7, [r_mT, r_wout], [bb[hf][1]])
            hin, r_hin = rings["hin"].next()
            B.dma("sp", hin, hsrc(gt), [], [r_hin])
            et, r_et = rings["et"].next()
            for hf in range(2):
                B.tt("dve", et[:, hf * 512:(hf + 1) * 512], bb[hf][0], gate[:, hf * 512:(hf + 1) * 512], ALU.mult,
                     [r_gate], [bb[hf][1], r_et])
            ho, r_ho = rings["hout"].next()
            B.tt("pool", ho, et, hin, ALU.add, [r_et, r_hin], [r_ho])
            B.dma("sp", hsrc(gt), ho, [r_ho], [])

        def load_V(dst, r_dst, kt0, nw, nk):
            for w in range(nw):
                B.dma("sp", dst[:, w, :, 0:64],
                      v_d[(kt0 + w) * 128:(kt0 + w + 1) * 128, 0:nk * 64].rearrange("p (k d) -> p k d", k=nk), [], [r_dst])

        def phase_even_attn(li):
            B.phase_begin()
            mvs = load_mod_vecs(li, None, None, 5, 1, 1.0)
            wout = A.alloc([8, D], BF16)
            r_wout = S.res()
            B.dma("pool", wout, I["ev_w_out"].rearrange("(k p) n -> p k n", p=128), [], [r_wout])
            wsn = A.alloc([8, 128], BF16)
            r_wsn = S.res()
            B.dma("pool", wsn, I["b_ws"].rearrange("g i j -> i g j"), [], [r_wsn])
            wsT = A.alloc([8, 128], BF16)
            r_wsT = S.res()
            bk, r_bk = B.bank()
            bkb = bk.bitcast(BF16)
            for g in range(8):
                B.tr(bkb[:, g * 128:(g + 1) * 128], wsn[:, g, :], [r_wsn], [r_bk])
            B.cp("act", wsT, bkb.rearrange("p (g t) -> p g t", g=8), [], [r_bk, r_wsT])
            bias_sb = A.alloc([128], F32, parts=8)
            r_bsb = S.res()
            B.dma("sp", bias_sb, I["b_bias"][:, :], [], [r_bsb])
            biasT = A.alloc([8], F32)
            r_biasT = S.res()
            bk, r_bk = B.bank()
            B.mm(bk[:, 0:8], bias_sb, B.identf[0:8, 0:8], True, True, [r_bsb, B.r_ident], [r_bk])
            B.cp("dve", biasT, bk[:, 0:8], [], [r_bk, r_biasT])
            esink, r_es = B.load_bc("sp", I["a_sink"][0:1, :], 8)
            B.act(esink, esink, AF.Exp, [r_es], [r_es])
            amask = A.alloc([2, 128], BF16)
            r_am = S.res()
            B.dma("pool", amask, I["k_amask"][:, :, :], [], [r_am])
            kTc = A.alloc([2, 256], BF16, parts=64)
            r_kTc = S.res()
            B.dma("sp", kTc, kT_d.rearrange("h d t -> d h t")[:, 0:2, NTL * 128:NT * 128], [], [r_kTc])
            Vc = A.alloc([2, 2, 65], BF16)
            r_Vc = S.res()
            B.memset("dve", Vc[:, :, :, 64:65], 1.0, [r_Vc])
            load_V(Vc, r_Vc, NTL, 2, 2)
            kTwr = Ring(B, 4, [2, 384], BF16, parts=64)
            Vwr = Ring(B, 5, [3, 2, 65], BF16)
            for vb_, r_ in Vwr.bufs:
                B.memset("dve", vb_[:, :, :, 64:65], 1.0, [r_])
            qTr = Ring(B, 4, [8, 128], BF16, parts=64)
            pTr = Ring(B, 16, [512], BF16)
            etr = Ring(B, 2, [512], BF16)
            mixr = Ring(B, 4, [D], BF16)
            str_ = Ring(B, 4, [8], F32)
            vvr = Ring(B, 5, [512], BF16)
            ur = Ring(B, 5, [512], F32)
            btr = Ring(B, 2, [512], F32)
            rings = {"mixT": Ring(B, 2, [8, 128], BF16), "hin": Ring(B, 2, [D], F32), "et": Ring(B, 1, [D], F32),
                     "hout": Ring(B, 2, [D], F32)}

            def tile_gen(n):
                ty = 0 if n < NTL else 1
                qT, r_qT = qTr.next()
                B.dma("sp", qT, qT_d.rearrange("h d t -> d h t")[:, :, n * 128:(n + 1) * 128], [], [r_qT])
                keys = []
                if ty == 0:
                    kt0 = min(max(n - 1, 0), NTL - 3)
                    kTw, r_kTw = kTwr.next()
                    B.dma("sp", kTw, kT_d.rearrange("h d t -> d h t")[:, 0:2, kt0 * 128:(kt0 + 3) * 128], [], [r_kTw])
                    Vw, r_Vw = Vwr.next()
                    load_V(Vw, r_Vw, kt0, 3, 2)
                    for kt, mk in ((n - 1, 0), (n, None), (n + 1, 1)):
                        if 0 <= kt < NTL:
                            s_ = kt - kt0
                            keys.append((kTw, s_, Vw, s_, mk, [r_kTw], [r_Vw]))
                for s_ in range(2):
                    keys.append((kTc, s_, Vc, s_, None, [r_kTc], [r_Vc]))
                vv, r_vv = vvr.next()
                B.dma("sp", vv, vv_d[n * 128:(n + 1) * 128, :], [], [r_vv])
                u_, r_u = ur.next()
                B.dma("sp", u_, u_d[n * 128:(n + 1) * 128, :], [], [r_u])
                yield

                def qk(kv):
                    pts = []
                    for (kTa, ks, Va, vs, mk, rk, rv) in keys:
                        bk, r_bk = B.bank()
                        B.mm(bk.rearrange("p (h q) -> p h q", h=4), kTa[:, kv, ks * 128:(ks + 1) * 128], qT[:, 4 * kv:4 * kv + 4, :],
                             True, True, rk + [r_qT], [r_bk])
                        pT, r_pT = pTr.next()
                        if mk is None:
                            B.act(pT, bk, AF.Exp, [], [r_bk, r_pT], scale=0.125)
                        else:
                            et, r_et = etr.next()
                            B.act(et, bk, AF.Exp, [], [r_bk, r_et], scale=0.125)
                            B.tt("dve", pT.rearrange("p (h q) -> p h q", h=4), et.rearrange("p (h q) -> p h q", h=4),
                                 amask[:, mk, :].unsqueeze(1).to_broadcast([128, 4, 128]), ALU.mult, [r_et, r_am], [r_pT])
                        pts.append((pT, r_pT, Va, vs, rv))
                    return pts

                def pv(pkv, pts, mix, r_mix):
                    ob, r_ob = B.bank()
                    for hh in range(4):
                        for ei, (pT, r_pT, Va, vs, rv) in enumerate(pts):
                            B.mm(ob[:, hh * 65:(hh + 1) * 65], pT[:, hh * 128:(hh + 1) * 128], Va[:, vs, pkv, :],
                                 ei == 0, ei == len(pts) - 1, [r_pT] + rv, [r_ob])
                    ob3 = ob[:, 0:260].rearrange("p (h e) -> p h e", h=4)
                    sd, r_sd = str_.next()
                    B.tt("dve", sd[:, 0:4], ob3[:, :, 64], esink[:, 4 * pkv:4 * pkv + 4], ALU.add, [r_es], [r_ob, r_sd])
                    B.recip(sd[:, 4:8], sd[:, 0:4], [r_sd], [r_sd])
                    B.tt("dve", mix[:, pkv * 256:(pkv + 1) * 256].rearrange("p (h d) -> p h d", h=4), ob3[:, :, 0:64],
                         sd[:, 4:8].unsqueeze(2).to_broadcast([128, 4, 64]), ALU.mult, [r_sd], [r_ob, r_mix])

                pts0 = qk(0)
                yield
                pts1 = qk(1)
                mix, r_mix = mixr.next()
                pv(0, pts0, mix, r_mix)
                yield
                pv(1, pts1, mix, r_mix)
                bk, r_bk = B.bank()
                for g in range(8):
                    B.mm(bk[:, g * 64:(g + 1) * 64], wsT[:, g, :], vv[:, g * 64:(g + 1) * 64], True, True, [r_wsT, r_vv], [r_bk])
                bt, r_bt = btr.next()
                B.tt("dve", bt.rearrange("p (g d) -> p g d", g=8), bk.rearrange("p (g d) -> p g d", g=8),
                     biasT.unsqueeze(2).to_broadcast([128, 8, 64]), ALU.add, [r_biasT], [r_bk, r_bt])
                B.tt("pool", mix[:, 512:1024], bt, u_, ALU.mult, [r_bt, r_u], [r_mix])
                yield
                outproj_residual(mix, r_mix, wout, r_wout, mvs[ty][2], mvs[ty][3], n, rings)

            pipeline(tile_gen, range(NT))

        def phase_odd_prep(li):
            B.phase_begin()
            mvs = load_mod_vecs(li, 3, 4, None, 1, 1.0)
            win = A.alloc([8, 2048], BF16)
            r_win = S.res()
            for hf in range(2):
                B.dma("pool", win[:, :, hf * 1024:(hf + 1) * 1024],
                      I["od_w_in"][:, hf * 1024:(hf + 1) * 1024].rearrange("(k p) n -> p k n", p=128), [], [r_win])
            qg_bc, r_qg = B.load_bc("sp", I["d_q_gain"][0:1, :], 64)
            kg_bc, r_kg = B.load_bc("sp", I["d_k_gain"][0:1, :], 64)
            hinr = Ring(B, 3, [D], F32)
            rings = {"sqj": (A.alloc([D], BF16), S.res()), "st": Ring(B, 6, [30], F32),
                     "z1": Ring(B, 2, [D], F32), "ztok": Ring(B, 4, [D], BF16),
                     "sq": Ring(B, 3, [512], F32), "qn": Ring(B, 2, [512], F32), "kn": Ring(B, 2, [512], F32),
                     "hT": Ring(B, 3, [8, 128], BF16, parts=64)}
            zTr = Ring(B, 3, [8, 128], BF16)
            qbr = Ring(B, 4, [512], BF16)
            kbr = Ring(B, 4, [512], BF16)
            vbr = Ring(B, 3, [512], BF16)
            xbr = Ring(B, 3, [512], BF16)

            def tile_gen(gt):
                ty = 0 if gt < NTL else 1
                hin, r_hin = hinr.next()
                B.dma("sp", hin, hsrc(gt), [], [r_hin])
                zt, r_zt = norm_tile(hin, r_hin, mvs[ty], rings)
                yield
                zT, r_zT = zTr.next()
                transpose8(zt, r_zt, zT, r_zT)
                nbs = [0, 1, 2, 3] if ty == 0 else [2, 3]
                bks = {i: B.bank() for i in nbs}
                for k in range(8):
                    for i in nbs:
                        B.mm(bks[i][0], zT[:, k, :], win[:, k, i * 512:(i + 1) * 512], k == 0, k == 7, [r_zT, r_win], [bks[i][1]])
                yield
                qb = r_qb = None
                if ty == 0:
                    xb, r_xb = xbr.next()
                    B.cp("act", xb, bks[0][0], [], [bks[0][1], r_xb])
                    B.dma("sp", xp_d[gt * 128:(gt + 1) * 128, :], xb, [r_xb], [])
                    qn, r_qn = head_norm(bks[1][0], bks[1][1], 8, qg_bc, r_qg, rings, "qn")
                    qb, r_qb = qbr.next()
                    B.cp("dve", qb, qn, [r_qn], [r_qb])
                kn, r_kn = head_norm(bks[2][0], bks[2][1], 8, kg_bc, r_kg, rings, "kn")
                kb, r_kb = kbr.next()
                B.cp("dve", kb, kn, [r_kn], [r_kb])
                vb, r_vb = vbr.next()
                B.cp("act", vb, bks[3][0], [], [bks[3][1], r_vb])
                B.dma("sp", v_d[gt * 128:(gt + 1) * 128, :], vb, [r_vb], [])
                yield
                if ty == 0:
                    head_transposes(qb, r_qb, 8, qT_d, gt, rings)
                head_transposes(kb, r_kb, 8, kT_d, gt, rings)

            pipeline(tile_gen, range(NT))

        def phase_odd_attn(li):
            B.phase_begin()
            mvs = load_mod_vecs(li, None, None, 5, 1, 1.0, ntypes=1)
            wout = A.alloc([8, D], BF16)
            r_wout = S.res()
            B.dma("pool", wout, I["od_w_out"].rearrange("(k p) n -> p k n", p=128), [], [r_wout])
            wpool = A.alloc([4, 128], BF16)
            r_wpool = S.res()
            B.dma("pool", wpool, I["c_w_pool"].rearrange("g c d -> c g d"), [], [r_wpool])
            csc, r_csc = B.load_bc("sp", I["c_scale"][0:1, :], 512)
            band = A.alloc([4, 5, 128], BF16)
            r_band = S.res()
            B.dma("pool", band, I["k_band"][:, :, :, :], [], [r_band])
            kTc = A.alloc([8, 256], BF16, parts=64)
            r_kTc = S.res()
            B.dma("sp", kTc, kT_d.rearrange("h d t -> d h t")[:, :, NTL * 128:NT * 128], [], [r_kTc])
            Vc = A.alloc([2, 8, 65], BF16)
            r_Vc = S.res()
            B.memset("dve", Vc[:, :, :, 64:65], 1.0, [r_Vc])
            load_V(Vc, r_Vc, NTL, 2, 8)
            biasr = Ring(B, 2, [8, 7, 128], F32)
            for bb_, r_ in biasr.bufs:
                B.memset("pool", bb_[:, :, 5:7, :], 0.0, [r_])
            kTwr = Ring(B, 3, [8, 640], BF16, parts=64)
            Vwr = Ring(B, 3, [5, 8, 65], BF16)
            for vb_, r_ in Vwr.bufs:
                B.memset("dve", vb_[:, :, :, 64:65], 1.0, [r_])
            qTr = Ring(B, 3, [8, 128], BF16, parts=64)
            xpr = Ring(B, 2, [3, 512], BF16)
            ppr = Ring(B, 2, [4, 128], BF16)
            tAr = Ring(B, 2, [512], F32)
            tBr = Ring(B, 2, [384], F32)
            pAr = Ring(B, 3, [512], BF16)
            pBr = Ring(B, 3, [384], BF16)
            mixr = Ring(B, 5, [D], BF16)
            str_ = Ring(B, 4, [8], F32)
            rings = {"mixT": Ring(B, 2, [8, 128], BF16), "hin": Ring(B, 2, [D], F32), "et": Ring(B, 1, [D], F32),
                     "hout": Ring(B, 2, [D], F32)}
            state = {"case": None, "bias": None, "r_bias": None}
            B.nrr = 6

            def case_of(n):
                return 0 if n == 0 else 1 if n == 1 else 3 if n == NTL - 2 else 4 if n == NTL - 1 else 2

            def tile_gen(n):
                mix, r_mix = mixr.next()
                case = case_of(n)
                if case != state["case"]:
                    bias_, r_bias_ = biasr.next()
                    B.dma("sp", bias_[:, :, 0:5, :], I["k_dbias"][case], [], [r_bias_])
                    state["case"] = case
                    state["bias"] = bias_
                    state["r_bias"] = r_bias_
                bias = state["bias"]
                r_bias = state["r_bias"]
                kt0 = min(max(n - 2, 0), NTL - 5)
                kTw, r_kTw = kTwr.next()
                B.dma("sp", kTw, kT_d.rearrange("h d t -> d h t")[:, :, kt0 * 128:(kt0 + 5) * 128], [], [r_kTw])
                Vw, r_Vw = Vwr.next()
                load_V(Vw, r_Vw, kt0, 5, 8)
                qT, r_qT = qTr.next()
                B.dma("sp", qT, qT_d.rearrange("h d t -> d h t")[:, :, n * 128:(n + 1) * 128], [], [r_qT])
                xw, r_xw = xpr.next()
                jts = [j for j in (n - 1, n, n + 1) if 0 <= j < NTL]
                j0 = jts[0]
                B.dma("sp", xw[:, 0:len(jts), :], xp_d[j0 * 128:(j0 + len(jts)) * 128, :].rearrange("(w p) f -> p w f", p=128), [], [r_xw])
                bk, r_bk = B.bank()
                for g in range(4):
                    for ji, j in enumerate(jts):
                        if j == n - 1:
                            typ = 0
                        elif j == n + 1:
                            typ = 2
                        else:
                            typ = 3 if n == 0 else (4 if n == NTL - 1 else 1)
                        B.mm(bk[:, g * 128:(g + 1) * 128], xw[:, ji, g * 128:(g + 1) * 128], band[:, g, typ, :],
                             ji == 0, ji == len(jts) - 1, [r_xw, r_band], [r_bk])
                pp, r_pp = ppr.next()
                B.cp("act", pp, bk.rearrange("p (g t) -> p g t", g=4), [], [r_bk, r_pp])
                bk2, r_bk2 = B.bank()
                for g in range(4):
                    B.mm(bk2[:, g * 128:(g + 1) * 128], pp[:, g, :], wpool[:, g, :], True, True, [r_pp, r_wpool], [r_bk2])
                B.tt("dve", mix[:, 0:512], bk2, csc, ALU.mult, [r_csc], [r_bk2, r_mix])
                yield

                def heads(hq):
                    pend = None
                    ob, r_ob = B.bank_fixed(6 + hq)
                    for hi in range(5):
                        cur = None
                        if hi < 4:
                            h = hq * 4 + hi
                            bA, r_bA = B.bank()
                            bB, r_bB = B.bank()
                            for s_ in range(4):
                                B.mm(bA[:, s_ * 128:(s_ + 1) * 128], kTw[:, h, s_ * 128:(s_ + 1) * 128], qT[:, h, :], True, True, [r_kTw, r_qT], [r_bA])
                            B.mm(bB[:, 0:128], kTw[:, h, 512:640], qT[:, h, :], True, True, [r_kTw, r_qT], [r_bB])
                            for s_ in range(2):
                                B.mm(bB[:, (1 + s_) * 128:(2 + s_) * 128], kTc[:, h, s_ * 128:(s_ + 1) * 128], qT[:, h, :], True, True, [r_kTc, r_qT], [r_bB])
                            tA, r_tA = tAr.next()
                            tB, r_tB = tBr.next()
                            B.stt("dve", tA, bA, 0.125, bias[:, h, 0:4, :].rearrange("p s q -> p (s q)"), ALU.mult, ALU.add, [r_bias], [r_bA, r_tA])
                            B.stt("dve", tB, bB[:, 0:384], 0.125, bias[:, h, 4:7, :].rearrange("p s q -> p (s q)"), ALU.mult, ALU.add, [r_bias], [r_bB, r_tB])
                            pA, r_pA = pAr.next()
                            pB, r_pB = pBr.next()
                            B.act(pA, tA, AF.Exp, [r_tA], [r_pA])
                            B.act(pB, tB, AF.Exp, [r_tB], [r_pB])
                            cur = (h, pA, r_pA, pB, r_pB)
                        if pend is not None:
                            ph, pA, r_pA, pB, r_pB = pend
                            osl = ob[:, (ph % 4) * 65:(ph % 4 + 1) * 65]
                            for s_ in range(7):
                                if s_ < 4:
                                    lhs = pA[:, s_ * 128:(s_ + 1) * 128]
                                    rp = r_pA
                                else:
                                    lhs = pB[:, (s_ - 4) * 128:(s_ - 3) * 128]
                                    rp = r_pB
                                if s_ < 5:
                                    rhs = Vw[:, s_, ph, :]
                                    rv = r_Vw
                                else:
                                    rhs = Vc[:, s_ - 5, ph, :]
                                    rv = r_Vc
                                B.mm(osl, lhs, rhs, s_ == 0, s_ == 6, [rp, rv], [r_ob])
                        pend = cur
                    ob3 = ob[:, 0:260].rearrange("p (h e) -> p h e", h=4)
                    sd, r_sd = str_.next()
                    B.recip(sd[:, 0:4], ob3[:, :, 64], [], [r_ob, r_sd])
                    B.tt("dve", mix[:, 512 + hq * 256:512 + (hq + 1) * 256].rearrange("p (h d) -> p h d", h=4), ob3[:, :, 0:64],
                         sd[:, 0:4].unsqueeze(2).to_broadcast([128, 4, 64]), ALU.mult, [r_sd], [r_ob, r_mix])

                heads(0)
                yield
                heads(1)
                yield
                outproj_residual(mix, r_mix, wout, r_wout, mvs[0][2], mvs[0][3], n, rings)

            pipeline(tile_gen, range(NTL), drain_before=lambda n: case_of(n) != state["case"])
            B.nrr = 8

        plist = [
            lambda: phase_mod(0),
            lambda: phase_ffn(0, 0, NT, src0, hsrc),
            lambda: phase_even_prep(0),
            lambda: phase_even_attn(0),
            lambda: phase_ffn(0, 1, NT, hsrc, hsrc),
            lambda: phase_mod(1),
            lambda: phase_ffn(1, 0, NT, hsrc, hsrc),
            lambda: phase_odd_prep(1),
            lambda: phase_odd_attn(1),
            lambda: phase_ffn(1, 1, NTL, hsrc, osrc),
        ]
        if dbg_phases is not None:
            plist = plist[:dbg_phases]
        for p in plist:
            p()
        S.emit()
        build_program.stats = S.stats
    return nc


def _rope_table():
    t = np.arange(S_LAT)
    row = (t // 64).astype(np.float32)
    col = (t % 64).astype(np.float32)
    m = 16
    inv = (1.0 / (10000.0 ** (np.arange(m, dtype=np.float32) / m))).astype(np.float32)
    ar = row[:, None] * inv[None, :]
    ac = col[:, None] * inv[None, :]
    cos = np.concatenate([np.cos(ar), np.cos(ar), np.cos(ac), np.cos(ac)], axis=1)
    sin = np.concatenate([-np.sin(ar), np.sin(ar), -np.sin(ac), np.sin(ac)], axis=1)
    return np.stack([cos, sin], axis=1).astype(np.float32)


def _amask():
    pj = np.arange(128)[:, None]
    pi = np.arange(128)[None, :]
    prev = (pj >= pi).astype(np.float32)
    nxt = (pj <= pi).astype(np.float32)
    return np.stack([prev, nxt], axis=1)


def _band():
    out = np.zeros((128, 4, 5, 128), np.float32)
    for gi, w in enumerate((2, 4, 8, 16)):
        def mat(n, jn):
            tg = n * 128 + np.arange(128)
            lo = np.clip(tg - w // 2, 0, S_LAT)
            hi = np.clip(tg + w - w // 2, 0, S_LAT)
            cnt = (hi - lo).astype(np.float32)
            jg = jn * 128 + np.arange(128)
            m = ((jg[:, None] >= lo[None, :]) & (jg[:, None] < hi[None, :])).astype(np.float32) / cnt[None, :]
            m = m - (jg[:, None] == tg[None, :]).astype(np.float32)
            return m
        out[:, gi, 0] = mat(5, 4)
        out[:, gi, 1] = mat(5, 5)
        out[:, gi, 2] = mat(5, 6)
        out[:, gi, 3] = mat(0, 0)
        out[:, gi, 4] = mat(NTL - 1, NTL - 1)
    return out


def _dbias(rpb):
    out = np.full((5, 128, 8, 5, 128), NEGB, np.float32)
    for case, n in enumerate((0, 1, 5, NTL - 2, NTL - 1)):
        kt0 = min(max(n - 2, 0), NTL - 5)
        i = np.arange(128)
        r = 2 * n + i // 64
        c = i % 64
        r0 = np.clip(r - 4, 0, 56)
        q0 = np.clip(c - 8, 0, 48)
        for s in range(5):
            kt = kt0 + s
            j = np.arange(128)
            kr = 2 * kt + j // 64
            kc = j % 64
            valid = ((kr[:, None] >= r0[None, :]) & (kr[:, None] < r0[None, :] + 8) &
                     (kc[:, None] >= q0[None, :]) & (kc[:, None] < q0[None, :] + 16))
            ri = np.clip(kr[:, None] - r[None, :] + 7, 0, 14)
            ci = np.clip(kc[:, None] - c[None, :] + 15, 0, 30)
            g = rpb[:, ri, ci]
            g = np.where(valid[None], g, np.float32(NEGB))
            out[case, :, :, s, :] = np.transpose(g, (1, 0, 2))
    return out


_CACHE = {}


def kernel(x, c, ctx, c_ctx, ada_w, ada_b, norm_g, ffn_w_gu, ffn_w_down,
           ev_w_in, ev_w_out, a_q_gain, a_k_gain, a_sink, b_v_gain, b_ws, b_bias,
           od_w_in, od_w_out, c_w_pool, c_scale, d_q_gain, d_k_gain, d_rpb, _dbg_phases=None, _dbg=False):
    f = lambda a: np.ascontiguousarray(np.asarray(a, dtype=np.float32))
    key = (_dbg_phases, _dbg)
    if key not in _CACHE:
        _CACHE[key] = build_program(_dbg_phases, _dbg)
    nc = _CACHE[key]
    shared = {
        "c_ctx": f(c_ctx).reshape(1, D), "ada_w": f(ada_w), "ada_b": f(ada_b), "norm_g": f(norm_g),
        "ffn_w_gu": f(ffn_w_gu), "ffn_w_down": f(ffn_w_down),
        "ev_w_in": f(ev_w_in)[0], "ev_w_out": f(ev_w_out)[0],
        "a_q_gain": f(a_q_gain), "a_k_gain": f(a_k_gain), "a_sink": f(a_sink),
        "b_v_gain": f(b_v_gain), "b_ws": f(b_ws)[0], "b_bias": f(b_bias)[0],
        "od_w_in": f(od_w_in)[0], "od_w_out": f(od_w_out)[0],
        "c_w_pool": f(c_w_pool)[0], "c_scale": f(c_scale),
        "d_q_gain": f(d_q_gain), "d_k_gain": f(d_k_gain),
        "k_ident": np.eye(128, dtype=np.float32), "k_rope": _rope_table(), "k_amask": _amask(),
        "k_band": _band(), "k_dbias": _dbias(f(d_rpb)[0]),
    }
    x = f(x); c = f(c); ctx = f(ctx)
    in_maps = []
    for b in range(8):
        m = dict(shared)
        m["x"] = x[b]
        m["ctx"] = ctx[b]
        m["c"] = c[b].reshape(1, D)
        in_maps.append(m)
    res = run_bass_kernel_spmd(nc, in_maps, core_ids=list(range(8)))
    kernel.last = res
    return np.stack([r["out"] for r in res.results], axis=0)
```

```python
import contextlib
import numpy as np
import concourse.bass as bass
import concourse.mybir as mybir
from concourse.bass_utils import run_bass_kernel_spmd

F32 = mybir.dt.float32
BF16 = mybir.dt.bfloat16
AF = mybir.ActivationFunctionType
ALU = mybir.AluOpType
AX = mybir.AxisListType

D = 1024
S_LAT = 4096
S_CTX = 256
NTL = 32
NT = 34
DFF = 2816
NFF = 22
EPS = 1e-6
NEGB = -30000.0

ENGS = ("sp", "act", "pool", "dve", "pe")
NDMA_SEM = 8


class Res:
    __slots__ = ("name", "last_w", "readers", "gen")

    def __init__(self, name):
        self.name = name
        self.last_w = None
        self.readers = {}
        self.gen = 0

    def bump(self):
        self.gen += 1
        return Ref(self, self.gen)


class Ref:
    __slots__ = ("phys", "gen")

    def __init__(self, phys, gen):
        self.phys = phys
        self.gen = gen


def _norm_res(lst):
    out = []
    for r in lst:
        if isinstance(r, Ref):
            assert r.gen == r.phys.gen, f"stale buffer reference {r.phys.name}"
            r = r.phys
        out.append(r)
    return out


def pipeline(make_gen, items, drain_before=None):
    active = []
    for it in items:
        if drain_before is not None and drain_before(it):
            while active:
                nxt = []
                for g in active:
                    try:
                        next(g)
                        nxt.append(g)
                    except StopIteration:
                        pass
                active = nxt
        nxt = []
        for g in active:
            try:
                next(g)
                nxt.append(g)
            except StopIteration:
                pass
        active = nxt
        g = make_gen(it)
        try:
            next(g)
            active.append(g)
        except StopIteration:
            pass
    while active:
        nxt = []
        for g in active:
            try:
                next(g)
                nxt.append(g)
            except StopIteration:
                pass
        active = nxt


class Op:
    __slots__ = ("eng", "fn", "deps", "dma", "signal", "sem", "val", "prewait")

    def __init__(self, eng, fn, dma):
        self.eng = eng
        self.fn = fn
        self.dma = dma
        self.deps = []
        self.signal = False
        self.sem = None
        self.val = 0
        self.prewait = None


class Sched:
    def __init__(self, nc):
        self.nc = nc
        self.ops = {e: [] for e in ENGS}
        self.bar = {}
        self.nres = 0

    def res(self, name=None):
        self.nres += 1
        return Res(name or f"r{self.nres}")

    def op(self, eng, fn, reads=(), writes=(), dma=False):
        reads = _norm_res(reads)
        writes = _norm_res(writes)
        o = Op(eng, fn, dma)
        deps = {}
        for r in reads:
            if r.last_w is not None:
                deps[id(r.last_w)] = r.last_w
        for r in writes:
            if r.last_w is not None:
                deps[id(r.last_w)] = r.last_w
            for rd in r.readers.values():
                if isinstance(rd, list):
                    for x in rd:
                        deps[id(x)] = x
                else:
                    deps[id(rd)] = rd
        for r in reads:
            if dma:
                r.readers.setdefault(("dma", eng), []).append(o)
            else:
                r.readers[eng] = o
        for r in writes:
            r.last_w = o
            r.readers = {}
        b = self.bar.pop(eng, None)
        if b:
            for x in b:
                deps[id(x)] = x
        dl = []
        for d in deps.values():
            if d is o:
                continue
            if (not dma) and (not d.dma) and d.eng == "pe" and eng == "pe":
                continue
            dl.append(d)
        o.deps = dl
        self.ops[eng].append(o)
        return o

    def dma(self, q, out, in_, reads=(), writes=(), **kw):
        return self.op(q, lambda e: e.dma_start(out=out, in_=in_, **kw), reads, writes, dma=True)

    def barrier(self):
        tails = []
        for e in ENGS:
            ops = self.ops[e]
            for o in reversed(ops):
                if not o.dma:
                    tails.append(o)
                    break
            cnt = 0
            for o in reversed(ops):
                if o.dma:
                    tails.append(o)
                    cnt += 1
                    if cnt >= NDMA_SEM:
                        break
        self.bar = {e: list(tails) for e in ENGS}

    def emit(self):
        nc = self.nc
        with contextlib.ExitStack() as st:
            csem = {e: st.enter_context(nc.semaphore(f"c_{e}")) for e in ENGS}
            dsem = {e: [st.enter_context(nc.semaphore(f"d_{e}{i}")) for i in range(NDMA_SEM)]
                    for e in ("sp", "act", "pool")}
            for e in ENGS:
                for o in self.ops[e]:
                    for d in o.deps:
                        d.signal = True
            self.stats = {}
            for e in ENGS:
                cnt = 0
                nd = 0
                for o in self.ops[e]:
                    if o.dma:
                        slot = nd % NDMA_SEM
                        o.sem = dsem[e][slot]
                        o.val = 16 * (nd // NDMA_SEM + 1)
                        if nd >= NDMA_SEM:
                            o.prewait = (dsem[e][slot], 16 * (nd // NDMA_SEM))
                        nd += 1
                    elif o.signal:
                        cnt += 1
                        o.sem = csem[e]
                        o.val = cnt
                self.stats[e] = (len(self.ops[e]), cnt, nd)
            block = st.enter_context(nc.Block())

            def run(eng_name, eng):
                seen = {}
                lastdma = {}
                for o in self.ops[eng_name]:
                    waits = []
                    if o.prewait is not None:
                        waits.append(o.prewait)
                    for d in o.deps:
                        waits.append((d.sem, d.val))
                    for sem, val in waits:
                        k = id(sem)
                        if seen.get(k, 0) >= val:
                            continue
                        seen[k] = val
                        eng.wait_ge(sem, val)
                    inst = o.fn(eng)
                    if o.dma:
                        inst.then_inc(o.sem, 16)
                        lastdma[id(o.sem)] = (o.sem, o.val)
                    elif o.signal:
                        inst.then_inc(o.sem, 1)
                for sem, val in lastdma.values():
                    if seen.get(id(sem), 0) < val:
                        eng.wait_ge(sem, val)

            @block.sync
            def _(e):
                run("sp", e)

            @block.scalar
            def _(e):
                run("act", e)

            @block.gpsimd
            def _(e):
                run("pool", e)

            @block.vector
            def _(e):
                run("dve", e)

            @block.tensor
            def _(e):
                run("pe", e)


class Arena:
    def __init__(self, nc, st, nbytes):
        self.nbytes = nbytes
        self.t = st.enter_context(nc.sbuf_tensor("arena", [128, nbytes // 2], BF16))
        self.off = 0
        self.base = 0

    def alloc(self, free, dtype, parts=128):
        free = list(free)
        n = int(np.prod(free))
        sz = n * (4 if dtype == F32 else 2)
        off = (self.off + 63) // 64 * 64
        assert off + sz <= self.nbytes, f"arena overflow {off + sz} > {self.nbytes}"
        self.off = off + sz
        ap = self.t[0:parts, off // 2: (off + sz) // 2]
        if dtype == F32:
            ap = ap.bitcast(F32)
        if len(free) == 2:
            ap = ap.rearrange("p (a b) -> p a b", a=free[0])
        elif len(free) == 3:
            ap = ap.rearrange("p (a b c) -> p a b c", a=free[0], b=free[1])
        elif len(free) == 4:
            ap = ap.rearrange("p (a b c d) -> p a b c d", a=free[0], b=free[1], c=free[2])
        return ap

    def mark_persistent(self):
        self.base = self.off

    def reset(self):
        self.off = self.base


class Ring:
    def __init__(self, B, n, free, dtype, parts=128):
        self.bufs = [(B.A.alloc(free, dtype, parts), B.S.res()) for _ in range(n)]
        self.i = 0

    def next(self):
        r = self.bufs[self.i % len(self.bufs)]
        self.i += 1
        return r[0], r[1].bump()


class Builder:
    def __init__(self, nc, st, dbg):
        self.nc = nc
        self.st = st
        self.S = Sched(nc)
        self.A = Arena(nc, st, 206 * 1024)
        self.banks = []
        for i in range(8):
            t = st.enter_context(nc.psum_tensor(f"bank{i}", [128, 512], F32))
            self.banks.append((t, self.S.res(f"bank{i}")))
        self.bi = 0
        self.nrr = 8
        self.dbg = dbg

    def bank(self):
        r = self.banks[self.bi % self.nrr]
        self.bi += 1
        return r[0][:], r[1].bump()

    def bank_fixed(self, idx):
        r = self.banks[idx]
        return r[0][:], r[1].bump()

    def dma(self, q, out, in_, reads=(), writes=(), **kw):
        return self.S.dma(q, out, in_, reads, writes, **kw)

    def mm(self, out, lhsT, rhs, start, stop, reads, writes):
        return self.S.op("pe", lambda e: e.matmul(out, lhsT=lhsT, rhs=rhs, start=start, stop=stop), reads, writes)

    def tr(self, out, in_, reads, writes):
        idn = self.ident
        return self.S.op("pe", lambda e: e.transpose(out, in_, idn), list(reads) + [self.r_ident], writes)

    def act(self, out, in_, func, reads, writes, scale=None, bias=None, accum=None):
        kw = {}
        if scale is not None:
            kw["scale"] = scale
        if bias is not None:
            kw["bias"] = bias
        if accum is not None:
            kw["accum_out"] = accum
        return self.S.op("act", lambda e: e.activation(out=out, in_=in_, func=func, **kw), reads, writes)

    def tt(self, eng, out, in0, in1, op, reads, writes):
        return self.S.op(eng, lambda e: e.tensor_tensor(out=out, in0=in0, in1=in1, op=op), reads, writes)

    def ts(self, eng, out, in0, s1, s2, op0, op1, reads, writes):
        if op1 is None:
            return self.S.op(eng, lambda e: e.tensor_scalar(out=out, in0=in0, scalar1=s1, scalar2=None, op0=op0), reads, writes)
        return self.S.op(eng, lambda e: e.tensor_scalar(out=out, in0=in0, scalar1=s1, scalar2=s2, op0=op0, op1=op1), reads, writes)

    def stt(self, eng, out, in0, scalar, in1, op0, op1, reads, writes):
        return self.S.op(eng, lambda e: e.scalar_tensor_tensor(out=out, in0=in0, scalar=scalar, in1=in1, op0=op0, op1=op1), reads, writes)

    def cp(self, eng, out, in_, reads, writes):
        if eng == "act":
            return self.S.op("act", lambda e: e.activation(out=out, in_=in_, func=AF.Copy), reads, writes)
        return self.S.op(eng, lambda e: e.tensor_copy(out=out, in_=in_), reads, writes)

    def recip(self, out, in_, reads, writes):
        return self.S.op("dve", lambda e: e.reciprocal(out=out, in_=in_), reads, writes)

    def memset(self, eng, out, val, writes):
        return self.S.op(eng, lambda e: e.memset(out, val), [], writes)

    def reduce_sum(self, out, in_, reads, writes):
        return self.S.op("dve", lambda e: e.tensor_reduce(out=out, in_=in_, axis=AX.X, op=ALU.add), reads, writes)

    def rstd(self, st, r_st, w, inv_n):
        self.ts("dve", st[:, w:2 * w], st[:, 0:w], inv_n, EPS, ALU.mult, ALU.add, [r_st], [r_st])
        self.act(st[:, w:2 * w], st[:, w:2 * w], AF.Sqrt, [r_st], [r_st])
        self.recip(st[:, 2 * w:3 * w], st[:, w:2 * w], [r_st], [r_st])
        return st[:, 2 * w:3 * w]

    def phase_begin(self):
        self.S.barrier()
        self.A.reset()

    def load_bc(self, q, src_1xn, n, name=None):
        t = self.A.alloc([n], F32)
        r = self.S.res(name)
        self.dma(q, t, src_1xn.partition_broadcast(128), [], [r])
        return t, r


def build_program(dbg_phases=None, dbg=False):
    nc = bass.Bass("TRN2", target_bir_lowering=False)
    I = {}

    def inp(name, shape, dt=F32):
        I[name] = nc.dram_tensor(name, list(shape), dt, kind="ExternalInput").ap()
        return I[name]

    inp("x", [S_LAT, D]); inp("ctx", [S_CTX, D]); inp("c", [1, D]); inp("c_ctx", [1, D])
    inp("ada_w", [2, D, 9 * D]); inp("ada_b", [2, 9 * D]); inp("norm_g", [2, 3, D])
    inp("ffn_w_gu", [2, 2, D, 2 * DFF]); inp("ffn_w_down", [2, 2, DFF, D])
    inp("ev_w_in", [D, 1792]); inp("ev_w_out", [D, D])
    inp("a_q_gain", [1, 64]); inp("a_k_gain", [1, 64]); inp("a_sink", [1, 8])
    inp("b_v_gain", [1, 512]); inp("b_ws", [8, 128, 128]); inp("b_bias", [8, 128])
    inp("od_w_in", [D, 2048]); inp("od_w_out", [D, D])
    inp("c_w_pool", [4, 128, 128]); inp("c_scale", [1, 512])
    inp("d_q_gain", [1, 64]); inp("d_k_gain", [1, 64])
    inp("k_ident", [128, 128]); inp("k_rope", [S_LAT, 2, 64]); inp("k_amask", [128, 2, 128])
    inp("k_band", [128, 4, 5, 128]); inp("k_dbias", [5, 128, 8, 5, 128])
    out = nc.dram_tensor("out", [S_LAT, D], F32, kind="ExternalOutput").ap()
    skind = "ExternalOutput" if dbg else "Internal"

    def scr(name, shape, dt):
        return nc.dram_tensor(name, list(shape), dt, kind=skind).ap()

    hA = scr("hA", [NT * 128, D], F32)
    mod_d = scr("mod_d", [2, 2, 9 * D], F32)
    qT_d = scr("qT_d", [8, 64, NT * 128], BF16)
    kT_d = scr("kT_d", [8, 64, NT * 128], BF16)
    v_d = scr("v_d", [NT * 128, 512], BF16)
    u_d = scr("u_d", [NT * 128, 512], F32)
    vv_d = scr("vv_d", [NT * 128, 512], BF16)
    xp_d = scr("xp_d", [S_LAT, 512], BF16)
    wgc_d = nc.dram_tensor("wgc_d", [NFF // 2, 128, 8 * 2 * 256], BF16, kind="Internal").ap()

    st = contextlib.ExitStack()
    with st:
        B = Builder(nc, st, dbg)
        S, A = B.S, B.A
        B.ident = A.alloc([128], BF16)
        B.r_ident = S.res("ident")
        B.dma("pool", B.ident, I["k_ident"][:, :], [], [B.r_ident])
        B.identf = A.alloc([128], F32)
        B.dma("sp", B.identf, I["k_ident"][:, :], [], [B.r_ident])
        A.mark_persistent()

        def src0(gt):
            if gt < NTL:
                return I["x"][gt * 128:(gt + 1) * 128, :]
            return I["ctx"][(gt - NTL) * 128:(gt - NTL + 1) * 128, :]

        def hsrc(gt):
            return hA[gt * 128:(gt + 1) * 128, :]

        def osrc(gt):
            return out[gt * 128:(gt + 1) * 128, :]

        phases = []

        def phase_mod(li):
            B.phase_begin()
            cc = A.alloc([2, 128], F32, parts=8)
            r_cc = S.res()
            B.dma("sp", cc[:, 0, :], I["c"][0, :].rearrange("(k p) -> k p", p=128), [], [r_cc])
            B.dma("sp", cc[:, 1, :], I["c_ctx"][0, :].rearrange("(k p) -> k p", p=128), [], [r_cc])
            ccb = A.alloc([2, 128], BF16, parts=8)
            r_ccb = S.res()
            B.act(ccb, cc, AF.Silu, [r_cc], [r_ccb])
            cs = A.alloc([8, 2], BF16)
            r_cs = S.res()
            bk, r_bk = B.bank()
            bkb = bk.bitcast(BF16)
            for j in range(2):
                B.S.op("pe", lambda e, j=j: e.transpose(bkb[:, j * 8:(j + 1) * 8], ccb[:, j, :], B.ident[0:8, 0:8]), [r_ccb, B.r_ident], [r_bk])
            B.cp("dve", cs, bkb[:, 0:16].rearrange("p (j k) -> p k j", j=2), [], [r_bk, r_cs])
            adab = A.alloc([9 * D], F32, parts=2)
            r_adab = S.res()
            B.dma("sp", adab, I["ada_b"][li:li + 1, :].partition_broadcast(2), [], [r_adab])
            msb = A.alloc([9 * D], F32, parts=2)
            r_msb = S.res()
            wr = Ring(B, 3, [8, 512], BF16)
            for nb in range(18):
                w, r_w = wr.next()
                B.dma("pool", w, I["ada_w"][li, :, nb * 512:(nb + 1) * 512].rearrange("(k p) n -> p k n", p=128), [], [r_w])
                bk, r_bk = B.bank()
                for k in range(8):
                    B.mm(bk[0:2, :], cs[:, k, :], w[:, k, :], k == 0, k == 7, [r_cs, r_w], [r_bk])
                B.tt("dve", msb[:, nb * 512:(nb + 1) * 512], bk[0:2, :], adab[:, nb * 512:(nb + 1) * 512], ALU.add,
                     [r_adab], [r_bk, r_msb])
            B.dma("sp", mod_d[li], msb, [r_msb], [])

        def load_mod_vecs(li, j_shift, j_scale, j_gate, gi, gate_mul, ntypes=2):
            outl = []
            gbc, r_g = (None, None)
            if j_scale is not None:
                gbc, r_g = B.load_bc("sp", I["norm_g"][li, gi:gi + 1, :], D)
            for ty in range(ntypes):
                r = S.res()
                sh = Gm = gt_ = None
                if j_shift is not None:
                    sh = A.alloc([D], F32)
                    B.dma("sp", sh, mod_d[li, ty:ty + 1, j_shift * D:(j_shift + 1) * D].partition_broadcast(128), [], [r])
                if j_scale is not None:
                    Gm = A.alloc([D], F32)
                    B.dma("sp", Gm, mod_d[li, ty:ty + 1, j_scale * D:(j_scale + 1) * D].partition_broadcast(128), [], [r])
                    B.stt("dve", Gm, Gm, 1.0, gbc, ALU.add, ALU.mult, [r_g, r], [r])
                if j_gate is not None:
                    gt_ = A.alloc([D], F32)
                    B.dma("sp", gt_, mod_d[li, ty:ty + 1, j_gate * D:(j_gate + 1) * D].partition_broadcast(128), [], [r])
                    if gate_mul != 1.0:
                        B.ts("dve", gt_, gt_, gate_mul, None, ALU.mult, None, [r], [r])
                outl.append((sh, Gm, gt_, r))
            return outl

        def norm_tile(hin, r_hin, mv, rings):
            sh, Gm, _, r_mv = mv
            sqj, r_sqj = rings["sqj"]
            st_, r_st = rings["st"].next()
            B.memset("dve", st_[:, 0:1], 0.0, [r_st])
            B.act(sqj, hin, AF.Square, [r_hin, r_st], [r_sqj, r_st], accum=st_[:, 0:1])
            rs = B.rstd(st_, r_st, 1, 1.0 / D)
            z1, r_z1 = rings["z1"].next()
            B.stt("dve", z1, hin, rs, Gm, ALU.mult, ALU.mult, [r_hin, r_st, r_mv], [r_z1])
            zt, r_zt = rings["ztok"].next()
            B.tt("pool", zt, z1, sh, ALU.add, [r_z1, r_mv], [r_zt])
            return zt, r_zt

        def transpose8(src, r_src, dst3, r_dst, eng="act"):
            bk, r_bk = B.bank()
            bkb = bk.bitcast(BF16)
            for k in range(8):
                B.tr(bkb[:, k * 128:(k + 1) * 128], src[:, k * 128:(k + 1) * 128], [r_src], [r_bk])
            B.cp(eng, dst3, bkb.rearrange("p (k t) -> p k t", k=8), [], [r_bk, r_dst])

        def phase_ffn(li, which, ntiles, srcf, dstf):
            B.phase_begin()
            j0 = 0 if which == 0 else 6
            gi = 0 if which == 0 else 2
            mvs = load_mod_vecs(li, j0, j0 + 1, j0 + 2, gi, 0.5, ntypes=2 if ntiles > NTL else 1)
            wd = A.alloc([NFF, D], BF16)
            r_wd = S.res()
            for c4 in range(0, NFF, 2):
                B.dma("pool", wd[:, c4:c4 + 2, :],
                      I["ffn_w_down"][li, which, c4 * 128:(c4 + 2) * 128, :].rearrange("(c p) n -> p c n", p=128), [], [r_wd])
            if ntiles == NT:
                groups = [list(range(0, 9)), list(range(9, 18)), list(range(18, 26)), list(range(26, 34))]
            else:
                groups = [list(range(g * 8, g * 8 + 8)) for g in range(4)]
            wc_res = [S.res() for _ in range(NFF // 2)]
            GM = max(len(g) for g in groups)
            zTs = [A.alloc([8, GM * 128], BF16) for _ in range(2)]
            zress = [[S.res() for _ in range(GM)] for _ in range(2)]
            actT = A.alloc([NFF, GM * 128], BF16)
            wgr = Ring(B, 2, [8, 2, 256], BF16)
            hinr = Ring(B, 2, [D], F32)
            rings = {"sqj": (A.alloc([D], BF16), S.res()), "st": Ring(B, 2, [3], F32),
                     "z1": Ring(B, 1, [D], F32), "ztok": Ring(B, 2, [D], BF16)}
            stmpr = Ring(B, 2, [512], F32)
            etmpr = Ring(B, 1, [D], F32)
            houtr = Ring(B, 1, [D], F32)
            wgu = I["ffn_w_gu"][li, which]

            def stage1_tile(gidx, lt, gt):
                ty = 0 if gt < NTL else 1
                hin, r_hin = hinr.next()
                B.dma("sp", hin, srcf(gt), [], [r_hin])
                zt, r_zt = norm_tile(hin, r_hin, mvs[ty], rings)
                transpose8(zt, r_zt, zTs[gidx % 2][:, :, lt * 128:(lt + 1) * 128], zress[gidx % 2][lt])

            for lt, gt in enumerate(groups[0]):
                stage1_tile(0, lt, gt)
            for gidx, grp in enumerate(groups):
                zT = zTs[gidx % 2]
                zres = zress[gidx % 2]
                T = len(grp) * 128
                nblk = (T + 511) // 512
                bs = T // nblk
                blocks = [(i * bs, (i + 1) * bs if i < nblk - 1 else T) for i in range(nblk)]
                ares = [S.res() for _ in blocks]
                pending = list(enumerate(groups[gidx + 1])) if gidx + 1 < len(groups) else []
                npend = len(pending)
                nunits = NFF * nblk
                unit = 0
                emitted = 0
                for cp_ in range(NFF // 2):
                    wg, r_wg = wgr.next()
                    if gidx == 0:
                        B.dma("pool", wg[:, :, 0, :], wgu[:, cp_ * 256:(cp_ + 1) * 256].rearrange("(k p) n -> p k n", p=128), [], [r_wg])
                        B.dma("pool", wg[:, :, 1, :], wgu[:, DFF + cp_ * 256:DFF + (cp_ + 1) * 256].rearrange("(k p) n -> p k n", p=128), [], [r_wg])
                        B.dma("sp", wgc_d[cp_], wg.rearrange("p k g n -> p (k g n)"), [r_wg], [wc_res[cp_]])
                    else:
                        B.dma("sp", wg.rearrange("p k g n -> p (k g n)"), wgc_d[cp_], [wc_res[cp_]], [r_wg])
                    for ci in range(2):
                        c = cp_ * 2 + ci
                        for bi_, (a, b_) in enumerate(blocks):
                            n = b_ - a
                            zr = [zres[t] for t in range(a // 128, (b_ - 1) // 128 + 1)]
                            bg, r_bg = B.bank()
                            bu, r_bu = B.bank()
                            for k in range(8):
                                B.mm(bg[:, 0:n], wg[:, k, 0, ci * 128:(ci + 1) * 128], zT[:, k, a:b_], k == 0, k == 7, [r_wg] + zr, [r_bg])
                            for k in range(8):
                                B.mm(bu[:, 0:n], wg[:, k, 1, ci * 128:(ci + 1) * 128], zT[:, k, a:b_], k == 0, k == 7, [r_wg] + zr, [r_bu])
                            stp, r_stp = stmpr.next()
                            B.act(stp[:, 0:n], bg[:, 0:n], AF.Silu, [], [r_bg, r_stp])
                            B.tt("dve", actT[:, c, a:b_], stp[:, 0:n], bu[:, 0:n], ALU.mult, [r_stp], [r_bu, ares[bi_]])
                            unit += 1
                            while emitted < npend and unit * npend >= (emitted + 1) * int(nunits * 0.7):
                                lt2, gt2 = pending[emitted]
                                stage1_tile(gidx + 1, lt2, gt2)
                                emitted += 1
                while emitted < npend:
                    lt2, gt2 = pending[emitted]
                    stage1_tile(gidx + 1, lt2, gt2)
                    emitted += 1
                for lt, gt in enumerate(grp):
                    ty = 0 if gt < NTL else 1
                    gate = mvs[ty][2]
                    r_mv = mvs[ty][3]
                    ar = [ares[i] for i, (a, b_) in enumerate(blocks) if a < (lt + 1) * 128 and b_ > lt * 128]
                    b0, r_b0 = B.bank()
                    b1, r_b1 = B.bank()
                    bb = [(b0, r_b0), (b1, r_b1)]
                    for c in range(NFF):
                        for hf in range(2):
                            B.mm(bb[hf][0], actT[:, c, lt * 128:(lt + 1) * 128], wd[:, c, hf * 512:(hf + 1) * 512],
                                 c == 0, c == NFF - 1, ar + [r_wd], [bb[hf][1]])
                    hin, r_hin = hinr.next()
                    B.dma("sp", hin, srcf(gt), [], [r_hin])
                    et, r_et = etmpr.next()
                    for hf in range(2):
                        B.tt("dve", et[:, hf * 512:(hf + 1) * 512], bb[hf][0], gate[:, hf * 512:(hf + 1) * 512], ALU.mult,
                             [r_mv], [bb[hf][1], r_et])
                    ho, r_ho = houtr.next()
                    B.tt("pool", ho, et, hin, ALU.add, [r_et, r_hin], [r_ho])
                    B.dma("sp", dstf(gt), ho, [r_ho], [])

        def head_norm(bank_ap, r_bank, nh, gain_bc, r_gain, rings, name):
            sq, r_sq = rings["sq"].next()
            w = nh * 64
            B.act(sq[:, 0:w], bank_ap, AF.Square, [], [r_bank, r_sq])
            st_, r_st = rings["st"].next()
            B.reduce_sum(st_[:, 0:nh], sq[:, 0:w].rearrange("p (h d) -> p h d", h=nh), [r_sq], [r_st])
            rs = B.rstd(st_, r_st, nh, 1.0 / 64)
            qn, r_qn = rings[name].next()
            qn3 = qn[:, 0:w].rearrange("p (h d) -> p h d", h=nh)
            B.tt("dve", qn3, bank_ap.rearrange("p (h d) -> p h d", h=nh), rs.unsqueeze(2).to_broadcast([128, nh, 64]), ALU.mult,
                 [r_st], [r_bank, r_qn])
            B.tt("pool", qn3, qn3, gain_bc.unsqueeze(1).to_broadcast([128, nh, 64]), ALU.mult, [r_gain, r_qn], [r_qn])
            return qn, r_qn

        def rope(qn, r_qn, nh, ropt, r_ropt, outb, r_out, rings):
            w = nh * 64
            a_, r_a = rings["ra"].next()
            b_, r_b = rings["rb"].next()
            q3 = qn[:, 0:w].rearrange("p (h d) -> p h d", h=nh)
            a3 = a_[:, 0:w].rearrange("p (h d) -> p h d", h=nh)
            B.tt("pool", a3, q3, ropt[:, 0, :].unsqueeze(1).to_broadcast([128, nh, 64]), ALU.mult, [r_qn, r_ropt], [r_a])
            q5 = qn[:, 0:w].rearrange("p (h a s d) -> p h a s d", h=nh, a=2, s=2)
            b5 = b_[:, 0:w].rearrange("p (h a s d) -> p h a s d", h=nh, a=2, s=2)
            s4 = ropt[:, 1, :].rearrange("p (a s d) -> p a s d", a=2, s=2)
            for ax in range(2):
                for s in range(2):
                    B.tt("dve", b5[:, :, ax, s, :], q5[:, :, ax, 1 - s, :],
                         s4[:, ax, s, :].unsqueeze(1).to_broadcast([128, nh, 16]), ALU.mult, [r_qn, r_ropt], [r_b])
            B.tt("dve", outb[:, 0:w], a_[:, 0:w], b_[:, 0:w], ALU.add, [r_a, r_b], [r_out])

        def head_transposes(src, r_src, nh, dst_dram, gt, rings):
            bk, r_bk = B.bank()
            bkb = bk.bitcast(BF16)
            for h in range(nh):
                B.tr(bkb[0:64, h * 128:(h + 1) * 128], src[:, h * 64:(h + 1) * 64], [r_src], [r_bk])
            ts_, r_ts = rings["hT"].next()
            B.cp("act", ts_[:, 0:nh, :], bkb[0:64, 0:nh * 128].rearrange("p (h t) -> p h t", h=nh), [], [r_bk, r_ts])
            B.dma("sp", dst_dram.rearrange("h d t -> d h t")[:, 0:nh, gt * 128:(gt + 1) * 128], ts_[:, 0:nh, :], [r_ts], [])

        def phase_even_prep(li):
            B.phase_begin()
            mvs = load_mod_vecs(li, 3, 4, None, 1, 1.0)
            win = A.alloc([8, 1792], BF16)
            r_win = S.res()
            B.dma("pool", win, I["ev_w_in"].rearrange("(k p) n -> p k n", p=128), [], [r_win])
            qg_bc, r_qg = B.load_bc("sp", I["a_q_gain"][0:1, :], 64)
            kg_bc, r_kg = B.load_bc("sp", I["a_k_gain"][0:1, :], 64)
            vg_bc, r_vg = B.load_bc("sp", I["b_v_gain"][0:1, :], 512)
            hinr = Ring(B, 3, [D], F32)
            rings = {"sqj": (A.alloc([D], BF16), S.res()), "st": Ring(B, 6, [30], F32),
                     "z1": Ring(B, 2, [D], F32), "ztok": Ring(B, 4, [D], BF16),
                     "sq": Ring(B, 3, [512], F32), "qn": Ring(B, 2, [512], F32), "kn": Ring(B, 2, [128], F32),
                     "gv": Ring(B, 2, [512], F32),
                     "ra": Ring(B, 2, [512], F32), "rb": Ring(B, 2, [512], F32), "hT": Ring(B, 3, [8, 128], BF16, parts=64)}
            zTr = Ring(B, 3, [8, 128], BF16)
            ropr = Ring(B, 5, [2, 64], F32)
            qbr = Ring(B, 4, [512], BF16)
            kbr = Ring(B, 4, [128], BF16)
            vbr = Ring(B, 3, [128], BF16)
            ur = Ring(B, 3, [512], F32)
            vvr = Ring(B, 3, [512], BF16)
            nsl = [(0, 512), (512, 768), (768, 1280), (1280, 1792)]

            def tile_gen(gt):
                ty = 0 if gt < NTL else 1
                hin, r_hin = hinr.next()
                B.dma("sp", hin, hsrc(gt), [], [r_hin])
                if ty == 0:
                    rt, r_rt = ropr.next()
                    B.dma("sp", rt, I["k_rope"][gt * 128:(gt + 1) * 128, :, :], [], [r_rt])
                zt, r_zt = norm_tile(hin, r_hin, mvs[ty], rings)
                yield
                zT, r_zT = zTr.next()
                transpose8(zt, r_zt, zT, r_zT)
                bks = [B.bank() for _ in range(4)]
                for k in range(8):
                    for i, (n0, n1) in enumerate(nsl):
                        B.mm(bks[i][0][:, 0:n1 - n0], zT[:, k, :], win[:, k, n0:n1], k == 0, k == 7, [r_zT, r_win], [bks[i][1]])
                (bq, r_bq), (bkv, r_bkv), (bbu, r_bbu), (bbv, r_bbv) = bks
                yield
                qn, r_qn = head_norm(bq, r_bq, 8, qg_bc, r_qg, rings, "qn")
                kn, r_kn = head_norm(bkv[:, 0:128], r_bkv, 2, kg_bc, r_kg, rings, "kn")
                qb, r_qb = qbr.next()
                kb, r_kb = kbr.next()
                if ty == 0:
                    rope(qn, r_qn, 8, rt, r_rt, qb, r_qb, rings)
                    rope(kn, r_kn, 2, rt, r_rt, kb, r_kb, rings)
                else:
                    B.cp("dve", qb, qn, [r_qn], [r_qb])
                    B.cp("dve", kb, kn[:, 0:128], [r_kn], [r_kb])
                vb, r_vb = vbr.next()
                B.cp("act", vb, bkv[:, 128:256], [], [r_bkv, r_vb])
                B.dma("sp", v_d[gt * 128:(gt + 1) * 128, 0:128], vb, [r_vb], [])
                u_, r_u = ur.next()
                B.act(u_, bbu, AF.Gelu_apprx_tanh, [], [r_bbu, r_u])
                B.dma("sp", u_d[gt * 128:(gt + 1) * 128, :], u_, [r_u], [])
                gv, r_gv = rings["gv"].next()
                B.act(gv, bbv, AF.Gelu_apprx_tanh, [], [r_bbv, r_gv])
                sq, r_sq = rings["sq"].next()
                B.act(sq, gv, AF.Square, [r_gv], [r_sq])
                st_, r_st = rings["st"].next()
                B.reduce_sum(st_[:, 0:8], sq.rearrange("p (h d) -> p h d", h=8), [r_sq], [r_st])
                rs = B.rstd(st_, r_st, 8, 1.0 / 64)
                gv3 = gv.rearrange("p (h d) -> p h d", h=8)
                B.tt("dve", gv3, gv3, rs.unsqueeze(2).to_broadcast([128, 8, 64]), ALU.mult, [r_st, r_gv], [r_gv])
                vv, r_vv = vvr.next()
                B.tt("pool", vv, gv, vg_bc, ALU.mult, [r_gv, r_vg], [r_vv])
                B.dma("sp", vv_d[gt * 128:(gt + 1) * 128, :], vv, [r_vv], [])
                yield
                head_transposes(qb, r_qb, 8, qT_d, gt, rings)
                head_transposes(kb, r_kb, 2, kT_d, gt, rings)

            pipeline(tile_gen, range(NT))

        def outproj_residual(mix, r_mix, wout, r_wout, gate, r_gate, gt, rings):
            mT, r_mT = rings["mixT"].next()
            transpose8(mix, r_mix, mT, r_mT)
            b0, r_b0 = B.bank()
            b1, r_b1 = B.bank()
            bb = [(b0, r_b0), (b1, r_b1)]
            for k in range(8):
                for hf in range(2):
                    B.mm(bb[hf][0], mT[:, k, :], wout[:, k, hf * 512:(hf + 1) * 512], k == 0, k == 7, [r_mT, r_wout], [bb[hf][1]])
            hin, r_hin = rings["hin"].next()
            B.dma("sp", hin, hsrc(gt), [], [r_hin])
            et, r_et = rings["et"].next()
            for hf in range(2):
                B.tt("dve", et[:, hf * 512:(hf + 1) * 512], bb[hf][0], gate[:, hf * 512:(hf + 1) * 512], ALU.mult,
                     [r_gate], [bb[hf][1], r_et])
            ho, r_ho = rings["hout"].next()
            B.tt("pool", ho, et, hin, ALU.add, [r_et, r_hin], [r_ho])
            B.dma("sp", hsrc(gt), ho, [r_ho], [])

        def load_V(dst, r_dst, kt0, nw, nk):
            for w in range(nw):
                B.dma("sp", dst[:, w, :, 0:64],
                      v_d[(kt0 + w) * 128:(kt0 + w + 1) * 128, 0:nk * 64].rearrange("p (k d) -> p k d", k=nk), [], [r_dst])

        def phase_even_attn(li):
            B.phase_begin()
            mvs = load_mod_vecs(li, None, None, 5, 1, 1.0)
            wout = A.alloc([8, D], BF16)
            r_wout = S.res()
            B.dma("pool", wout, I["ev_w_out"].rearrange("(k p) n -> p k n", p=128), [], [r_wout])
            wsn = A.alloc([8, 128], BF16)
            r_wsn = S.res()
            B.dma("pool", wsn, I["b_ws"].rearrange("g i j -> i g j"), [], [r_wsn])
            wsT = A.alloc([8, 128], BF16)
            r_wsT = S.res()
            bk, r_bk = B.bank()
            bkb = bk.bitcast(BF16)
            for g in range(8):
                B.tr(bkb[:, g * 128:(g + 1) * 128], wsn[:, g, :], [r_wsn], [r_bk])
            B.cp("act", wsT, bkb.rearrange("p (g t) -> p g t", g=8), [], [r_bk, r_wsT])
            bias_sb = A.alloc([128], F32, parts=8)
            r_bsb = S.res()
            B.dma("sp", bias_sb, I["b_bias"][:, :], [], [r_bsb])
            biasT = A.alloc([8], F32)
            r_biasT = S.res()
            bk, r_bk = B.bank()
            B.mm(bk[:, 0:8], bias_sb, B.identf[0:8, 0:8], True, True, [r_bsb, B.r_ident], [r_bk])
            B.cp("dve", biasT, bk[:, 0:8], [], [r_bk, r_biasT])
            esink, r_es = B.load_bc("sp", I["a_sink"][0:1, :], 8)
            B.act(esink, esink, AF.Exp, [r_es], [r_es])
            amask = A.alloc([2, 128], BF16)
            r_am = S.res()
            B.dma("pool", amask, I["k_amask"][:, :, :], [], [r_am])
            kTc = A.alloc([2, 256], BF16, parts=64)
            r_kTc = S.res()
            B.dma("sp", kTc, kT_d.rearrange("h d t -> d h t")[:, 0:2, NTL * 128:NT * 128], [], [r_kTc])
            Vc = A.alloc([2, 2, 65], BF16)
            r_Vc = S.res()
            B.memset("dve", Vc[:, :, :, 64:65], 1.0, [r_Vc])
            load_V(Vc, r_Vc, NTL, 2, 2)
            kTwr = Ring(B, 4, [2, 384], BF16, parts=64)
            Vwr = Ring(B, 5, [3, 2, 65], BF16)
            for vb_, r_ in Vwr.bufs:
                B.memset("dve", vb_[:, :, :, 64:65], 1.0, [r_])
            qTr = Ring(B, 4, [8, 128], BF16, parts=64)
            pTr = Ring(B, 16, [512], BF16)
            etr = Ring(B, 2, [512], BF16)
            mixr = Ring(B, 4, [D], BF16)
            str_ = Ring(B, 4, [8], F32)
            vvr = Ring(B, 5, [512], BF16)
            ur = Ring(B, 5, [512], F32)
            btr = Ring(B, 2, [512], F32)
            rings = {"mixT": Ring(B, 2, [8, 128], BF16), "hin": Ring(B, 2, [D], F32), "et": Ring(B, 1, [D], F32),
                     "hout": Ring(B, 2, [D], F32)}

            def tile_gen(n):
                ty = 0 if n < NTL else 1
                qT, r_qT = qTr.next()
                B.dma("sp", qT, qT_d.rearrange("h d t -> d h t")[:, :, n * 128:(n + 1) * 128], [], [r_qT])
                keys = []
                if ty == 0:
                    kt0 = min(max(n - 1, 0), NTL - 3)
                    kTw, r_kTw = kTwr.next()
                    B.dma("sp", kTw, kT_d.rearrange("h d t -> d h t")[:, 0:2, kt0 * 128:(kt0 + 3) * 128], [], [r_kTw])
                    Vw, r_Vw = Vwr.next()
                    load_V(Vw, r_Vw, kt0, 3, 2)
                    for kt, mk in ((n - 1, 0), (n, None), (n + 1, 1)):
                        if 0 <= kt < NTL:
                            s_ = kt - kt0
                            keys.append((kTw, s_, Vw, s_, mk, [r_kTw], [r_Vw]))
                for s_ in range(2):
                    keys.append((kTc, s_, Vc, s_, None, [r_kTc], [r_Vc]))
                vv, r_vv = vvr.next()
                B.dma("sp", vv, vv_d[n * 128:(n + 1) * 128, :], [], [r_vv])
                u_, r_u = ur.next()
                B.dma("sp", u_, u_d[n * 128:(n + 1) * 128, :], [], [r_u])
                yield

                def qk(kv):
                    pts = []
                    for (kTa, ks, Va, vs, mk, rk, rv) in keys:
                        bk, r_bk = B.bank()
                        B.mm(bk.rearrange("p (h q) -> p h q", h=4), kTa[:, kv, ks * 128:(ks + 1) * 128], qT[:, 4 * kv:4 * kv + 4, :],
                             True, True, rk + [r_qT], [r_bk])
                        pT, r_pT = pTr.next()
                        if mk is None:
                            B.act(pT, bk, AF.Exp, [], [r_bk, r_pT], scale=0.125)
                        else:
                            et, r_et = etr.next()
                            B.act(et, bk, AF.Exp, [], [r_bk, r_et], scale=0.125)
                            B.tt("dve", pT.rearrange("p (h q) -> p h q", h=4), et.rearrange("p (h q) -> p h q", h=4),
                                 amask[:, mk, :].unsqueeze(1).to_broadcast([128, 4, 128]), ALU.mult, [r_et, r_am], [r_pT])
                        pts.append((pT, r_pT, Va, vs, rv))
                    return pts

                def pv(pkv, pts, mix, r_mix):
                    ob, r_ob = B.bank()
                    for hh in range(4):
                        for ei, (pT, r_pT, Va, vs, rv) in enumerate(pts):
                            B.mm(ob[:, hh * 65:(hh + 1) * 65], pT[:, hh * 128:(hh + 1) * 128], Va[:, vs, pkv, :],
                                 ei == 0, ei == len(pts) - 1, [r_pT] + rv, [r_ob])
                    ob3 = ob[:, 0:260].rearrange("p (h e) -> p h e", h=4)
                    sd, r_sd = str_.next()
                    B.tt("dve", sd[:, 0:4], ob3[:, :, 64], esink[:, 4 * pkv:4 * pkv + 4], ALU.add, [r_es], [r_ob, r_sd])
                    B.recip(sd[:, 4:8], sd[:, 0:4], [r_sd], [r_sd])
                    B.tt("dve", mix[:, pkv * 256:(pkv + 1) * 256].rearrange("p (h d) -> p h d", h=4), ob3[:, :, 0:64],
                         sd[:, 4:8].unsqueeze(2).to_broadcast([128, 4, 64]), ALU.mult, [r_sd], [r_ob, r_mix])

                pts0 = qk(0)
                yield
                pts1 = qk(1)
                mix, r_mix = mixr.next()
                pv(0, pts0, mix, r_mix)
                yield
                pv(1, pts1, mix, r_mix)
                bk, r_bk = B.bank()
                for g in range(8):
                    B.mm(bk[:, g * 64:(g + 1) * 64], wsT[:, g, :], vv[:, g * 64:(g + 1) * 64], True, True, [r_wsT, r_vv], [r_bk])
                bt, r_bt = btr.next()
                B.tt("dve", bt.rearrange("p (g d) -> p g d", g=8), bk.rearrange("p (g d) -> p g d", g=8),
                     biasT.unsqueeze(2).to_broadcast([128, 8, 64]), ALU.add, [r_biasT], [r_bk, r_bt])
                B.tt("pool", mix[:, 512:1024], bt, u_, ALU.mult, [r_bt, r_u], [r_mix])
                yield
                outproj_residual(mix, r_mix, wout, r_wout, mvs[ty][2], mvs[ty][3], n, rings)

            pipeline(tile_gen, range(NT))

        def phase_odd_prep(li):
            B.phase_begin()
            mvs = load_mod_vecs(li, 3, 4, None, 1, 1.0)
            win = A.alloc([8, 2048], BF16)
            r_win = S.res()
            for hf in range(2):
                B.dma("pool", win[:, :, hf * 1024:(hf + 1) * 1024],
                      I["od_w_in"][:, hf * 1024:(hf + 1) * 1024].rearrange("(k p) n -> p k n", p=128), [], [r_win])
            qg_bc, r_qg = B.load_bc("sp", I["d_q_gain"][0:1, :], 64)
            kg_bc, r_kg = B.load_bc("sp", I["d_k_gain"][0:1, :], 64)
            hinr = Ring(B, 3, [D], F32)
            rings = {"sqj": (A.alloc([D], BF16), S.res()), "st": Ring(B, 6, [30], F32),
                     "z1": Ring(B, 2, [D], F32), "ztok": Ring(B, 4, [D], BF16),
                     "sq": Ring(B, 3, [512], F32), "qn": Ring(B, 2, [512], F32), "kn": Ring(B, 2, [512], F32),
                     "hT": Ring(B, 3, [8, 128], BF16, parts=64)}
            zTr = Ring(B, 3, [8, 128], BF16)
            qbr = Ring(B, 4, [512], BF16)
            kbr = Ring(B, 4, [512], BF16)
            vbr = Ring(B, 3, [512], BF16)
            xbr = Ring(B, 3, [512], BF16)

            def tile_gen(gt):
                ty = 0 if gt < NTL else 1
                hin, r_hin = hinr.next()
                B.dma("sp", hin, hsrc(gt), [], [r_hin])
                zt, r_zt = norm_tile(hin, r_hin, mvs[ty], rings)
                yield
                zT, r_zT = zTr.next()
                transpose8(zt, r_zt, zT, r_zT)
                nbs = [0, 1, 2, 3] if ty == 0 else [2, 3]
                bks = {i: B.bank() for i in nbs}
                for k in range(8):
                    for i in nbs:
                        B.mm(bks[i][0], zT[:, k, :], win[:, k, i * 512:(i + 1) * 512], k == 0, k == 7, [r_zT, r_win], [bks[i][1]])
                yield
                qb = r_qb = None
                if ty == 0:
                    xb, r_xb = xbr.next()
                    B.cp("act", xb, bks[0][0], [], [bks[0][1], r_xb])
                    B.dma("sp", xp_d[gt * 128:(gt + 1) * 128, :], xb, [r_xb], [])
                    qn, r_qn = head_norm(bks[1][0], bks[1][1], 8, qg_bc, r_qg, rings, "qn")
                    qb, r_qb = qbr.next()
                    B.cp("dve", qb, qn, [r_qn], [r_qb])
                kn, r_kn = head_norm(bks[2][0], bks[2][1], 8, kg_bc, r_kg, rings, "kn")
                kb, r_kb = kbr.next()
                B.cp("dve", kb, kn, [r_kn], [r_kb])
                vb, r_vb = vbr.next()
                B.cp("act", vb, bks[3][0], [], [bks[3][1], r_vb])
                B.dma("sp", v_d[gt * 128:(gt + 1) * 128, :], vb, [r_vb], [])
                yield
                if ty == 0:
                    head_transposes(qb, r_qb, 8, qT_d, gt, rings)
                head_transposes(kb, r_kb, 8, kT_d, gt, rings)

            pipeline(tile_gen, range(NT))

        def phase_odd_attn(li):
            B.phase_begin()
            mvs = load_mod_vecs(li, None, None, 5, 1, 1.0, ntypes=1)
            wout = A.alloc([8, D], BF16)
            r_wout = S.res()
            B.dma("pool", wout, I["od_w_out"].rearrange("(k p) n -> p k n", p=128), [], [r_wout])
            wpool = A.alloc([4, 128], BF16)
            r_wpool = S.res()
            B.dma("pool", wpool, I["c_w_pool"].rearrange("g c d -> c g d"), [], [r_wpool])
            csc, r_csc = B.load_bc("sp", I["c_scale"][0:1, :], 512)
            band = A.alloc([4, 5, 128], BF16)
            r_band = S.res()
            B.dma("pool", band, I["k_band"][:, :, :, :], [], [r_band])
            kTc = A.alloc([8, 256], BF16, parts=64)
            r_kTc = S.res()
            B.dma("sp", kTc, kT_d.rearrange("h d t -> d h t")[:, :, NTL * 128:NT * 128], [], [r_kTc])
            Vc = A.alloc([2, 8, 65], BF16)
            r_Vc = S.res()
            B.memset("dve", Vc[:, :, :, 64:65], 1.0, [r_Vc])
            load_V(Vc, r_Vc, NTL, 2, 8)
            biasr = Ring(B, 2, [8, 7, 128], F32)
            for bb_, r_ in biasr.bufs:
                B.memset("pool", bb_[:, :, 5:7, :], 0.0, [r_])
            kTwr = Ring(B, 3, [8, 640], BF16, parts=64)
            Vwr = Ring(B, 3, [5, 8, 65], BF16)
            for vb_, r_ in Vwr.bufs:
                B.memset("dve", vb_[:, :, :, 64:65], 1.0, [r_])
            qTr = Ring(B, 3, [8, 128], BF16, parts=64)
            xpr = Ring(B, 2, [3, 512], BF16)
            ppr = Ring(B, 2, [4, 128], BF16)
            tAr = Ring(B, 2, [512], F32)
            tBr = Ring(B, 2, [384], F32)
            pAr = Ring(B, 3, [512], BF16)
            pBr = Ring(B, 3, [384], BF16)
            mixr = Ring(B, 5, [D], BF16)
            str_ = Ring(B, 4, [8], F32)
            rings = {"mixT": Ring(B, 2, [8, 128], BF16), "hin": Ring(B, 2, [D], F32), "et": Ring(B, 1, [D], F32),
                     "hout": Ring(B, 2, [D], F32)}
            state = {"case": None, "bias": None, "r_bias": None}
            B.nrr = 6

            def case_of(n):
                return 0 if n == 0 else 1 if n == 1 else 3 if n == NTL - 2 else 4 if n == NTL - 1 else 2

            def tile_gen(n):
                mix, r_mix = mixr.next()
                case = case_of(n)
                if case != state["case"]:
                    bias_, r_bias_ = biasr.next()
                    B.dma("sp", bias_[:, :, 0:5, :], I["k_dbias"][case], [], [r_bias_])
                    state["case"] = case
                    state["bias"] = bias_
                    state["r_bias"] = r_bias_
                bias = state["bias"]
                r_bias = state["r_bias"]
                kt0 = min(max(n - 2, 0), NTL - 5)
                kTw, r_kTw = kTwr.next()
                B.dma("sp", kTw, kT_d.rearrange("h d t -> d h t")[:, :, kt0 * 128:(kt0 + 5) * 128], [], [r_kTw])
                Vw, r_Vw = Vwr.next()
                load_V(Vw, r_Vw, kt0, 5, 8)
                qT, r_qT = qTr.next()
                B.dma("sp", qT, qT_d.rearrange("h d t -> d h t")[:, :, n * 128:(n + 1) * 128], [], [r_qT])
                xw, r_xw = xpr.next()
                jts = [j for j in (n - 1, n, n + 1) if 0 <= j < NTL]
                j0 = jts[0]
                B.dma("sp", xw[:, 0:len(jts), :], xp_d[j0 * 128:(j0 + len(jts)) * 128, :].rearrange("(w p) f -> p w f", p=128), [], [r_xw])
                bk, r_bk = B.bank()
                for g in range(4):
                    for ji, j in enumerate(jts):
                        if j == n - 1:
                            typ = 0
                        elif j == n + 1:
                            typ = 2
                        else:
                            typ = 3 if n == 0 else (4 if n == NTL - 1 else 1)
                        B.mm(bk[:, g * 128:(g + 1) * 128], xw[:, ji, g * 128:(g + 1) * 128], band[:, g, typ, :],
                             ji == 0, ji == len(jts) - 1, [r_xw, r_band], [r_bk])
                pp, r_pp = ppr.next()
                B.cp("act", pp, bk.rearrange("p (g t) -> p g t", g=4), [], [r_bk, r_pp])
                bk2, r_bk2 = B.bank()
                for g in range(4):
                    B.mm(bk2[:, g * 128:(g + 1) * 128], pp[:, g, :], wpool[:, g, :], True, True, [r_pp, r_wpool], [r_bk2])
                B.tt("dve", mix[:, 0:512], bk2, csc, ALU.mult, [r_csc], [r_bk2, r_mix])
                yield

                def heads(hq):
                    pend = None
                    ob, r_ob = B.bank_fixed(6 + hq)
                    for hi in range(5):
                        cur = None
                        if hi < 4:
                            h = hq * 4 + hi
                            bA, r_bA = B.bank()
                            bB, r_bB = B.bank()
                            for s_ in range(4):
                                B.mm(bA[:, s_ * 128:(s_ + 1) * 128], kTw[:, h, s_ * 128:(s_ + 1) * 128], qT[:, h, :], True, True, [r_kTw, r_qT], [r_bA])
                            B.mm(bB[:, 0:128], kTw[:, h, 512:640], qT[:, h, :], True, True, [r_kTw, r_qT], [r_bB])
                            for s_ in range(2):
                                B.mm(bB[:, (1 + s_) * 128:(2 + s_) * 128], kTc[:, h, s_ * 128:(s_ + 1) * 128], qT[:, h, :], True, True, [r_kTc, r_qT], [r_bB])
                            tA, r_tA = tAr.next()
                            tB, r_tB = tBr.next()
                            B.stt("dve", tA, bA, 0.125, bias[:, h, 0:4, :].rearrange("p s q -> p (s q)"), ALU.mult, ALU.add, [r_bias], [r_bA, r_tA])
                            B.stt("dve", tB, bB[:, 0:384], 0.125, bias[:, h, 4:7, :].rearrange("p s q -> p (s q)"), ALU.mult, ALU.add, [r_bias], [r_bB, r_tB])
                            pA, r_pA = pAr.next()
                            pB, r_pB = pBr.next()
                            B.act(pA, tA, AF.Exp, [r_tA], [r_pA])
                            B.act(pB, tB, AF.Exp, [r_tB], [r_pB])
                            cur = (h, pA, r_pA, pB, r_pB)
                        if pend is not None:
                            ph, pA, r_pA, pB, r_pB = pend
                            osl = ob[:, (ph % 4) * 65:(ph % 4 + 1) * 65]
                            for s_ in range(7):
                                if s_ < 4:
                                    lhs = pA[:, s_ * 128:(s_ + 1) * 128]
                                    rp = r_pA
                                else:
                                    lhs = pB[:, (s_ - 4) * 128:(s_ - 3) * 128]
                                    rp = r_pB
                                if s_ < 5:
                                    rhs = Vw[:, s_, ph, :]
                                    rv = r_Vw
                                else:
                                    rhs = Vc[:, s_ - 5, ph, :]
                                    rv = r_Vc
                                B.mm(osl, lhs, rhs, s_ == 0, s_ == 6, [rp, rv], [r_ob])
                        pend = cur
                    ob3 = ob[:, 0:260].rearrange("p (h e) -> p h e", h=4)
                    sd, r_sd = str_.next()
                    B.recip(sd[:, 0:4], ob3[:, :, 64], [], [r_ob, r_sd])
                    B.tt("dve", mix[:, 512 + hq * 256:512 + (hq + 1) * 256].rearrange("p (h d) -> p h d", h=4), ob3[:, :, 0:64],
                         sd[:, 0:4].unsqueeze(2).to_broadcast([128, 4, 64]), ALU.mult, [r_sd], [r_ob, r_mix])

                heads(0)
                yield
                heads(1)
                yield
                outproj_residual(mix, r_mix, wout, r_wout, mvs[0][2], mvs[0][3], n, rings)

            pipeline(tile_gen, range(NTL), drain_before=lambda n: case_of(n) != state["case"])
            B.nrr = 8

        plist = [
            lambda: phase_mod(0),
            lambda: phase_ffn(0, 0, NT, src0, hsrc),
            lambda: phase_even_prep(0),
            lambda: phase_even_attn(0),
            lambda: phase_ffn(0, 1, NT, hsrc, hsrc),
            lambda: phase_mod(1),
            lambda: phase_ffn(1, 0, NT, hsrc, hsrc),
            lambda: phase_odd_prep(1),
            lambda: phase_odd_attn(1),
            lambda: phase_ffn(1, 1, NTL, hsrc, osrc),
        ]
        if dbg_phases is not None:
            plist = plist[:dbg_phases]
        for p in plist:
            p()
        S.emit()
        build_program.stats = S.stats
    return nc


def _rope_table():
    t = np.arange(S_LAT)
    row = (t // 64).astype(np.float32)
    col = (t % 64).astype(np.float32)
    m = 16
    inv = (1.0 / (10000.0 ** (np.arange(m, dtype=np.float32) / m))).astype(np.float32)
    ar = row[:, None] * inv[None, :]
    ac = col[:, None] * inv[None, :]
    cos = np.concatenate([np.cos(ar), np.cos(ar), np.cos(ac), np.cos(ac)], axis=1)
    sin = np.concatenate([-np.sin(ar), np.sin(ar), -np.sin(ac), np.sin(ac)], axis=1)
    return np.stack([cos, sin], axis=1).astype(np.float32)


def _amask():
    pj = np.arange(128)[:, None]
    pi = np.arange(128)[None, :]
    prev = (pj >= pi).astype(np.float32)
    nxt = (pj <= pi).astype(np.float32)
    return np.stack([prev, nxt], axis=1)


def _band():
    out = np.zeros((128, 4, 5, 128), np.float32)
    for gi, w in enumerate((2, 4, 8, 16)):
        def mat(n, jn):
            tg = n * 128 + np.arange(128)
            lo = np.clip(tg - w // 2, 0, S_LAT)
            hi = np.clip(tg + w - w // 2, 0, S_LAT)
            cnt = (hi - lo).astype(np.float32)
            jg = jn * 128 + np.arange(128)
            m = ((jg[:, None] >= lo[None, :]) & (jg[:, None] < hi[None, :])).astype(np.float32) / cnt[None, :]
            m = m - (jg[:, None] == tg[None, :]).astype(np.float32)
            return m
        out[:, gi, 0] = mat(5, 4)
        out[:, gi, 1] = mat(5, 5)
        out[:, gi, 2] = mat(5, 6)
        out[:, gi, 3] = mat(0, 0)
        out[:, gi, 4] = mat(NTL - 1, NTL - 1)
    return out


def _dbias(rpb):
    out = np.full((5, 128, 8, 5, 128), NEGB, np.float32)
    for case, n in enumerate((0, 1, 5, NTL - 2, NTL - 1)):
        kt0 = min(max(n - 2, 0), NTL - 5)
        i = np.arange(128)
        r = 2 * n + i // 64
        c = i % 64
        r0 = np.clip(r - 4, 0, 56)
        q0 = np.clip(c - 8, 0, 48)
        for s in range(5):
            kt = kt0 + s
            j = np.arange(128)
            kr = 2 * kt + j // 64
            kc = j % 64
            valid = ((kr[:, None] >= r0[None, :]) & (kr[:, None] < r0[None, :] + 8) &
                     (kc[:, None] >= q0[None, :]) & (kc[:, None] < q0[None, :] + 16))
            ri = np.clip(kr[:, None] - r[None, :] + 7, 0, 14)
            ci = np.clip(kc[:, None] - c[None, :] + 15, 0, 30)
            g = rpb[:, ri, ci]
            g = np.where(valid[None], g, np.float32(NEGB))
            out[case, :, :, s, :] = np.transpose(g, (1, 0, 2))
    return out


_CACHE = {}


def kernel(x, c, ctx, c_ctx, ada_w, ada_b, norm_g, ffn_w_gu, ffn_w_down,
           ev_w_in, ev_w_out, a_q_gain, a_k_gain, a_sink, b_v_gain, b_ws, b_bias,
           od_w_in, od_w_out, c_w_pool, c_scale, d_q_gain, d_k_gain, d_rpb, _dbg_phases=None, _dbg=False):
    f = lambda a: np.ascontiguousarray(np.asarray(a, dtype=np.float32))
    key = (_dbg_phases, _dbg)
    if key not in _CACHE:
        _CACHE[key] = build_program(_dbg_phases, _dbg)
    nc = _CACHE[key]
    shared = {
        "c_ctx": f(c_ctx).reshape(1, D), "ada_w": f(ada_w), "ada_b": f(ada_b), "norm_g": f(norm_g),
        "ffn_w_gu": f(ffn_w_gu), "ffn_w_down": f(ffn_w_down),
        "ev_w_in": f(ev_w_in)[0], "ev_w_out": f(ev_w_out)[0],
        "a_q_gain": f(a_q_gain), "a_k_gain": f(a_k_gain), "a_sink": f(a_sink),
        "b_v_gain": f(b_v_gain), "b_ws": f(b_ws)[0], "b_bias": f(b_bias)[0],
        "od_w_in": f(od_w_in)[0], "od_w_out": f(od_w_out)[0],
        "c_w_pool": f(c_w_pool)[0], "c_scale": f(c_scale),
        "d_q_gain": f(d_q_gain), "d_k_gain": f(d_k_gain),
        "k_ident": np.eye(128, dtype=np.float32), "k_rope": _rope_table(), "k_amask": _amask(),
        "k_band": _band(), "k_dbias": _dbias(f(d_rpb)[0]),
    }
    x = f(x); c = f(c); ctx = f(ctx)
    in_maps = []
    for b in range(8):
        m = dict(shared)
        m["x"] = x[b]
        m["ctx"] = ctx[b]
        m["c"] = c[b].reshape(1, D)
        in_maps.append(m)
    res = run_bass_kernel_spmd(nc, in_maps, core_ids=list(range(8)))
    kernel.last = res
    return np.stack([r["out"] for r in res.results], axis=0)
```

```python
import contextlib
import numpy as np
import concourse.bass as bass
import concourse.mybir as mybir
from concourse.bass_utils import run_bass_kernel_spmd

F32 = mybir.dt.float32
BF16 = mybir.dt.bfloat16
AF = mybir.ActivationFunctionType
ALU = mybir.AluOpType
AX = mybir.AxisListType

D = 1024
S_LAT = 4096
S_CTX = 256
NTL = 32
NT = 34
DFF = 2816
NFF = 22
EPS = 1e-6
NEGB = -30000.0

ENGS = ("sp", "act", "pool", "dve", "pe")
NDMA_SEM = 8


class Res:
    __slots__ = ("name", "last_w", "readers", "gen")

    def __init__(self, name):
        self.name = name
        self.last_w = None
        self.readers = {}
        self.gen = 0

    def bump(self):
        self.gen += 1
        return Ref(self, self.gen)


class Ref:
    __slots__ = ("phys", "gen")

    def __init__(self, phys, gen):
        self.phys = phys
        self.gen = gen


def _norm_res(lst):
    out = []
    for r in lst:
        if isinstance(r, Ref):
            assert r.gen == r.phys.gen, f"stale buffer reference {r.phys.name}"
            r = r.phys
        out.append(r)
    return out


def pipeline(make_gen, items, drain_before=None):
    active = []
    for it in items:
        if drain_before is not None and drain_before(it):
            while active:
                nxt = []
                for g in active:
                    try:
                        next(g)
                        nxt.append(g)
                    except StopIteration:
                        pass
                active = nxt
        nxt = []
        for g in active:
            try:
                next(g)
                nxt.append(g)
            except StopIteration:
                pass
        active = nxt
        g = make_gen(it)
        try:
            next(g)
            active.append(g)
        except StopIteration:
            pass
    while active:
        nxt = []
        for g in active:
            try:
                next(g)
                nxt.append(g)
            except StopIteration:
                pass
        active = nxt


class Op:
    __slots__ = ("eng", "fn", "deps", "dma", "signal", "sem", "val", "prewait", "bg")

    def __init__(self, eng, fn, dma):
        self.eng = eng
        self.fn = fn
        self.dma = dma
        self.deps = []
        self.signal = False
        self.sem = None
        self.val = 0
        self.prewait = None
        self.bg = False


class Sched:
    def __init__(self, nc):
        self.nc = nc
        self.ops = {e: [] for e in ENGS}
        self.bar = {}
        self.nres = 0

    def res(self, name=None):
        self.nres += 1
        return Res(name or f"r{self.nres}")

    def op(self, eng, fn, reads=(), writes=(), dma=False):
        reads = _norm_res(reads)
        writes = _norm_res(writes)
        o = Op(eng, fn, dma)
        deps = {}
        for r in reads:
            if r.last_w is not None:
                deps[id(r.last_w)] = r.last_w
        for r in writes:
            if r.last_w is not None:
                deps[id(r.last_w)] = r.last_w
            for rd in r.readers.values():
                if isinstance(rd, list):
                    for x in rd:
                        deps[id(x)] = x
                else:
                    deps[id(rd)] = rd
        for r in reads:
            if dma:
                r.readers.setdefault(("dma", eng), []).append(o)
            else:
                r.readers[eng] = o
        for r in writes:
            r.last_w = o
            r.readers = {}
        b = self.bar.pop(eng, None)
        if b:
            for x in b:
                deps[id(x)] = x
        dl = []
        for d in deps.values():
            if d is o:
                continue
            if (not dma) and (not d.dma) and d.eng == "pe" and eng == "pe":
                continue
            dl.append(d)
        o.deps = dl
        self.ops[eng].append(o)
        return o

    def dma(self, q, out, in_, reads=(), writes=(), **kw):
        return self.op(q, lambda e: e.dma_start(out=out, in_=in_, **kw), reads, writes, dma=True)

    def barrier(self):
        tails = []
        for e in ENGS:
            ops = self.ops[e]
            for o in reversed(ops):
                if not o.dma:
                    tails.append(o)
                    break
            cnt = 0
            for o in reversed(ops):
                if o.dma:
                    if not o.bg:
                        tails.append(o)
                    cnt += 1
                    if cnt >= NDMA_SEM:
                        break
        self.bar = {e: list(tails) for e in ENGS}

    def emit(self):
        nc = self.nc
        with contextlib.ExitStack() as st:
            csem = {e: st.enter_context(nc.semaphore(f"c_{e}")) for e in ENGS}
            dsem = {e: [st.enter_context(nc.semaphore(f"d_{e}{i}")) for i in range(NDMA_SEM)]
                    for e in ("sp", "act", "pool")}
            for e in ENGS:
                for o in self.ops[e]:
                    for d in o.deps:
                        d.signal = True
            self.stats = {}
            for e in ENGS:
                cnt = 0
                nd = 0
                for o in self.ops[e]:
                    if o.dma:
                        slot = nd % NDMA_SEM
                        o.sem = dsem[e][slot]
                        o.val = 16 * (nd // NDMA_SEM + 1)
                        if nd >= NDMA_SEM:
                            o.prewait = (dsem[e][slot], 16 * (nd // NDMA_SEM))
                        nd += 1
                    elif o.signal:
                        cnt += 1
                        o.sem = csem[e]
                        o.val = cnt
                self.stats[e] = (len(self.ops[e]), cnt, nd)
            block = st.enter_context(nc.Block())

            def run(eng_name, eng):
                seen = {}
                lastdma = {}
                for o in self.ops[eng_name]:
                    waits = []
                    if o.prewait is not None:
                        waits.append(o.prewait)
                    for d in o.deps:
                        waits.append((d.sem, d.val))
                    for sem, val in waits:
                        k = id(sem)
                        if seen.get(k, 0) >= val:
                            continue
                        seen[k] = val
                        eng.wait_ge(sem, val)
                    inst = o.fn(eng)
                    if o.dma:
                        inst.then_inc(o.sem, 16)
                        lastdma[id(o.sem)] = (o.sem, o.val)
                    elif o.signal:
                        inst.then_inc(o.sem, 1)
                for sem, val in lastdma.values():
                    if seen.get(id(sem), 0) < val:
                        eng.wait_ge(sem, val)

            @block.sync
            def _(e):
                run("sp", e)

            @block.scalar
            def _(e):
                run("act", e)

            @block.gpsimd
            def _(e):
                run("pool", e)

            @block.vector
            def _(e):
                run("dve", e)

            @block.tensor
            def _(e):
                run("pe", e)


class Arena:
    def __init__(self, nc, st, nbytes):
        self.nbytes = nbytes
        self.t = st.enter_context(nc.sbuf_tensor("arena", [128, nbytes // 2], BF16))
        self.off = 0
        self.base = 0

    def alloc(self, free, dtype, parts=128):
        free = list(free)
        n = int(np.prod(free))
        sz = n * (4 if dtype == F32 else 2)
        off = (self.off + 63) // 64 * 64
        assert off + sz <= self.nbytes, f"arena overflow {off + sz} > {self.nbytes}"
        self.off = off + sz
        ap = self.t[0:parts, off // 2: (off + sz) // 2]
        if dtype == F32:
            ap = ap.bitcast(F32)
        if len(free) == 2:
            ap = ap.rearrange("p (a b) -> p a b", a=free[0])
        elif len(free) == 3:
            ap = ap.rearrange("p (a b c) -> p a b c", a=free[0], b=free[1])
        elif len(free) == 4:
            ap = ap.rearrange("p (a b c d) -> p a b c d", a=free[0], b=free[1], c=free[2])
        return ap

    def mark_persistent(self):
        self.base = self.off

    def reset(self):
        self.off = self.base


class Ring:
    def __init__(self, B, n, free, dtype, parts=128):
        self.bufs = [(B.A.alloc(free, dtype, parts), B.S.res()) for _ in range(n)]
        self.i = 0

    def next(self):
        r = self.bufs[self.i % len(self.bufs)]
        self.i += 1
        return r[0], r[1].bump()


class Builder:
    def __init__(self, nc, st, dbg):
        self.nc = nc
        self.st = st
        self.S = Sched(nc)
        self.A = Arena(nc, st, 206 * 1024)
        self.banks = []
        for i in range(8):
            t = st.enter_context(nc.psum_tensor(f"bank{i}", [128, 512], F32))
            self.banks.append((t, self.S.res(f"bank{i}")))
        self.bi = 0
        self.nrr = 8
        self.dbg = dbg

    def bank(self):
        r = self.banks[self.bi % self.nrr]
        self.bi += 1
        return r[0][:], r[1].bump()

    def bank_fixed(self, idx):
        r = self.banks[idx]
        return r[0][:], r[1].bump()

    def dma(self, q, out, in_, reads=(), writes=(), **kw):
        return self.S.dma(q, out, in_, reads, writes, **kw)

    def mm(self, out, lhsT, rhs, start, stop, reads, writes):
        return self.S.op("pe", lambda e: e.matmul(out, lhsT=lhsT, rhs=rhs, start=start, stop=stop), reads, writes)

    def tr(self, out, in_, reads, writes):
        idn = self.ident
        return self.S.op("pe", lambda e: e.transpose(out, in_, idn), list(reads) + [self.r_ident], writes)

    def act(self, out, in_, func, reads, writes, scale=None, bias=None, accum=None):
        kw = {}
        if scale is not None:
            kw["scale"] = scale
        if bias is not None:
            kw["bias"] = bias
        if accum is not None:
            kw["accum_out"] = accum
        return self.S.op("act", lambda e: e.activation(out=out, in_=in_, func=func, **kw), reads, writes)

    def tt(self, eng, out, in0, in1, op, reads, writes):
        return self.S.op(eng, lambda e: e.tensor_tensor(out=out, in0=in0, in1=in1, op=op), reads, writes)

    def ts(self, eng, out, in0, s1, s2, op0, op1, reads, writes):
        if op1 is None:
            return self.S.op(eng, lambda e: e.tensor_scalar(out=out, in0=in0, scalar1=s1, scalar2=None, op0=op0), reads, writes)
        return self.S.op(eng, lambda e: e.tensor_scalar(out=out, in0=in0, scalar1=s1, scalar2=s2, op0=op0, op1=op1), reads, writes)

    def stt(self, eng, out, in0, scalar, in1, op0, op1, reads, writes):
        return self.S.op(eng, lambda e: e.scalar_tensor_tensor(out=out, in0=in0, scalar=scalar, in1=in1, op0=op0, op1=op1), reads, writes)

    def cp(self, eng, out, in_, reads, writes):
        if eng == "act":
            return self.S.op("act", lambda e: e.activation(out=out, in_=in_, func=AF.Copy), reads, writes)
        return self.S.op(eng, lambda e: e.tensor_copy(out=out, in_=in_), reads, writes)

    def recip(self, out, in_, reads, writes):
        return self.S.op("dve", lambda e: e.reciprocal(out=out, in_=in_), reads, writes)

    def memset(self, eng, out, val, writes):
        return self.S.op(eng, lambda e: e.memset(out, val), [], writes)

    def reduce_sum(self, out, in_, reads, writes):
        return self.S.op("dve", lambda e: e.tensor_reduce(out=out, in_=in_, axis=AX.X, op=ALU.add), reads, writes)

    def rstd(self, st, r_st, w, inv_n):
        self.ts("dve", st[:, w:2 * w], st[:, 0:w], inv_n, EPS, ALU.mult, ALU.add, [r_st], [r_st])
        self.act(st[:, w:2 * w], st[:, w:2 * w], AF.Sqrt, [r_st], [r_st])
        self.recip(st[:, 2 * w:3 * w], st[:, w:2 * w], [r_st], [r_st])
        return st[:, 2 * w:3 * w]

    def phase_begin(self):
        self.S.barrier()
        self.A.reset()

    def load_bc(self, q, src_1xn, n, name=None):
        t = self.A.alloc([n], F32)
        r = self.S.res(name)
        self.dma(q, t, src_1xn.partition_broadcast(128), [], [r])
        return t, r


def build_program(dbg_phases=None, dbg=False):
    nc = bass.Bass("TRN2", target_bir_lowering=False)
    I = {}

    def inp(name, shape, dt=F32):
        I[name] = nc.dram_tensor(name, list(shape), dt, kind="ExternalInput").ap()
        return I[name]

    inp("x", [S_LAT, D]); inp("ctx", [S_CTX, D]); inp("c", [1, D]); inp("c_ctx", [1, D])
    inp("ada_w", [2, D, 9 * D]); inp("ada_b", [2, 9 * D]); inp("norm_g", [2, 3, D])
    inp("ffn_w_gu", [2, 2, D, 2 * DFF]); inp("ffn_w_down", [2, 2, DFF, D])
    inp("ev_w_in", [D, 1792]); inp("ev_w_out", [D, D])
    inp("a_q_gain", [1, 64]); inp("a_k_gain", [1, 64]); inp("a_sink", [1, 8])
    inp("b_v_gain", [1, 512]); inp("b_ws", [8, 128, 128]); inp("b_bias", [8, 128])
    inp("od_w_in", [D, 2048]); inp("od_w_out", [D, D])
    inp("c_w_pool", [4, 128, 128]); inp("c_scale", [1, 512])
    inp("d_q_gain", [1, 64]); inp("d_k_gain", [1, 64])
    inp("k_ident", [128, 128]); inp("k_rope", [S_LAT, 2, 64]); inp("k_amask", [128, 2, 128])
    inp("k_band", [128, 4, 5, 128]); inp("k_dbias", [5, 128, 8, 5, 128])
    out = nc.dram_tensor("out", [S_LAT, D], F32, kind="ExternalOutput").ap()
    skind = "ExternalOutput" if dbg else "Internal"

    def scr(name, shape, dt):
        return nc.dram_tensor(name, list(shape), dt, kind=skind).ap()

    hA = scr("hA", [NT * 128, D], F32)
    mod_d = scr("mod_d", [2, 2, 9 * D], F32)
    qT_d = scr("qT_d", [8, 64, NT * 128], BF16)
    kT_d = scr("kT_d", [8, 64, NT * 128], BF16)
    v_d = scr("v_d", [NT * 128, 512], BF16)
    u_d = scr("u_d", [NT * 128, 512], F32)
    vv_d = scr("vv_d", [NT * 128, 512], BF16)
    xp_d = scr("xp_d", [S_LAT, 512], BF16)
    wgc_all = [nc.dram_tensor(f"wgc_d{f}", [NFF // 2, 128, 8 * 2 * 256], BF16, kind="Internal").ap() for f in range(4)]
    wdc_all = [nc.dram_tensor(f"wdc_d{f}", [128, NFF * D], BF16, kind="Internal").ap() for f in range(4)]
    adac_d = nc.dram_tensor("adac_d", [18, 128, 8 * 512], BF16, kind="Internal").ap()

    st = contextlib.ExitStack()
    with st:
        B = Builder(nc, st, dbg)
        S, A = B.S, B.A
        B.ident = A.alloc([128], BF16)
        B.r_ident = S.res("ident")
        B.dma("pool", B.ident, I["k_ident"][:, :], [], [B.r_ident])
        B.identf = A.alloc([128], F32)
        B.dma("sp", B.identf, I["k_ident"][:, :], [], [B.r_ident])
        A.mark_persistent()

        bgq = []
        wres = {}

        def bg_add_ffn(f, li, which):
            wgu_ = I["ffn_w_gu"][li, which]
            for cp_ in range(NFF // 2):
                dst4 = wgc_all[f][cp_].rearrange("p (k g n) -> p k g n", k=8, g=2)
                for g_ in range(2):
                    r = S.res()
                    wres[("gu", f, cp_, g_)] = r
                    bgq.append((dst4[:, :, g_, :],
                                wgu_[:, g_ * DFF + cp_ * 256:g_ * DFF + (cp_ + 1) * 256].rearrange("(k p) n -> p k n", p=128), r))
            wd3 = wdc_all[f].rearrange("p (c n) -> p c n", c=NFF)
            for c4 in range(0, NFF, 2):
                r = S.res()
                wres[("wd", f, c4)] = r
                bgq.append((wd3[:, c4:c4 + 2, :],
                            I["ffn_w_down"][li, which, c4 * 128:(c4 + 2) * 128, :].rearrange("(c p) n -> p c n", p=128), r))

        def bg_add_ada(li):
            for nb in range(18):
                r = S.res()
                wres[("ada", li, nb)] = r
                bgq.append((adac_d[nb].rearrange("p (k n) -> p k n", k=8),
                            I["ada_w"][li, :, nb * 512:(nb + 1) * 512].rearrange("(k p) n -> p k n", p=128), r))

        def bg_need(pred):
            last = -1
            for i, (_, _, r) in enumerate(bgq):
                if pred(r):
                    last = i
            if last >= 0:
                bg_tick(last + 1)

        def bg_tick(n=1):
            for _ in range(n):
                if not bgq:
                    return
                dst, src, r = bgq.pop(0)
                o = B.dma("pool", dst, src, [], [r])
                o.bg = True

        bg_add_ffn(1, 0, 1)
        bg_add_ada(1)
        bg_add_ffn(2, 1, 0)
        bg_add_ffn(3, 1, 1)

        def src0(gt):
            if gt < NTL:
                return I["x"][gt * 128:(gt + 1) * 128, :]
            return I["ctx"][(gt - NTL) * 128:(gt - NTL + 1) * 128, :]

        def hsrc(gt):
            return hA[gt * 128:(gt + 1) * 128, :]

        def osrc(gt):
            return out[gt * 128:(gt + 1) * 128, :]

        phases = []

        def phase_mod(li):
            B.phase_begin()
            mine_ = {id(r) for k_, r in wres.items() if k_[0] == "ada" and k_[1] == li}
            bg_need(lambda r: id(r) in mine_)
            cc = A.alloc([2, 128], F32, parts=8)
            r_cc = S.res()
            B.dma("sp", cc[:, 0, :], I["c"][0, :].rearrange("(k p) -> k p", p=128), [], [r_cc])
            B.dma("sp", cc[:, 1, :], I["c_ctx"][0, :].rearrange("(k p) -> k p", p=128), [], [r_cc])
            ccb = A.alloc([2, 128], BF16, parts=8)
            r_ccb = S.res()
            B.act(ccb, cc, AF.Silu, [r_cc], [r_ccb])
            cs = A.alloc([8, 2], BF16)
            r_cs = S.res()
            bk, r_bk = B.bank()
            bkb = bk.bitcast(BF16)
            for j in range(2):
                B.S.op("pe", lambda e, j=j: e.transpose(bkb[:, j * 8:(j + 1) * 8], ccb[:, j, :], B.ident[0:8, 0:8]), [r_ccb, B.r_ident], [r_bk])
            B.cp("dve", cs, bkb[:, 0:16].rearrange("p (j k) -> p k j", j=2), [], [r_bk, r_cs])
            adab = A.alloc([9 * D], F32, parts=2)
            r_adab = S.res()
            B.dma("sp", adab, I["ada_b"][li:li + 1, :].partition_broadcast(2), [], [r_adab])
            msb = A.alloc([9 * D], F32, parts=2)
            r_msb = S.res()
            wr = Ring(B, 3, [8, 512], BF16)
            for nb in range(18):
                w, r_w = wr.next()
                if ("ada", li, nb) in wres:
                    B.dma("sp", w.rearrange("p k n -> p (k n)"), adac_d[nb], [wres[("ada", li, nb)]], [r_w])
                else:
                    B.dma("pool", w, I["ada_w"][li, :, nb * 512:(nb + 1) * 512].rearrange("(k p) n -> p k n", p=128), [], [r_w])
                bk, r_bk = B.bank()
                for k in range(8):
                    B.mm(bk[0:2, :], cs[:, k, :], w[:, k, :], k == 0, k == 7, [r_cs, r_w], [r_bk])
                B.tt("dve", msb[:, nb * 512:(nb + 1) * 512], bk[0:2, :], adab[:, nb * 512:(nb + 1) * 512], ALU.add,
                     [r_adab], [r_bk, r_msb])
            B.dma("sp", mod_d[li], msb, [r_msb], [])

        def load_mod_vecs(li, j_shift, j_scale, j_gate, gi, gate_mul, ntypes=2):
            outl = []
            gbc, r_g = (None, None)
            if j_scale is not None:
                gbc, r_g = B.load_bc("sp", I["norm_g"][li, gi:gi + 1, :], D)
            for ty in range(ntypes):
                r = S.res()
                sh = Gm = gt_ = None
                if j_shift is not None:
                    sh = A.alloc([D], F32)
                    B.dma("sp", sh, mod_d[li, ty:ty + 1, j_shift * D:(j_shift + 1) * D].partition_broadcast(128), [], [r])
                if j_scale is not None:
                    Gm = A.alloc([D], F32)
                    B.dma("sp", Gm, mod_d[li, ty:ty + 1, j_scale * D:(j_scale + 1) * D].partition_broadcast(128), [], [r])
                    B.stt("dve", Gm, Gm, 1.0, gbc, ALU.add, ALU.mult, [r_g, r], [r])
                if j_gate is not None:
                    gt_ = A.alloc([D], F32)
                    B.dma("sp", gt_, mod_d[li, ty:ty + 1, j_gate * D:(j_gate + 1) * D].partition_broadcast(128), [], [r])
                    if gate_mul != 1.0:
                        B.ts("dve", gt_, gt_, gate_mul, None, ALU.mult, None, [r], [r])
                outl.append((sh, Gm, gt_, r))
            return outl

        def norm_tile(hin, r_hin, mv, rings):
            sh, Gm, _, r_mv = mv
            sqj, r_sqj = rings["sqj"]
            st_, r_st = rings["st"].next()
            B.memset("dve", st_[:, 0:1], 0.0, [r_st])
            B.act(sqj, hin, AF.Square, [r_hin, r_st], [r_sqj, r_st], accum=st_[:, 0:1])
            rs = B.rstd(st_, r_st, 1, 1.0 / D)
            z1, r_z1 = rings["z1"].next()
            B.stt("dve", z1, hin, rs, Gm, ALU.mult, ALU.mult, [r_hin, r_st, r_mv], [r_z1])
            zt, r_zt = rings["ztok"].next()
            B.tt("pool", zt, z1, sh, ALU.add, [r_z1, r_mv], [r_zt])
            return zt, r_zt

        def transpose8(src, r_src, dst3, r_dst, eng="act"):
            bk, r_bk = B.bank()
            bkb = bk.bitcast(BF16)
            for k in range(8):
                B.tr(bkb[:, k * 128:(k + 1) * 128], src[:, k * 128:(k + 1) * 128], [r_src], [r_bk])
            B.cp(eng, dst3, bkb.rearrange("p (k t) -> p k t", k=8), [], [r_bk, r_dst])

        def phase_ffn(li, which, ntiles, srcf, dstf):
            B.phase_begin()
            f = li * 2 + which
            cached = ("wd", f, 0) in wres
            if cached:
                mine = {id(r) for k_, r in wres.items() if k_[0] in ("gu", "wd") and k_[1] == f}
                bg_need(lambda r: id(r) in mine)
            wgc_d = wgc_all[f]
            j0 = 0 if which == 0 else 6
            gi = 0 if which == 0 else 2
            mvs = load_mod_vecs(li, j0, j0 + 1, j0 + 2, gi, 0.5, ntypes=2 if ntiles > NTL else 1)
            wd = A.alloc([NFF, D], BF16)
            r_wd = S.res()
            for c4 in range(0, NFF, 2):
                if cached:
                    B.dma("sp", wd[:, c4:c4 + 2, :], wdc_all[f].rearrange("p (c n) -> p c n", c=NFF)[:, c4:c4 + 2, :],
                          [wres[("wd", f, c4)]], [r_wd])
                else:
                    B.dma("pool", wd[:, c4:c4 + 2, :],
                          I["ffn_w_down"][li, which, c4 * 128:(c4 + 2) * 128, :].rearrange("(c p) n -> p c n", p=128), [], [r_wd])
            if ntiles == NT:
                groups = [list(range(0, 9)), list(range(9, 18)), list(range(18, 26)), list(range(26, 34))]
            else:
                groups = [list(range(g * 8, g * 8 + 8)) for g in range(4)]
            wc_res = [S.res() for _ in range(NFF // 2)]
            GM = max(len(g) for g in groups)
            zTs = [A.alloc([8, GM * 128], BF16) for _ in range(2)]
            zress = [[S.res() for _ in range(GM)] for _ in range(2)]
            actT = A.alloc([NFF, GM * 128], BF16)
            wgr = Ring(B, 2, [8, 2, 256], BF16)
            hinr = Ring(B, 2, [D], F32)
            rings = {"sqj": (A.alloc([D], BF16), S.res()), "st": Ring(B, 2, [3], F32),
                     "z1": Ring(B, 1, [D], F32), "ztok": Ring(B, 2, [D], BF16)}
            stmpr = Ring(B, 2, [512], F32)
            etmpr = Ring(B, 1, [D], F32)
            houtr = Ring(B, 1, [D], F32)
            wgu = I["ffn_w_gu"][li, which]

            def stage1_tile(gidx, lt, gt):
                ty = 0 if gt < NTL else 1
                hin, r_hin = hinr.next()
                B.dma("sp", hin, srcf(gt), [], [r_hin])
                zt, r_zt = norm_tile(hin, r_hin, mvs[ty], rings)
                transpose8(zt, r_zt, zTs[gidx % 2][:, :, lt * 128:(lt + 1) * 128], zress[gidx % 2][lt])

            for lt, gt in enumerate(groups[0]):
                stage1_tile(0, lt, gt)
            for gidx, grp in enumerate(groups):
                zT = zTs[gidx % 2]
                zres = zress[gidx % 2]
                T = len(grp) * 128
                nblk = (T + 511) // 512
                bs = T // nblk
                blocks = [(i * bs, (i + 1) * bs if i < nblk - 1 else T) for i in range(nblk)]
                ares = [S.res() for _ in blocks]
                pending = list(enumerate(groups[gidx + 1])) if gidx + 1 < len(groups) else []
                npend = len(pending)
                nunits = NFF * nblk
                unit = 0
                emitted = 0
                for cp_ in range(NFF // 2):
                    wg, r_wg = wgr.next()
                    if cached:
                        B.dma("sp", wg.rearrange("p k g n -> p (k g n)"), wgc_d[cp_],
                              [wres[("gu", f, cp_, 0)], wres[("gu", f, cp_, 1)]], [r_wg])
                    elif gidx == 0:
                        B.dma("pool", wg[:, :, 0, :], wgu[:, cp_ * 256:(cp_ + 1) * 256].rearrange("(k p) n -> p k n", p=128), [], [r_wg])
                        B.dma("pool", wg[:, :, 1, :], wgu[:, DFF + cp_ * 256:DFF + (cp_ + 1) * 256].rearrange("(k p) n -> p k n", p=128), [], [r_wg])
                        B.dma("sp", wgc_d[cp_], wg.rearrange("p k g n -> p (k g n)"), [r_wg], [wc_res[cp_]])
                    else:
                        B.dma("sp", wg.rearrange("p k g n -> p (k g n)"), wgc_d[cp_], [wc_res[cp_]], [r_wg])
                    for ci in range(2):
                        c = cp_ * 2 + ci
                        for bi_, (a, b_) in enumerate(blocks):
                            n = b_ - a
                            zr = [zres[t] for t in range(a // 128, (b_ - 1) // 128 + 1)]
                            bg, r_bg = B.bank()
                            bu, r_bu = B.bank()
                            for k in range(8):
                                B.mm(bg[:, 0:n], wg[:, k, 0, ci * 128:(ci + 1) * 128], zT[:, k, a:b_], k == 0, k == 7, [r_wg] + zr, [r_bg])
                            for k in range(8):
                                B.mm(bu[:, 0:n], wg[:, k, 1, ci * 128:(ci + 1) * 128], zT[:, k, a:b_], k == 0, k == 7, [r_wg] + zr, [r_bu])
                            stp, r_stp = stmpr.next()
                            B.act(stp[:, 0:n], bg[:, 0:n], AF.Silu, [], [r_bg, r_stp])
                            B.tt("dve", actT[:, c, a:b_], stp[:, 0:n], bu[:, 0:n], ALU.mult, [r_stp], [r_bu, ares[bi_]])
                            unit += 1
                            if unit % 4 == 0 and (cached or gidx > 0):
                                bg_tick()
                            while emitted < npend and unit * npend >= (emitted + 1) * int(nunits * 0.7):
                                lt2, gt2 = pending[emitted]
                                stage1_tile(gidx + 1, lt2, gt2)
                                emitted += 1
                while emitted < npend:
                    lt2, gt2 = pending[emitted]
                    stage1_tile(gidx + 1, lt2, gt2)
                    emitted += 1
                for lt, gt in enumerate(grp):
                    ty = 0 if gt < NTL else 1
                    gate = mvs[ty][2]
                    r_mv = mvs[ty][3]
                    ar = [ares[i] for i, (a, b_) in enumerate(blocks) if a < (lt + 1) * 128 and b_ > lt * 128]
                    b0, r_b0 = B.bank()
                    b1, r_b1 = B.bank()
                    bb = [(b0, r_b0), (b1, r_b1)]
                    for c in range(NFF):
                        for hf in range(2):
                            B.mm(bb[hf][0], actT[:, c, lt * 128:(lt + 1) * 128], wd[:, c, hf * 512:(hf + 1) * 512],
                                 c == 0, c == NFF - 1, ar + [r_wd], [bb[hf][1]])
                    hin, r_hin = hinr.next()
                    B.dma("sp", hin, srcf(gt), [], [r_hin])
                    et, r_et = etmpr.next()
                    for hf in range(2):
                        B.tt("dve", et[:, hf * 512:(hf + 1) * 512], bb[hf][0], gate[:, hf * 512:(hf + 1) * 512], ALU.mult,
                             [r_mv], [bb[hf][1], r_et])
                    ho, r_ho = houtr.next()
                    B.tt("pool", ho, et, hin, ALU.add, [r_et, r_hin], [r_ho])
                    B.dma("sp", dstf(gt), ho, [r_ho], [])

        def head_norm(bank_ap, r_bank, nh, gain_bc, r_gain, rings, name):
            sq, r_sq = rings["sq"].next()
            w = nh * 64
            B.act(sq[:, 0:w], bank_ap, AF.Square, [], [r_bank, r_sq])
            st_, r_st = rings["st"].next()
            B.reduce_sum(st_[:, 0:nh], sq[:, 0:w].rearrange("p (h d) -> p h d", h=nh), [r_sq], [r_st])
            rs = B.rstd(st_, r_st, nh, 1.0 / 64)
            qn, r_qn = rings[name].next()
            qn3 = qn[:, 0:w].rearrange("p (h d) -> p h d", h=nh)
            B.tt("dve", qn3, bank_ap.rearrange("p (h d) -> p h d", h=nh), rs.unsqueeze(2).to_broadcast([128, nh, 64]), ALU.mult,
                 [r_st], [r_bank, r_qn])
            B.tt("pool", qn3, qn3, gain_bc.unsqueeze(1).to_broadcast([128, nh, 64]), ALU.mult, [r_gain, r_qn], [r_qn])
            return qn, r_qn

        def rope(qn, r_qn, nh, ropt, r_ropt, outb, r_out, rings):
            w = nh * 64
            a_, r_a = rings["ra"].next()
            b_, r_b = rings["rb"].next()
            q3 = qn[:, 0:w].rearrange("p (h d) -> p h d", h=nh)
            a3 = a_[:, 0:w].rearrange("p (h d) -> p h d", h=nh)
            B.tt("pool", a3, q3, ropt[:, 0, :].unsqueeze(1).to_broadcast([128, nh, 64]), ALU.mult, [r_qn, r_ropt], [r_a])
            q5 = qn[:, 0:w].rearrange("p (h a s d) -> p h a s d", h=nh, a=2, s=2)
            b5 = b_[:, 0:w].rearrange("p (h a s d) -> p h a s d", h=nh, a=2, s=2)
            s4 = ropt[:, 1, :].rearrange("p (a s d) -> p a s d", a=2, s=2)
            for ax in range(2):
                for s in range(2):
                    B.tt("dve", b5[:, :, ax, s, :], q5[:, :, ax, 1 - s, :],
                         s4[:, ax, s, :].unsqueeze(1).to_broadcast([128, nh, 16]), ALU.mult, [r_qn, r_ropt], [r_b])
            B.tt("dve", outb[:, 0:w], a_[:, 0:w], b_[:, 0:w], ALU.add, [r_a, r_b], [r_out])

        def head_transposes(src, r_src, nh, dst_dram, gt, rings):
            bk, r_bk = B.bank()
            bkb = bk.bitcast(BF16)
            for h in range(nh):
                B.tr(bkb[0:64, h * 128:(h + 1) * 128], src[:, h * 64:(h + 1) * 64], [r_src], [r_bk])
            ts_, r_ts = rings["hT"].next()
            B.cp("act", ts_[:, 0:nh, :], bkb[0:64, 0:nh * 128].rearrange("p (h t) -> p h t", h=nh), [], [r_bk, r_ts])
            B.dma("sp", dst_dram.rearrange("h d t -> d h t")[:, 0:nh, gt * 128:(gt + 1) * 128], ts_[:, 0:nh, :], [r_ts], [])

        def phase_even_prep(li):
            B.phase_begin()
            mvs = load_mod_vecs(li, 3, 4, None, 1, 1.0)
            win = A.alloc([8, 1792], BF16)
            r_win = S.res()
            B.dma("pool", win, I["ev_w_in"].rearrange("(k p) n -> p k n", p=128), [], [r_win])
            qg_bc, r_qg = B.load_bc("sp", I["a_q_gain"][0:1, :], 64)
            kg_bc, r_kg = B.load_bc("sp", I["a_k_gain"][0:1, :], 64)
            vg_bc, r_vg = B.load_bc("sp", I["b_v_gain"][0:1, :], 512)
            hinr = Ring(B, 4, [D], F32)
            rings = {"sqj": (A.alloc([D], BF16), S.res()), "st": Ring(B, 6, [30], F32),
                     "z1": Ring(B, 2, [D], F32), "ztok": Ring(B, 4, [D], BF16),
                     "sq": Ring(B, 3, [512], F32), "qn": Ring(B, 2, [512], F32), "kn": Ring(B, 2, [128], F32),
                     "gv": Ring(B, 2, [512], F32),
                     "ra": Ring(B, 2, [512], F32), "rb": Ring(B, 2, [512], F32), "hT": Ring(B, 3, [8, 128], BF16, parts=64)}
            zTr = Ring(B, 3, [8, 128], BF16)
            ropr = Ring(B, 6, [2, 64], F32)
            qbr = Ring(B, 4, [512], BF16)
            kbr = Ring(B, 4, [128], BF16)
            vbr = Ring(B, 3, [128], BF16)
            ur = Ring(B, 3, [512], F32)
            vvr = Ring(B, 3, [512], BF16)
            nsl = [(0, 512), (512, 768), (768, 1280), (1280, 1792)]

            def tile_gen(gt):
                bg_tick()
                ty = 0 if gt < NTL else 1
                hin, r_hin = hinr.next()
                B.dma("sp", hin, hsrc(gt), [], [r_hin])
                if ty == 0:
                    rt, r_rt = ropr.next()
                    B.dma("sp", rt, I["k_rope"][gt * 128:(gt + 1) * 128, :, :], [], [r_rt])
                yield
                zt, r_zt = norm_tile(hin, r_hin, mvs[ty], rings)
                yield
                zT, r_zT = zTr.next()
                transpose8(zt, r_zt, zT, r_zT)
                bks = [B.bank() for _ in range(4)]
                for k in range(8):
                    for i, (n0, n1) in enumerate(nsl):
                        B.mm(bks[i][0][:, 0:n1 - n0], zT[:, k, :], win[:, k, n0:n1], k == 0, k == 7, [r_zT, r_win], [bks[i][1]])
                (bq, r_bq), (bkv, r_bkv), (bbu, r_bbu), (bbv, r_bbv) = bks
                yield
                qn, r_qn = head_norm(bq, r_bq, 8, qg_bc, r_qg, rings, "qn")
                kn, r_kn = head_norm(bkv[:, 0:128], r_bkv, 2, kg_bc, r_kg, rings, "kn")
                qb, r_qb = qbr.next()
                kb, r_kb = kbr.next()
                if ty == 0:
                    rope(qn, r_qn, 8, rt, r_rt, qb, r_qb, rings)
                    rope(kn, r_kn, 2, rt, r_rt, kb, r_kb, rings)
                else:
                    B.cp("dve", qb, qn, [r_qn], [r_qb])
                    B.cp("dve", kb, kn[:, 0:128], [r_kn], [r_kb])
                vb, r_vb = vbr.next()
                B.cp("act", vb, bkv[:, 128:256], [], [r_bkv, r_vb])
                B.dma("sp", v_d[gt * 128:(gt + 1) * 128, 0:128], vb, [r_vb], [])
                u_, r_u = ur.next()
                B.act(u_, bbu, AF.Gelu_apprx_tanh, [], [r_bbu, r_u])
                B.dma("sp", u_d[gt * 128:(gt + 1) * 128, :], u_, [r_u], [])
                gv, r_gv = rings["gv"].next()
                B.act(gv, bbv, AF.Gelu_apprx_tanh, [], [r_bbv, r_gv])
                sq, r_sq = rings["sq"].next()
                B.act(sq, gv, AF.Square, [r_gv], [r_sq])
                st_, r_st = rings["st"].next()
                B.reduce_sum(st_[:, 0:8], sq.rearrange("p (h d) -> p h d", h=8), [r_sq], [r_st])
                rs = B.rstd(st_, r_st, 8, 1.0 / 64)
                gv3 = gv.rearrange("p (h d) -> p h d", h=8)
                B.tt("dve", gv3, gv3, rs.unsqueeze(2).to_broadcast([128, 8, 64]), ALU.mult, [r_st, r_gv], [r_gv])
                vv, r_vv = vvr.next()
                B.tt("pool", vv, gv, vg_bc, ALU.mult, [r_gv, r_vg], [r_vv])
                B.dma("sp", vv_d[gt * 128:(gt + 1) * 128, :], vv, [r_vv], [])
                yield
                head_transposes(qb, r_qb, 8, qT_d, gt, rings)
                head_transposes(kb, r_kb, 2, kT_d, gt, rings)

            pipeline(tile_gen, range(NT))

        def outproj_residual(mix, r_mix, wout, r_wout, gate, r_gate, gt, rings):
            mT, r_mT = rings["mixT"].next()
            transpose8(mix, r_mix, mT, r_mT)
            b0, r_b0 = B.bank()
            b1, r_b1 = B.bank()
            bb = [(b0, r_b0), (b1, r_b1)]
            for k in range(8):
                for hf in range(2):
                    B.mm(bb[hf][0], mT[:, k, :], wout[:, k, hf * 512:(hf + 1) * 512], k == 0, k == 7, [r_mT, r_wout], [bb[hf][1]])
            hin, r_hin = rings["hin"].next()
            B.dma("sp", hin, hsrc(gt), [], [r_hin])
            et, r_et = rings["et"].next()
            for hf in range(2):
                B.tt("dve", et[:, hf * 512:(hf + 1) * 512], bb[hf][0], gate[:, hf * 512:(hf + 1) * 512], ALU.mult,
                     [r_gate], [bb[hf][1], r_et])
            ho, r_ho = rings["hout"].next()
            B.tt("pool", ho, et, hin, ALU.add, [r_et, r_hin], [r_ho])
            B.dma("sp", hsrc(gt), ho, [r_ho], [])

        def load_V(dst, r_dst, kt0, nw, nk):
            for w in range(nw):
                B.dma("sp", dst[:, w, :, 0:64],
                      v_d[(kt0 + w) * 128:(kt0 + w + 1) * 128, 0:nk * 64].rearrange("p (k d) -> p k d", k=nk), [], [r_dst])

        def phase_even_attn(li):
            B.phase_begin()
            mvs = load_mod_vecs(li, None, None, 5, 1, 1.0)
            wout = A.alloc([8, D], BF16)
            r_wout = S.res()
            B.dma("pool", wout, I["ev_w_out"].rearrange("(k p) n -> p k n", p=128), [], [r_wout])
            wsn = A.alloc([8, 128], BF16)
            r_wsn = S.res()
            B.dma("pool", wsn, I["b_ws"].rearrange("g i j -> i g j"), [], [r_wsn])
            wsT = A.alloc([8, 128], BF16)
            r_wsT = S.res()
            bk, r_bk = B.bank()
            bkb = bk.bitcast(BF16)
            for g in range(8):
                B.tr(bkb[:, g * 128:(g + 1) * 128], wsn[:, g, :], [r_wsn], [r_bk])
            B.cp("act", wsT, bkb.rearrange("p (g t) -> p g t", g=8), [], [r_bk, r_wsT])
            bias_sb = A.alloc([128], F32, parts=8)
            r_bsb = S.res()
            B.dma("sp", bias_sb, I["b_bias"][:, :], [], [r_bsb])
            biasT = A.alloc([8], F32)
            r_biasT = S.res()
            bk, r_bk = B.bank()
            B.mm(bk[:, 0:8], bias_sb, B.identf[0:8, 0:8], True, True, [r_bsb, B.r_ident], [r_bk])
            B.cp("dve", biasT, bk[:, 0:8], [], [r_bk, r_biasT])
            esink, r_es = B.load_bc("sp", I["a_sink"][0:1, :], 8)
            B.act(esink, esink, AF.Exp, [r_es], [r_es])
            amask = A.alloc([2, 128], BF16)
            r_am = S.res()
            B.dma("pool", amask, I["k_amask"][:, :, :], [], [r_am])
            kTc = A.alloc([2, 256], BF16, parts=64)
            r_kTc = S.res()
            B.dma("sp", kTc, kT_d.rearrange("h d t -> d h t")[:, 0:2, NTL * 128:NT * 128], [], [r_kTc])
            Vc = A.alloc([2, 2, 65], BF16)
            r_Vc = S.res()
            B.memset("dve", Vc[:, :, :, 64:65], 1.0, [r_Vc])
            load_V(Vc, r_Vc, NTL, 2, 2)
            kTwr = Ring(B, 4, [2, 384], BF16, parts=64)
            Vwr = Ring(B, 5, [3, 2, 65], BF16)
            for vb_, r_ in Vwr.bufs:
                B.memset("dve", vb_[:, :, :, 64:65], 1.0, [r_])
            qTr = Ring(B, 4, [8, 128], BF16, parts=64)
            pTr = Ring(B, 16, [512], BF16)
            etr = Ring(B, 2, [512], BF16)
            mixr = Ring(B, 4, [D], BF16)
            str_ = Ring(B, 4, [8], F32)
            vvr = Ring(B, 5, [512], BF16)
            ur = Ring(B, 5, [512], F32)
            btr = Ring(B, 2, [512], F32)
            rings = {"mixT": Ring(B, 2, [8, 128], BF16), "hin": Ring(B, 2, [D], F32), "et": Ring(B, 1, [D], F32),
                     "hout": Ring(B, 2, [D], F32)}

            def tile_gen(n):
                bg_tick()
                ty = 0 if n < NTL else 1
                qT, r_qT = qTr.next()
                B.dma("sp", qT, qT_d.rearrange("h d t -> d h t")[:, :, n * 128:(n + 1) * 128], [], [r_qT])
                keys = []
                if ty == 0:
                    kt0 = min(max(n - 1, 0), NTL - 3)
                    kTw, r_kTw = kTwr.next()
                    B.dma("sp", kTw, kT_d.rearrange("h d t -> d h t")[:, 0:2, kt0 * 128:(kt0 + 3) * 128], [], [r_kTw])
                    Vw, r_Vw = Vwr.next()
                    load_V(Vw, r_Vw, kt0, 3, 2)
                    for kt, mk in ((n - 1, 0), (n, None), (n + 1, 1)):
                        if 0 <= kt < NTL:
                            s_ = kt - kt0
                            keys.append((kTw, s_, Vw, s_, mk, [r_kTw], [r_Vw]))
                for s_ in range(2):
                    keys.append((kTc, s_, Vc, s_, None, [r_kTc], [r_Vc]))
                vv, r_vv = vvr.next()
                B.dma("sp", vv, vv_d[n * 128:(n + 1) * 128, :], [], [r_vv])
                u_, r_u = ur.next()
                B.dma("sp", u_, u_d[n * 128:(n + 1) * 128, :], [], [r_u])
                yield

                def qk(kv):
                    pts = []
                    for (kTa, ks, Va, vs, mk, rk, rv) in keys:
                        bk, r_bk = B.bank()
                        B.mm(bk.rearrange("p (h q) -> p h q", h=4), kTa[:, kv, ks * 128:(ks + 1) * 128], qT[:, 4 * kv:4 * kv + 4, :],
                             True, True, rk + [r_qT], [r_bk])
                        pT, r_pT = pTr.next()
                        if mk is None:
                            B.act(pT, bk, AF.Exp, [], [r_bk, r_pT], scale=0.125)
                        else:
                            et, r_et = etr.next()
                            B.act(et, bk, AF.Exp, [], [r_bk, r_et], scale=0.125)
                            B.tt("dve", pT.rearrange("p (h q) -> p h q", h=4), et.rearrange("p (h q) -> p h q", h=4),
                                 amask[:, mk, :].unsqueeze(1).to_broadcast([128, 4, 128]), ALU.mult, [r_et, r_am], [r_pT])
                        pts.append((pT, r_pT, Va, vs, rv))
                    return pts

                def pv(pkv, pts, mix, r_mix):
                    ob, r_ob = B.bank()
                    for hh in range(4):
                        for ei, (pT, r_pT, Va, vs, rv) in enumerate(pts):
                            B.mm(ob[:, hh * 65:(hh + 1) * 65], pT[:, hh * 128:(hh + 1) * 128], Va[:, vs, pkv, :],
                                 ei == 0, ei == len(pts) - 1, [r_pT] + rv, [r_ob])
                    ob3 = ob[:, 0:260].rearrange("p (h e) -> p h e", h=4)
                    sd, r_sd = str_.next()
                    B.tt("dve", sd[:, 0:4], ob3[:, :, 64], esink[:, 4 * pkv:4 * pkv + 4], ALU.add, [r_es], [r_ob, r_sd])
                    B.recip(sd[:, 4:8], sd[:, 0:4], [r_sd], [r_sd])
                    B.tt("dve", mix[:, pkv * 256:(pkv + 1) * 256].rearrange("p (h d) -> p h d", h=4), ob3[:, :, 0:64],
                         sd[:, 4:8].unsqueeze(2).to_broadcast([128, 4, 64]), ALU.mult, [r_sd], [r_ob, r_mix])

                pts0 = qk(0)
                yield
                pts1 = qk(1)
                mix, r_mix = mixr.next()
                pv(0, pts0, mix, r_mix)
                yield
                pv(1, pts1, mix, r_mix)
                bk, r_bk = B.bank()
                for g in range(8):
                    B.mm(bk[:, g * 64:(g + 1) * 64], wsT[:, g, :], vv[:, g * 64:(g + 1) * 64], True, True, [r_wsT, r_vv], [r_bk])
                bt, r_bt = btr.next()
                B.tt("dve", bt.rearrange("p (g d) -> p g d", g=8), bk.rearrange("p (g d) -> p g d", g=8),
                     biasT.unsqueeze(2).to_broadcast([128, 8, 64]), ALU.add, [r_biasT], [r_bk, r_bt])
                B.tt("pool", mix[:, 512:1024], bt, u_, ALU.mult, [r_bt, r_u], [r_mix])
                yield
                outproj_residual(mix, r_mix, wout, r_wout, mvs[ty][2], mvs[ty][3], n, rings)

            pipeline(tile_gen, range(NT))

        def phase_odd_prep(li):
            B.phase_begin()
            mvs = load_mod_vecs(li, 3, 4, None, 1, 1.0)
            win = A.alloc([8, 2048], BF16)
            r_win = S.res()
            for hf in range(2):
                B.dma("pool", win[:, :, hf * 1024:(hf + 1) * 1024],
                      I["od_w_in"][:, hf * 1024:(hf + 1) * 1024].rearrange("(k p) n -> p k n", p=128), [], [r_win])
            qg_bc, r_qg = B.load_bc("sp", I["d_q_gain"][0:1, :], 64)
            kg_bc, r_kg = B.load_bc("sp", I["d_k_gain"][0:1, :], 64)
            hinr = Ring(B, 4, [D], F32)
            rings = {"sqj": (A.alloc([D], BF16), S.res()), "st": Ring(B, 6, [30], F32),
                     "z1": Ring(B, 2, [D], F32), "ztok": Ring(B, 4, [D], BF16),
                     "sq": Ring(B, 3, [512], F32), "qn": Ring(B, 2, [512], F32), "kn": Ring(B, 2, [512], F32),
                     "hT": Ring(B, 3, [8, 128], BF16, parts=64)}
            zTr = Ring(B, 3, [8, 128], BF16)
            qbr = Ring(B, 4, [512], BF16)
            kbr = Ring(B, 4, [512], BF16)
            vbr = Ring(B, 3, [512], BF16)
            xbr = Ring(B, 3, [512], BF16)

            def tile_gen(gt):
                bg_tick()
                ty = 0 if gt < NTL else 1
                hin, r_hin = hinr.next()
                B.dma("sp", hin, hsrc(gt), [], [r_hin])
                yield
                zt, r_zt = norm_tile(hin, r_hin, mvs[ty], rings)
                yield
                zT, r_zT = zTr.next()
                transpose8(zt, r_zt, zT, r_zT)
                nbs = [0, 1, 2, 3] if ty == 0 else [2, 3]
                bks = {i: B.bank() for i in nbs}
                for k in range(8):
                    for i in nbs:
                        B.mm(bks[i][0], zT[:, k, :], win[:, k, i * 512:(i + 1) * 512], k == 0, k == 7, [r_zT, r_win], [bks[i][1]])
                yield
                qb = r_qb = None
                if ty == 0:
                    xb, r_xb = xbr.next()
                    B.cp("act", xb, bks[0][0], [], [bks[0][1], r_xb])
                    B.dma("sp", xp_d[gt * 128:(gt + 1) * 128, :], xb, [r_xb], [])
                    qn, r_qn = head_norm(bks[1][0], bks[1][1], 8, qg_bc, r_qg, rings, "qn")
                    qb, r_qb = qbr.next()
                    B.cp("dve", qb, qn, [r_qn], [r_qb])
                kn, r_kn = head_norm(bks[2][0], bks[2][1], 8, kg_bc, r_kg, rings, "kn")
                kb, r_kb = kbr.next()
                B.cp("dve", kb, kn, [r_kn], [r_kb])
                vb, r_vb = vbr.next()
                B.cp("act", vb, bks[3][0], [], [bks[3][1], r_vb])
                B.dma("sp", v_d[gt * 128:(gt + 1) * 128, :], vb, [r_vb], [])
                yield
                if ty == 0:
                    head_transposes(qb, r_qb, 8, qT_d, gt, rings)
                head_transposes(kb, r_kb, 8, kT_d, gt, rings)

            pipeline(tile_gen, range(NT))

        def phase_odd_attn(li):
            B.phase_begin()
            mvs = load_mod_vecs(li, None, None, 5, 1, 1.0, ntypes=1)
            wout = A.alloc([8, D], BF16)
            r_wout = S.res()
            B.dma("pool", wout, I["od_w_out"].rearrange("(k p) n -> p k n", p=128), [], [r_wout])
            wpool = A.alloc([4, 128], BF16)
            r_wpool = S.res()
            B.dma("pool", wpool, I["c_w_pool"].rearrange("g c d -> c g d"), [], [r_wpool])
            csc, r_csc = B.load_bc("sp", I["c_scale"][0:1, :], 512)
            band = A.alloc([4, 5, 128], BF16)
            r_band = S.res()
            B.dma("pool", band, I["k_band"][:, :, :, :], [], [r_band])
            kTc = A.alloc([8, 256], BF16, parts=64)
            r_kTc = S.res()
            B.dma("sp", kTc, kT_d.rearrange("h d t -> d h t")[:, :, NTL * 128:NT * 128], [], [r_kTc])
            Vc = A.alloc([2, 8, 65], BF16)
            r_Vc = S.res()
            B.memset("dve", Vc[:, :, :, 64:65], 1.0, [r_Vc])
            load_V(Vc, r_Vc, NTL, 2, 8)
            biasr = Ring(B, 1, [8, 7, 128], F32)
            for bb_, r_ in biasr.bufs:
                B.memset("pool", bb_[:, :, 5:7, :], 0.0, [r_])
            kTwr = Ring(B, 4, [8, 640], BF16, parts=64)
            Vwr = Ring(B, 4, [5, 8, 65], BF16)
            for vb_, r_ in Vwr.bufs:
                B.memset("dve", vb_[:, :, :, 64:65], 1.0, [r_])
            qTr = Ring(B, 4, [8, 128], BF16, parts=64)
            xpr = Ring(B, 3, [3, 512], BF16)
            ppr = Ring(B, 2, [4, 128], BF16)
            tAr = Ring(B, 2, [512], F32)
            tBr = Ring(B, 2, [384], F32)
            pAr = Ring(B, 3, [512], BF16)
            pBr = Ring(B, 3, [384], BF16)
            mixr = Ring(B, 5, [D], BF16)
            str_ = Ring(B, 4, [8], F32)
            rings = {"mixT": Ring(B, 2, [8, 128], BF16), "hin": Ring(B, 2, [D], F32), "et": Ring(B, 1, [D], F32),
                     "hout": Ring(B, 2, [D], F32)}
            state = {"case": None, "bias": None, "r_bias": None}
            B.nrr = 6

            def case_of(n):
                return 0 if n == 0 else 1 if n == 1 else 3 if n == NTL - 2 else 4 if n == NTL - 1 else 2

            def tile_gen(n):
                bg_tick()
                case = case_of(n)
                if case != state["case"]:
                    bias_, r_bias_ = biasr.next()
                    B.dma("sp", bias_[:, :, 0:5, :], I["k_dbias"][case], [], [r_bias_])
                    state["case"] = case
                    state["bias"] = bias_
                    state["r_bias"] = r_bias_
                bias = state["bias"]
                r_bias = state["r_bias"]
                kt0 = min(max(n - 2, 0), NTL - 5)
                kTw, r_kTw = kTwr.next()
                B.dma("sp", kTw, kT_d.rearrange("h d t -> d h t")[:, :, kt0 * 128:(kt0 + 5) * 128], [], [r_kTw])
                Vw, r_Vw = Vwr.next()
                load_V(Vw, r_Vw, kt0, 5, 8)
                qT, r_qT = qTr.next()
                B.dma("sp", qT, qT_d.rearrange("h d t -> d h t")[:, :, n * 128:(n + 1) * 128], [], [r_qT])
                xw, r_xw = xpr.next()
                jts = [j for j in (n - 1, n, n + 1) if 0 <= j < NTL]
                j0 = jts[0]
                B.dma("sp", xw[:, 0:len(jts), :], xp_d[j0 * 128:(j0 + len(jts)) * 128, :].rearrange("(w p) f -> p w f", p=128), [], [r_xw])
                yield
                mix, r_mix = mixr.next()
                bk, r_bk = B.bank()
                for g in range(4):
                    for ji, j in enumerate(jts):
                        if j == n - 1:
                            typ = 0
                        elif j == n + 1:
                            typ = 2
                        else:
                            typ = 3 if n == 0 else (4 if n == NTL - 1 else 1)
                        B.mm(bk[:, g * 128:(g + 1) * 128], xw[:, ji, g * 128:(g + 1) * 128], band[:, g, typ, :],
                             ji == 0, ji == len(jts) - 1, [r_xw, r_band], [r_bk])
                pp, r_pp = ppr.next()
                B.cp("act", pp, bk.rearrange("p (g t) -> p g t", g=4), [], [r_bk, r_pp])
                bk2, r_bk2 = B.bank()
                for g in range(4):
                    B.mm(bk2[:, g * 128:(g + 1) * 128], pp[:, g, :], wpool[:, g, :], True, True, [r_pp, r_wpool], [r_bk2])
                B.tt("dve", mix[:, 0:512], bk2, csc, ALU.mult, [r_csc], [r_bk2, r_mix])
                yield

                def heads(hq):
                    pend = None
                    ob, r_ob = B.bank_fixed(6 + hq)
                    for hi in range(5):
                        cur = None
                        if hi < 4:
                            h = hq * 4 + hi
                            bA, r_bA = B.bank()
                            bB, r_bB = B.bank()
                            for s_ in range(4):
                                B.mm(bA[:, s_ * 128:(s_ + 1) * 128], kTw[:, h, s_ * 128:(s_ + 1) * 128], qT[:, h, :], True, True, [r_kTw, r_qT], [r_bA])
                            B.mm(bB[:, 0:128], kTw[:, h, 512:640], qT[:, h, :], True, True, [r_kTw, r_qT], [r_bB])
                            for s_ in range(2):
                                B.mm(bB[:, (1 + s_) * 128:(2 + s_) * 128], kTc[:, h, s_ * 128:(s_ + 1) * 128], qT[:, h, :], True, True, [r_kTc, r_qT], [r_bB])
                            tA, r_tA = tAr.next()
                            tB, r_tB = tBr.next()
                            B.stt("dve", tA, bA, 0.125, bias[:, h, 0:4, :].rearrange("p s q -> p (s q)"), ALU.mult, ALU.add, [r_bias], [r_bA, r_tA])
                            B.stt("dve", tB, bB[:, 0:384], 0.125, bias[:, h, 4:7, :].rearrange("p s q -> p (s q)"), ALU.mult, ALU.add, [r_bias], [r_bB, r_tB])
                            pA, r_pA = pAr.next()
                            pB, r_pB = pBr.next()
                            B.act(pA, tA, AF.Exp, [r_tA], [r_pA])
                            B.act(pB, tB, AF.Exp, [r_tB], [r_pB])
                            cur = (h, pA, r_pA, pB, r_pB)
                        if pend is not None:
                            ph, pA, r_pA, pB, r_pB = pend
                            osl = ob[:, (ph % 4) * 65:(ph % 4 + 1) * 65]
                            for s_ in range(7):
                                if s_ < 4:
                                    lhs = pA[:, s_ * 128:(s_ + 1) * 128]
                                    rp = r_pA
                                else:
                                    lhs = pB[:, (s_ - 4) * 128:(s_ - 3) * 128]
                                    rp = r_pB
                                if s_ < 5:
                                    rhs = Vw[:, s_, ph, :]
                                    rv = r_Vw
                                else:
                                    rhs = Vc[:, s_ - 5, ph, :]
                                    rv = r_Vc
                                B.mm(osl, lhs, rhs, s_ == 0, s_ == 6, [rp, rv], [r_ob])
                        pend = cur
                    ob3 = ob[:, 0:260].rearrange("p (h e) -> p h e", h=4)
                    sd, r_sd = str_.next()
                    B.recip(sd[:, 0:4], ob3[:, :, 64], [], [r_ob, r_sd])
                    B.tt("dve", mix[:, 512 + hq * 256:512 + (hq + 1) * 256].rearrange("p (h d) -> p h d", h=4), ob3[:, :, 0:64],
                         sd[:, 0:4].unsqueeze(2).to_broadcast([128, 4, 64]), ALU.mult, [r_sd], [r_ob, r_mix])

                heads(0)
                yield
                heads(1)
                yield
                outproj_residual(mix, r_mix, wout, r_wout, mvs[0][2], mvs[0][3], n, rings)

            pipeline(tile_gen, range(NTL), drain_before=lambda n: case_of(n) != state["case"])
            B.nrr = 8

        plist = [
            lambda: phase_mod(0),
            lambda: phase_ffn(0, 0, NT, src0, hsrc),
            lambda: phase_even_prep(0),
            lambda: phase_even_attn(0),
            lambda: phase_ffn(0, 1, NT, hsrc, hsrc),
            lambda: phase_mod(1),
            lambda: phase_ffn(1, 0, NT, hsrc, hsrc),
            lambda: phase_odd_prep(1),
            lambda: phase_odd_attn(1),
            lambda: phase_ffn(1, 1, NTL, hsrc, osrc),
        ]
        if dbg_phases is not None:
            plist = plist[:dbg_phases]
        for p in plist:
            p()
        S.emit()
        build_program.stats = S.stats
    return nc


def _rope_table():
    t = np.arange(S_LAT)
    row = (t // 64).astype(np.float32)
    col = (t % 64).astype(np.float32)
    m = 16
    inv = (1.0 / (10000.0 ** (np.arange(m, dtype=np.float32) / m))).astype(np.float32)
    ar = row[:, None] * inv[None, :]
    ac = col[:, None] * inv[None, :]
    cos = np.concatenate([np.cos(ar), np.cos(ar), np.cos(ac), np.cos(ac)], axis=1)
    sin = np.concatenate([-np.sin(ar), np.sin(ar), -np.sin(ac), np.sin(ac)], axis=1)
    return np.stack([cos, sin], axis=1).astype(np.float32)


def _amask():
    pj = np.arange(128)[:, None]
    pi = np.arange(128)[None, :]
    prev = (pj >= pi).astype(np.float32)
    nxt = (pj <= pi).astype(np.float32)
    return np.stack([prev, nxt], axis=1)


def _band():
    out = np.zeros((128, 4, 5, 128), np.float32)
    for gi, w in enumerate((2, 4, 8, 16)):
        def mat(n, jn):
            tg = n * 128 + np.arange(128)
            lo = np.clip(tg - w // 2, 0, S_LAT)
            hi = np.clip(tg + w - w // 2, 0, S_LAT)
            cnt = (hi - lo).astype(np.float32)
            jg = jn * 128 + np.arange(128)
            m = ((jg[:, None] >= lo[None, :]) & (jg[:, None] < hi[None, :])).astype(np.float32) / cnt[None, :]
            m = m - (jg[:, None] == tg[None, :]).astype(np.float32)
            return m
        out[:, gi, 0] = mat(5, 4)
        out[:, gi, 1] = mat(5, 5)
        out[:, gi, 2] = mat(5, 6)
        out[:, gi, 3] = mat(0, 0)
        out[:, gi, 4] = mat(NTL - 1, NTL - 1)
    return out


def _dbias(rpb):
    out = np.full((5, 128, 8, 5, 128), NEGB, np.float32)
    for case, n in enumerate((0, 1, 5, NTL - 2, NTL - 1)):
        kt0 = min(max(n - 2, 0), NTL - 5)
        i = np.arange(128)
        r = 2 * n + i // 64
        c = i % 64
        r0 = np.clip(r - 4, 0, 56)
        q0 = np.clip(c - 8, 0, 48)
        for s in range(5):
            kt = kt0 + s
            j = np.arange(128)
            kr = 2 * kt + j // 64
            kc = j % 64
            valid = ((kr[:, None] >= r0[None, :]) & (kr[:, None] < r0[None, :] + 8) &
                     (kc[:, None] >= q0[None, :]) & (kc[:, None] < q0[None, :] + 16))
            ri = np.clip(kr[:, None] - r[None, :] + 7, 0, 14)
            ci = np.clip(kc[:, None] - c[None, :] + 15, 0, 30)
            g = rpb[:, ri, ci]
            g = np.where(valid[None], g, np.float32(NEGB))
            out[case, :, :, s, :] = np.transpose(g, (1, 0, 2))
    return out


_CACHE = {}


def kernel(x, c, ctx, c_ctx, ada_w, ada_b, norm_g, ffn_w_gu, ffn_w_down,
           ev_w_in, ev_w_out, a_q_gain, a_k_gain, a_sink, b_v_gain, b_ws, b_bias,
           od_w_in, od_w_out, c_w_pool, c_scale, d_q_gain, d_k_gain, d_rpb, _dbg_phases=None, _dbg=False):
    f = lambda a: np.ascontiguousarray(np.asarray(a, dtype=np.float32))
    key = (_dbg_phases, _dbg)
    if key not in _CACHE:
        _CACHE[key] = build_program(_dbg_phases, _dbg)
    nc = _CACHE[key]
    shared = {
        "c_ctx": f(c_ctx).reshape(1, D), "ada_w": f(ada_w), "ada_b": f(ada_b), "norm_g": f(norm_g),
        "ffn_w_gu": f(ffn_w_gu), "ffn_w_down": f(ffn_w_down),
        "ev_w_in": f(ev_w_in)[0], "ev_w_out": f(ev_w_out)[0],
        "a_q_gain": f(a_q_gain), "a_k_gain": f(a_k_gain), "a_sink": f(a_sink),
        "b_v_gain": f(b_v_gain), "b_ws": f(b_ws)[0], "b_bias": f(b_bias)[0],
        "od_w_in": f(od_w_in)[0], "od_w_out": f(od_w_out)[0],
        "c_w_pool": f(c_w_pool)[0], "c_scale": f(c_scale),
        "d_q_gain": f(d_q_gain), "d_k_gain": f(d_k_gain),
        "k_ident": np.eye(128, dtype=np.float32), "k_rope": _rope_table(), "k_amask": _amask(),
        "k_band": _band(), "k_dbias": _dbias(f(d_rpb)[0]),
    }
    x = f(x); c = f(c); ctx = f(ctx)
    in_maps = []
    for b in range(8):
        m = dict(shared)
        m["x"] = x[b]
        m["ctx"] = ctx[b]
        m["c"] = c[b].reshape(1, D)
        in_maps.append(m)
    res = run_bass_kernel_spmd(nc, in_maps, core_ids=list(range(8)))
    kernel.last = res
    return np.stack([r["out"] for r in res.results], axis=0)
```

```python
import contextlib
import numpy as np
import concourse.bass as bass
import concourse.mybir as mybir
from concourse.bass_utils import run_bass_kernel_spmd

F32 = mybir.dt.float32
BF16 = mybir.dt.bfloat16
AF = mybir.ActivationFunctionType
ALU = mybir.AluOpType
AX = mybir.AxisListType

D = 1024
S_LAT = 4096
S_CTX = 256
NTL = 32
NT = 34
DFF = 2816
NFF = 22
EPS = 1e-6
NEGB = -30000.0

ENGS = ("sp", "act", "pool", "dve", "pe")
NDMA_SEM = 8


class Res:
    __slots__ = ("name", "last_w", "readers", "gen")

    def __init__(self, name):
        self.name = name
        self.last_w = None
        self.readers = {}
        self.gen = 0

    def bump(self):
        self.gen += 1
        return Ref(self, self.gen)


class Ref:
    __slots__ = ("phys", "gen")

    def __init__(self, phys, gen):
        self.phys = phys
        self.gen = gen


def _norm_res(lst):
    out = []
    for r in lst:
        if isinstance(r, Ref):
            assert r.gen == r.phys.gen, f"stale buffer reference {r.phys.name}"
            r = r.phys
        out.append(r)
    return out


def pipeline(make_gen, items, drain_before=None):
    active = []
    for it in items:
        if drain_before is not None and drain_before(it):
            while active:
                nxt = []
                for g in active:
                    try:
                        next(g)
                        nxt.append(g)
                    except StopIteration:
                        pass
                active = nxt
        nxt = []
        for g in active:
            try:
                next(g)
                nxt.append(g)
            except StopIteration:
                pass
        active = nxt
        g = make_gen(it)
        try:
            next(g)
            active.append(g)
        except StopIteration:
            pass
    while active:
        nxt = []
        for g in active:
            try:
                next(g)
                nxt.append(g)
            except StopIteration:
                pass
        active = nxt


class Op:
    __slots__ = ("eng", "fn", "deps", "dma", "signal", "sem", "val", "prewait", "bg")

    def __init__(self, eng, fn, dma):
        self.eng = eng
        self.fn = fn
        self.dma = dma
        self.deps = []
        self.signal = False
        self.sem = None
        self.val = 0
        self.prewait = None
        self.bg = False


class Sched:
    def __init__(self, nc):
        self.nc = nc
        self.ops = {e: [] for e in ENGS}
        self.bar = {}
        self.nres = 0

    def res(self, name=None):
        self.nres += 1
        return Res(name or f"r{self.nres}")

    def op(self, eng, fn, reads=(), writes=(), dma=False):
        reads = _norm_res(reads)
        writes = _norm_res(writes)
        o = Op(eng, fn, dma)
        deps = {}
        for r in reads:
            if r.last_w is not None:
                deps[id(r.last_w)] = r.last_w
        for r in writes:
            if r.last_w is not None:
                deps[id(r.last_w)] = r.last_w
            for rd in r.readers.values():
                if isinstance(rd, list):
                    for x in rd:
                        deps[id(x)] = x
                else:
                    deps[id(rd)] = rd
        for r in reads:
            if dma:
                r.readers.setdefault(("dma", eng), []).append(o)
            else:
                r.readers[eng] = o
        for r in writes:
            r.last_w = o
            r.readers = {}
        b = self.bar.pop(eng, None)
        if b:
            for x in b:
                deps[id(x)] = x
        dl = []
        for d in deps.values():
            if d is o:
                continue
            if (not dma) and (not d.dma) and d.eng == "pe" and eng == "pe":
                continue
            dl.append(d)
        o.deps = dl
        self.ops[eng].append(o)
        return o

    def dma(self, q, out, in_, reads=(), writes=(), **kw):
        return self.op(q, lambda e: e.dma_start(out=out, in_=in_, **kw), reads, writes, dma=True)

    def barrier(self):
        tails = []
        for e in ENGS:
            ops = self.ops[e]
            for o in reversed(ops):
                if not o.dma:
                    tails.append(o)
                    break
            cnt = 0
            for o in reversed(ops):
                if o.dma:
                    if not o.bg:
                        tails.append(o)
                    cnt += 1
                    if cnt >= NDMA_SEM:
                        break
        self.bar = {e: list(tails) for e in ENGS}

    def emit(self):
        nc = self.nc
        with contextlib.ExitStack() as st:
            csem = {e: st.enter_context(nc.semaphore(f"c_{e}")) for e in ENGS}
            dsem = {e: [st.enter_context(nc.semaphore(f"d_{e}{i}")) for i in range(NDMA_SEM)]
                    for e in ("sp", "act", "pool")}
            for e in ENGS:
                for o in self.ops[e]:
                    for d in o.deps:
                        d.signal = True
            self.stats = {}
            for e in ENGS:
                cnt = 0
                nd = 0
                for o in self.ops[e]:
                    if o.dma:
                        slot = nd % NDMA_SEM
                        o.sem = dsem[e][slot]
                        o.val = 16 * (nd // NDMA_SEM + 1)
                        if nd >= NDMA_SEM:
                            o.prewait = (dsem[e][slot], 16 * (nd // NDMA_SEM))
                        nd += 1
                    elif o.signal:
                        cnt += 1
                        o.sem = csem[e]
                        o.val = cnt
                self.stats[e] = (len(self.ops[e]), cnt, nd)
            block = st.enter_context(nc.Block())

            def run(eng_name, eng):
                seen = {}
                lastdma = {}
                for o in self.ops[eng_name]:
                    waits = []
                    if o.prewait is not None:
                        waits.append(o.prewait)
                    for d in o.deps:
                        waits.append((d.sem, d.val))
                    for sem, val in waits:
                        k = id(sem)
                        if seen.get(k, 0) >= val:
                            continue
                        seen[k] = val
                        eng.wait_ge(sem, val)
                    inst = o.fn(eng)
                    if o.dma:
                        inst.then_inc(o.sem, 16)
                        lastdma[id(o.sem)] = (o.sem, o.val)
                    elif o.signal:
                        inst.then_inc(o.sem, 1)
                for sem, val in lastdma.values():
                    if seen.get(id(sem), 0) < val:
                        eng.wait_ge(sem, val)

            @block.sync
            def _(e):
                run("sp", e)

            @block.scalar
            def _(e):
                run("act", e)

            @block.gpsimd
            def _(e):
                run("pool", e)

            @block.vector
            def _(e):
                run("dve", e)

            @block.tensor
            def _(e):
                run("pe", e)


class Arena:
    def __init__(self, nc, st, nbytes):
        self.nbytes = nbytes
        self.t = st.enter_context(nc.sbuf_tensor("arena", [128, nbytes // 2], BF16))
        self.off = 0
        self.base = 0

    def alloc(self, free, dtype, parts=128):
        free = list(free)
        n = int(np.prod(free))
        sz = n * (4 if dtype == F32 else 2)
        off = (self.off + 63) // 64 * 64
        assert off + sz <= self.nbytes, f"arena overflow {off + sz} > {self.nbytes}"
        self.off = off + sz
        ap = self.t[0:parts, off // 2: (off + sz) // 2]
        if dtype == F32:
            ap = ap.bitcast(F32)
        if len(free) == 2:
            ap = ap.rearrange("p (a b) -> p a b", a=free[0])
        elif len(free) == 3:
            ap = ap.rearrange("p (a b c) -> p a b c", a=free[0], b=free[1])
        elif len(free) == 4:
            ap = ap.rearrange("p (a b c d) -> p a b c d", a=free[0], b=free[1], c=free[2])
        return ap

    def mark_persistent(self):
        self.base = self.off

    def reset(self):
        self.off = self.base


class Ring:
    def __init__(self, B, n, free, dtype, parts=128):
        self.bufs = [(B.A.alloc(free, dtype, parts), B.S.res()) for _ in range(n)]
        self.i = 0

    def next(self):
        r = self.bufs[self.i % len(self.bufs)]
        self.i += 1
        return r[0], r[1].bump()


class Builder:
    def __init__(self, nc, st, dbg):
        self.nc = nc
        self.st = st
        self.S = Sched(nc)
        self.A = Arena(nc, st, 206 * 1024)
        self.banks = []
        for i in range(8):
            t = st.enter_context(nc.psum_tensor(f"bank{i}", [128, 512], F32))
            self.banks.append((t, self.S.res(f"bank{i}")))
        self.bi = 0
        self.nrr = 8
        self.dbg = dbg

    def bank(self):
        r = self.banks[self.bi % self.nrr]
        self.bi += 1
        return r[0][:], r[1].bump()

    def bank_fixed(self, idx):
        r = self.banks[idx]
        return r[0][:], r[1].bump()

    def dma(self, q, out, in_, reads=(), writes=(), **kw):
        return self.S.dma(q, out, in_, reads, writes, **kw)

    def mm(self, out, lhsT, rhs, start, stop, reads, writes):
        return self.S.op("pe", lambda e: e.matmul(out, lhsT=lhsT, rhs=rhs, start=start, stop=stop), reads, writes)

    def tr(self, out, in_, reads, writes):
        idn = self.ident
        return self.S.op("pe", lambda e: e.transpose(out, in_, idn), list(reads) + [self.r_ident], writes)

    def act(self, out, in_, func, reads, writes, scale=None, bias=None, accum=None):
        kw = {}
        if scale is not None:
            kw["scale"] = scale
        if bias is not None:
            kw["bias"] = bias
        if accum is not None:
            kw["accum_out"] = accum
        return self.S.op("act", lambda e: e.activation(out=out, in_=in_, func=func, **kw), reads, writes)

    def tt(self, eng, out, in0, in1, op, reads, writes):
        return self.S.op(eng, lambda e: e.tensor_tensor(out=out, in0=in0, in1=in1, op=op), reads, writes)

    def ts(self, eng, out, in0, s1, s2, op0, op1, reads, writes):
        if op1 is None:
            return self.S.op(eng, lambda e: e.tensor_scalar(out=out, in0=in0, scalar1=s1, scalar2=None, op0=op0), reads, writes)
        return self.S.op(eng, lambda e: e.tensor_scalar(out=out, in0=in0, scalar1=s1, scalar2=s2, op0=op0, op1=op1), reads, writes)

    def stt(self, eng, out, in0, scalar, in1, op0, op1, reads, writes):
        return self.S.op(eng, lambda e: e.scalar_tensor_tensor(out=out, in0=in0, scalar=scalar, in1=in1, op0=op0, op1=op1), reads, writes)

    def cp(self, eng, out, in_, reads, writes):
        if eng == "act":
            return self.S.op("act", lambda e: e.activation(out=out, in_=in_, func=AF.Copy), reads, writes)
        return self.S.op(eng, lambda e: e.tensor_copy(out=out, in_=in_), reads, writes)

    def recip(self, out, in_, reads, writes):
        return self.S.op("dve", lambda e: e.reciprocal(out=out, in_=in_), reads, writes)

    def memset(self, eng, out, val, writes):
        return self.S.op(eng, lambda e: e.memset(out, val), [], writes)

    def reduce_sum(self, out, in_, reads, writes):
        return self.S.op("dve", lambda e: e.tensor_reduce(out=out, in_=in_, axis=AX.X, op=ALU.add), reads, writes)

    def rstd(self, st, r_st, w, inv_n):
        self.ts("dve", st[:, w:2 * w], st[:, 0:w], inv_n, EPS, ALU.mult, ALU.add, [r_st], [r_st])
        self.act(st[:, w:2 * w], st[:, w:2 * w], AF.Sqrt, [r_st], [r_st])
        self.recip(st[:, 2 * w:3 * w], st[:, w:2 * w], [r_st], [r_st])
        return st[:, 2 * w:3 * w]

    def phase_begin(self):
        self.S.barrier()
        self.A.reset()

    def load_bc(self, q, src_1xn, n, name=None):
        t = self.A.alloc([n], F32)
        r = self.S.res(name)
        self.dma(q, t, src_1xn.partition_broadcast(128), [], [r])
        return t, r


def build_program(dbg_phases=None, dbg=False):
    nc = bass.Bass("TRN2", target_bir_lowering=False)
    I = {}

    def inp(name, shape, dt=F32):
        I[name] = nc.dram_tensor(name, list(shape), dt, kind="ExternalInput").ap()
        return I[name]

    inp("x", [S_LAT, D]); inp("ctx", [S_CTX, D]); inp("c", [1, D]); inp("c_ctx", [1, D])
    inp("ada_w", [2, D, 9 * D]); inp("ada_b", [2, 9 * D]); inp("norm_g", [2, 3, D])
    inp("ffn_w_gu", [2, 2, D, 2 * DFF]); inp("ffn_w_down", [2, 2, DFF, D])
    inp("ev_w_in", [D, 1792]); inp("ev_w_out", [D, D])
    inp("a_q_gain", [1, 64]); inp("a_k_gain", [1, 64]); inp("a_sink", [1, 8])
    inp("b_v_gain", [1, 512]); inp("b_ws", [8, 128, 128]); inp("b_bias", [8, 128])
    inp("od_w_in", [D, 2048]); inp("od_w_out", [D, D])
    inp("c_w_pool", [4, 128, 128]); inp("c_scale", [1, 512])
    inp("d_q_gain", [1, 64]); inp("d_k_gain", [1, 64])
    inp("k_ident", [128, 128]); inp("k_rope", [S_LAT, 2, 64]); inp("k_amask", [128, 2, 128])
    inp("k_band", [128, 4, 5, 128]); inp("k_dbias", [5, 128, 8, 5, 128])
    out = nc.dram_tensor("out", [S_LAT, D], F32, kind="ExternalOutput").ap()
    skind = "ExternalOutput" if dbg else "Internal"

    def scr(name, shape, dt):
        return nc.dram_tensor(name, list(shape), dt, kind=skind).ap()

    hA = scr("hA", [NT * 128, D], F32)
    mod_d = scr("mod_d", [2, 2, 9 * D], F32)
    qT_d = scr("qT_d", [8, 64, NT * 128], BF16)
    kT_d = scr("kT_d", [8, 64, NT * 128], BF16)
    v_d = scr("v_d", [NT * 128, 512], BF16)
    u_d = scr("u_d", [NT * 128, 512], F32)
    vv_d = scr("vv_d", [NT * 128, 512], BF16)
    xp_d = scr("xp_d", [S_LAT, 512], BF16)
    wgc_all = [nc.dram_tensor(f"wgc_d{f}", [NFF // 2, 128, 8 * 2 * 256], BF16, kind="Internal").ap() for f in range(4)]
    wdc_all = [nc.dram_tensor(f"wdc_d{f}", [128, NFF * D], BF16, kind="Internal").ap() for f in range(4)]
    adac_all = [nc.dram_tensor(f"adac_d{l_}", [18, 128, 8 * 512], BF16, kind="Internal").ap() for l_ in range(2)]

    st = contextlib.ExitStack()
    with st:
        B = Builder(nc, st, dbg)
        S, A = B.S, B.A
        B.ident = A.alloc([128], BF16)
        B.r_ident = S.res("ident")
        B.dma("pool", B.ident, I["k_ident"][:, :], [], [B.r_ident])
        B.identf = A.alloc([128], F32)
        B.dma("sp", B.identf, I["k_ident"][:, :], [], [B.r_ident])
        A.mark_persistent()

        bgq = []
        wres = {}

        def bg_add_ffn(f, li, which):
            wgu_ = I["ffn_w_gu"][li, which]
            for cp_ in range(NFF // 2):
                dst4 = wgc_all[f][cp_].rearrange("p (k g n) -> p k g n", k=8, g=2)
                for g_ in range(2):
                    r = S.res()
                    wres[("gu", f, cp_, g_)] = r
                    bgq.append((dst4[:, :, g_, :],
                                wgu_[:, g_ * DFF + cp_ * 256:g_ * DFF + (cp_ + 1) * 256].rearrange("(k p) n -> p k n", p=128), r))
            wd3 = wdc_all[f].rearrange("p (c n) -> p c n", c=NFF)
            for c4 in range(0, NFF, 2):
                r = S.res()
                wres[("wd", f, c4)] = r
                bgq.append((wd3[:, c4:c4 + 2, :],
                            I["ffn_w_down"][li, which, c4 * 128:(c4 + 2) * 128, :].rearrange("(c p) n -> p c n", p=128), r))

        def bg_add_ada(li, nb0=0, nb1=18):
            for nb in range(nb0, nb1):
                r = S.res()
                wres[("ada", li, nb)] = r
                bgq.append((adac_all[li][nb].rearrange("p (k n) -> p k n", k=8),
                            I["ada_w"][li, :, nb * 512:(nb + 1) * 512].rearrange("(k p) n -> p k n", p=128), r))

        def bg_need(pred):
            last = -1
            for i, (_, _, r) in enumerate(bgq):
                if pred(r):
                    last = i
            if last >= 0:
                bg_tick(last + 1)

        def bg_tick(n=1):
            for _ in range(n):
                if not bgq:
                    return
                dst, src, r = bgq.pop(0)
                o = B.dma("pool", dst, src, [], [r])
                o.bg = True

        bg_add_ada(0, 6, 18)
        bg_add_ffn(1, 0, 1)
        bg_add_ada(1)
        bg_add_ffn(2, 1, 0)
        bg_add_ffn(3, 1, 1)

        def src0(gt):
            if gt < NTL:
                return I["x"][gt * 128:(gt + 1) * 128, :]
            return I["ctx"][(gt - NTL) * 128:(gt - NTL + 1) * 128, :]

        def hsrc(gt):
            return hA[gt * 128:(gt + 1) * 128, :]

        def osrc(gt):
            return out[gt * 128:(gt + 1) * 128, :]

        phases = []

        def phase_mod(li, nb0=0, nb1=18):
            B.phase_begin()
            mine_ = {id(r) for k_, r in wres.items() if k_[0] == "ada" and k_[1] == li and nb0 <= k_[2] < nb1}
            bg_need(lambda r: id(r) in mine_)
            cc = A.alloc([2, 128], F32, parts=8)
            r_cc = S.res()
            B.dma("sp", cc[:, 0, :], I["c"][0, :].rearrange("(k p) -> k p", p=128), [], [r_cc])
            B.dma("sp", cc[:, 1, :], I["c_ctx"][0, :].rearrange("(k p) -> k p", p=128), [], [r_cc])
            ccb = A.alloc([2, 128], BF16, parts=8)
            r_ccb = S.res()
            B.act(ccb, cc, AF.Silu, [r_cc], [r_ccb])
            cs = A.alloc([8, 2], BF16)
            r_cs = S.res()
            bk, r_bk = B.bank()
            bkb = bk.bitcast(BF16)
            for j in range(2):
                B.S.op("pe", lambda e, j=j: e.transpose(bkb[:, j * 8:(j + 1) * 8], ccb[:, j, :], B.ident[0:8, 0:8]), [r_ccb, B.r_ident], [r_bk])
            B.cp("dve", cs, bkb[:, 0:16].rearrange("p (j k) -> p k j", j=2), [], [r_bk, r_cs])
            adab = A.alloc([9 * D], F32, parts=2)
            r_adab = S.res()
            B.dma("sp", adab, I["ada_b"][li:li + 1, :].partition_broadcast(2), [], [r_adab])
            msb = A.alloc([9 * D], F32, parts=2)
            r_msb = S.res()
            wr = Ring(B, 3, [8, 512], BF16)
            for nb in range(nb0, nb1):
                w, r_w = wr.next()
                if ("ada", li, nb) in wres:
                    B.dma("sp", w.rearrange("p k n -> p (k n)"), adac_all[li][nb], [wres[("ada", li, nb)]], [r_w])
                else:
                    B.dma("pool", w, I["ada_w"][li, :, nb * 512:(nb + 1) * 512].rearrange("(k p) n -> p k n", p=128), [], [r_w])
                bk, r_bk = B.bank()
                for k in range(8):
                    B.mm(bk[0:2, :], cs[:, k, :], w[:, k, :], k == 0, k == 7, [r_cs, r_w], [r_bk])
                B.tt("dve", msb[:, nb * 512:(nb + 1) * 512], bk[0:2, :], adab[:, nb * 512:(nb + 1) * 512], ALU.add,
                     [r_adab], [r_bk, r_msb])
            B.dma("sp", mod_d[li, :, nb0 * 512:nb1 * 512], msb[:, nb0 * 512:nb1 * 512], [r_msb], [])

        def load_mod_vecs(li, j_shift, j_scale, j_gate, gi, gate_mul, ntypes=2):
            outl = []
            gbc, r_g = (None, None)
            if j_scale is not None:
                gbc, r_g = B.load_bc("sp", I["norm_g"][li, gi:gi + 1, :], D)
            for ty in range(ntypes):
                r = S.res()
                sh = Gm = gt_ = None
                if j_shift is not None:
                    sh = A.alloc([D], F32)
                    B.dma("sp", sh, mod_d[li, ty:ty + 1, j_shift * D:(j_shift + 1) * D].partition_broadcast(128), [], [r])
                if j_scale is not None:
                    Gm = A.alloc([D], F32)
                    B.dma("sp", Gm, mod_d[li, ty:ty + 1, j_scale * D:(j_scale + 1) * D].partition_broadcast(128), [], [r])
                    B.stt("dve", Gm, Gm, 1.0, gbc, ALU.add, ALU.mult, [r_g, r], [r])
                if j_gate is not None:
                    gt_ = A.alloc([D], F32)
                    B.dma("sp", gt_, mod_d[li, ty:ty + 1, j_gate * D:(j_gate + 1) * D].partition_broadcast(128), [], [r])
                    if gate_mul != 1.0:
                        B.ts("dve", gt_, gt_, gate_mul, None, ALU.mult, None, [r], [r])
                outl.append((sh, Gm, gt_, r))
            return outl

        def norm_tile(hin, r_hin, mv, rings):
            sh, Gm, _, r_mv = mv
            sqj, r_sqj = rings["sqj"]
            st_, r_st = rings["st"].next()
            B.memset("dve", st_[:, 0:1], 0.0, [r_st])
            B.act(sqj, hin, AF.Square, [r_hin, r_st], [r_sqj, r_st], accum=st_[:, 0:1])
            rs = B.rstd(st_, r_st, 1, 1.0 / D)
            z1, r_z1 = rings["z1"].next()
            B.stt("dve", z1, hin, rs, Gm, ALU.mult, ALU.mult, [r_hin, r_st, r_mv], [r_z1])
            zt, r_zt = rings["ztok"].next()
            B.tt("pool", zt, z1, sh, ALU.add, [r_z1, r_mv], [r_zt])
            return zt, r_zt

        def transpose8(src, r_src, dst3, r_dst, eng="act"):
            bk, r_bk = B.bank()
            bkb = bk.bitcast(BF16)
            for k in range(8):
                B.tr(bkb[:, k * 128:(k + 1) * 128], src[:, k * 128:(k + 1) * 128], [r_src], [r_bk])
            B.cp(eng, dst3, bkb.rearrange("p (k t) -> p k t", k=8), [], [r_bk, r_dst])

        def phase_ffn(li, which, ntiles, srcf, dstf):
            B.phase_begin()
            f = li * 2 + which
            cached = ("wd", f, 0) in wres
            if cached:
                mine = {id(r) for k_, r in wres.items() if k_[0] in ("gu", "wd") and k_[1] == f}
                bg_need(lambda r: id(r) in mine)
            wgc_d = wgc_all[f]
            j0 = 0 if which == 0 else 6
            gi = 0 if which == 0 else 2
            mvs = load_mod_vecs(li, j0, j0 + 1, j0 + 2, gi, 0.5, ntypes=2 if ntiles > NTL else 1)
            wd = A.alloc([NFF, D], BF16)
            r_wd = S.res()
            for c4 in range(0, NFF, 2):
                if cached:
                    B.dma("sp", wd[:, c4:c4 + 2, :], wdc_all[f].rearrange("p (c n) -> p c n", c=NFF)[:, c4:c4 + 2, :],
                          [wres[("wd", f, c4)]], [r_wd])
                else:
                    B.dma("pool", wd[:, c4:c4 + 2, :],
                          I["ffn_w_down"][li, which, c4 * 128:(c4 + 2) * 128, :].rearrange("(c p) n -> p c n", p=128), [], [r_wd])
            if ntiles == NT:
                groups = [list(range(0, 9)), list(range(9, 18)), list(range(18, 26)), list(range(26, 34))]
            else:
                groups = [list(range(g * 8, g * 8 + 8)) for g in range(4)]
            wc_res = [S.res() for _ in range(NFF // 2)]
            GM = max(len(g) for g in groups)
            zTs = [A.alloc([8, GM * 128], BF16) for _ in range(2)]
            zress = [[S.res() for _ in range(GM)] for _ in range(2)]
            actT = A.alloc([NFF, GM * 128], BF16)
            wgr = Ring(B, 2, [8, 2, 256], BF16)
            hinr = Ring(B, 2, [D], F32)
            rings = {"sqj": (A.alloc([D], BF16), S.res()), "st": Ring(B, 2, [3], F32),
                     "z1": Ring(B, 1, [D], F32), "ztok": Ring(B, 2, [D], BF16)}
            stmpr = Ring(B, 2, [512], F32)
            etmpr = Ring(B, 1, [D], F32)
            houtr = Ring(B, 1, [D], F32)
            wgu = I["ffn_w_gu"][li, which]

            def stage1_tile(gidx, lt, gt):
                ty = 0 if gt < NTL else 1
                hin, r_hin = hinr.next()
                B.dma("sp", hin, srcf(gt), [], [r_hin])
                zt, r_zt = norm_tile(hin, r_hin, mvs[ty], rings)
                transpose8(zt, r_zt, zTs[gidx % 2][:, :, lt * 128:(lt + 1) * 128], zress[gidx % 2][lt])

            for lt, gt in enumerate(groups[0]):
                stage1_tile(0, lt, gt)
            for gidx, grp in enumerate(groups):
                zT = zTs[gidx % 2]
                zres = zress[gidx % 2]
                T = len(grp) * 128
                nblk = (T + 511) // 512
                bs = T // nblk
                blocks = [(i * bs, (i + 1) * bs if i < nblk - 1 else T) for i in range(nblk)]
                ares = [S.res() for _ in blocks]
                pending = list(enumerate(groups[gidx + 1])) if gidx + 1 < len(groups) else []
                npend = len(pending)
                nunits = NFF * nblk
                unit = 0
                emitted = 0
                for cp_ in range(NFF // 2):
                    wg, r_wg = wgr.next()
                    if cached:
                        B.dma("sp", wg.rearrange("p k g n -> p (k g n)"), wgc_d[cp_],
                              [wres[("gu", f, cp_, 0)], wres[("gu", f, cp_, 1)]], [r_wg])
                    elif gidx == 0:
                        B.dma("pool", wg[:, :, 0, :], wgu[:, cp_ * 256:(cp_ + 1) * 256].rearrange("(k p) n -> p k n", p=128), [], [r_wg])
                        B.dma("pool", wg[:, :, 1, :], wgu[:, DFF + cp_ * 256:DFF + (cp_ + 1) * 256].rearrange("(k p) n -> p k n", p=128), [], [r_wg])
                        B.dma("pool", wgc_d[cp_], wg.rearrange("p k g n -> p (k g n)"), [r_wg], [wc_res[cp_]])
                    else:
                        B.dma("sp", wg.rearrange("p k g n -> p (k g n)"), wgc_d[cp_], [wc_res[cp_]], [r_wg])
                    for ci in range(2):
                        c = cp_ * 2 + ci
                        for bi_, (a, b_) in enumerate(blocks):
                            n = b_ - a
                            zr = [zres[t] for t in range(a // 128, (b_ - 1) // 128 + 1)]
                            bg, r_bg = B.bank()
                            bu, r_bu = B.bank()
                            for k in range(8):
                                B.mm(bg[:, 0:n], wg[:, k, 0, ci * 128:(ci + 1) * 128], zT[:, k, a:b_], k == 0, k == 7, [r_wg] + zr, [r_bg])
                            for k in range(8):
                                B.mm(bu[:, 0:n], wg[:, k, 1, ci * 128:(ci + 1) * 128], zT[:, k, a:b_], k == 0, k == 7, [r_wg] + zr, [r_bu])
                            stp, r_stp = stmpr.next()
                            B.act(stp[:, 0:n], bg[:, 0:n], AF.Silu, [], [r_bg, r_stp])
                            B.tt("dve", actT[:, c, a:b_], stp[:, 0:n], bu[:, 0:n], ALU.mult, [r_stp], [r_bu, ares[bi_]])
                            unit += 1
                            if unit % 4 == 0 and (cached or gidx > 0):
                                bg_tick()
                            while emitted < npend and unit * npend >= (emitted + 1) * int(nunits * 0.7):
                                lt2, gt2 = pending[emitted]
                                stage1_tile(gidx + 1, lt2, gt2)
                                emitted += 1
                while emitted < npend:
                    lt2, gt2 = pending[emitted]
                    stage1_tile(gidx + 1, lt2, gt2)
                    emitted += 1
                for lt, gt in enumerate(grp):
                    ty = 0 if gt < NTL else 1
                    gate = mvs[ty][2]
                    r_mv = mvs[ty][3]
                    ar = [ares[i] for i, (a, b_) in enumerate(blocks) if a < (lt + 1) * 128 and b_ > lt * 128]
                    b0, r_b0 = B.bank()
                    b1, r_b1 = B.bank()
                    bb = [(b0, r_b0), (b1, r_b1)]
                    for c in range(NFF):
                        for hf in range(2):
                            B.mm(bb[hf][0], actT[:, c, lt * 128:(lt + 1) * 128], wd[:, c, hf * 512:(hf + 1) * 512],
                                 c == 0, c == NFF - 1, ar + [r_wd], [bb[hf][1]])
                    hin, r_hin = hinr.next()
                    B.dma("sp", hin, srcf(gt), [], [r_hin])
                    et, r_et = etmpr.next()
                    for hf in range(2):
                        B.tt("dve", et[:, hf * 512:(hf + 1) * 512], bb[hf][0], gate[:, hf * 512:(hf + 1) * 512], ALU.mult,
                             [r_mv], [bb[hf][1], r_et])
                    ho, r_ho = houtr.next()
                    B.tt("pool", ho, et, hin, ALU.add, [r_et, r_hin], [r_ho])
                    B.dma("pool", dstf(gt), ho, [r_ho], [])

        def head_norm(bank_ap, r_bank, nh, gain_bc, r_gain, rings, name):
            sq, r_sq = rings["sq"].next()
            w = nh * 64
            B.act(sq[:, 0:w], bank_ap, AF.Square, [], [r_bank, r_sq])
            st_, r_st = rings["st"].next()
            B.reduce_sum(st_[:, 0:nh], sq[:, 0:w].rearrange("p (h d) -> p h d", h=nh), [r_sq], [r_st])
            rs = B.rstd(st_, r_st, nh, 1.0 / 64)
            qn, r_qn = rings[name].next()
            qn3 = qn[:, 0:w].rearrange("p (h d) -> p h d", h=nh)
            B.tt("dve", qn3, bank_ap.rearrange("p (h d) -> p h d", h=nh), rs.unsqueeze(2).to_broadcast([128, nh, 64]), ALU.mult,
                 [r_st], [r_bank, r_qn])
            B.tt("pool", qn3, qn3, gain_bc.unsqueeze(1).to_broadcast([128, nh, 64]), ALU.mult, [r_gain, r_qn], [r_qn])
            return qn, r_qn

        def rope(qn, r_qn, nh, ropt, r_ropt, outb, r_out, rings):
            w = nh * 64
            a_, r_a = rings["ra"].next()
            b_, r_b = rings["rb"].next()
            q3 = qn[:, 0:w].rearrange("p (h d) -> p h d", h=nh)
            a3 = a_[:, 0:w].rearrange("p (h d) -> p h d", h=nh)
            B.tt("pool", a3, q3, ropt[:, 0, :].unsqueeze(1).to_broadcast([128, nh, 64]), ALU.mult, [r_qn, r_ropt], [r_a])
            q5 = qn[:, 0:w].rearrange("p (h a s d) -> p h a s d", h=nh, a=2, s=2)
            b5 = b_[:, 0:w].rearrange("p (h a s d) -> p h a s d", h=nh, a=2, s=2)
            s4 = ropt[:, 1, :].rearrange("p (a s d) -> p a s d", a=2, s=2)
            for ax in range(2):
                for s in range(2):
                    B.tt("dve", b5[:, :, ax, s, :], q5[:, :, ax, 1 - s, :],
                         s4[:, ax, s, :].unsqueeze(1).to_broadcast([128, nh, 16]), ALU.mult, [r_qn, r_ropt], [r_b])
            B.tt("dve", outb[:, 0:w], a_[:, 0:w], b_[:, 0:w], ALU.add, [r_a, r_b], [r_out])

        def head_transposes(src, r_src, nh, dst_dram, gt, rings):
            bk, r_bk = B.bank()
            bkb = bk.bitcast(BF16)
            for h in range(nh):
                B.tr(bkb[0:64, h * 128:(h + 1) * 128], src[:, h * 64:(h + 1) * 64], [r_src], [r_bk])
            ts_, r_ts = rings["hT"].next()
            B.cp("act", ts_[:, 0:nh, :], bkb[0:64, 0:nh * 128].rearrange("p (h t) -> p h t", h=nh), [], [r_bk, r_ts])
            B.dma("pool", dst_dram.rearrange("h d t -> d h t")[:, 0:nh, gt * 128:(gt + 1) * 128], ts_[:, 0:nh, :], [r_ts], [])

        def phase_even_prep(li):
            B.phase_begin()
            mvs = load_mod_vecs(li, 3, 4, None, 1, 1.0)
            win = A.alloc([8, 1792], BF16)
            r_win = S.res()
            B.dma("pool", win, I["ev_w_in"].rearrange("(k p) n -> p k n", p=128), [], [r_win])
            qg_bc, r_qg = B.load_bc("sp", I["a_q_gain"][0:1, :], 64)
            kg_bc, r_kg = B.load_bc("sp", I["a_k_gain"][0:1, :], 64)
            vg_bc, r_vg = B.load_bc("sp", I["b_v_gain"][0:1, :], 512)
            hinr = Ring(B, 4, [D], F32)
            rings = {"sqj": (A.alloc([D], BF16), S.res()), "st": Ring(B, 6, [30], F32),
                     "z1": Ring(B, 2, [D], F32), "ztok": Ring(B, 4, [D], BF16),
                     "sq": Ring(B, 3, [512], F32), "qn": Ring(B, 2, [512], F32), "kn": Ring(B, 2, [128], F32),
                     "gv": Ring(B, 2, [512], F32),
                     "ra": Ring(B, 2, [512], F32), "rb": Ring(B, 2, [512], F32), "hT": Ring(B, 3, [8, 128], BF16, parts=64)}
            zTr = Ring(B, 3, [8, 128], BF16)
            ropr = Ring(B, 6, [2, 64], F32)
            qbr = Ring(B, 4, [512], BF16)
            kbr = Ring(B, 4, [128], BF16)
            vbr = Ring(B, 3, [128], BF16)
            ur = Ring(B, 3, [512], F32)
            vvr = Ring(B, 3, [512], BF16)
            nsl = [(0, 512), (512, 768), (768, 1280), (1280, 1792)]

            def tile_gen(gt):
                bg_tick()
                ty = 0 if gt < NTL else 1
                hin, r_hin = hinr.next()
                B.dma("sp", hin, hsrc(gt), [], [r_hin])
                if ty == 0:
                    rt, r_rt = ropr.next()
                    B.dma("sp", rt, I["k_rope"][gt * 128:(gt + 1) * 128, :, :], [], [r_rt])
                yield
                zt, r_zt = norm_tile(hin, r_hin, mvs[ty], rings)
                yield
                zT, r_zT = zTr.next()
                transpose8(zt, r_zt, zT, r_zT)
                bks = [B.bank() for _ in range(4)]
                for k in range(8):
                    for i, (n0, n1) in enumerate(nsl):
                        B.mm(bks[i][0][:, 0:n1 - n0], zT[:, k, :], win[:, k, n0:n1], k == 0, k == 7, [r_zT, r_win], [bks[i][1]])
                (bq, r_bq), (bkv, r_bkv), (bbu, r_bbu), (bbv, r_bbv) = bks
                yield
                qn, r_qn = head_norm(bq, r_bq, 8, qg_bc, r_qg, rings, "qn")
                kn, r_kn = head_norm(bkv[:, 0:128], r_bkv, 2, kg_bc, r_kg, rings, "kn")
                qb, r_qb = qbr.next()
                kb, r_kb = kbr.next()
                if ty == 0:
                    rope(qn, r_qn, 8, rt, r_rt, qb, r_qb, rings)
                    rope(kn, r_kn, 2, rt, r_rt, kb, r_kb, rings)
                else:
                    B.cp("dve", qb, qn, [r_qn], [r_qb])
                    B.cp("dve", kb, kn[:, 0:128], [r_kn], [r_kb])
                vb, r_vb = vbr.next()
                B.cp("act", vb, bkv[:, 128:256], [], [r_bkv, r_vb])
                B.dma("pool", v_d[gt * 128:(gt + 1) * 128, 0:128], vb, [r_vb], [])
                u_, r_u = ur.next()
                B.act(u_, bbu, AF.Gelu_apprx_tanh, [], [r_bbu, r_u])
                B.dma("pool", u_d[gt * 128:(gt + 1) * 128, :], u_, [r_u], [])
                gv, r_gv = rings["gv"].next()
                B.act(gv, bbv, AF.Gelu_apprx_tanh, [], [r_bbv, r_gv])
                sq, r_sq = rings["sq"].next()
                B.act(sq, gv, AF.Square, [r_gv], [r_sq])
                st_, r_st = rings["st"].next()
                B.reduce_sum(st_[:, 0:8], sq.rearrange("p (h d) -> p h d", h=8), [r_sq], [r_st])
                rs = B.rstd(st_, r_st, 8, 1.0 / 64)
                gv3 = gv.rearrange("p (h d) -> p h d", h=8)
                B.tt("dve", gv3, gv3, rs.unsqueeze(2).to_broadcast([128, 8, 64]), ALU.mult, [r_st, r_gv], [r_gv])
                vv, r_vv = vvr.next()
                B.tt("pool", vv, gv, vg_bc, ALU.mult, [r_gv, r_vg], [r_vv])
                B.dma("pool", vv_d[gt * 128:(gt + 1) * 128, :], vv, [r_vv], [])
                yield
                head_transposes(qb, r_qb, 8, qT_d, gt, rings)
                head_transposes(kb, r_kb, 2, kT_d, gt, rings)

            pipeline(tile_gen, range(NT))

        def outproj_residual(mix, r_mix, wout, r_wout, gate, r_gate, gt, rings):
            mT, r_mT = rings["mixT"].next()
            transpose8(mix, r_mix, mT, r_mT)
            b0, r_b0 = B.bank()
            b1, r_b1 = B.bank()
            bb = [(b0, r_b0), (b1, r_b1)]
            for k in range(8):
                for hf in range(2):
                    B.mm(bb[hf][0], mT[:, k, :], wout[:, k, hf * 512:(hf + 1) * 512], k == 0, k == 7, [r_mT, r_wout], [bb[hf][1]])
            hin, r_hin = rings["hin"].next()
            B.dma("sp", hin, hsrc(gt), [], [r_hin])
            et, r_et = rings["et"].next()
            for hf in range(2):
                B.tt("dve", et[:, hf * 512:(hf + 1) * 512], bb[hf][0], gate[:, hf * 512:(hf + 1) * 512], ALU.mult,
                     [r_gate], [bb[hf][1], r_et])
            ho, r_ho = rings["hout"].next()
            B.tt("pool", ho, et, hin, ALU.add, [r_et, r_hin], [r_ho])
            B.dma("pool", hsrc(gt), ho, [r_ho], [])

        def load_V(dst, r_dst, kt0, nw, nk):
            for w in range(nw):
                B.dma("sp", dst[:, w, :, 0:64],
                      v_d[(kt0 + w) * 128:(kt0 + w + 1) * 128, 0:nk * 64].rearrange("p (k d) -> p k d", k=nk), [], [r_dst])

        def phase_even_attn(li):
            B.phase_begin()
            mvs = load_mod_vecs(li, None, None, 5, 1, 1.0)
            wout = A.alloc([8, D], BF16)
            r_wout = S.res()
            B.dma("pool", wout, I["ev_w_out"].rearrange("(k p) n -> p k n", p=128), [], [r_wout])
            wsn = A.alloc([8, 128], BF16)
            r_wsn = S.res()
            B.dma("pool", wsn, I["b_ws"].rearrange("g i j -> i g j"), [], [r_wsn])
            wsT = A.alloc([8, 128], BF16)
            r_wsT = S.res()
            bk, r_bk = B.bank()
            bkb = bk.bitcast(BF16)
            for g in range(8):
                B.tr(bkb[:, g * 128:(g + 1) * 128], wsn[:, g, :], [r_wsn], [r_bk])
            B.cp("act", wsT, bkb.rearrange("p (g t) -> p g t", g=8), [], [r_bk, r_wsT])
            bias_sb = A.alloc([128], F32, parts=8)
            r_bsb = S.res()
            B.dma("sp", bias_sb, I["b_bias"][:, :], [], [r_bsb])
            biasT = A.alloc([8], F32)
            r_biasT = S.res()
            bk, r_bk = B.bank()
            B.mm(bk[:, 0:8], bias_sb, B.identf[0:8, 0:8], True, True, [r_bsb, B.r_ident], [r_bk])
            B.cp("dve", biasT, bk[:, 0:8], [], [r_bk, r_biasT])
            esink, r_es = B.load_bc("sp", I["a_sink"][0:1, :], 8)
            B.act(esink, esink, AF.Exp, [r_es], [r_es])
            amask = A.alloc([2, 128], BF16)
            r_am = S.res()
            B.dma("pool", amask, I["k_amask"][:, :, :], [], [r_am])
            kTc = A.alloc([2, 256], BF16, parts=64)
            r_kTc = S.res()
            B.dma("sp", kTc, kT_d.rearrange("h d t -> d h t")[:, 0:2, NTL * 128:NT * 128], [], [r_kTc])
            Vc = A.alloc([2, 2, 65], BF16)
            r_Vc = S.res()
            B.memset("dve", Vc[:, :, :, 64:65], 1.0, [r_Vc])
            load_V(Vc, r_Vc, NTL, 2, 2)
            kTwr = Ring(B, 4, [2, 384], BF16, parts=64)
            Vwr = Ring(B, 5, [3, 2, 65], BF16)
            for vb_, r_ in Vwr.bufs:
                B.memset("dve", vb_[:, :, :, 64:65], 1.0, [r_])
            qTr = Ring(B, 4, [8, 128], BF16, parts=64)
            pTr = Ring(B, 16, [512], BF16)
            etr = Ring(B, 2, [512], BF16)
            mixr = Ring(B, 4, [D], BF16)
            str_ = Ring(B, 4, [8], F32)
            vvr = Ring(B, 5, [512], BF16)
            ur = Ring(B, 5, [512], F32)
            btr = Ring(B, 2, [512], F32)
            rings = {"mixT": Ring(B, 2, [8, 128], BF16), "hin": Ring(B, 2, [D], F32), "et": Ring(B, 1, [D], F32),
                     "hout": Ring(B, 2, [D], F32)}

            def tile_gen(n):
                bg_tick()
                ty = 0 if n < NTL else 1
                qT, r_qT = qTr.next()
                B.dma("sp", qT, qT_d.rearrange("h d t -> d h t")[:, :, n * 128:(n + 1) * 128], [], [r_qT])
                keys = []
                if ty == 0:
                    kt0 = min(max(n - 1, 0), NTL - 3)
                    kTw, r_kTw = kTwr.next()
                    B.dma("sp", kTw, kT_d.rearrange("h d t -> d h t")[:, 0:2, kt0 * 128:(kt0 + 3) * 128], [], [r_kTw])
                    Vw, r_Vw = Vwr.next()
                    load_V(Vw, r_Vw, kt0, 3, 2)
                    for kt, mk in ((n - 1, 0), (n, None), (n + 1, 1)):
                        if 0 <= kt < NTL:
                            s_ = kt - kt0
                            keys.append((kTw, s_, Vw, s_, mk, [r_kTw], [r_Vw]))
                for s_ in range(2):
                    keys.append((kTc, s_, Vc, s_, None, [r_kTc], [r_Vc]))
                vv, r_vv = vvr.next()
                B.dma("sp", vv, vv_d[n * 128:(n + 1) * 128, :], [], [r_vv])
                u_, r_u = ur.next()
                B.dma("sp", u_, u_d[n * 128:(n + 1) * 128, :], [], [r_u])
                yield

                def qk(kv):
                    pts = []
                    for (kTa, ks, Va, vs, mk, rk, rv) in keys:
                        bk, r_bk = B.bank()
                        B.mm(bk.rearrange("p (h q) -> p h q", h=4), kTa[:, kv, ks * 128:(ks + 1) * 128], qT[:, 4 * kv:4 * kv + 4, :],
                             True, True, rk + [r_qT], [r_bk])
                        pT, r_pT = pTr.next()
                        if mk is None:
                            B.act(pT, bk, AF.Exp, [], [r_bk, r_pT], scale=0.125)
                        else:
                            et, r_et = etr.next()
                            B.act(et, bk, AF.Exp, [], [r_bk, r_et], scale=0.125)
                            B.tt("dve", pT.rearrange("p (h q) -> p h q", h=4), et.rearrange("p (h q) -> p h q", h=4),
                                 amask[:, mk, :].unsqueeze(1).to_broadcast([128, 4, 128]), ALU.mult, [r_et, r_am], [r_pT])
                        pts.append((pT, r_pT, Va, vs, rv))
                    return pts

                def pv(pkv, pts, mix, r_mix):
                    ob, r_ob = B.bank()
                    for hh in range(4):
                        for ei, (pT, r_pT, Va, vs, rv) in enumerate(pts):
                            B.mm(ob[:, hh * 65:(hh + 1) * 65], pT[:, hh * 128:(hh + 1) * 128], Va[:, vs, pkv, :],
                                 ei == 0, ei == len(pts) - 1, [r_pT] + rv, [r_ob])
                    ob3 = ob[:, 0:260].rearrange("p (h e) -> p h e", h=4)
                    sd, r_sd = str_.next()
                    B.tt("dve", sd[:, 0:4], ob3[:, :, 64], esink[:, 4 * pkv:4 * pkv + 4], ALU.add, [r_es], [r_ob, r_sd])
                    B.recip(sd[:, 4:8], sd[:, 0:4], [r_sd], [r_sd])
                    B.tt("dve", mix[:, pkv * 256:(pkv + 1) * 256].rearrange("p (h d) -> p h d", h=4), ob3[:, :, 0:64],
                         sd[:, 4:8].unsqueeze(2).to_broadcast([128, 4, 64]), ALU.mult, [r_sd], [r_ob, r_mix])

                pts0 = qk(0)
                yield
                pts1 = qk(1)
                mix, r_mix = mixr.next()
                pv(0, pts0, mix, r_mix)
                yield
                pv(1, pts1, mix, r_mix)
                bk, r_bk = B.bank()
                for g in range(8):
                    B.mm(bk[:, g * 64:(g + 1) * 64], wsT[:, g, :], vv[:, g * 64:(g + 1) * 64], True, True, [r_wsT, r_vv], [r_bk])
                bt, r_bt = btr.next()
                B.tt("dve", bt.rearrange("p (g d) -> p g d", g=8), bk.rearrange("p (g d) -> p g d", g=8),
                     biasT.unsqueeze(2).to_broadcast([128, 8, 64]), ALU.add, [r_biasT], [r_bk, r_bt])
                B.tt("pool", mix[:, 512:1024], bt, u_, ALU.mult, [r_bt, r_u], [r_mix])
                yield
                outproj_residual(mix, r_mix, wout, r_wout, mvs[ty][2], mvs[ty][3], n, rings)

            pipeline(tile_gen, range(NT))

        def phase_odd_prep(li):
            B.phase_begin()
            mvs = load_mod_vecs(li, 3, 4, None, 1, 1.0)
            win = A.alloc([8, 2048], BF16)
            r_win = S.res()
            for hf in range(2):
                B.dma("pool", win[:, :, hf * 1024:(hf + 1) * 1024],
                      I["od_w_in"][:, hf * 1024:(hf + 1) * 1024].rearrange("(k p) n -> p k n", p=128), [], [r_win])
            qg_bc, r_qg = B.load_bc("sp", I["d_q_gain"][0:1, :], 64)
            kg_bc, r_kg = B.load_bc("sp", I["d_k_gain"][0:1, :], 64)
            hinr = Ring(B, 4, [D], F32)
            rings = {"sqj": (A.alloc([D], BF16), S.res()), "st": Ring(B, 6, [30], F32),
                     "z1": Ring(B, 2, [D], F32), "ztok": Ring(B, 4, [D], BF16),
                     "sq": Ring(B, 3, [512], F32), "qn": Ring(B, 2, [512], F32), "kn": Ring(B, 2, [512], F32),
                     "hT": Ring(B, 3, [8, 128], BF16, parts=64)}
            zTr = Ring(B, 3, [8, 128], BF16)
            qbr = Ring(B, 4, [512], BF16)
            kbr = Ring(B, 4, [512], BF16)
            vbr = Ring(B, 3, [512], BF16)
            xbr = Ring(B, 3, [512], BF16)

            def tile_gen(gt):
                bg_tick()
                ty = 0 if gt < NTL else 1
                hin, r_hin = hinr.next()
                B.dma("sp", hin, hsrc(gt), [], [r_hin])
                yield
                zt, r_zt = norm_tile(hin, r_hin, mvs[ty], rings)
                yield
                zT, r_zT = zTr.next()
                transpose8(zt, r_zt, zT, r_zT)
                nbs = [0, 1, 2, 3] if ty == 0 else [2, 3]
                bks = {i: B.bank() for i in nbs}
                for k in range(8):
                    for i in nbs:
                        B.mm(bks[i][0], zT[:, k, :], win[:, k, i * 512:(i + 1) * 512], k == 0, k == 7, [r_zT, r_win], [bks[i][1]])
                yield
                qb = r_qb = None
                if ty == 0:
                    xb, r_xb = xbr.next()
                    B.cp("act", xb, bks[0][0], [], [bks[0][1], r_xb])
                    B.dma("pool", xp_d[gt * 128:(gt + 1) * 128, :], xb, [r_xb], [])
                    qn, r_qn = head_norm(bks[1][0], bks[1][1], 8, qg_bc, r_qg, rings, "qn")
                    qb, r_qb = qbr.next()
                    B.cp("dve", qb, qn, [r_qn], [r_qb])
                kn, r_kn = head_norm(bks[2][0], bks[2][1], 8, kg_bc, r_kg, rings, "kn")
                kb, r_kb = kbr.next()
                B.cp("dve", kb, kn, [r_kn], [r_kb])
                vb, r_vb = vbr.next()
                B.cp("act", vb, bks[3][0], [], [bks[3][1], r_vb])
                B.dma("pool", v_d[gt * 128:(gt + 1) * 128, :], vb, [r_vb], [])
                yield
                if ty == 0:
                    head_transposes(qb, r_qb, 8, qT_d, gt, rings)
                head_transposes(kb, r_kb, 8, kT_d, gt, rings)

            pipeline(tile_gen, range(NT))

        def phase_odd_attn(li):
            B.phase_begin()
            mvs = load_mod_vecs(li, None, None, 5, 1, 1.0, ntypes=1)
            wout = A.alloc([8, D], BF16)
            r_wout = S.res()
            B.dma("pool", wout, I["od_w_out"].rearrange("(k p) n -> p k n", p=128), [], [r_wout])
            wpool = A.alloc([4, 128], BF16)
            r_wpool = S.res()
            B.dma("pool", wpool, I["c_w_pool"].rearrange("g c d -> c g d"), [], [r_wpool])
            csc, r_csc = B.load_bc("sp", I["c_scale"][0:1, :], 512)
            band = A.alloc([4, 5, 128], BF16)
            r_band = S.res()
            B.dma("pool", band, I["k_band"][:, :, :, :], [], [r_band], max_dma_last_dim=2048)
            kTc = A.alloc([8, 256], BF16, parts=64)
            r_kTc = S.res()
            B.dma("sp", kTc, kT_d.rearrange("h d t -> d h t")[:, :, NTL * 128:NT * 128], [], [r_kTc])
            Vc = A.alloc([2, 8, 65], BF16)
            r_Vc = S.res()
            B.memset("dve", Vc[:, :, :, 64:65], 1.0, [r_Vc])
            load_V(Vc, r_Vc, NTL, 2, 8)
            biasr = Ring(B, 1, [8, 7, 128], F32)
            for bb_, r_ in biasr.bufs:
                B.memset("pool", bb_[:, :, 5:7, :], 0.0, [r_])
            kTwr = Ring(B, 4, [8, 640], BF16, parts=64)
            Vwr = Ring(B, 4, [5, 8, 65], BF16)
            for vb_, r_ in Vwr.bufs:
                B.memset("dve", vb_[:, :, :, 64:65], 1.0, [r_])
            qTr = Ring(B, 4, [8, 128], BF16, parts=64)
            xpr = Ring(B, 3, [3, 512], BF16)
            ppr = Ring(B, 2, [4, 128], BF16)
            tAr = Ring(B, 2, [512], F32)
            tBr = Ring(B, 2, [384], F32)
            pAr = Ring(B, 3, [512], BF16)
            pBr = Ring(B, 3, [384], BF16)
            mixr = Ring(B, 5, [D], BF16)
            str_ = Ring(B, 4, [8], F32)
            rings = {"mixT": Ring(B, 2, [8, 128], BF16), "hin": Ring(B, 2, [D], F32), "et": Ring(B, 1, [D], F32),
                     "hout": Ring(B, 2, [D], F32)}
            state = {"case": None, "bias": None, "r_bias": None}
            B.nrr = 6

            def case_of(n):
                return 0 if n == 0 else 1 if n == 1 else 3 if n == NTL - 2 else 4 if n == NTL - 1 else 2

            def tile_gen(n):
                bg_tick()
                case = case_of(n)
                if case != state["case"]:
                    bias_, r_bias_ = biasr.next()
                    B.dma("sp", bias_[:, :, 0:5, :], I["k_dbias"][case], [], [r_bias_])
                    state["case"] = case
                    state["bias"] = bias_
                    state["r_bias"] = r_bias_
                bias = state["bias"]
                r_bias = state["r_bias"]
                kt0 = min(max(n - 2, 0), NTL - 5)
                kTw, r_kTw = kTwr.next()
                B.dma("sp", kTw, kT_d.rearrange("h d t -> d h t")[:, :, kt0 * 128:(kt0 + 5) * 128], [], [r_kTw])
                Vw, r_Vw = Vwr.next()
                load_V(Vw, r_Vw, kt0, 5, 8)
                qT, r_qT = qTr.next()
                B.dma("sp", qT, qT_d.rearrange("h d t -> d h t")[:, :, n * 128:(n + 1) * 128], [], [r_qT])
                xw, r_xw = xpr.next()
                jts = [j for j in (n - 1, n, n + 1) if 0 <= j < NTL]
                j0 = jts[0]
                B.dma("sp", xw[:, 0:len(jts), :], xp_d[j0 * 128:(j0 + len(jts)) * 128, :].rearrange("(w p) f -> p w f", p=128), [], [r_xw])
                yield
                mix, r_mix = mixr.next()
                bk, r_bk = B.bank()
                for g in range(4):
                    for ji, j in enumerate(jts):
                        if j == n - 1:
                            typ = 0
                        elif j == n + 1:
                            typ = 2
                        else:
                            typ = 3 if n == 0 else (4 if n == NTL - 1 else 1)
                        B.mm(bk[:, g * 128:(g + 1) * 128], xw[:, ji, g * 128:(g + 1) * 128], band[:, g, typ, :],
                             ji == 0, ji == len(jts) - 1, [r_xw, r_band], [r_bk])
                pp, r_pp = ppr.next()
                B.cp("act", pp, bk.rearrange("p (g t) -> p g t", g=4), [], [r_bk, r_pp])
                bk2, r_bk2 = B.bank()
                for g in range(4):
                    B.mm(bk2[:, g * 128:(g + 1) * 128], pp[:, g, :], wpool[:, g, :], True, True, [r_pp, r_wpool], [r_bk2])
                B.tt("dve", mix[:, 0:512], bk2, csc, ALU.mult, [r_csc], [r_bk2, r_mix])
                yield

                def heads(hq):
                    pend = None
                    ob, r_ob = B.bank_fixed(6 + hq)
                    for hi in range(5):
                        cur = None
                        if hi < 4:
                            h = hq * 4 + hi
                            bA, r_bA = B.bank()
                            bB, r_bB = B.bank()
                            for s_ in range(4):
                                B.mm(bA[:, s_ * 128:(s_ + 1) * 128], kTw[:, h, s_ * 128:(s_ + 1) * 128], qT[:, h, :], True, True, [r_kTw, r_qT], [r_bA])
                            B.mm(bB[:, 0:128], kTw[:, h, 512:640], qT[:, h, :], True, True, [r_kTw, r_qT], [r_bB])
                            for s_ in range(2):
                                B.mm(bB[:, (1 + s_) * 128:(2 + s_) * 128], kTc[:, h, s_ * 128:(s_ + 1) * 128], qT[:, h, :], True, True, [r_kTc, r_qT], [r_bB])
                            tA, r_tA = tAr.next()
                            tB, r_tB = tBr.next()
                            B.stt("dve", tA, bA, 0.125, bias[:, h, 0:4, :].rearrange("p s q -> p (s q)"), ALU.mult, ALU.add, [r_bias], [r_bA, r_tA])
                            B.stt("dve", tB, bB[:, 0:384], 0.125, bias[:, h, 4:7, :].rearrange("p s q -> p (s q)"), ALU.mult, ALU.add, [r_bias], [r_bB, r_tB])
                            pA, r_pA = pAr.next()
                            pB, r_pB = pBr.next()
                            B.act(pA, tA, AF.Exp, [r_tA], [r_pA])
                            B.act(pB, tB, AF.Exp, [r_tB], [r_pB])
                            cur = (h, pA, r_pA, pB, r_pB)
                        if pend is not None:
                            ph, pA, r_pA, pB, r_pB = pend
                            osl = ob[:, (ph % 4) * 65:(ph % 4 + 1) * 65]
                            for s_ in range(7):
                                if s_ < 4:
                                    lhs = pA[:, s_ * 128:(s_ + 1) * 128]
                                    rp = r_pA
                                else:
                                    lhs = pB[:, (s_ - 4) * 128:(s_ - 3) * 128]
                                    rp = r_pB
                                if s_ < 5:
                                    rhs = Vw[:, s_, ph, :]
                                    rv = r_Vw
                                else:
                                    rhs = Vc[:, s_ - 5, ph, :]
                                    rv = r_Vc
                                B.mm(osl, lhs, rhs, s_ == 0, s_ == 6, [rp, rv], [r_ob])
                        pend = cur
                    ob3 = ob[:, 0:260].rearrange("p (h e) -> p h e", h=4)
                    sd, r_sd = str_.next()
                    B.recip(sd[:, 0:4], ob3[:, :, 64], [], [r_ob, r_sd])
                    B.tt("dve", mix[:, 512 + hq * 256:512 + (hq + 1) * 256].rearrange("p (h d) -> p h d", h=4), ob3[:, :, 0:64],
                         sd[:, 0:4].unsqueeze(2).to_broadcast([128, 4, 64]), ALU.mult, [r_sd], [r_ob, r_mix])

                heads(0)
                yield
                heads(1)
                yield
                outproj_residual(mix, r_mix, wout, r_wout, mvs[0][2], mvs[0][3], n, rings)

            pipeline(tile_gen, range(NTL), drain_before=lambda n: case_of(n) != state["case"])
            B.nrr = 8

        plist = [
            lambda: phase_mod(0, 0, 6),
            lambda: phase_ffn(0, 0, NT, src0, hsrc),
            lambda: phase_mod(0, 6, 18),
            lambda: phase_even_prep(0),
            lambda: phase_even_attn(0),
            lambda: phase_ffn(0, 1, NT, hsrc, hsrc),
            lambda: phase_mod(1),
            lambda: phase_ffn(1, 0, NT, hsrc, hsrc),
            lambda: phase_odd_prep(1),
            lambda: phase_odd_attn(1),
            lambda: phase_ffn(1, 1, NTL, hsrc, osrc),
        ]
        if dbg_phases is not None:
            plist = plist[:dbg_phases]
        for p in plist:
            p()
        S.emit()
        build_program.stats = S.stats
    return nc


def _rope_table():
    t = np.arange(S_LAT)
    row = (t // 64).astype(np.float32)
    col = (t % 64).astype(np.float32)
    m = 16
    inv = (1.0 / (10000.0 ** (np.arange(m, dtype=np.float32) / m))).astype(np.float32)
    ar = row[:, None] * inv[None, :]
    ac = col[:, None] * inv[None, :]
    cos = np.concatenate([np.cos(ar), np.cos(ar), np.cos(ac), np.cos(ac)], axis=1)
    sin = np.concatenate([-np.sin(ar), np.sin(ar), -np.sin(ac), np.sin(ac)], axis=1)
    return np.stack([cos, sin], axis=1).astype(np.float32)


def _amask():
    pj = np.arange(128)[:, None]
    pi = np.arange(128)[None, :]
    prev = (pj >= pi).astype(np.float32)
    nxt = (pj <= pi).astype(np.float32)
    return np.stack([prev, nxt], axis=1)


def _band():
    out = np.zeros((128, 4, 5, 128), np.float32)
    for gi, w in enumerate((2, 4, 8, 16)):
        def mat(n, jn):
            tg = n * 128 + np.arange(128)
            lo = np.clip(tg - w // 2, 0, S_LAT)
            hi = np.clip(tg + w - w // 2, 0, S_LAT)
            cnt = (hi - lo).astype(np.float32)
            jg = jn * 128 + np.arange(128)
            m = ((jg[:, None] >= lo[None, :]) & (jg[:, None] < hi[None, :])).astype(np.float32) / cnt[None, :]
            m = m - (jg[:, None] == tg[None, :]).astype(np.float32)
            return m
        out[:, gi, 0] = mat(5, 4)
        out[:, gi, 1] = mat(5, 5)
        out[:, gi, 2] = mat(5, 6)
        out[:, gi, 3] = mat(0, 0)
        out[:, gi, 4] = mat(NTL - 1, NTL - 1)
    return out


def _dbias(rpb):
    out = np.full((5, 128, 8, 5, 128), NEGB, np.float32)
    for case, n in enumerate((0, 1, 5, NTL - 2, NTL - 1)):
        kt0 = min(max(n - 2, 0), NTL - 5)
        i = np.arange(128)
        r = 2 * n + i // 64
        c = i % 64
        r0 = np.clip(r - 4, 0, 56)
        q0 = np.clip(c - 8, 0, 48)
        for s in range(5):
            kt = kt0 + s
            j = np.arange(128)
            kr = 2 * kt + j // 64
            kc = j % 64
            valid = ((kr[:, None] >= r0[None, :]) & (kr[:, None] < r0[None, :] + 8) &
                     (kc[:, None] >= q0[None, :]) & (kc[:, None] < q0[None, :] + 16))
            ri = np.clip(kr[:, None] - r[None, :] + 7, 0, 14)
            ci = np.clip(kc[:, None] - c[None, :] + 15, 0, 30)
            g = rpb[:, ri, ci]
            g = np.where(valid[None], g, np.float32(NEGB))
            out[case, :, :, s, :] = np.transpose(g, (1, 0, 2))
    return out


_CACHE = {}


def kernel(x, c, ctx, c_ctx, ada_w, ada_b, norm_g, ffn_w_gu, ffn_w_down,
           ev_w_in, ev_w_out, a_q_gain, a_k_gain, a_sink, b_v_gain, b_ws, b_bias,
           od_w_in, od_w_out, c_w_pool, c_scale, d_q_gain, d_k_gain, d_rpb, _dbg_phases=None, _dbg=False):
    f = lambda a: np.ascontiguousarray(np.asarray(a, dtype=np.float32))
    key = (_dbg_phases, _dbg)
    if key not in _CACHE:
        _CACHE[key] = build_program(_dbg_phases, _dbg)
    nc = _CACHE[key]
    shared = {
        "c_ctx": f(c_ctx).reshape(1, D), "ada_w": f(ada_w), "ada_b": f(ada_b), "norm_g": f(norm_g),
        "ffn_w_gu": f(ffn_w_gu), "ffn_w_down": f(ffn_w_down),
        "ev_w_in": f(ev_w_in)[0], "ev_w_out": f(ev_w_out)[0],
        "a_q_gain": f(a_q_gain), "a_k_gain": f(a_k_gain), "a_sink": f(a_sink),
        "b_v_gain": f(b_v_gain), "b_ws": f(b_ws)[0], "b_bias": f(b_bias)[0],
        "od_w_in": f(od_w_in)[0], "od_w_out": f(od_w_out)[0],
        "c_w_pool": f(c_w_pool)[0], "c_scale": f(c_scale),
        "d_q_gain": f(d_q_gain), "d_k_gain": f(d_k_gain),
        "k_ident": np.eye(128, dtype=np.float32), "k_rope": _rope_table(), "k_amask": _amask(),
        "k_band": _band(), "k_dbias": _dbias(f(d_rpb)[0]),
    }
    x = f(x); c = f(c); ctx = f(ctx)
    in_maps = []
    for b in range(8):
        m = dict(shared)
        m["x"] = x[b]
        m["ctx"] = ctx[b]
        m["c"] = c[b].reshape(1, D)
        in_maps.append(m)
    res = run_bass_kernel_spmd(nc, in_maps, core_ids=list(range(8)))
    kernel.last = res
    return np.stack([r["out"] for r in res.results], axis=0)
```

```python
import contextlib
import numpy as np
import concourse.bass as bass
import concourse.mybir as mybir
from concourse.bass_utils import run_bass_kernel_spmd

F32 = mybir.dt.float32
BF16 = mybir.dt.bfloat16
AF = mybir.ActivationFunctionType
ALU = mybir.AluOpType
AX = mybir.AxisListType

D = 1024
S_LAT = 4096
S_CTX = 256
NTL = 32
NT = 34
DFF = 2816
NFF = 22
EPS = 1e-6
NEGB = -30000.0

ENGS = ("sp", "act", "pool", "dve", "pe")
NDMA_SEM = 8


class Res:
    __slots__ = ("name", "last_w", "readers", "gen")

    def __init__(self, name):
        self.name = name
        self.last_w = None
        self.readers = {}
        self.gen = 0

    def bump(self):
        self.gen += 1
        return Ref(self, self.gen)


class Ref:
    __slots__ = ("phys", "gen")

    def __init__(self, phys, gen):
        self.phys = phys
        self.gen = gen


def _norm_res(lst):
    out = []
    for r in lst:
        if isinstance(r, Ref):
            assert r.gen == r.phys.gen, f"stale buffer reference {r.phys.name}"
            r = r.phys
        out.append(r)
    return out


def pipeline(make_gen, items, drain_before=None):
    active = []
    for it in items:
        if drain_before is not None and drain_before(it):
            while active:
                nxt = []
                for g in active:
                    try:
                        next(g)
                        nxt.append(g)
                    except StopIteration:
                        pass
                active = nxt
        nxt = []
        for g in active:
            try:
                next(g)
                nxt.append(g)
            except StopIteration:
                pass
        active = nxt
        g = make_gen(it)
        try:
            next(g)
            active.append(g)
        except StopIteration:
            pass
    while active:
        nxt = []
        for g in active:
            try:
                next(g)
                nxt.append(g)
            except StopIteration:
                pass
        active = nxt


class Op:
    __slots__ = ("eng", "fn", "deps", "dma", "signal", "sem", "val", "prewait", "bg")

    def __init__(self, eng, fn, dma):
        self.eng = eng
        self.fn = fn
        self.dma = dma
        self.deps = []
        self.signal = False
        self.sem = None
        self.val = 0
        self.prewait = None
        self.bg = False


class Sched:
    def __init__(self, nc):
        self.nc = nc
        self.ops = {e: [] for e in ENGS}
        self.bar = {}
        self.nres = 0

    def res(self, name=None):
        self.nres += 1
        return Res(name or f"r{self.nres}")

    def op(self, eng, fn, reads=(), writes=(), dma=False):
        reads = _norm_res(reads)
        writes = _norm_res(writes)
        o = Op(eng, fn, dma)
        deps = {}
        for r in reads:
            if r.last_w is not None:
                deps[id(r.last_w)] = r.last_w
        for r in writes:
            if r.last_w is not None:
                deps[id(r.last_w)] = r.last_w
            for rd in r.readers.values():
                if isinstance(rd, list):
                    for x in rd:
                        deps[id(x)] = x
                else:
                    deps[id(rd)] = rd
        for r in reads:
            if dma:
                r.readers.setdefault(("dma", eng), []).append(o)
            else:
                r.readers[eng] = o
        for r in writes:
            r.last_w = o
            r.readers = {}
        b = self.bar.pop(eng, None)
        if b:
            for x in b:
                deps[id(x)] = x
        dl = []
        for d in deps.values():
            if d is o:
                continue
            if (not dma) and (not d.dma) and d.eng == "pe" and eng == "pe":
                continue
            dl.append(d)
        o.deps = dl
        self.ops[eng].append(o)
        return o

    def dma(self, q, out, in_, reads=(), writes=(), **kw):
        return self.op(q, lambda e: e.dma_start(out=out, in_=in_, **kw), reads, writes, dma=True)

    def barrier(self):
        tails = []
        for e in ENGS:
            ops = self.ops[e]
            for o in reversed(ops):
                if not o.dma:
                    tails.append(o)
                    break
            cnt = 0
            for o in reversed(ops):
                if o.dma:
                    if not o.bg:
                        tails.append(o)
                    cnt += 1
                    if cnt >= NDMA_SEM:
                        break
        self.bar = {e: list(tails) for e in ENGS}

    def emit(self):
        nc = self.nc
        with contextlib.ExitStack() as st:
            csem = {e: st.enter_context(nc.semaphore(f"c_{e}")) for e in ENGS}
            dsem = {e: [st.enter_context(nc.semaphore(f"d_{e}{i}")) for i in range(NDMA_SEM)]
                    for e in ("sp", "act", "pool")}
            for e in ENGS:
                for o in self.ops[e]:
                    for d in o.deps:
                        d.signal = True
            self.stats = {}
            for e in ENGS:
                cnt = 0
                nd = 0
                for o in self.ops[e]:
                    if o.dma:
                        slot = nd % NDMA_SEM
                        o.sem = dsem[e][slot]
                        o.val = 16 * (nd // NDMA_SEM + 1)
                        if nd >= NDMA_SEM:
                            o.prewait = (dsem[e][slot], 16 * (nd // NDMA_SEM))
                        nd += 1
                    elif o.signal:
                        cnt += 1
                        o.sem = csem[e]
                        o.val = cnt
                self.stats[e] = (len(self.ops[e]), cnt, nd)
            block = st.enter_context(nc.Block())

            def run(eng_name, eng):
                seen = {}
                lastdma = {}
                for o in self.ops[eng_name]:
                    waits = []
                    if o.prewait is not None:
                        waits.append(o.prewait)
                    for d in o.deps:
                        waits.append((d.sem, d.val))
                    for sem, val in waits:
                        k = id(sem)
                        if seen.get(k, 0) >= val:
                            continue
                        seen[k] = val
                        eng.wait_ge(sem, val)
                    inst = o.fn(eng)
                    if o.dma:
                        inst.then_inc(o.sem, 16)
                        lastdma[id(o.sem)] = (o.sem, o.val)
                    elif o.signal:
                        inst.then_inc(o.sem, 1)
                for sem, val in lastdma.values():
                    if seen.get(id(sem), 0) < val:
                        eng.wait_ge(sem, val)

            @block.sync
            def _(e):
                run("sp", e)

            @block.scalar
            def _(e):
                run("act", e)

            @block.gpsimd
            def _(e):
                run("pool", e)

            @block.vector
            def _(e):
                run("dve", e)

            @block.tensor
            def _(e):
                run("pe", e)


class Arena:
    def __init__(self, nc, st, nbytes):
        self.nbytes = nbytes
        self.t = st.enter_context(nc.sbuf_tensor("arena", [128, nbytes // 2], BF16))
        self.off = 0
        self.base = 0

    def alloc(self, free, dtype, parts=128):
        free = list(free)
        n = int(np.prod(free))
        sz = n * (4 if dtype == F32 else 2)
        off = (self.off + 63) // 64 * 64
        assert off + sz <= self.nbytes, f"arena overflow {off + sz} > {self.nbytes}"
        self.off = off + sz
        ap = self.t[0:parts, off // 2: (off + sz) // 2]
        if dtype == F32:
            ap = ap.bitcast(F32)
        if len(free) == 2:
            ap = ap.rearrange("p (a b) -> p a b", a=free[0])
        elif len(free) == 3:
            ap = ap.rearrange("p (a b c) -> p a b c", a=free[0], b=free[1])
        elif len(free) == 4:
            ap = ap.rearrange("p (a b c d) -> p a b c d", a=free[0], b=free[1], c=free[2])
        return ap

    def mark_persistent(self):
        self.base = self.off

    def reset(self):
        self.off = self.base


class Ring:
    def __init__(self, B, n, free, dtype, parts=128):
        self.bufs = [(B.A.alloc(free, dtype, parts), B.S.res()) for _ in range(n)]
        self.i = 0

    def next(self):
        r = self.bufs[self.i % len(self.bufs)]
        self.i += 1
        return r[0], r[1].bump()


class Builder:
    def __init__(self, nc, st, dbg):
        self.nc = nc
        self.st = st
        self.S = Sched(nc)
        self.A = Arena(nc, st, 206 * 1024)
        self.banks = []
        for i in range(8):
            t = st.enter_context(nc.psum_tensor(f"bank{i}", [128, 512], F32))
            self.banks.append((t, self.S.res(f"bank{i}")))
        self.bi = 0
        self.nrr = 8
        self.dbg = dbg

    def bank(self):
        r = self.banks[self.bi % self.nrr]
        self.bi += 1
        return r[0][:], r[1].bump()

    def bank_fixed(self, idx):
        r = self.banks[idx]
        return r[0][:], r[1].bump()

    def dma(self, q, out, in_, reads=(), writes=(), **kw):
        return self.S.dma(q, out, in_, reads, writes, **kw)

    def mm(self, out, lhsT, rhs, start, stop, reads, writes):
        return self.S.op("pe", lambda e: e.matmul(out, lhsT=lhsT, rhs=rhs, start=start, stop=stop), reads, writes)

    def tr(self, out, in_, reads, writes):
        idn = self.ident
        return self.S.op("pe", lambda e: e.transpose(out, in_, idn), list(reads) + [self.r_ident], writes)

    def act(self, out, in_, func, reads, writes, scale=None, bias=None, accum=None):
        kw = {}
        if scale is not None:
            kw["scale"] = scale
        if bias is not None:
            kw["bias"] = bias
        if accum is not None:
            kw["accum_out"] = accum
        return self.S.op("act", lambda e: e.activation(out=out, in_=in_, func=func, **kw), reads, writes)

    def tt(self, eng, out, in0, in1, op, reads, writes):
        return self.S.op(eng, lambda e: e.tensor_tensor(out=out, in0=in0, in1=in1, op=op), reads, writes)

    def ts(self, eng, out, in0, s1, s2, op0, op1, reads, writes):
        if op1 is None:
            return self.S.op(eng, lambda e: e.tensor_scalar(out=out, in0=in0, scalar1=s1, scalar2=None, op0=op0), reads, writes)
        return self.S.op(eng, lambda e: e.tensor_scalar(out=out, in0=in0, scalar1=s1, scalar2=s2, op0=op0, op1=op1), reads, writes)

    def stt(self, eng, out, in0, scalar, in1, op0, op1, reads, writes):
        return self.S.op(eng, lambda e: e.scalar_tensor_tensor(out=out, in0=in0, scalar=scalar, in1=in1, op0=op0, op1=op1), reads, writes)

    def cp(self, eng, out, in_, reads, writes):
        if eng == "act":
            return self.S.op("act", lambda e: e.activation(out=out, in_=in_, func=AF.Copy), reads, writes)
        return self.S.op(eng, lambda e: e.tensor_copy(out=out, in_=in_), reads, writes)

    def recip(self, out, in_, reads, writes):
        return self.S.op("dve", lambda e: e.reciprocal(out=out, in_=in_), reads, writes)

    def memset(self, eng, out, val, writes):
        return self.S.op(eng, lambda e: e.memset(out, val), [], writes)

    def reduce_sum(self, out, in_, reads, writes):
        return self.S.op("dve", lambda e: e.tensor_reduce(out=out, in_=in_, axis=AX.X, op=ALU.add), reads, writes)

    def rstd(self, st, r_st, w, inv_n):
        self.ts("dve", st[:, w:2 * w], st[:, 0:w], inv_n, EPS, ALU.mult, ALU.add, [r_st], [r_st])
        self.act(st[:, w:2 * w], st[:, w:2 * w], AF.Sqrt, [r_st], [r_st])
        self.recip(st[:, 2 * w:3 * w], st[:, w:2 * w], [r_st], [r_st])
        return st[:, 2 * w:3 * w]

    def phase_begin(self):
        self.S.barrier()
        self.A.reset()

    def load_bc(self, q, src_1xn, n, name=None):
        t = self.A.alloc([n], F32)
        r = self.S.res(name)
        self.dma(q, t, src_1xn.partition_broadcast(128), [], [r])
        return t, r


def build_program(dbg_phases=None, dbg=False):
    nc = bass.Bass("TRN2", target_bir_lowering=False)
    I = {}

    def inp(name, shape, dt=F32):
        I[name] = nc.dram_tensor(name, list(shape), dt, kind="ExternalInput").ap()
        return I[name]

    inp("x", [S_LAT, D]); inp("ctx", [S_CTX, D]); inp("c", [1, D]); inp("c_ctx", [1, D])
    inp("ada_w", [2, D, 9 * D]); inp("ada_b", [2, 9 * D]); inp("norm_g", [2, 3, D])
    inp("ffn_w_gu", [2, 2, D, 2 * DFF]); inp("ffn_w_down", [2, 2, DFF, D])
    inp("ev_w_in", [D, 1792]); inp("ev_w_out", [D, D])
    inp("a_q_gain", [1, 64]); inp("a_k_gain", [1, 64]); inp("a_sink", [1, 8])
    inp("b_v_gain", [1, 512]); inp("b_ws", [8, 128, 128]); inp("b_bias", [8, 128])
    inp("od_w_in", [D, 2048]); inp("od_w_out", [D, D])
    inp("c_w_pool", [4, 128, 128]); inp("c_scale", [1, 512])
    inp("d_q_gain", [1, 64]); inp("d_k_gain", [1, 64])
    inp("k_ident", [128, 128]); inp("k_rope", [S_LAT, 2, 64]); inp("k_amask", [128, 2, 128])
    inp("k_band", [128, 4, 5, 128]); inp("k_dbias", [5, 128, 8, 5, 128])
    out = nc.dram_tensor("out", [S_LAT, D], F32, kind="ExternalOutput").ap()
    skind = "ExternalOutput" if dbg else "Internal"

    def scr(name, shape, dt):
        return nc.dram_tensor(name, list(shape), dt, kind=skind).ap()

    hA = scr("hA", [NT * 128, D], F32)
    mod_d = scr("mod_d", [2, 2, 9 * D], F32)
    qT_d = scr("qT_d", [8, 64, NT * 128], BF16)
    kT_d = scr("kT_d", [8, 64, NT * 128], BF16)
    v_d = scr("v_d", [NT * 128, 512], BF16)
    u_d = scr("u_d", [NT * 128, 512], F32)
    vv_d = scr("vv_d", [NT * 128, 512], BF16)
    xp_d = scr("xp_d", [S_LAT, 512], BF16)
    wgc_all = [nc.dram_tensor(f"wgc_d{f}", [NFF // 2, 128, 8 * 2 * 256], BF16, kind="Internal").ap() for f in range(4)]
    wdc_all = [nc.dram_tensor(f"wdc_d{f}", [128, NFF * D], BF16, kind="Internal").ap() for f in range(4)]
    adac_all = [nc.dram_tensor(f"adac_d{l_}", [18, 128, 8 * 512], BF16, kind="Internal").ap() for l_ in range(2)]

    st = contextlib.ExitStack()
    with st:
        B = Builder(nc, st, dbg)
        S, A = B.S, B.A
        B.ident = A.alloc([128], BF16)
        B.r_ident = S.res("ident")
        B.dma("pool", B.ident, I["k_ident"][:, :], [], [B.r_ident])
        B.identf = A.alloc([128], F32)
        B.dma("sp", B.identf, I["k_ident"][:, :], [], [B.r_ident])
        A.mark_persistent()

        bgq = []
        wres = {}

        def bg_add_ffn(f, li, which):
            wgu_ = I["ffn_w_gu"][li, which]
            for cp_ in range(NFF // 2):
                dst4 = wgc_all[f][cp_].rearrange("p (k g n) -> p k g n", k=8, g=2)
                for g_ in range(2):
                    r = S.res()
                    wres[("gu", f, cp_, g_)] = r
                    bgq.append((dst4[:, :, g_, :],
                                wgu_[:, g_ * DFF + cp_ * 256:g_ * DFF + (cp_ + 1) * 256].rearrange("(k p) n -> p k n", p=128), r))
            wd3 = wdc_all[f].rearrange("p (c n) -> p c n", c=NFF)
            for c4 in range(0, NFF, 2):
                r = S.res()
                wres[("wd", f, c4)] = r
                bgq.append((wd3[:, c4:c4 + 2, :],
                            I["ffn_w_down"][li, which, c4 * 128:(c4 + 2) * 128, :].rearrange("(c p) n -> p c n", p=128), r))

        def bg_add_ada(li, nb0=0, nb1=18):
            for nb in range(nb0, nb1):
                r = S.res()
                wres[("ada", li, nb)] = r
                bgq.append((adac_all[li][nb].rearrange("p (k n) -> p k n", k=8),
                            I["ada_w"][li, :, nb * 512:(nb + 1) * 512].rearrange("(k p) n -> p k n", p=128), r))

        def bg_need(pred):
            last = -1
            for i, (_, _, r) in enumerate(bgq):
                if pred(r):
                    last = i
            if last >= 0:
                bg_tick(last + 1)

        def bg_tick(n=1):
            for _ in range(n):
                if not bgq:
                    return
                dst, src, r = bgq.pop(0)
                o = B.dma("pool", dst, src, [], [r])
                o.bg = True

        wcache = {}

        def bg_add_w(name, src, ncols, split):
            t = nc.dram_tensor(f"wc_{name}", [128, 8 * ncols], BF16, kind="Internal").ap()
            rl = []
            w_ = ncols // split
            for i in range(split):
                r = S.res()
                rl.append(r)
                bgq.append((t.rearrange("p (k n) -> p k n", k=8)[:, :, i * w_:(i + 1) * w_],
                            src[:, i * w_:(i + 1) * w_].rearrange("(k p) n -> p k n", p=128), r))
            wcache[name] = (t, rl)

        def load_w_bf16(name, dst, r_dst, src, ncols):
            if name in wcache:
                t, rl = wcache[name]
                ids = {id(r) for r in rl}
                bg_need(lambda r: id(r) in ids)
                B.dma("sp", dst.rearrange("p k n -> p (k n)"), t, rl, [r_dst])
            else:
                step = 1024 if ncols > 1792 else ncols
                for c0 in range(0, ncols, step):
                    B.dma("pool", dst[:, :, c0:c0 + step], src[:, c0:c0 + step].rearrange("(k p) n -> p k n", p=128), [], [r_dst])

        bg_add_ada(0, 6, 18)
        bg_add_w("ev_w_in", I["ev_w_in"], 1792, 2)
        bg_add_w("ev_w_out", I["ev_w_out"], 1024, 1)
        bg_add_ffn(1, 0, 1)
        bg_add_ada(1)
        bg_add_ffn(2, 1, 0)
        bg_add_w("od_w_in", I["od_w_in"], 2048, 2)
        bg_add_w("od_w_out", I["od_w_out"], 1024, 1)
        bg_add_ffn(3, 1, 1)

        def src0(gt):
            if gt < NTL:
                return I["x"][gt * 128:(gt + 1) * 128, :]
            return I["ctx"][(gt - NTL) * 128:(gt - NTL + 1) * 128, :]

        def hsrc(gt):
            return hA[gt * 128:(gt + 1) * 128, :]

        def osrc(gt):
            return out[gt * 128:(gt + 1) * 128, :]

        phases = []

        def phase_mod(li, nb0=0, nb1=18):
            B.phase_begin()
            mine_ = {id(r) for k_, r in wres.items() if k_[0] == "ada" and k_[1] == li and nb0 <= k_[2] < nb1}
            bg_need(lambda r: id(r) in mine_)
            cc = A.alloc([2, 128], F32, parts=8)
            r_cc = S.res()
            B.dma("sp", cc[:, 0, :], I["c"][0, :].rearrange("(k p) -> k p", p=128), [], [r_cc])
            B.dma("sp", cc[:, 1, :], I["c_ctx"][0, :].rearrange("(k p) -> k p", p=128), [], [r_cc])
            ccb = A.alloc([2, 128], BF16, parts=8)
            r_ccb = S.res()
            B.act(ccb, cc, AF.Silu, [r_cc], [r_ccb])
            cs = A.alloc([8, 2], BF16)
            r_cs = S.res()
            bk, r_bk = B.bank()
            bkb = bk.bitcast(BF16)
            for j in range(2):
                B.S.op("pe", lambda e, j=j: e.transpose(bkb[:, j * 8:(j + 1) * 8], ccb[:, j, :], B.ident[0:8, 0:8]), [r_ccb, B.r_ident], [r_bk])
            B.cp("dve", cs, bkb[:, 0:16].rearrange("p (j k) -> p k j", j=2), [], [r_bk, r_cs])
            adab = A.alloc([9 * D], F32, parts=2)
            r_adab = S.res()
            B.dma("sp", adab, I["ada_b"][li:li + 1, :].partition_broadcast(2), [], [r_adab])
            msb = A.alloc([9 * D], F32, parts=2)
            r_msb = S.res()
            wr = Ring(B, 3, [8, 512], BF16)
            for nb in range(nb0, nb1):
                w, r_w = wr.next()
                if ("ada", li, nb) in wres:
                    B.dma("sp", w.rearrange("p k n -> p (k n)"), adac_all[li][nb], [wres[("ada", li, nb)]], [r_w])
                else:
                    B.dma("pool", w, I["ada_w"][li, :, nb * 512:(nb + 1) * 512].rearrange("(k p) n -> p k n", p=128), [], [r_w])
                bk, r_bk = B.bank()
                for k in range(8):
                    B.mm(bk[0:2, :], cs[:, k, :], w[:, k, :], k == 0, k == 7, [r_cs, r_w], [r_bk])
                B.tt("dve", msb[:, nb * 512:(nb + 1) * 512], bk[0:2, :], adab[:, nb * 512:(nb + 1) * 512], ALU.add,
                     [r_adab], [r_bk, r_msb])
            B.dma("sp", mod_d[li, :, nb0 * 512:nb1 * 512], msb[:, nb0 * 512:nb1 * 512], [r_msb], [])

        def load_mod_vecs(li, j_shift, j_scale, j_gate, gi, gate_mul, ntypes=2):
            outl = []
            gbc, r_g = (None, None)
            if j_scale is not None:
                gbc, r_g = B.load_bc("sp", I["norm_g"][li, gi:gi + 1, :], D)
            for ty in range(ntypes):
                r = S.res()
                sh = Gm = gt_ = None
                if j_shift is not None:
                    sh = A.alloc([D], F32)
                    B.dma("sp", sh, mod_d[li, ty:ty + 1, j_shift * D:(j_shift + 1) * D].partition_broadcast(128), [], [r])
                if j_scale is not None:
                    Gm = A.alloc([D], F32)
                    B.dma("sp", Gm, mod_d[li, ty:ty + 1, j_scale * D:(j_scale + 1) * D].partition_broadcast(128), [], [r])
                    B.stt("dve", Gm, Gm, 1.0, gbc, ALU.add, ALU.mult, [r_g, r], [r])
                if j_gate is not None:
                    gt_ = A.alloc([D], F32)
                    B.dma("sp", gt_, mod_d[li, ty:ty + 1, j_gate * D:(j_gate + 1) * D].partition_broadcast(128), [], [r])
                    if gate_mul != 1.0:
                        B.ts("dve", gt_, gt_, gate_mul, None, ALU.mult, None, [r], [r])
                outl.append((sh, Gm, gt_, r))
            return outl

        def norm_tile(hin, r_hin, mv, rings):
            sh, Gm, _, r_mv = mv
            sqj, r_sqj = rings["sqj"]
            st_, r_st = rings["st"].next()
            B.memset("dve", st_[:, 0:1], 0.0, [r_st])
            B.act(sqj, hin, AF.Square, [r_hin, r_st], [r_sqj, r_st], accum=st_[:, 0:1])
            rs = B.rstd(st_, r_st, 1, 1.0 / D)
            z1, r_z1 = rings["z1"].next()
            B.stt("dve", z1, hin, rs, Gm, ALU.mult, ALU.mult, [r_hin, r_st, r_mv], [r_z1])
            zt, r_zt = rings["ztok"].next()
            B.tt("pool", zt, z1, sh, ALU.add, [r_z1, r_mv], [r_zt])
            return zt, r_zt

        def norm_tile_g(hin, r_hin, mv, rings):
            sh, Gm, _, r_mv = mv
            sqj, r_sqj = rings["sqj"]
            st_, r_st = rings["st"].next()
            B.memset("dve", st_[:, 0:1], 0.0, [r_st])
            B.act(sqj, hin, AF.Square, [r_hin, r_st], [r_sqj, r_st], accum=st_[:, 0:1])
            yield
            B.ts("dve", st_[:, 1:2], st_[:, 0:1], 1.0 / D, EPS, ALU.mult, ALU.add, [r_st], [r_st])
            yield
            B.act(st_[:, 1:2], st_[:, 1:2], AF.Sqrt, [r_st], [r_st])
            yield
            B.recip(st_[:, 2:3], st_[:, 1:2], [r_st], [r_st])
            z1, r_z1 = rings["z1"].next()
            B.stt("dve", z1, hin, st_[:, 2:3], Gm, ALU.mult, ALU.mult, [r_hin, r_st, r_mv], [r_z1])
            yield
            zt, r_zt = rings["ztok"].next()
            B.tt("pool", zt, z1, sh, ALU.add, [r_z1, r_mv], [r_zt])
            return zt, r_zt

        def chains_g(items, rings):
            sts = []
            for it in items:
                w = it["nh"] * 64
                sq, r_sq = rings["sq"].next()
                B.act(sq[:, 0:w], it["src"][:, 0:w], AF.Square, [it["r_src"]], [r_sq])
                it["sq"], it["r_sq"] = sq, r_sq
            yield
            for it in items:
                nh = it["nh"]
                w = nh * 64
                st_, r_st = rings["st"].next()
                B.reduce_sum(st_[:, 0:nh], it["sq"][:, 0:w].rearrange("p (h d) -> p h d", h=nh), [it["r_sq"]], [r_st])
                B.ts("dve", st_[:, nh:2 * nh], st_[:, 0:nh], 1.0 / 64, EPS, ALU.mult, ALU.add, [r_st], [r_st])
                it["st"], it["r_st"] = st_, r_st
            yield
            for it in items:
                nh = it["nh"]
                B.act(it["st"][:, nh:2 * nh], it["st"][:, nh:2 * nh], AF.Sqrt, [it["r_st"]], [it["r_st"]])
            yield
            for it in items:
                nh = it["nh"]
                w = nh * 64
                st_ = it["st"]
                B.recip(st_[:, 2 * nh:3 * nh], st_[:, nh:2 * nh], [it["r_st"]], [it["r_st"]])
                x3 = it["src"][:, 0:w].rearrange("p (h d) -> p h d", h=nh)
                B.tt("dve", x3, x3, st_[:, 2 * nh:3 * nh].unsqueeze(2).to_broadcast([128, nh, 64]), ALU.mult,
                     [it["r_st"], it["r_src"]], [it["r_src"]])
            yield
            for it in items:
                nh = it["nh"]
                w = nh * 64
                if it.get("gain_full"):
                    B.tt("pool", it["dst"], it["src"][:, 0:w], it["gain"], ALU.mult, [it["r_gain"], it["r_src"]], [it["r_dst"]])
                elif it.get("dst") is not None:
                    B.tt("pool", it["dst"].rearrange("p (h d) -> p h d", h=nh), it["src"][:, 0:w].rearrange("p (h d) -> p h d", h=nh),
                         it["gain"].unsqueeze(1).to_broadcast([128, nh, 64]), ALU.mult, [it["r_gain"], it["r_src"]], [it["r_dst"]])
                else:
                    x3 = it["src"][:, 0:w].rearrange("p (h d) -> p h d", h=nh)
                    B.tt("pool", x3, x3, it["gain"].unsqueeze(1).to_broadcast([128, nh, 64]), ALU.mult,
                         [it["r_gain"], it["r_src"]], [it["r_src"]])

        def rope_g(items, ropt, r_ropt, rings):
            tmp = []
            for (qn, r_qn, nh, outb, r_out) in items:
                w = nh * 64
                a_, r_a = rings["ra"].next()
                q3 = qn[:, 0:w].rearrange("p (h d) -> p h d", h=nh)
                a3 = a_[:, 0:w].rearrange("p (h d) -> p h d", h=nh)
                B.tt("pool", a3, q3, ropt[:, 0, :].unsqueeze(1).to_broadcast([128, nh, 64]), ALU.mult, [r_qn, r_ropt], [r_a])
                b_, r_b = rings["rb"].next()
                q5 = qn[:, 0:w].rearrange("p (h a s d) -> p h a s d", h=nh, a=2, s=2)
                b5 = b_[:, 0:w].rearrange("p (h a s d) -> p h a s d", h=nh, a=2, s=2)
                s4 = ropt[:, 1, :].rearrange("p (a s d) -> p a s d", a=2, s=2)
                for ax in range(2):
                    for s_ in range(2):
                        B.tt("dve", b5[:, :, ax, s_, :], q5[:, :, ax, 1 - s_, :],
                             s4[:, ax, s_, :].unsqueeze(1).to_broadcast([128, nh, 16]), ALU.mult, [r_qn, r_ropt], [r_b])
                tmp.append((a_, r_a, b_, r_b))
            yield
            for (qn, r_qn, nh, outb, r_out), (a_, r_a, b_, r_b) in zip(items, tmp):
                w = nh * 64
                B.tt("dve", outb[:, 0:w], a_[:, 0:w], b_[:, 0:w], ALU.add, [r_a, r_b], [r_out])

        def transpose8(src, r_src, dst3, r_dst, eng="act"):
            bk, r_bk = B.bank()
            bkb = bk.bitcast(BF16)
            for k in range(8):
                B.tr(bkb[:, k * 128:(k + 1) * 128], src[:, k * 128:(k + 1) * 128], [r_src], [r_bk])
            B.cp(eng, dst3, bkb.rearrange("p (k t) -> p k t", k=8), [], [r_bk, r_dst])

        def phase_ffn(li, which, ntiles, srcf, dstf):
            B.phase_begin()
            f = li * 2 + which
            cached = ("wd", f, 0) in wres
            if cached:
                mine = {id(r) for k_, r in wres.items() if k_[0] in ("gu", "wd") and k_[1] == f}
                bg_need(lambda r: id(r) in mine)
            wgc_d = wgc_all[f]
            j0 = 0 if which == 0 else 6
            gi = 0 if which == 0 else 2
            mvs = load_mod_vecs(li, j0, j0 + 1, j0 + 2, gi, 0.5, ntypes=2 if ntiles > NTL else 1)
            wd = A.alloc([NFF, D], BF16)
            r_wd = S.res()

            def load_wd(c4):
                if cached:
                    B.dma("sp", wd[:, c4:c4 + 2, :], wdc_all[f].rearrange("p (c n) -> p c n", c=NFF)[:, c4:c4 + 2, :],
                          [wres[("wd", f, c4)]], [r_wd])
                else:
                    B.dma("pool", wd[:, c4:c4 + 2, :],
                          I["ffn_w_down"][li, which, c4 * 128:(c4 + 2) * 128, :].rearrange("(c p) n -> p c n", p=128), [], [r_wd])
            if ntiles == NT:
                groups = [list(range(0, 9)), list(range(9, 18)), list(range(18, 26)), list(range(26, 34))]
            else:
                groups = [list(range(g * 8, g * 8 + 8)) for g in range(4)]
            wc_res = [S.res() for _ in range(NFF // 2)]
            GM = max(len(g) for g in groups)
            zTs = [A.alloc([8, GM * 128], BF16) for _ in range(2)]
            zress = [[S.res() for _ in range(GM)] for _ in range(2)]
            actT = A.alloc([NFF, GM * 128], BF16)
            wgr = Ring(B, 2, [8, 2, 256], BF16)
            hinr = Ring(B, 2, [D], F32)
            rings = {"sqj": (A.alloc([D], BF16), S.res()), "st": Ring(B, 2, [3], F32),
                     "z1": Ring(B, 1, [D], F32), "ztok": Ring(B, 2, [D], BF16)}
            stmpr = Ring(B, 2, [512], F32)
            etmpr = Ring(B, 1, [D], F32)
            houtr = Ring(B, 1, [D], F32)
            wgu = I["ffn_w_gu"][li, which]

            def stage1_tile(gidx, lt, gt):
                ty = 0 if gt < NTL else 1
                hin, r_hin = hinr.next()
                B.dma("sp", hin, srcf(gt), [], [r_hin])
                zt, r_zt = norm_tile(hin, r_hin, mvs[ty], rings)
                transpose8(zt, r_zt, zTs[gidx % 2][:, :, lt * 128:(lt + 1) * 128], zress[gidx % 2][lt])

            for lt, gt in enumerate(groups[0]):
                stage1_tile(0, lt, gt)
            for gidx, grp in enumerate(groups):
                zT = zTs[gidx % 2]
                zres = zress[gidx % 2]
                T = len(grp) * 128
                nblk = (T + 511) // 512
                bs = T // nblk
                blocks = [(i * bs, (i + 1) * bs if i < nblk - 1 else T) for i in range(nblk)]
                ares = [S.res() for _ in blocks]
                pending = list(enumerate(groups[gidx + 1])) if gidx + 1 < len(groups) else []
                npend = len(pending)
                nunits = NFF * nblk
                unit = 0
                emitted = 0
                for cp_ in range(NFF // 2):
                    wg, r_wg = wgr.next()
                    if cached:
                        B.dma("sp", wg.rearrange("p k g n -> p (k g n)"), wgc_d[cp_],
                              [wres[("gu", f, cp_, 0)], wres[("gu", f, cp_, 1)]], [r_wg])
                    elif gidx == 0:
                        B.dma("pool", wg[:, :, 0, :], wgu[:, cp_ * 256:(cp_ + 1) * 256].rearrange("(k p) n -> p k n", p=128), [], [r_wg])
                        B.dma("pool", wg[:, :, 1, :], wgu[:, DFF + cp_ * 256:DFF + (cp_ + 1) * 256].rearrange("(k p) n -> p k n", p=128), [], [r_wg])
                        B.dma("pool", wgc_d[cp_], wg.rearrange("p k g n -> p (k g n)"), [r_wg], [wc_res[cp_]])
                    else:
                        B.dma("sp", wg.rearrange("p k g n -> p (k g n)"), wgc_d[cp_], [wc_res[cp_]], [r_wg])
                    if gidx == 0:
                        load_wd(cp_ * 2)
                    for ci in range(2):
                        c = cp_ * 2 + ci
                        for bi_, (a, b_) in enumerate(blocks):
                            n = b_ - a
                            zr = [zres[t] for t in range(a // 128, (b_ - 1) // 128 + 1)]
                            bg, r_bg = B.bank()
                            bu, r_bu = B.bank()
                            for k in range(8):
                                B.mm(bg[:, 0:n], wg[:, k, 0, ci * 128:(ci + 1) * 128], zT[:, k, a:b_], k == 0, k == 7, [r_wg] + zr, [r_bg])
                            for k in range(8):
                                B.mm(bu[:, 0:n], wg[:, k, 1, ci * 128:(ci + 1) * 128], zT[:, k, a:b_], k == 0, k == 7, [r_wg] + zr, [r_bu])
                            stp, r_stp = stmpr.next()
                            B.act(stp[:, 0:n], bg[:, 0:n], AF.Silu, [], [r_bg, r_stp])
                            B.tt("dve", actT[:, c, a:b_], stp[:, 0:n], bu[:, 0:n], ALU.mult, [r_stp], [r_bu, ares[bi_]])
                            unit += 1
                            if unit % 4 == 0 and (cached or gidx > 0):
                                bg_tick()
                            while emitted < npend and unit * npend >= (emitted + 1) * int(nunits * 0.7):
                                lt2, gt2 = pending[emitted]
                                stage1_tile(gidx + 1, lt2, gt2)
                                emitted += 1
                while emitted < npend:
                    lt2, gt2 = pending[emitted]
                    stage1_tile(gidx + 1, lt2, gt2)
                    emitted += 1
                for lt, gt in enumerate(grp):
                    ty = 0 if gt < NTL else 1
                    gate = mvs[ty][2]
                    r_mv = mvs[ty][3]
                    ar = [ares[i] for i, (a, b_) in enumerate(blocks) if a < (lt + 1) * 128 and b_ > lt * 128]
                    b0, r_b0 = B.bank()
                    b1, r_b1 = B.bank()
                    bb = [(b0, r_b0), (b1, r_b1)]
                    for c in range(NFF):
                        for hf in range(2):
                            B.mm(bb[hf][0], actT[:, c, lt * 128:(lt + 1) * 128], wd[:, c, hf * 512:(hf + 1) * 512],
                                 c == 0, c == NFF - 1, ar + [r_wd], [bb[hf][1]])
                    hin, r_hin = hinr.next()
                    B.dma("sp", hin, srcf(gt), [], [r_hin])
                    et, r_et = etmpr.next()
                    for hf in range(2):
                        B.tt("dve", et[:, hf * 512:(hf + 1) * 512], bb[hf][0], gate[:, hf * 512:(hf + 1) * 512], ALU.mult,
                             [r_mv], [bb[hf][1], r_et])
                    ho, r_ho = houtr.next()
                    B.tt("pool", ho, et, hin, ALU.add, [r_et, r_hin], [r_ho])
                    B.dma("pool", dstf(gt), ho, [r_ho], [])

        def head_norm(bank_ap, r_bank, nh, gain_bc, r_gain, rings, name):
            sq, r_sq = rings["sq"].next()
            w = nh * 64
            B.act(sq[:, 0:w], bank_ap, AF.Square, [], [r_bank, r_sq])
            st_, r_st = rings["st"].next()
            B.reduce_sum(st_[:, 0:nh], sq[:, 0:w].rearrange("p (h d) -> p h d", h=nh), [r_sq], [r_st])
            rs = B.rstd(st_, r_st, nh, 1.0 / 64)
            qn, r_qn = rings[name].next()
            qn3 = qn[:, 0:w].rearrange("p (h d) -> p h d", h=nh)
            B.tt("dve", qn3, bank_ap.rearrange("p (h d) -> p h d", h=nh), rs.unsqueeze(2).to_broadcast([128, nh, 64]), ALU.mult,
                 [r_st], [r_bank, r_qn])
            B.tt("pool", qn3, qn3, gain_bc.unsqueeze(1).to_broadcast([128, nh, 64]), ALU.mult, [r_gain, r_qn], [r_qn])
            return qn, r_qn

        def rope(qn, r_qn, nh, ropt, r_ropt, outb, r_out, rings):
            w = nh * 64
            a_, r_a = rings["ra"].next()
            b_, r_b = rings["rb"].next()
            q3 = qn[:, 0:w].rearrange("p (h d) -> p h d", h=nh)
            a3 = a_[:, 0:w].rearrange("p (h d) -> p h d", h=nh)
            B.tt("pool", a3, q3, ropt[:, 0, :].unsqueeze(1).to_broadcast([128, nh, 64]), ALU.mult, [r_qn, r_ropt], [r_a])
            q5 = qn[:, 0:w].rearrange("p (h a s d) -> p h a s d", h=nh, a=2, s=2)
            b5 = b_[:, 0:w].rearrange("p (h a s d) -> p h a s d", h=nh, a=2, s=2)
            s4 = ropt[:, 1, :].rearrange("p (a s d) -> p a s d", a=2, s=2)
            for ax in range(2):
                for s in range(2):
                    B.tt("dve", b5[:, :, ax, s, :], q5[:, :, ax, 1 - s, :],
                         s4[:, ax, s, :].unsqueeze(1).to_broadcast([128, nh, 16]), ALU.mult, [r_qn, r_ropt], [r_b])
            B.tt("dve", outb[:, 0:w], a_[:, 0:w], b_[:, 0:w], ALU.add, [r_a, r_b], [r_out])

        def head_transposes(src, r_src, nh, dst_dram, gt, rings):
            bk, r_bk = B.bank()
            bkb = bk.bitcast(BF16)
            for h in range(nh):
                B.tr(bkb[0:64, h * 128:(h + 1) * 128], src[:, h * 64:(h + 1) * 64], [r_src], [r_bk])
            ts_, r_ts = rings["hT"].next()
            B.cp("act", ts_[:, 0:nh, :], bkb[0:64, 0:nh * 128].rearrange("p (h t) -> p h t", h=nh), [], [r_bk, r_ts])
            B.dma("sp", dst_dram.rearrange("h d t -> d h t")[:, 0:nh, gt * 128:(gt + 1) * 128], ts_[:, 0:nh, :], [r_ts], [])

        def phase_even_prep(li):
            B.phase_begin()
            mvs = load_mod_vecs(li, 3, 4, None, 1, 1.0)
            win = A.alloc([8, 1792], BF16)
            r_win = S.res()
            load_w_bf16("ev_w_in", win, r_win, I["ev_w_in"], 1792)
            qg_bc, r_qg = B.load_bc("sp", I["a_q_gain"][0:1, :], 64)
            kg_bc, r_kg = B.load_bc("sp", I["a_k_gain"][0:1, :], 64)
            vg_bc, r_vg = B.load_bc("sp", I["b_v_gain"][0:1, :], 512)
            hinr = Ring(B, 6, [D], F32)
            rings = {"sqj": (A.alloc([D], BF16), S.res()), "st": Ring(B, 24, [30], F32),
                     "z1": Ring(B, 2, [D], F32), "ztok": Ring(B, 3, [D], BF16),
                     "sq": Ring(B, 6, [512], F32),
                     "ra": Ring(B, 3, [512], F32), "rb": Ring(B, 3, [512], F32), "hT": Ring(B, 3, [8, 128], BF16, parts=64)}
            zTr = Ring(B, 3, [8, 128], BF16)
            ropr = Ring(B, 14, [2, 64], F32)
            qrr = Ring(B, 6, [512], F32)
            krr = Ring(B, 6, [128], F32)
            gvr = Ring(B, 6, [512], F32)
            qbr = Ring(B, 9, [512], BF16)
            kbr = Ring(B, 9, [128], BF16)
            vbr = Ring(B, 4, [128], BF16)
            ur = Ring(B, 3, [512], F32)
            vvr = Ring(B, 8, [512], BF16)
            nsl = [(0, 512), (512, 768), (768, 1280), (1280, 1792)]

            def tile_gen(gt):
                bg_tick()
                ty = 0 if gt < NTL else 1
                hin, r_hin = hinr.next()
                B.dma("sp", hin, hsrc(gt), [], [r_hin])
                if ty == 0:
                    rt, r_rt = ropr.next()
                    B.dma("sp", rt, I["k_rope"][gt * 128:(gt + 1) * 128, :, :], [], [r_rt])
                yield
                zt, r_zt = yield from norm_tile_g(hin, r_hin, mvs[ty], rings)
                yield
                zT, r_zT = zTr.next()
                transpose8(zt, r_zt, zT, r_zT)
                bks = [B.bank() for _ in range(4)]
                for k in range(8):
                    for i, (n0, n1) in enumerate(nsl):
                        B.mm(bks[i][0][:, 0:n1 - n0], zT[:, k, :], win[:, k, n0:n1], k == 0, k == 7, [r_zT, r_win], [bks[i][1]])
                (bq, r_bq), (bkv, r_bkv), (bbu, r_bbu), (bbv, r_bbv) = bks
                yield
                qr, r_qr = qrr.next()
                B.cp("act", qr, bq, [], [r_bq, r_qr])
                kr, r_kr = krr.next()
                B.cp("act", kr, bkv[:, 0:128], [], [r_bkv, r_kr])
                vb, r_vb = vbr.next()
                B.cp("act", vb, bkv[:, 128:256], [], [r_bkv, r_vb])
                u_, r_u = ur.next()
                B.act(u_, bbu, AF.Gelu_apprx_tanh, [], [r_bbu, r_u])
                gv, r_gv = gvr.next()
                B.act(gv, bbv, AF.Gelu_apprx_tanh, [], [r_bbv, r_gv])
                yield
                B.dma("sp", v_d[gt * 128:(gt + 1) * 128, 0:128], vb, [r_vb], [])
                B.dma("sp", u_d[gt * 128:(gt + 1) * 128, :], u_, [r_u], [])
                vv, r_vv = vvr.next()
                items = [dict(src=qr, r_src=r_qr, nh=8, gain=qg_bc, r_gain=r_qg),
                         dict(src=kr, r_src=r_kr, nh=2, gain=kg_bc, r_gain=r_kg),
                         dict(src=gv, r_src=r_gv, nh=8, gain=vg_bc, r_gain=r_vg, gain_full=True, dst=vv, r_dst=r_vv)]
                qb, r_qb = qbr.next()
                kb, r_kb = kbr.next()
                if ty == 1:
                    items[0]["dst"], items[0]["r_dst"] = qb, r_qb
                    items[1]["dst"], items[1]["r_dst"] = kb, r_kb
                yield from chains_g(items, rings)
                yield
                B.dma("sp", vv_d[gt * 128:(gt + 1) * 128, :], vv, [r_vv], [])
                if ty == 0:
                    yield from rope_g([(qr, r_qr, 8, qb, r_qb), (kr, r_kr, 2, kb, r_kb)], rt, r_rt, rings)
                    yield
                head_transposes(qb, r_qb, 8, qT_d, gt, rings)
                head_transposes(kb, r_kb, 2, kT_d, gt, rings)

            pipeline(tile_gen, range(NT))

        def outproj_residual(mix, r_mix, wout, r_wout, gate, r_gate, gt, rings):
            mT, r_mT = rings["mixT"].next()
            transpose8(mix, r_mix, mT, r_mT)
            b0, r_b0 = B.bank()
            b1, r_b1 = B.bank()
            bb = [(b0, r_b0), (b1, r_b1)]
            for k in range(8):
                for hf in range(2):
                    B.mm(bb[hf][0], mT[:, k, :], wout[:, k, hf * 512:(hf + 1) * 512], k == 0, k == 7, [r_mT, r_wout], [bb[hf][1]])
            hin, r_hin = rings["hin"].next()
            B.dma("sp", hin, hsrc(gt), [], [r_hin])
            et, r_et = rings["et"].next()
            for hf in range(2):
                B.tt("dve", et[:, hf * 512:(hf + 1) * 512], bb[hf][0], gate[:, hf * 512:(hf + 1) * 512], ALU.mult,
                     [r_gate], [bb[hf][1], r_et])
            ho, r_ho = rings["hout"].next()
            B.tt("pool", ho, et, hin, ALU.add, [r_et, r_hin], [r_ho])
            B.dma("pool", hsrc(gt), ho, [r_ho], [])

        def load_V(dst, r_dst, kt0, nw, nk):
            for w in range(nw):
                B.dma("sp", dst[:, w, :, 0:64],
                      v_d[(kt0 + w) * 128:(kt0 + w + 1) * 128, 0:nk * 64].rearrange("p (k d) -> p k d", k=nk), [], [r_dst])

        def phase_even_attn(li):
            B.phase_begin()
            mvs = load_mod_vecs(li, None, None, 5, 1, 1.0)
            wout = A.alloc([8, D], BF16)
            r_wout = S.res()
            load_w_bf16("ev_w_out", wout, r_wout, I["ev_w_out"], 1024)
            wsn = A.alloc([8, 128], BF16)
            r_wsn = S.res()
            B.dma("pool", wsn, I["b_ws"].rearrange("g i j -> i g j"), [], [r_wsn])
            wsT = A.alloc([8, 128], BF16)
            r_wsT = S.res()
            bk, r_bk = B.bank()
            bkb = bk.bitcast(BF16)
            for g in range(8):
                B.tr(bkb[:, g * 128:(g + 1) * 128], wsn[:, g, :], [r_wsn], [r_bk])
            B.cp("act", wsT, bkb.rearrange("p (g t) -> p g t", g=8), [], [r_bk, r_wsT])
            bias_sb = A.alloc([128], F32, parts=8)
            r_bsb = S.res()
            B.dma("sp", bias_sb, I["b_bias"][:, :], [], [r_bsb])
            biasT = A.alloc([8], F32)
            r_biasT = S.res()
            bk, r_bk = B.bank()
            B.mm(bk[:, 0:8], bias_sb, B.identf[0:8, 0:8], True, True, [r_bsb, B.r_ident], [r_bk])
            B.cp("dve", biasT, bk[:, 0:8], [], [r_bk, r_biasT])
            esink, r_es = B.load_bc("sp", I["a_sink"][0:1, :], 8)
            B.act(esink, esink, AF.Exp, [r_es], [r_es])
            amask = A.alloc([2, 128], BF16)
            r_am = S.res()
            B.dma("pool", amask, I["k_amask"][:, :, :], [], [r_am])
            kTc = A.alloc([2, 256], BF16, parts=64)
            r_kTc = S.res()
            B.dma("sp", kTc, kT_d.rearrange("h d t -> d h t")[:, 0:2, NTL * 128:NT * 128], [], [r_kTc])
            Vc = A.alloc([2, 2, 65], BF16)
            r_Vc = S.res()
            B.memset("dve", Vc[:, :, :, 64:65], 1.0, [r_Vc])
            load_V(Vc, r_Vc, NTL, 2, 2)
            kTwr = Ring(B, 4, [2, 384], BF16, parts=64)
            Vwr = Ring(B, 5, [3, 2, 65], BF16)
            for vb_, r_ in Vwr.bufs:
                B.memset("dve", vb_[:, :, :, 64:65], 1.0, [r_])
            qTr = Ring(B, 4, [8, 128], BF16, parts=64)
            pTr = Ring(B, 16, [512], BF16)
            etr = Ring(B, 2, [512], BF16)
            mixr = Ring(B, 4, [D], BF16)
            str_ = Ring(B, 4, [8], F32)
            vvr = Ring(B, 5, [512], BF16)
            ur = Ring(B, 5, [512], F32)
            btr = Ring(B, 2, [512], F32)
            rings = {"mixT": Ring(B, 2, [8, 128], BF16), "hin": Ring(B, 2, [D], F32), "et": Ring(B, 1, [D], F32),
                     "hout": Ring(B, 2, [D], F32)}

            def tile_gen(n):
                bg_tick()
                ty = 0 if n < NTL else 1
                qT, r_qT = qTr.next()
                B.dma("sp", qT, qT_d.rearrange("h d t -> d h t")[:, :, n * 128:(n + 1) * 128], [], [r_qT])
                keys = []
                if ty == 0:
                    kt0 = min(max(n - 1, 0), NTL - 3)
                    kTw, r_kTw = kTwr.next()
                    B.dma("sp", kTw, kT_d.rearrange("h d t -> d h t")[:, 0:2, kt0 * 128:(kt0 + 3) * 128], [], [r_kTw])
                    Vw, r_Vw = Vwr.next()
                    load_V(Vw, r_Vw, kt0, 3, 2)
                    for kt, mk in ((n - 1, 0), (n, None), (n + 1, 1)):
                        if 0 <= kt < NTL:
                            s_ = kt - kt0
                            keys.append((kTw, s_, Vw, s_, mk, [r_kTw], [r_Vw]))
                for s_ in range(2):
                    keys.append((kTc, s_, Vc, s_, None, [r_kTc], [r_Vc]))
                vv, r_vv = vvr.next()
                B.dma("sp", vv, vv_d[n * 128:(n + 1) * 128, :], [], [r_vv])
                u_, r_u = ur.next()
                B.dma("sp", u_, u_d[n * 128:(n + 1) * 128, :], [], [r_u])
                yield

                def qk(kv):
                    pts = []
                    for (kTa, ks, Va, vs, mk, rk, rv) in keys:
                        bk, r_bk = B.bank()
                        B.mm(bk.rearrange("p (h q) -> p h q", h=4), kTa[:, kv, ks * 128:(ks + 1) * 128], qT[:, 4 * kv:4 * kv + 4, :],
                             True, True, rk + [r_qT], [r_bk])
                        pT, r_pT = pTr.next()
                        if mk is None:
                            B.act(pT, bk, AF.Exp, [], [r_bk, r_pT], scale=0.125)
                        else:
                            et, r_et = etr.next()
                            B.act(et, bk, AF.Exp, [], [r_bk, r_et], scale=0.125)
                            B.tt("dve", pT.rearrange("p (h q) -> p h q", h=4), et.rearrange("p (h q) -> p h q", h=4),
                                 amask[:, mk, :].unsqueeze(1).to_broadcast([128, 4, 128]), ALU.mult, [r_et, r_am], [r_pT])
                        pts.append((pT, r_pT, Va, vs, rv))
                    return pts

                def pv(pkv, pts, mix, r_mix):
                    ob, r_ob = B.bank()
                    for hh in range(4):
                        for ei, (pT, r_pT, Va, vs, rv) in enumerate(pts):
                            B.mm(ob[:, hh * 65:(hh + 1) * 65], pT[:, hh * 128:(hh + 1) * 128], Va[:, vs, pkv, :],
                                 ei == 0, ei == len(pts) - 1, [r_pT] + rv, [r_ob])
                    ob3 = ob[:, 0:260].rearrange("p (h e) -> p h e", h=4)
                    sd, r_sd = str_.next()
                    B.tt("dve", sd[:, 0:4], ob3[:, :, 64], esink[:, 4 * pkv:4 * pkv + 4], ALU.add, [r_es], [r_ob, r_sd])
                    B.recip(sd[:, 4:8], sd[:, 0:4], [r_sd], [r_sd])
                    B.tt("dve", mix[:, pkv * 256:(pkv + 1) * 256].rearrange("p (h d) -> p h d", h=4), ob3[:, :, 0:64],
                         sd[:, 4:8].unsqueeze(2).to_broadcast([128, 4, 64]), ALU.mult, [r_sd], [r_ob, r_mix])

                pts0 = qk(0)
                yield
                pts1 = qk(1)
                mix, r_mix = mixr.next()
                pv(0, pts0, mix, r_mix)
                yield
                pv(1, pts1, mix, r_mix)
                bk, r_bk = B.bank()
                for g in range(8):
                    B.mm(bk[:, g * 64:(g + 1) * 64], wsT[:, g, :], vv[:, g * 64:(g + 1) * 64], True, True, [r_wsT, r_vv], [r_bk])
                bt, r_bt = btr.next()
                B.tt("dve", bt.rearrange("p (g d) -> p g d", g=8), bk.rearrange("p (g d) -> p g d", g=8),
                     biasT.unsqueeze(2).to_broadcast([128, 8, 64]), ALU.add, [r_biasT], [r_bk, r_bt])
                B.tt("pool", mix[:, 512:1024], bt, u_, ALU.mult, [r_bt, r_u], [r_mix])
                yield
                outproj_residual(mix, r_mix, wout, r_wout, mvs[ty][2], mvs[ty][3], n, rings)

            pipeline(tile_gen, range(NT))

        def phase_odd_prep(li):
            B.phase_begin()
            mvs = load_mod_vecs(li, 3, 4, None, 1, 1.0)
            win = A.alloc([8, 2048], BF16)
            r_win = S.res()
            load_w_bf16("od_w_in", win, r_win, I["od_w_in"], 2048)
            qg_bc, r_qg = B.load_bc("sp", I["d_q_gain"][0:1, :], 64)
            kg_bc, r_kg = B.load_bc("sp", I["d_k_gain"][0:1, :], 64)
            hinr = Ring(B, 6, [D], F32)
            rings = {"sqj": (A.alloc([D], BF16), S.res()), "st": Ring(B, 24, [30], F32),
                     "z1": Ring(B, 2, [D], F32), "ztok": Ring(B, 3, [D], BF16),
                     "sq": Ring(B, 6, [512], F32),
                     "hT": Ring(B, 3, [8, 128], BF16, parts=64)}
            zTr = Ring(B, 3, [8, 128], BF16)
            qrr = Ring(B, 7, [512], F32)
            krr = Ring(B, 7, [512], F32)
            qbr = Ring(B, 8, [512], BF16)
            kbr = Ring(B, 8, [512], BF16)
            vbr = Ring(B, 4, [512], BF16)
            xbr = Ring(B, 4, [512], BF16)

            def tile_gen(gt):
                bg_tick()
                ty = 0 if gt < NTL else 1
                hin, r_hin = hinr.next()
                B.dma("sp", hin, hsrc(gt), [], [r_hin])
                yield
                zt, r_zt = yield from norm_tile_g(hin, r_hin, mvs[ty], rings)
                yield
                zT, r_zT = zTr.next()
                transpose8(zt, r_zt, zT, r_zT)
                nbs = [0, 1, 2, 3] if ty == 0 else [2, 3]
                bks = {i: B.bank() for i in nbs}
                for k in range(8):
                    for i in nbs:
                        B.mm(bks[i][0], zT[:, k, :], win[:, k, i * 512:(i + 1) * 512], k == 0, k == 7, [r_zT, r_win], [bks[i][1]])
                yield
                items = []
                qb = r_qb = None
                if ty == 0:
                    xb, r_xb = xbr.next()
                    B.cp("act", xb, bks[0][0], [], [bks[0][1], r_xb])
                    qr, r_qr = qrr.next()
                    B.cp("act", qr, bks[1][0], [], [bks[1][1], r_qr])
                    qb, r_qb = qbr.next()
                    items.append(dict(src=qr, r_src=r_qr, nh=8, gain=qg_bc, r_gain=r_qg, dst=qb, r_dst=r_qb))
                kr, r_kr = krr.next()
                B.cp("act", kr, bks[2][0], [], [bks[2][1], r_kr])
                kb, r_kb = kbr.next()
                items.append(dict(src=kr, r_src=r_kr, nh=8, gain=kg_bc, r_gain=r_kg, dst=kb, r_dst=r_kb))
                vb, r_vb = vbr.next()
                B.cp("act", vb, bks[3][0], [], [bks[3][1], r_vb])
                yield
                if ty == 0:
                    B.dma("sp", xp_d[gt * 128:(gt + 1) * 128, :], xb, [r_xb], [])
                B.dma("sp", v_d[gt * 128:(gt + 1) * 128, :], vb, [r_vb], [])
                yield from chains_g(items, rings)
                yield
                if ty == 0:
                    head_transposes(qb, r_qb, 8, qT_d, gt, rings)
                head_transposes(kb, r_kb, 8, kT_d, gt, rings)

            pipeline(tile_gen, range(NT))

        def phase_odd_attn(li):
            B.phase_begin()
            mvs = load_mod_vecs(li, None, None, 5, 1, 1.0, ntypes=1)
            wout = A.alloc([8, D], BF16)
            r_wout = S.res()
            load_w_bf16("od_w_out", wout, r_wout, I["od_w_out"], 1024)
            wpool = A.alloc([4, 128], BF16)
            r_wpool = S.res()
            B.dma("pool", wpool, I["c_w_pool"].rearrange("g c d -> c g d"), [], [r_wpool])
            csc, r_csc = B.load_bc("sp", I["c_scale"][0:1, :], 512)
            band = A.alloc([4, 5, 128], BF16)
            r_band = S.res()
            B.dma("pool", band, I["k_band"][:, :, :, :], [], [r_band], max_dma_last_dim=2048)
            kTc = A.alloc([8, 256], BF16, parts=64)
            r_kTc = S.res()
            B.dma("sp", kTc, kT_d.rearrange("h d t -> d h t")[:, :, NTL * 128:NT * 128], [], [r_kTc])
            Vc = A.alloc([2, 8, 65], BF16)
            r_Vc = S.res()
            B.memset("dve", Vc[:, :, :, 64:65], 1.0, [r_Vc])
            load_V(Vc, r_Vc, NTL, 2, 8)
            biasr = Ring(B, 1, [8, 7, 128], F32)
            for bb_, r_ in biasr.bufs:
                B.memset("pool", bb_[:, :, 5:7, :], 0.0, [r_])
            kTwr = Ring(B, 4, [8, 640], BF16, parts=64)
            Vwr = Ring(B, 4, [5, 8, 65], BF16)
            for vb_, r_ in Vwr.bufs:
                B.memset("dve", vb_[:, :, :, 64:65], 1.0, [r_])
            qTr = Ring(B, 4, [8, 128], BF16, parts=64)
            xpr = Ring(B, 3, [3, 512], BF16)
            ppr = Ring(B, 2, [4, 128], BF16)
            tAr = Ring(B, 2, [512], F32)
            tBr = Ring(B, 2, [384], F32)
            pAr = Ring(B, 3, [512], BF16)
            pBr = Ring(B, 3, [384], BF16)
            mixr = Ring(B, 5, [D], BF16)
            str_ = Ring(B, 4, [8], F32)
            rings = {"mixT": Ring(B, 2, [8, 128], BF16), "hin": Ring(B, 2, [D], F32), "et": Ring(B, 1, [D], F32),
                     "hout": Ring(B, 2, [D], F32)}
            state = {"case": None, "bias": None, "r_bias": None}
            B.nrr = 6

            def case_of(n):
                return 0 if n == 0 else 1 if n == 1 else 3 if n == NTL - 2 else 4 if n == NTL - 1 else 2

            def tile_gen(n):
                bg_tick()
                case = case_of(n)
                if case != state["case"]:
                    bias_, r_bias_ = biasr.next()
                    B.dma("sp", bias_[:, :, 0:5, :], I["k_dbias"][case], [], [r_bias_])
                    state["case"] = case
                    state["bias"] = bias_
                    state["r_bias"] = r_bias_
                bias = state["bias"]
                r_bias = state["r_bias"]
                kt0 = min(max(n - 2, 0), NTL - 5)
                kTw, r_kTw = kTwr.next()
                B.dma("sp", kTw, kT_d.rearrange("h d t -> d h t")[:, :, kt0 * 128:(kt0 + 5) * 128], [], [r_kTw])
                Vw, r_Vw = Vwr.next()
                load_V(Vw, r_Vw, kt0, 5, 8)
                qT, r_qT = qTr.next()
                B.dma("sp", qT, qT_d.rearrange("h d t -> d h t")[:, :, n * 128:(n + 1) * 128], [], [r_qT])
                xw, r_xw = xpr.next()
                jts = [j for j in (n - 1, n, n + 1) if 0 <= j < NTL]
                j0 = jts[0]
                B.dma("sp", xw[:, 0:len(jts), :], xp_d[j0 * 128:(j0 + len(jts)) * 128, :].rearrange("(w p) f -> p w f", p=128), [], [r_xw])
                yield
                mix, r_mix = mixr.next()
                bk, r_bk = B.bank()
                for g in range(4):
                    for ji, j in enumerate(jts):
                        if j == n - 1:
                            typ = 0
                        elif j == n + 1:
                            typ = 2
                        else:
                            typ = 3 if n == 0 else (4 if n == NTL - 1 else 1)
                        B.mm(bk[:, g * 128:(g + 1) * 128], xw[:, ji, g * 128:(g + 1) * 128], band[:, g, typ, :],
                             ji == 0, ji == len(jts) - 1, [r_xw, r_band], [r_bk])
                pp, r_pp = ppr.next()
                B.cp("act", pp, bk.rearrange("p (g t) -> p g t", g=4), [], [r_bk, r_pp])
                bk2, r_bk2 = B.bank()
                for g in range(4):
                    B.mm(bk2[:, g * 128:(g + 1) * 128], pp[:, g, :], wpool[:, g, :], True, True, [r_pp, r_wpool], [r_bk2])
                B.tt("dve", mix[:, 0:512], bk2, csc, ALU.mult, [r_csc], [r_bk2, r_mix])
                yield

                def heads(hq):
                    pend = None
                    ob, r_ob = B.bank_fixed(6 + hq)
                    for hi in range(5):
                        cur = None
                        if hi < 4:
                            h = hq * 4 + hi
                            bA, r_bA = B.bank()
                            bB, r_bB = B.bank()
                            for s_ in range(4):
                                B.mm(bA[:, s_ * 128:(s_ + 1) * 128], kTw[:, h, s_ * 128:(s_ + 1) * 128], qT[:, h, :], True, True, [r_kTw, r_qT], [r_bA])
                            B.mm(bB[:, 0:128], kTw[:, h, 512:640], qT[:, h, :], True, True, [r_kTw, r_qT], [r_bB])
                            for s_ in range(2):
                                B.mm(bB[:, (1 + s_) * 128:(2 + s_) * 128], kTc[:, h, s_ * 128:(s_ + 1) * 128], qT[:, h, :], True, True, [r_kTc, r_qT], [r_bB])
                            tA, r_tA = tAr.next()
                            tB, r_tB = tBr.next()
                            B.stt("dve", tA, bA, 0.125, bias[:, h, 0:4, :].rearrange("p s q -> p (s q)"), ALU.mult, ALU.add, [r_bias], [r_bA, r_tA])
                            B.stt("dve", tB, bB[:, 0:384], 0.125, bias[:, h, 4:7, :].rearrange("p s q -> p (s q)"), ALU.mult, ALU.add, [r_bias], [r_bB, r_tB])
                            pA, r_pA = pAr.next()
                            pB, r_pB = pBr.next()
                            B.act(pA, tA, AF.Exp, [r_tA], [r_pA])
                            B.act(pB, tB, AF.Exp, [r_tB], [r_pB])
                            cur = (h, pA, r_pA, pB, r_pB)
                        if pend is not None:
                            ph, pA, r_pA, pB, r_pB = pend
                            osl = ob[:, (ph % 4) * 65:(ph % 4 + 1) * 65]
                            for s_ in range(7):
                                if s_ < 4:
                                    lhs = pA[:, s_ * 128:(s_ + 1) * 128]
                                    rp = r_pA
                                else:
                                    lhs = pB[:, (s_ - 4) * 128:(s_ - 3) * 128]
                                    rp = r_pB
                                if s_ < 5:
                                    rhs = Vw[:, s_, ph, :]
                                    rv = r_Vw
                                else:
                                    rhs = Vc[:, s_ - 5, ph, :]
                                    rv = r_Vc
                                B.mm(osl, lhs, rhs, s_ == 0, s_ == 6, [rp, rv], [r_ob])
                        pend = cur
                    ob3 = ob[:, 0:260].rearrange("p (h e) -> p h e", h=4)
                    sd, r_sd = str_.next()
                    B.recip(sd[:, 0:4], ob3[:, :, 64], [], [r_ob, r_sd])
                    B.tt("dve", mix[:, 512 + hq * 256:512 + (hq + 1) * 256].rearrange("p (h d) -> p h d", h=4), ob3[:, :, 0:64],
                         sd[:, 0:4].unsqueeze(2).to_broadcast([128, 4, 64]), ALU.mult, [r_sd], [r_ob, r_mix])

                heads(0)
                yield
                heads(1)
                yield
                outproj_residual(mix, r_mix, wout, r_wout, mvs[0][2], mvs[0][3], n, rings)

            pipeline(tile_gen, range(NTL), drain_before=lambda n: case_of(n) != state["case"])
            B.nrr = 8

        plist = [
            lambda: phase_mod(0, 0, 6),
            lambda: phase_ffn(0, 0, NT, src0, hsrc),
            lambda: phase_mod(0, 6, 18),
            lambda: phase_even_prep(0),
            lambda: phase_even_attn(0),
            lambda: phase_ffn(0, 1, NT, hsrc, hsrc),
            lambda: phase_mod(1),
            lambda: phase_ffn(1, 0, NT, hsrc, hsrc),
            lambda: phase_odd_prep(1),
            lambda: phase_odd_attn(1),
            lambda: phase_ffn(1, 1, NTL, hsrc, osrc),
        ]
        if dbg_phases is not None:
            plist = plist[:dbg_phases]
        for p in plist:
            p()
        S.emit()
        build_program.stats = S.stats
    return nc


def _rope_table():
    t = np.arange(S_LAT)
    row = (t // 64).astype(np.float32)
    col = (t % 64).astype(np.float32)
    m = 16
    inv = (1.0 / (10000.0 ** (np.arange(m, dtype=np.float32) / m))).astype(np.float32)
    ar = row[:, None] * inv[None, :]
    ac = col[:, None] * inv[None, :]
    cos = np.concatenate([np.cos(ar), np.cos(ar), np.cos(ac), np.cos(ac)], axis=1)
    sin = np.concatenate([-np.sin(ar), np.sin(ar), -np.sin(ac), np.sin(ac)], axis=1)
    return np.stack([cos, sin], axis=1).astype(np.float32)


def _amask():
    pj = np.arange(128)[:, None]
    pi = np.arange(128)[None, :]
    prev = (pj >= pi).astype(np.float32)
    nxt = (pj <= pi).astype(np.float32)
    return np.stack([prev, nxt], axis=1)


def _band():
    out = np.zeros((128, 4, 5, 128), np.float32)
    for gi, w in enumerate((2, 4, 8, 16)):
        def mat(n, jn):
            tg = n * 128 + np.arange(128)
            lo = np.clip(tg - w // 2, 0, S_LAT)
            hi = np.clip(tg + w - w // 2, 0, S_LAT)
            cnt = (hi - lo).astype(np.float32)
            jg = jn * 128 + np.arange(128)
            m = ((jg[:, None] >= lo[None, :]) & (jg[:, None] < hi[None, :])).astype(np.float32) / cnt[None, :]
            m = m - (jg[:, None] == tg[None, :]).astype(np.float32)
            return m
        out[:, gi, 0] = mat(5, 4)
        out[:, gi, 1] = mat(5, 5)
        out[:, gi, 2] = mat(5, 6)
        out[:, gi, 3] = mat(0, 0)
        out[:, gi, 4] = mat(NTL - 1, NTL - 1)
    return out


def _dbias(rpb):
    out = np.full((5, 128, 8, 5, 128), NEGB, np.float32)
    for case, n in enumerate((0, 1, 5, NTL - 2, NTL - 1)):
        kt0 = min(max(n - 2, 0), NTL - 5)
        i = np.arange(128)
        r = 2 * n + i // 64
        c = i % 64
        r0 = np.clip(r - 4, 0, 56)
        q0 = np.clip(c - 8, 0, 48)
        for s in range(5):
            kt = kt0 + s
            j = np.arange(128)
            kr = 2 * kt + j // 64
            kc = j % 64
            valid = ((kr[:, None] >= r0[None, :]) & (kr[:, None] < r0[None, :] + 8) &
                     (kc[:, None] >= q0[None, :]) & (kc[:, None] < q0[None, :] + 16))
            ri = np.clip(kr[:, None] - r[None, :] + 7, 0, 14)
            ci = np.clip(kc[:, None] - c[None, :] + 15, 0, 30)
            g = rpb[:, ri, ci]
            g = np.where(valid[None], g, np.float32(NEGB))
            out[case, :, :, s, :] = np.transpose(g, (1, 0, 2))
    return out


_CACHE = {}


def kernel(x, c, ctx, c_ctx, ada_w, ada_b, norm_g, ffn_w_gu, ffn_w_down,
           ev_w_in, ev_w_out, a_q_gain, a_k_gain, a_sink, b_v_gain, b_ws, b_bias,
           od_w_in, od_w_out, c_w_pool, c_scale, d_q_gain, d_k_gain, d_rpb, _dbg_phases=None, _dbg=False):
    f = lambda a: np.ascontiguousarray(np.asarray(a, dtype=np.float32))
    key = (_dbg_phases, _dbg)
    if key not in _CACHE:
        _CACHE[key] = build_program(_dbg_phases, _dbg)
    nc = _CACHE[key]
    shared = {
        "c_ctx": f(c_ctx).reshape(1, D), "ada_w": f(ada_w), "ada_b": f(ada_b), "norm_g": f(norm_g),
        "ffn_w_gu": f(ffn_w_gu), "ffn_w_down": f(ffn_w_down),
        "ev_w_in": f(ev_w_in)[0], "ev_w_out": f(ev_w_out)[0],
        "a_q_gain": f(a_q_gain), "a_k_gain": f(a_k_gain), "a_sink": f(a_sink),
        "b_v_gain": f(b_v_gain), "b_ws": f(b_ws)[0], "b_bias": f(b_bias)[0],
        "od_w_in": f(od_w_in)[0], "od_w_out": f(od_w_out)[0],
        "c_w_pool": f(c_w_pool)[0], "c_scale": f(c_scale),
        "d_q_gain": f(d_q_gain), "d_k_gain": f(d_k_gain),
        "k_ident": np.eye(128, dtype=np.float32), "k_rope": _rope_table(), "k_amask": _amask(),
        "k_band": _band(), "k_dbias": _dbias(f(d_rpb)[0]),
    }
    x = f(x); c = f(c); ctx = f(ctx)
    in_maps = []
    for b in range(8):
        m = dict(shared)
        m["x"] = x[b]
        m["ctx"] = ctx[b]
        m["c"] = c[b].reshape(1, D)
        in_maps.append(m)
    res = run_bass_kernel_spmd(nc, in_maps, core_ids=list(range(8)))
    kernel.last = res
    return np.stack([r["out"] for r in res.results], axis=0)
```

```python
import contextlib
import numpy as np
import concourse.bass as bass
import concourse.mybir as mybir
from concourse.bass_utils import run_bass_kernel_spmd

F32 = mybir.dt.float32
BF16 = mybir.dt.bfloat16
AF = mybir.ActivationFunctionType
ALU = mybir.AluOpType
AX = mybir.AxisListType

D = 1024
S_LAT = 4096
S_CTX = 256
NTL = 32
NT = 34
DFF = 2816
NFF = 22
EPS = 1e-6
NEGB = -30000.0

ENGS = ("sp", "act", "pool", "dve", "pe")
NDMA_SEM = 8


class Res:
    __slots__ = ("name", "last_w", "readers", "gen")

    def __init__(self, name):
        self.name = name
        self.last_w = None
        self.readers = {}
        self.gen = 0

    def bump(self):
        self.gen += 1
        return Ref(self, self.gen)


class Ref:
    __slots__ = ("phys", "gen")

    def __init__(self, phys, gen):
        self.phys = phys
        self.gen = gen


def _norm_res(lst):
    out = []
    for r in lst:
        if isinstance(r, Ref):
            assert r.gen == r.phys.gen, f"stale buffer reference {r.phys.name}"
            r = r.phys
        out.append(r)
    return out


def pipeline(make_gen, items, drain_before=None):
    active = []
    for it in items:
        if drain_before is not None and drain_before(it):
            while active:
                nxt = []
                for g in active:
                    try:
                        next(g)
                        nxt.append(g)
                    except StopIteration:
                        pass
                active = nxt
        nxt = []
        for g in active:
            try:
                next(g)
                nxt.append(g)
            except StopIteration:
                pass
        active = nxt
        g = make_gen(it)
        try:
            next(g)
            active.append(g)
        except StopIteration:
            pass
    while active:
        nxt = []
        for g in active:
            try:
                next(g)
                nxt.append(g)
            except StopIteration:
                pass
        active = nxt


class Op:
    __slots__ = ("eng", "fn", "deps", "dma", "signal", "sem", "val", "prewait", "bg")

    def __init__(self, eng, fn, dma):
        self.eng = eng
        self.fn = fn
        self.dma = dma
        self.deps = []
        self.signal = False
        self.sem = None
        self.val = 0
        self.prewait = None
        self.bg = False


class Sched:
    def __init__(self, nc):
        self.nc = nc
        self.ops = {e: [] for e in ENGS}
        self.bar = {}
        self.nres = 0

    def res(self, name=None):
        self.nres += 1
        return Res(name or f"r{self.nres}")

    def op(self, eng, fn, reads=(), writes=(), dma=False):
        reads = _norm_res(reads)
        writes = _norm_res(writes)
        o = Op(eng, fn, dma)
        deps = {}
        for r in reads:
            if r.last_w is not None:
                deps[id(r.last_w)] = r.last_w
        for r in writes:
            if r.last_w is not None:
                deps[id(r.last_w)] = r.last_w
            for rd in r.readers.values():
                if isinstance(rd, list):
                    for x in rd:
                        deps[id(x)] = x
                else:
                    deps[id(rd)] = rd
        for r in reads:
            if dma:
                r.readers.setdefault(("dma", eng), []).append(o)
            else:
                r.readers[eng] = o
        for r in writes:
            r.last_w = o
            r.readers = {}
        b = self.bar.pop(eng, None)
        if b:
            for x in b:
                deps[id(x)] = x
        dl = []
        for d in deps.values():
            if d is o:
                continue
            if (not dma) and (not d.dma) and d.eng == "pe" and eng == "pe":
                continue
            dl.append(d)
        o.deps = dl
        self.ops[eng].append(o)
        return o

    def dma(self, q, out, in_, reads=(), writes=(), **kw):
        return self.op(q, lambda e: e.dma_start(out=out, in_=in_, **kw), reads, writes, dma=True)

    def barrier(self):
        tails = []
        for e in ENGS:
            ops = self.ops[e]
            for o in reversed(ops):
                if not o.dma:
                    tails.append(o)
                    break
            nd = sum(1 for o in ops if o.dma)
            seen_slots = set()
            idx = nd
            for o in reversed(ops):
                if not o.dma:
                    continue
                idx -= 1
                slot = idx % NDMA_SEM
                if slot in seen_slots or o.bg:
                    continue
                seen_slots.add(slot)
                tails.append(o)
                if len(seen_slots) >= NDMA_SEM:
                    break
        self.bar = {e: list(tails) for e in ENGS}

    def emit(self):
        nc = self.nc
        with contextlib.ExitStack() as st:
            csem = {e: st.enter_context(nc.semaphore(f"c_{e}")) for e in ENGS}
            dsem = {e: [st.enter_context(nc.semaphore(f"d_{e}{i}")) for i in range(NDMA_SEM)]
                    for e in ("sp", "act", "pool")}
            for e in ENGS:
                for o in self.ops[e]:
                    for d in o.deps:
                        d.signal = True
            self.stats = {}
            for e in ENGS:
                cnt = 0
                nd = 0
                for o in self.ops[e]:
                    if o.dma:
                        slot = nd % NDMA_SEM
                        o.sem = dsem[e][slot]
                        o.val = 16 * (nd // NDMA_SEM + 1)
                        if nd >= NDMA_SEM:
                            o.prewait = (dsem[e][slot], 16 * (nd // NDMA_SEM))
                        nd += 1
                    elif o.signal:
                        cnt += 1
                        o.sem = csem[e]
                        o.val = cnt
                self.stats[e] = (len(self.ops[e]), cnt, nd)
            block = st.enter_context(nc.Block())

            def run(eng_name, eng):
                seen = {}
                lastdma = {}
                for o in self.ops[eng_name]:
                    waits = []
                    if o.prewait is not None:
                        waits.append(o.prewait)
                    for d in o.deps:
                        waits.append((d.sem, d.val))
                    for sem, val in waits:
                        k = id(sem)
                        if seen.get(k, 0) >= val:
                            continue
                        seen[k] = val
                        eng.wait_ge(sem, val)
                    inst = o.fn(eng)
                    if o.dma:
                        inst.then_inc(o.sem, 16)
                        lastdma[id(o.sem)] = (o.sem, o.val)
                    elif o.signal:
                        inst.then_inc(o.sem, 1)
                for sem, val in lastdma.values():
                    if seen.get(id(sem), 0) < val:
                        eng.wait_ge(sem, val)

            @block.sync
            def _(e):
                run("sp", e)

            @block.scalar
            def _(e):
                run("act", e)

            @block.gpsimd
            def _(e):
                run("pool", e)

            @block.vector
            def _(e):
                run("dve", e)

            @block.tensor
            def _(e):
                run("pe", e)


class Arena:
    def __init__(self, nc, st, nbytes):
        self.nbytes = nbytes
        self.t = st.enter_context(nc.sbuf_tensor("arena", [128, nbytes // 2], BF16))
        self.off = 0
        self.base = 0

    def alloc(self, free, dtype, parts=128):
        free = list(free)
        n = int(np.prod(free))
        sz = n * (4 if dtype == F32 else 2)
        off = (self.off + 63) // 64 * 64
        assert off + sz <= self.nbytes, f"arena overflow {off + sz} > {self.nbytes}"
        self.off = off + sz
        ap = self.t[0:parts, off // 2: (off + sz) // 2]
        if dtype == F32:
            ap = ap.bitcast(F32)
        if len(free) == 2:
            ap = ap.rearrange("p (a b) -> p a b", a=free[0])
        elif len(free) == 3:
            ap = ap.rearrange("p (a b c) -> p a b c", a=free[0], b=free[1])
        elif len(free) == 4:
            ap = ap.rearrange("p (a b c d) -> p a b c d", a=free[0], b=free[1], c=free[2])
        return ap

    def mark_persistent(self):
        self.base = self.off

    def reset(self):
        self.off = self.base


class Ring:
    def __init__(self, B, n, free, dtype, parts=128):
        self.bufs = [(B.A.alloc(free, dtype, parts), B.S.res()) for _ in range(n)]
        self.i = 0

    def next(self):
        r = self.bufs[self.i % len(self.bufs)]
        self.i += 1
        return r[0], r[1].bump()


class Builder:
    def __init__(self, nc, st, dbg):
        self.nc = nc
        self.st = st
        self.S = Sched(nc)
        self.A = Arena(nc, st, 206 * 1024)
        self.banks = []
        for i in range(8):
            t = st.enter_context(nc.psum_tensor(f"bank{i}", [128, 512], F32))
            self.banks.append((t, self.S.res(f"bank{i}")))
        self.bi = 0
        self.nrr = 8
        self.dbg = dbg

    def bank(self):
        r = self.banks[self.bi % self.nrr]
        self.bi += 1
        return r[0][:], r[1].bump()

    def bank_fixed(self, idx):
        r = self.banks[idx]
        return r[0][:], r[1].bump()

    def dma(self, q, out, in_, reads=(), writes=(), **kw):
        return self.S.dma(q, out, in_, reads, writes, **kw)

    def mm(self, out, lhsT, rhs, start, stop, reads, writes):
        return self.S.op("pe", lambda e: e.matmul(out, lhsT=lhsT, rhs=rhs, start=start, stop=stop), reads, writes)

    def tr(self, out, in_, reads, writes):
        idn = self.ident
        return self.S.op("pe", lambda e: e.transpose(out, in_, idn), list(reads) + [self.r_ident], writes)

    def act(self, out, in_, func, reads, writes, scale=None, bias=None, accum=None):
        kw = {}
        if scale is not None:
            kw["scale"] = scale
        if bias is not None:
            kw["bias"] = bias
        if accum is not None:
            kw["accum_out"] = accum
        return self.S.op("act", lambda e: e.activation(out=out, in_=in_, func=func, **kw), reads, writes)

    def tt(self, eng, out, in0, in1, op, reads, writes):
        return self.S.op(eng, lambda e: e.tensor_tensor(out=out, in0=in0, in1=in1, op=op), reads, writes)

    def ts(self, eng, out, in0, s1, s2, op0, op1, reads, writes):
        if op1 is None:
            return self.S.op(eng, lambda e: e.tensor_scalar(out=out, in0=in0, scalar1=s1, scalar2=None, op0=op0), reads, writes)
        return self.S.op(eng, lambda e: e.tensor_scalar(out=out, in0=in0, scalar1=s1, scalar2=s2, op0=op0, op1=op1), reads, writes)

    def stt(self, eng, out, in0, scalar, in1, op0, op1, reads, writes):
        return self.S.op(eng, lambda e: e.scalar_tensor_tensor(out=out, in0=in0, scalar=scalar, in1=in1, op0=op0, op1=op1), reads, writes)

    def cp(self, eng, out, in_, reads, writes):
        if eng == "act":
            return self.S.op("act", lambda e: e.activation(out=out, in_=in_, func=AF.Copy), reads, writes)
        return self.S.op(eng, lambda e: e.tensor_copy(out=out, in_=in_), reads, writes)

    def recip(self, out, in_, reads, writes):
        return self.S.op("dve", lambda e: e.reciprocal(out=out, in_=in_), reads, writes)

    def memset(self, eng, out, val, writes):
        return self.S.op(eng, lambda e: e.memset(out, val), [], writes)

    def reduce_sum(self, out, in_, reads, writes):
        return self.S.op("dve", lambda e: e.tensor_reduce(out=out, in_=in_, axis=AX.X, op=ALU.add), reads, writes)

    def rstd(self, st, r_st, w, inv_n):
        self.ts("dve", st[:, w:2 * w], st[:, 0:w], inv_n, EPS, ALU.mult, ALU.add, [r_st], [r_st])
        self.act(st[:, w:2 * w], st[:, w:2 * w], AF.Sqrt, [r_st], [r_st])
        self.recip(st[:, 2 * w:3 * w], st[:, w:2 * w], [r_st], [r_st])
        return st[:, 2 * w:3 * w]

    def phase_begin(self):
        self.S.barrier()
        self.A.reset()

    def load_bc(self, q, src_1xn, n, name=None):
        t = self.A.alloc([n], F32)
        r = self.S.res(name)
        self.dma(q, t, src_1xn.partition_broadcast(128), [], [r])
        return t, r


def build_program(dbg_phases=None, dbg=False):
    nc = bass.Bass("TRN2", target_bir_lowering=False)
    I = {}

    def inp(name, shape, dt=F32):
        I[name] = nc.dram_tensor(name, list(shape), dt, kind="ExternalInput").ap()
        return I[name]

    inp("x", [S_LAT, D]); inp("ctx", [S_CTX, D]); inp("c", [1, D]); inp("c_ctx", [1, D])
    inp("ada_w", [2, D, 9 * D]); inp("ada_b", [2, 9 * D]); inp("norm_g", [2, 3, D])
    inp("ffn_w_gu", [2, 2, D, 2 * DFF]); inp("ffn_w_down", [2, 2, DFF, D])
    inp("ev_w_in", [D, 1792]); inp("ev_w_out", [D, D])
    inp("a_q_gain", [1, 64]); inp("a_k_gain", [1, 64]); inp("a_sink", [1, 8])
    inp("b_v_gain", [1, 512]); inp("b_ws", [8, 128, 128]); inp("b_bias", [8, 128])
    inp("od_w_in", [D, 2048]); inp("od_w_out", [D, D])
    inp("c_w_pool", [4, 128, 128]); inp("c_scale", [1, 512])
    inp("d_q_gain", [1, 64]); inp("d_k_gain", [1, 64])
    inp("k_ident", [128, 128]); inp("k_rope", [S_LAT, 2, 64]); inp("k_amask", [128, 2, 128])
    inp("k_band", [128, 4, 5, 128]); inp("k_dbias", [5, 128, 8, 5, 128])
    out = nc.dram_tensor("out", [S_LAT, D], F32, kind="ExternalOutput").ap()
    skind = "ExternalOutput" if dbg else "Internal"

    def scr(name, shape, dt):
        return nc.dram_tensor(name, list(shape), dt, kind=skind).ap()

    hA = scr("hA", [NT * 128, D], F32)
    mod_d = scr("mod_d", [2, 2, 9 * D], F32)
    qT_d = scr("qT_d", [8, 64, NT * 128], BF16)
    kT_d = scr("kT_d", [8, 64, NT * 128], BF16)
    v_d = scr("v_d", [NT * 128, 512], BF16)
    u_d = scr("u_d", [NT * 128, 512], F32)
    vv_d = scr("vv_d", [NT * 128, 512], BF16)
    xp_d = scr("xp_d", [S_LAT, 512], BF16)
    wgc_all = [nc.dram_tensor(f"wgc_d{f}", [NFF // 2, 128, 8 * 2 * 256], BF16, kind="Internal").ap() for f in range(4)]
    wdc_all = [nc.dram_tensor(f"wdc_d{f}", [128, NFF * D], BF16, kind="Internal").ap() for f in range(4)]
    adac_all = [nc.dram_tensor(f"adac_d{l_}", [18, 128, 8 * 512], BF16, kind="Internal").ap() for l_ in range(2)]

    st = contextlib.ExitStack()
    with st:
        B = Builder(nc, st, dbg)
        S, A = B.S, B.A
        B.ident = A.alloc([128], BF16)
        B.r_ident = S.res("ident")
        B.dma("pool", B.ident, I["k_ident"][:, :], [], [B.r_ident])
        B.identf = A.alloc([128], F32)
        B.dma("sp", B.identf, I["k_ident"][:, :], [], [B.r_ident])
        A.mark_persistent()

        bgq = []
        wres = {}

        def bg_add_ffn(f, li, which):
            wgu_ = I["ffn_w_gu"][li, which]
            for cp_ in range(NFF // 2):
                dst4 = wgc_all[f][cp_].rearrange("p (k g n) -> p k g n", k=8, g=2)
                for g_ in range(2):
                    r = S.res()
                    wres[("gu", f, cp_, g_)] = r
                    bgq.append((dst4[:, :, g_, :],
                                wgu_[:, g_ * DFF + cp_ * 256:g_ * DFF + (cp_ + 1) * 256].rearrange("(k p) n -> p k n", p=128), r))
            wd3 = wdc_all[f].rearrange("p (c n) -> p c n", c=NFF)
            for c4 in range(0, NFF, 2):
                r = S.res()
                wres[("wd", f, c4)] = r
                bgq.append((wd3[:, c4:c4 + 2, :],
                            I["ffn_w_down"][li, which, c4 * 128:(c4 + 2) * 128, :].rearrange("(c p) n -> p c n", p=128), r))

        def bg_add_ada(li, nb0=0, nb1=18):
            for nb in range(nb0, nb1):
                r = S.res()
                wres[("ada", li, nb)] = r
                bgq.append((adac_all[li][nb].rearrange("p (k n) -> p k n", k=8),
                            I["ada_w"][li, :, nb * 512:(nb + 1) * 512].rearrange("(k p) n -> p k n", p=128), r))

        def bg_need(pred):
            last = -1
            for i, (_, _, r) in enumerate(bgq):
                if pred(r):
                    last = i
            if last >= 0:
                bg_tick(last + 1)

        def bg_tick(n=1):
            for _ in range(n):
                if not bgq:
                    return
                dst, src, r = bgq.pop(0)
                o = B.dma("pool", dst, src, [], [r])
                o.bg = True

        wcache = {}

        def bg_add_w(name, src, ncols, split):
            t = nc.dram_tensor(f"wc_{name}", [128, 8 * ncols], BF16, kind="Internal").ap()
            rl = []
            w_ = ncols // split
            for i in range(split):
                r = S.res()
                rl.append(r)
                bgq.append((t.rearrange("p (k n) -> p k n", k=8)[:, :, i * w_:(i + 1) * w_],
                            src[:, i * w_:(i + 1) * w_].rearrange("(k p) n -> p k n", p=128), r))
            wcache[name] = (t, rl)

        def load_w_bf16(name, dst, r_dst, src, ncols):
            if name in wcache:
                t, rl = wcache[name]
                ids = {id(r) for r in rl}
                bg_need(lambda r: id(r) in ids)
                B.dma("sp", dst.rearrange("p k n -> p (k n)"), t, rl, [r_dst])
            else:
                step = 1024 if ncols > 1792 else ncols
                for c0 in range(0, ncols, step):
                    B.dma("pool", dst[:, :, c0:c0 + step], src[:, c0:c0 + step].rearrange("(k p) n -> p k n", p=128), [], [r_dst])

        bg_add_ada(0, 6, 18)
        bg_add_w("ev_w_in", I["ev_w_in"], 1792, 2)
        bg_add_w("ev_w_out", I["ev_w_out"], 1024, 1)
        bg_add_ffn(1, 0, 1)
        bg_add_ada(1)
        bg_add_ffn(2, 1, 0)
        bg_add_w("od_w_in", I["od_w_in"], 2048, 2)
        bg_add_w("od_w_out", I["od_w_out"], 1024, 1)
        bg_add_ffn(3, 1, 1)

        def src0(gt):
            if gt < NTL:
                return I["x"][gt * 128:(gt + 1) * 128, :]
            return I["ctx"][(gt - NTL) * 128:(gt - NTL + 1) * 128, :]

        def hsrc(gt):
            return hA[gt * 128:(gt + 1) * 128, :]

        def osrc(gt):
            return out[gt * 128:(gt + 1) * 128, :]

        phases = []

        def phase_mod(li, nb0=0, nb1=18):
            B.phase_begin()
            mine_ = {id(r) for k_, r in wres.items() if k_[0] == "ada" and k_[1] == li and nb0 <= k_[2] < nb1}
            bg_need(lambda r: id(r) in mine_)
            cc = A.alloc([2, 128], F32, parts=8)
            r_cc = S.res()
            B.dma("sp", cc[:, 0, :], I["c"][0, :].rearrange("(k p) -> k p", p=128), [], [r_cc])
            B.dma("sp", cc[:, 1, :], I["c_ctx"][0, :].rearrange("(k p) -> k p", p=128), [], [r_cc])
            ccb = A.alloc([2, 128], BF16, parts=8)
            r_ccb = S.res()
            B.act(ccb, cc, AF.Silu, [r_cc], [r_ccb])
            cs = A.alloc([8, 2], BF16)
            r_cs = S.res()
            bk, r_bk = B.bank()
            bkb = bk.bitcast(BF16)
            for j in range(2):
                B.S.op("pe", lambda e, j=j: e.transpose(bkb[:, j * 8:(j + 1) * 8], ccb[:, j, :], B.ident[0:8, 0:8]), [r_ccb, B.r_ident], [r_bk])
            B.cp("dve", cs, bkb[:, 0:16].rearrange("p (j k) -> p k j", j=2), [], [r_bk, r_cs])
            adab = A.alloc([9 * D], F32, parts=2)
            r_adab = S.res()
            B.dma("sp", adab, I["ada_b"][li:li + 1, :].partition_broadcast(2), [], [r_adab])
            msb = A.alloc([9 * D], F32, parts=2)
            r_msb = S.res()
            wr = Ring(B, 3, [8, 512], BF16)
            for nb in range(nb0, nb1):
                w, r_w = wr.next()
                if ("ada", li, nb) in wres:
                    B.dma("sp", w.rearrange("p k n -> p (k n)"), adac_all[li][nb], [wres[("ada", li, nb)]], [r_w])
                else:
                    B.dma("pool", w, I["ada_w"][li, :, nb * 512:(nb + 1) * 512].rearrange("(k p) n -> p k n", p=128), [], [r_w])
                bk, r_bk = B.bank()
                for k in range(8):
                    B.mm(bk[0:2, :], cs[:, k, :], w[:, k, :], k == 0, k == 7, [r_cs, r_w], [r_bk])
                B.tt("dve", msb[:, nb * 512:(nb + 1) * 512], bk[0:2, :], adab[:, nb * 512:(nb + 1) * 512], ALU.add,
                     [r_adab], [r_bk, r_msb])
            B.dma("sp", mod_d[li, :, nb0 * 512:nb1 * 512], msb[:, nb0 * 512:nb1 * 512], [r_msb], [])

        def load_mod_vecs(li, j_shift, j_scale, j_gate, gi, gate_mul, ntypes=2):
            outl = []
            gbc, r_g = (None, None)
            if j_scale is not None:
                gbc, r_g = B.load_bc("sp", I["norm_g"][li, gi:gi + 1, :], D)
            for ty in range(ntypes):
                r = S.res()
                sh = Gm = gt_ = None
                if j_shift is not None:
                    sh = A.alloc([D], F32)
                    B.dma("sp", sh, mod_d[li, ty:ty + 1, j_shift * D:(j_shift + 1) * D].partition_broadcast(128), [], [r])
                if j_scale is not None:
                    Gm = A.alloc([D], F32)
                    B.dma("sp", Gm, mod_d[li, ty:ty + 1, j_scale * D:(j_scale + 1) * D].partition_broadcast(128), [], [r])
                    B.stt("dve", Gm, Gm, 1.0, gbc, ALU.add, ALU.mult, [r_g, r], [r])
                if j_gate is not None:
                    gt_ = A.alloc([D], F32)
                    B.dma("sp", gt_, mod_d[li, ty:ty + 1, j_gate * D:(j_gate + 1) * D].partition_broadcast(128), [], [r])
                    if gate_mul != 1.0:
                        B.ts("dve", gt_, gt_, gate_mul, None, ALU.mult, None, [r], [r])
                outl.append((sh, Gm, gt_, r))
            return outl

        def norm_tile(hin, r_hin, mv, rings):
            sh, Gm, _, r_mv = mv
            sqj, r_sqj = rings["sqj"]
            st_, r_st = rings["st"].next()
            B.memset("dve", st_[:, 0:1], 0.0, [r_st])
            B.act(sqj, hin, AF.Square, [r_hin, r_st], [r_sqj, r_st], accum=st_[:, 0:1])
            rs = B.rstd(st_, r_st, 1, 1.0 / D)
            z1, r_z1 = rings["z1"].next()
            B.stt("dve", z1, hin, rs, Gm, ALU.mult, ALU.mult, [r_hin, r_st, r_mv], [r_z1])
            zt, r_zt = rings["ztok"].next()
            B.tt("pool", zt, z1, sh, ALU.add, [r_z1, r_mv], [r_zt])
            return zt, r_zt

        def norm_tile_g(hin, r_hin, mv, rings):
            sh, Gm, _, r_mv = mv
            sqj, r_sqj = rings["sqj"]
            st_, r_st = rings["st"].next()
            B.memset("dve", st_[:, 0:1], 0.0, [r_st])
            B.act(sqj, hin, AF.Square, [r_hin, r_st], [r_sqj, r_st], accum=st_[:, 0:1])
            yield
            B.ts("dve", st_[:, 1:2], st_[:, 0:1], 1.0 / D, EPS, ALU.mult, ALU.add, [r_st], [r_st])
            yield
            B.act(st_[:, 1:2], st_[:, 1:2], AF.Sqrt, [r_st], [r_st])
            yield
            B.recip(st_[:, 2:3], st_[:, 1:2], [r_st], [r_st])
            z1, r_z1 = rings["z1"].next()
            B.stt("dve", z1, hin, st_[:, 2:3], Gm, ALU.mult, ALU.mult, [r_hin, r_st, r_mv], [r_z1])
            yield
            zt, r_zt = rings["ztok"].next()
            B.tt("pool", zt, z1, sh, ALU.add, [r_z1, r_mv], [r_zt])
            return zt, r_zt

        def chains_g(items, rings):
            sts = []
            for it in items:
                w = it["nh"] * 64
                sq, r_sq = rings["sq"].next()
                B.act(sq[:, 0:w], it["src"][:, 0:w], AF.Square, [it["r_src"]], [r_sq])
                it["sq"], it["r_sq"] = sq, r_sq
            yield
            for it in items:
                nh = it["nh"]
                w = nh * 64
                st_, r_st = rings["st"].next()
                B.reduce_sum(st_[:, 0:nh], it["sq"][:, 0:w].rearrange("p (h d) -> p h d", h=nh), [it["r_sq"]], [r_st])
                B.ts("dve", st_[:, nh:2 * nh], st_[:, 0:nh], 1.0 / 64, EPS, ALU.mult, ALU.add, [r_st], [r_st])
                it["st"], it["r_st"] = st_, r_st
            yield
            for it in items:
                nh = it["nh"]
                B.act(it["st"][:, nh:2 * nh], it["st"][:, nh:2 * nh], AF.Sqrt, [it["r_st"]], [it["r_st"]])
            yield
            for it in items:
                nh = it["nh"]
                w = nh * 64
                st_ = it["st"]
                B.recip(st_[:, 2 * nh:3 * nh], st_[:, nh:2 * nh], [it["r_st"]], [it["r_st"]])
                x3 = it["src"][:, 0:w].rearrange("p (h d) -> p h d", h=nh)
                B.tt("dve", x3, x3, st_[:, 2 * nh:3 * nh].unsqueeze(2).to_broadcast([128, nh, 64]), ALU.mult,
                     [it["r_st"], it["r_src"]], [it["r_src"]])
            yield
            for it in items:
                nh = it["nh"]
                w = nh * 64
                if it.get("gain_full"):
                    B.tt("pool", it["dst"], it["src"][:, 0:w], it["gain"], ALU.mult, [it["r_gain"], it["r_src"]], [it["r_dst"]])
                elif it.get("dst") is not None:
                    B.tt("pool", it["dst"].rearrange("p (h d) -> p h d", h=nh), it["src"][:, 0:w].rearrange("p (h d) -> p h d", h=nh),
                         it["gain"].unsqueeze(1).to_broadcast([128, nh, 64]), ALU.mult, [it["r_gain"], it["r_src"]], [it["r_dst"]])
                else:
                    x3 = it["src"][:, 0:w].rearrange("p (h d) -> p h d", h=nh)
                    B.tt("pool", x3, x3, it["gain"].unsqueeze(1).to_broadcast([128, nh, 64]), ALU.mult,
                         [it["r_gain"], it["r_src"]], [it["r_src"]])

        def rope_g(items, ropt, r_ropt, rings):
            tmp = []
            for (qn, r_qn, nh, outb, r_out) in items:
                w = nh * 64
                a_, r_a = rings["ra"].next()
                q3 = qn[:, 0:w].rearrange("p (h d) -> p h d", h=nh)
                a3 = a_[:, 0:w].rearrange("p (h d) -> p h d", h=nh)
                B.tt("pool", a3, q3, ropt[:, 0, :].unsqueeze(1).to_broadcast([128, nh, 64]), ALU.mult, [r_qn, r_ropt], [r_a])
                b_, r_b = rings["rb"].next()
                q5 = qn[:, 0:w].rearrange("p (h a s d) -> p h a s d", h=nh, a=2, s=2)
                b5 = b_[:, 0:w].rearrange("p (h a s d) -> p h a s d", h=nh, a=2, s=2)
                s4 = ropt[:, 1, :].rearrange("p (a s d) -> p a s d", a=2, s=2)
                for ax in range(2):
                    for s_ in range(2):
                        B.tt("dve", b5[:, :, ax, s_, :], q5[:, :, ax, 1 - s_, :],
                             s4[:, ax, s_, :].unsqueeze(1).to_broadcast([128, nh, 16]), ALU.mult, [r_qn, r_ropt], [r_b])
                tmp.append((a_, r_a, b_, r_b))
            yield
            for (qn, r_qn, nh, outb, r_out), (a_, r_a, b_, r_b) in zip(items, tmp):
                w = nh * 64
                B.tt("dve", outb[:, 0:w], a_[:, 0:w], b_[:, 0:w], ALU.add, [r_a, r_b], [r_out])

        def transpose8(src, r_src, dst3, r_dst, eng="act"):
            bk, r_bk = B.bank()
            bkb = bk.bitcast(BF16)
            for k in range(8):
                B.tr(bkb[:, k * 128:(k + 1) * 128], src[:, k * 128:(k + 1) * 128], [r_src], [r_bk])
            B.cp(eng, dst3, bkb.rearrange("p (k t) -> p k t", k=8), [], [r_bk, r_dst])

        def phase_ffn(li, which, ntiles, srcf, dstf):
            B.phase_begin()
            f = li * 2 + which
            cached = ("wd", f, 0) in wres
            if cached:
                mine = {id(r) for k_, r in wres.items() if k_[0] in ("gu", "wd") and k_[1] == f}
                bg_need(lambda r: id(r) in mine)
            wgc_d = wgc_all[f]
            j0 = 0 if which == 0 else 6
            gi = 0 if which == 0 else 2
            mvs = load_mod_vecs(li, j0, j0 + 1, j0 + 2, gi, 0.5, ntypes=2 if ntiles > NTL else 1)
            wd = A.alloc([NFF, D], BF16)
            r_wd = S.res()

            def load_wd(c4):
                if cached:
                    B.dma("sp", wd[:, c4:c4 + 2, :], wdc_all[f].rearrange("p (c n) -> p c n", c=NFF)[:, c4:c4 + 2, :],
                          [wres[("wd", f, c4)]], [r_wd])
                else:
                    B.dma("pool", wd[:, c4:c4 + 2, :],
                          I["ffn_w_down"][li, which, c4 * 128:(c4 + 2) * 128, :].rearrange("(c p) n -> p c n", p=128), [], [r_wd])
            if ntiles == NT:
                groups = [list(range(0, 9)), list(range(9, 18)), list(range(18, 26)), list(range(26, 34))]
            else:
                groups = [list(range(g * 8, g * 8 + 8)) for g in range(4)]
            wc_res = [S.res() for _ in range(NFF // 2)]
            GM = max(len(g) for g in groups)
            zTs = [A.alloc([8, GM * 128], BF16) for _ in range(2)]
            zress = [[S.res() for _ in range(GM)] for _ in range(2)]
            actT = A.alloc([NFF, GM * 128], BF16)
            wgr = Ring(B, 2, [8, 2, 256], BF16)
            hinr = Ring(B, 2, [D], F32)
            rings = {"sqj": (A.alloc([D], BF16), S.res()), "st": Ring(B, 2, [3], F32),
                     "z1": Ring(B, 1, [D], F32), "ztok": Ring(B, 2, [D], BF16)}
            stmpr = Ring(B, 2, [512], F32)
            etmpr = Ring(B, 1, [D], F32)
            houtr = Ring(B, 1, [D], F32)
            wgu = I["ffn_w_gu"][li, which]

            def stage1_tile(gidx, lt, gt):
                ty = 0 if gt < NTL else 1
                hin, r_hin = hinr.next()
                B.dma("sp", hin, srcf(gt), [], [r_hin])
                zt, r_zt = norm_tile(hin, r_hin, mvs[ty], rings)
                transpose8(zt, r_zt, zTs[gidx % 2][:, :, lt * 128:(lt + 1) * 128], zress[gidx % 2][lt])

            for lt, gt in enumerate(groups[0]):
                stage1_tile(0, lt, gt)
            for gidx, grp in enumerate(groups):
                zT = zTs[gidx % 2]
                zres = zress[gidx % 2]
                T = len(grp) * 128
                nblk = (T + 511) // 512
                bs = T // nblk
                blocks = [(i * bs, (i + 1) * bs if i < nblk - 1 else T) for i in range(nblk)]
                ares = [S.res() for _ in blocks]
                pending = list(enumerate(groups[gidx + 1])) if gidx + 1 < len(groups) else []
                npend = len(pending)
                nunits = NFF * nblk
                unit = 0
                emitted = 0
                for cp_ in range(NFF // 2):
                    wg, r_wg = wgr.next()
                    if cached:
                        B.dma("sp", wg.rearrange("p k g n -> p (k g n)"), wgc_d[cp_],
                              [wres[("gu", f, cp_, 0)], wres[("gu", f, cp_, 1)]], [r_wg])
                    elif gidx == 0:
                        B.dma("pool", wg[:, :, 0, :], wgu[:, cp_ * 256:(cp_ + 1) * 256].rearrange("(k p) n -> p k n", p=128), [], [r_wg])
                        B.dma("pool", wg[:, :, 1, :], wgu[:, DFF + cp_ * 256:DFF + (cp_ + 1) * 256].rearrange("(k p) n -> p k n", p=128), [], [r_wg])
                        B.dma("pool", wgc_d[cp_], wg.rearrange("p k g n -> p (k g n)"), [r_wg], [wc_res[cp_]])
                    else:
                        B.dma("sp", wg.rearrange("p k g n -> p (k g n)"), wgc_d[cp_], [wc_res[cp_]], [r_wg])
                    if gidx == 0:
                        load_wd(cp_ * 2)
                    for ci in range(2):
                        c = cp_ * 2 + ci
                        for bi_, (a, b_) in enumerate(blocks):
                            n = b_ - a
                            zr = [zres[t] for t in range(a // 128, (b_ - 1) // 128 + 1)]
                            bg, r_bg = B.bank()
                            bu, r_bu = B.bank()
                            for k in range(8):
                                B.mm(bg[:, 0:n], wg[:, k, 0, ci * 128:(ci + 1) * 128], zT[:, k, a:b_], k == 0, k == 7, [r_wg] + zr, [r_bg])
                            for k in range(8):
                                B.mm(bu[:, 0:n], wg[:, k, 1, ci * 128:(ci + 1) * 128], zT[:, k, a:b_], k == 0, k == 7, [r_wg] + zr, [r_bu])
                            stp, r_stp = stmpr.next()
                            B.act(stp[:, 0:n], bg[:, 0:n], AF.Silu, [], [r_bg, r_stp])
                            B.tt("dve", actT[:, c, a:b_], stp[:, 0:n], bu[:, 0:n], ALU.mult, [r_stp], [r_bu, ares[bi_]])
                            unit += 1
                            if unit % 4 == 0 and (cached or gidx > 0):
                                bg_tick()
                            while emitted < npend and unit * npend >= (emitted + 1) * int(nunits * 0.7):
                                lt2, gt2 = pending[emitted]
                                stage1_tile(gidx + 1, lt2, gt2)
                                emitted += 1
                while emitted < npend:
                    lt2, gt2 = pending[emitted]
                    stage1_tile(gidx + 1, lt2, gt2)
                    emitted += 1
                for lt, gt in enumerate(grp):
                    ty = 0 if gt < NTL else 1
                    gate = mvs[ty][2]
                    r_mv = mvs[ty][3]
                    ar = [ares[i] for i, (a, b_) in enumerate(blocks) if a < (lt + 1) * 128 and b_ > lt * 128]
                    b0, r_b0 = B.bank()
                    b1, r_b1 = B.bank()
                    bb = [(b0, r_b0), (b1, r_b1)]
                    for c in range(NFF):
                        for hf in range(2):
                            B.mm(bb[hf][0], actT[:, c, lt * 128:(lt + 1) * 128], wd[:, c, hf * 512:(hf + 1) * 512],
                                 c == 0, c == NFF - 1, ar + [r_wd], [bb[hf][1]])
                    hin, r_hin = hinr.next()
                    B.dma("sp", hin, srcf(gt), [], [r_hin])
                    et, r_et = etmpr.next()
                    for hf in range(2):
                        B.tt("dve", et[:, hf * 512:(hf + 1) * 512], bb[hf][0], gate[:, hf * 512:(hf + 1) * 512], ALU.mult,
                             [r_mv], [bb[hf][1], r_et])
                    ho, r_ho = houtr.next()
                    B.tt("pool", ho, et, hin, ALU.add, [r_et, r_hin], [r_ho])
                    B.dma("pool", dstf(gt), ho, [r_ho], [])

        def head_norm(bank_ap, r_bank, nh, gain_bc, r_gain, rings, name):
            sq, r_sq = rings["sq"].next()
            w = nh * 64
            B.act(sq[:, 0:w], bank_ap, AF.Square, [], [r_bank, r_sq])
            st_, r_st = rings["st"].next()
            B.reduce_sum(st_[:, 0:nh], sq[:, 0:w].rearrange("p (h d) -> p h d", h=nh), [r_sq], [r_st])
            rs = B.rstd(st_, r_st, nh, 1.0 / 64)
            qn, r_qn = rings[name].next()
            qn3 = qn[:, 0:w].rearrange("p (h d) -> p h d", h=nh)
            B.tt("dve", qn3, bank_ap.rearrange("p (h d) -> p h d", h=nh), rs.unsqueeze(2).to_broadcast([128, nh, 64]), ALU.mult,
                 [r_st], [r_bank, r_qn])
            B.tt("pool", qn3, qn3, gain_bc.unsqueeze(1).to_broadcast([128, nh, 64]), ALU.mult, [r_gain, r_qn], [r_qn])
            return qn, r_qn

        def rope(qn, r_qn, nh, ropt, r_ropt, outb, r_out, rings):
            w = nh * 64
            a_, r_a = rings["ra"].next()
            b_, r_b = rings["rb"].next()
            q3 = qn[:, 0:w].rearrange("p (h d) -> p h d", h=nh)
            a3 = a_[:, 0:w].rearrange("p (h d) -> p h d", h=nh)
            B.tt("pool", a3, q3, ropt[:, 0, :].unsqueeze(1).to_broadcast([128, nh, 64]), ALU.mult, [r_qn, r_ropt], [r_a])
            q5 = qn[:, 0:w].rearrange("p (h a s d) -> p h a s d", h=nh, a=2, s=2)
            b5 = b_[:, 0:w].rearrange("p (h a s d) -> p h a s d", h=nh, a=2, s=2)
            s4 = ropt[:, 1, :].rearrange("p (a s d) -> p a s d", a=2, s=2)
            for ax in range(2):
                for s in range(2):
                    B.tt("dve", b5[:, :, ax, s, :], q5[:, :, ax, 1 - s, :],
                         s4[:, ax, s, :].unsqueeze(1).to_broadcast([128, nh, 16]), ALU.mult, [r_qn, r_ropt], [r_b])
            B.tt("dve", outb[:, 0:w], a_[:, 0:w], b_[:, 0:w], ALU.add, [r_a, r_b], [r_out])

        def head_transposes(src, r_src, nh, dst_dram, gt, rings):
            bk, r_bk = B.bank()
            bkb = bk.bitcast(BF16)
            for h in range(nh):
                B.tr(bkb[0:64, h * 128:(h + 1) * 128], src[:, h * 64:(h + 1) * 64], [r_src], [r_bk])
            ts_, r_ts = rings["hT"].next()
            B.cp("act", ts_[:, 0:nh, :], bkb[0:64, 0:nh * 128].rearrange("p (h t) -> p h t", h=nh), [], [r_bk, r_ts])
            B.dma("sp", dst_dram.rearrange("h d t -> d h t")[:, 0:nh, gt * 128:(gt + 1) * 128], ts_[:, 0:nh, :], [r_ts], [])

        def phase_even_prep(li):
            B.phase_begin()
            mvs = load_mod_vecs(li, 3, 4, None, 1, 1.0)
            win = A.alloc([8, 1792], BF16)
            r_win = S.res()
            load_w_bf16("ev_w_in", win, r_win, I["ev_w_in"], 1792)
            qg_bc, r_qg = B.load_bc("sp", I["a_q_gain"][0:1, :], 64)
            kg_bc, r_kg = B.load_bc("sp", I["a_k_gain"][0:1, :], 64)
            vg_bc, r_vg = B.load_bc("sp", I["b_v_gain"][0:1, :], 512)
            hinr = Ring(B, 6, [D], F32)
            rings = {"sqj": (A.alloc([D], BF16), S.res()), "st": Ring(B, 24, [30], F32),
                     "z1": Ring(B, 2, [D], F32), "ztok": Ring(B, 3, [D], BF16),
                     "sq": Ring(B, 6, [512], F32),
                     "ra": Ring(B, 3, [512], F32), "rb": Ring(B, 3, [512], F32), "hT": Ring(B, 3, [8, 128], BF16, parts=64)}
            zTr = Ring(B, 3, [8, 128], BF16)
            ropr = Ring(B, 14, [2, 64], F32)
            qrr = Ring(B, 6, [512], F32)
            krr = Ring(B, 6, [128], F32)
            gvr = Ring(B, 6, [512], F32)
            qbr = Ring(B, 9, [512], BF16)
            kbr = Ring(B, 9, [128], BF16)
            vbr = Ring(B, 4, [128], BF16)
            ur = Ring(B, 3, [512], F32)
            vvr = Ring(B, 8, [512], BF16)
            nsl = [(0, 512), (512, 768), (768, 1280), (1280, 1792)]

            def tile_gen(gt):
                bg_tick()
                ty = 0 if gt < NTL else 1
                hin, r_hin = hinr.next()
                B.dma("sp", hin, hsrc(gt), [], [r_hin])
                if ty == 0:
                    rt, r_rt = ropr.next()
                    B.dma("sp", rt, I["k_rope"][gt * 128:(gt + 1) * 128, :, :], [], [r_rt])
                yield
                zt, r_zt = yield from norm_tile_g(hin, r_hin, mvs[ty], rings)
                yield
                zT, r_zT = zTr.next()
                transpose8(zt, r_zt, zT, r_zT)
                yield
                bks = [B.bank() for _ in range(4)]
                for k in range(8):
                    for i, (n0, n1) in enumerate(nsl):
                        B.mm(bks[i][0][:, 0:n1 - n0], zT[:, k, :], win[:, k, n0:n1], k == 0, k == 7, [r_zT, r_win], [bks[i][1]])
                (bq, r_bq), (bkv, r_bkv), (bbu, r_bbu), (bbv, r_bbv) = bks
                yield
                qr, r_qr = qrr.next()
                B.cp("act", qr, bq, [], [r_bq, r_qr])
                kr, r_kr = krr.next()
                B.cp("act", kr, bkv[:, 0:128], [], [r_bkv, r_kr])
                vb, r_vb = vbr.next()
                B.cp("act", vb, bkv[:, 128:256], [], [r_bkv, r_vb])
                u_, r_u = ur.next()
                B.act(u_, bbu, AF.Gelu_apprx_tanh, [], [r_bbu, r_u])
                gv, r_gv = gvr.next()
                B.act(gv, bbv, AF.Gelu_apprx_tanh, [], [r_bbv, r_gv])
                yield
                B.dma("sp", v_d[gt * 128:(gt + 1) * 128, 0:128], vb, [r_vb], [])
                B.dma("sp", u_d[gt * 128:(gt + 1) * 128, :], u_, [r_u], [])
                vv, r_vv = vvr.next()
                items = [dict(src=qr, r_src=r_qr, nh=8, gain=qg_bc, r_gain=r_qg),
                         dict(src=kr, r_src=r_kr, nh=2, gain=kg_bc, r_gain=r_kg),
                         dict(src=gv, r_src=r_gv, nh=8, gain=vg_bc, r_gain=r_vg, gain_full=True, dst=vv, r_dst=r_vv)]
                qb, r_qb = qbr.next()
                kb, r_kb = kbr.next()
                if ty == 1:
                    items[0]["dst"], items[0]["r_dst"] = qb, r_qb
                    items[1]["dst"], items[1]["r_dst"] = kb, r_kb
                yield from chains_g(items, rings)
                yield
                B.dma("sp", vv_d[gt * 128:(gt + 1) * 128, :], vv, [r_vv], [])
                if ty == 0:
                    yield from rope_g([(qr, r_qr, 8, qb, r_qb), (kr, r_kr, 2, kb, r_kb)], rt, r_rt, rings)
                    yield
                head_transposes(qb, r_qb, 8, qT_d, gt, rings)
                head_transposes(kb, r_kb, 2, kT_d, gt, rings)

            pipeline(tile_gen, range(NT))

        def outproj_residual(mix, r_mix, wout, r_wout, gate, r_gate, gt, rings):
            hin, r_hin = rings["hin"].next()
            B.dma("sp", hin, hsrc(gt), [], [r_hin])
            mT, r_mT = rings["mixT"].next()
            transpose8(mix, r_mix, mT, r_mT)
            yield
            b0, r_b0 = B.bank()
            b1, r_b1 = B.bank()
            bb = [(b0, r_b0), (b1, r_b1)]
            for k in range(8):
                for hf in range(2):
                    B.mm(bb[hf][0], mT[:, k, :], wout[:, k, hf * 512:(hf + 1) * 512], k == 0, k == 7, [r_mT, r_wout], [bb[hf][1]])
            et, r_et = rings["et"].next()
            for hf in range(2):
                B.tt("dve", et[:, hf * 512:(hf + 1) * 512], bb[hf][0], gate[:, hf * 512:(hf + 1) * 512], ALU.mult,
                     [r_gate], [bb[hf][1], r_et])
            yield
            ho, r_ho = rings["hout"].next()
            B.tt("pool", ho, et, hin, ALU.add, [r_et, r_hin], [r_ho])
            yield
            B.dma("sp", hsrc(gt), ho, [r_ho], [])

        def load_V(dst, r_dst, kt0, nw, nk):
            for w in range(nw):
                B.dma("sp", dst[:, w, :, 0:64],
                      v_d[(kt0 + w) * 128:(kt0 + w + 1) * 128, 0:nk * 64].rearrange("p (k d) -> p k d", k=nk), [], [r_dst])

        def phase_even_attn(li):
            B.phase_begin()
            mvs = load_mod_vecs(li, None, None, 5, 1, 1.0)
            wout = A.alloc([8, D], BF16)
            r_wout = S.res()
            load_w_bf16("ev_w_out", wout, r_wout, I["ev_w_out"], 1024)
            wsn = A.alloc([8, 128], BF16)
            r_wsn = S.res()
            B.dma("pool", wsn, I["b_ws"].rearrange("g i j -> i g j"), [], [r_wsn])
            wsT = A.alloc([8, 128], BF16)
            r_wsT = S.res()
            bk, r_bk = B.bank()
            bkb = bk.bitcast(BF16)
            for g in range(8):
                B.tr(bkb[:, g * 128:(g + 1) * 128], wsn[:, g, :], [r_wsn], [r_bk])
            B.cp("act", wsT, bkb.rearrange("p (g t) -> p g t", g=8), [], [r_bk, r_wsT])
            bias_sb = A.alloc([128], F32, parts=8)
            r_bsb = S.res()
            B.dma("sp", bias_sb, I["b_bias"][:, :], [], [r_bsb])
            biasT = A.alloc([8], F32)
            r_biasT = S.res()
            bk, r_bk = B.bank()
            B.mm(bk[:, 0:8], bias_sb, B.identf[0:8, 0:8], True, True, [r_bsb, B.r_ident], [r_bk])
            B.cp("dve", biasT, bk[:, 0:8], [], [r_bk, r_biasT])
            esink, r_es = B.load_bc("sp", I["a_sink"][0:1, :], 8)
            B.act(esink, esink, AF.Exp, [r_es], [r_es])
            amask = A.alloc([2, 128], BF16)
            r_am = S.res()
            B.dma("pool", amask, I["k_amask"][:, :, :], [], [r_am])
            kTc = A.alloc([2, 256], BF16, parts=64)
            r_kTc = S.res()
            B.dma("sp", kTc, kT_d.rearrange("h d t -> d h t")[:, 0:2, NTL * 128:NT * 128], [], [r_kTc])
            Vc = A.alloc([2, 2, 65], BF16)
            r_Vc = S.res()
            B.memset("dve", Vc[:, :, :, 64:65], 1.0, [r_Vc])
            load_V(Vc, r_Vc, NTL, 2, 2)
            kTwr = Ring(B, 4, [2, 384], BF16, parts=64)
            Vwr = Ring(B, 5, [3, 2, 65], BF16)
            for vb_, r_ in Vwr.bufs:
                B.memset("dve", vb_[:, :, :, 64:65], 1.0, [r_])
            qTr = Ring(B, 4, [8, 128], BF16, parts=64)
            pTr = Ring(B, 16, [512], BF16)
            etr = Ring(B, 2, [512], BF16)
            mixr = Ring(B, 4, [D], BF16)
            str_ = Ring(B, 4, [8], F32)
            vvr = Ring(B, 5, [512], BF16)
            ur = Ring(B, 5, [512], F32)
            btr = Ring(B, 2, [512], F32)
            rings = {"mixT": Ring(B, 2, [8, 128], BF16), "hin": Ring(B, 3, [D], F32), "et": Ring(B, 2, [D], F32),
                     "hout": Ring(B, 2, [D], F32)}

            def tile_gen(n):
                bg_tick()
                ty = 0 if n < NTL else 1
                qT, r_qT = qTr.next()
                B.dma("sp", qT, qT_d.rearrange("h d t -> d h t")[:, :, n * 128:(n + 1) * 128], [], [r_qT])
                keys = []
                if ty == 0:
                    kt0 = min(max(n - 1, 0), NTL - 3)
                    kTw, r_kTw = kTwr.next()
                    B.dma("sp", kTw, kT_d.rearrange("h d t -> d h t")[:, 0:2, kt0 * 128:(kt0 + 3) * 128], [], [r_kTw])
                    Vw, r_Vw = Vwr.next()
                    load_V(Vw, r_Vw, kt0, 3, 2)
                    for kt, mk in ((n - 1, 0), (n, None), (n + 1, 1)):
                        if 0 <= kt < NTL:
                            s_ = kt - kt0
                            keys.append((kTw, s_, Vw, s_, mk, [r_kTw], [r_Vw]))
                for s_ in range(2):
                    keys.append((kTc, s_, Vc, s_, None, [r_kTc], [r_Vc]))
                vv, r_vv = vvr.next()
                B.dma("sp", vv, vv_d[n * 128:(n + 1) * 128, :], [], [r_vv])
                u_, r_u = ur.next()
                B.dma("sp", u_, u_d[n * 128:(n + 1) * 128, :], [], [r_u])
                yield

                def qk(kv):
                    pts = []
                    for (kTa, ks, Va, vs, mk, rk, rv) in keys:
                        bk, r_bk = B.bank()
                        B.mm(bk.rearrange("p (h q) -> p h q", h=4), kTa[:, kv, ks * 128:(ks + 1) * 128], qT[:, 4 * kv:4 * kv + 4, :],
                             True, True, rk + [r_qT], [r_bk])
                        pT, r_pT = pTr.next()
                        if mk is None:
                            B.act(pT, bk, AF.Exp, [], [r_bk, r_pT], scale=0.125)
                        else:
                            et, r_et = etr.next()
                            B.act(et, bk, AF.Exp, [], [r_bk, r_et], scale=0.125)
                            B.tt("dve", pT.rearrange("p (h q) -> p h q", h=4), et.rearrange("p (h q) -> p h q", h=4),
                                 amask[:, mk, :].unsqueeze(1).to_broadcast([128, 4, 128]), ALU.mult, [r_et, r_am], [r_pT])
                        pts.append((pT, r_pT, Va, vs, rv))
                    return pts

                def pv(pkv, pts, mix, r_mix):
                    ob, r_ob = B.bank()
                    for hh in range(4):
                        for ei, (pT, r_pT, Va, vs, rv) in enumerate(pts):
                            B.mm(ob[:, hh * 65:(hh + 1) * 65], pT[:, hh * 128:(hh + 1) * 128], Va[:, vs, pkv, :],
                                 ei == 0, ei == len(pts) - 1, [r_pT] + rv, [r_ob])
                    ob3 = ob[:, 0:260].rearrange("p (h e) -> p h e", h=4)
                    sd, r_sd = str_.next()
                    B.tt("dve", sd[:, 0:4], ob3[:, :, 64], esink[:, 4 * pkv:4 * pkv + 4], ALU.add, [r_es], [r_ob, r_sd])
                    B.recip(sd[:, 4:8], sd[:, 0:4], [r_sd], [r_sd])
                    B.tt("dve", mix[:, pkv * 256:(pkv + 1) * 256].rearrange("p (h d) -> p h d", h=4), ob3[:, :, 0:64],
                         sd[:, 4:8].unsqueeze(2).to_broadcast([128, 4, 64]), ALU.mult, [r_sd], [r_ob, r_mix])

                pts0 = qk(0)
                yield
                pts1 = qk(1)
                mix, r_mix = mixr.next()
                pv(0, pts0, mix, r_mix)
                yield
                pv(1, pts1, mix, r_mix)
                bk, r_bk = B.bank()
                for g in range(8):
                    B.mm(bk[:, g * 64:(g + 1) * 64], wsT[:, g, :], vv[:, g * 64:(g + 1) * 64], True, True, [r_wsT, r_vv], [r_bk])
                bt, r_bt = btr.next()
                B.tt("dve", bt.rearrange("p (g d) -> p g d", g=8), bk.rearrange("p (g d) -> p g d", g=8),
                     biasT.unsqueeze(2).to_broadcast([128, 8, 64]), ALU.add, [r_biasT], [r_bk, r_bt])
                B.tt("pool", mix[:, 512:1024], bt, u_, ALU.mult, [r_bt, r_u], [r_mix])
                yield
                yield from outproj_residual(mix, r_mix, wout, r_wout, mvs[ty][2], mvs[ty][3], n, rings)

            pipeline(tile_gen, range(NT))

        def phase_odd_prep(li):
            B.phase_begin()
            mvs = load_mod_vecs(li, 3, 4, None, 1, 1.0)
            win = A.alloc([8, 2048], BF16)
            r_win = S.res()
            load_w_bf16("od_w_in", win, r_win, I["od_w_in"], 2048)
            qg_bc, r_qg = B.load_bc("sp", I["d_q_gain"][0:1, :], 64)
            kg_bc, r_kg = B.load_bc("sp", I["d_k_gain"][0:1, :], 64)
            hinr = Ring(B, 6, [D], F32)
            rings = {"sqj": (A.alloc([D], BF16), S.res()), "st": Ring(B, 24, [30], F32),
                     "z1": Ring(B, 2, [D], F32), "ztok": Ring(B, 3, [D], BF16),
                     "sq": Ring(B, 6, [512], F32),
                     "hT": Ring(B, 3, [8, 128], BF16, parts=64)}
            zTr = Ring(B, 3, [8, 128], BF16)
            qrr = Ring(B, 7, [512], F32)
            krr = Ring(B, 7, [512], F32)
            qbr = Ring(B, 8, [512], BF16)
            kbr = Ring(B, 8, [512], BF16)
            vbr = Ring(B, 4, [512], BF16)
            xbr = Ring(B, 4, [512], BF16)

            def tile_gen(gt):
                bg_tick()
                ty = 0 if gt < NTL else 1
                hin, r_hin = hinr.next()
                B.dma("sp", hin, hsrc(gt), [], [r_hin])
                yield
                zt, r_zt = yield from norm_tile_g(hin, r_hin, mvs[ty], rings)
                yield
                zT, r_zT = zTr.next()
                transpose8(zt, r_zt, zT, r_zT)
                yield
                nbs = [0, 1, 2, 3] if ty == 0 else [2, 3]
                bks = {i: B.bank() for i in nbs}
                for k in range(8):
                    for i in nbs:
                        B.mm(bks[i][0], zT[:, k, :], win[:, k, i * 512:(i + 1) * 512], k == 0, k == 7, [r_zT, r_win], [bks[i][1]])
                yield
                items = []
                qb = r_qb = None
                if ty == 0:
                    xb, r_xb = xbr.next()
                    B.cp("act", xb, bks[0][0], [], [bks[0][1], r_xb])
                    qr, r_qr = qrr.next()
                    B.cp("act", qr, bks[1][0], [], [bks[1][1], r_qr])
                    qb, r_qb = qbr.next()
                    items.append(dict(src=qr, r_src=r_qr, nh=8, gain=qg_bc, r_gain=r_qg, dst=qb, r_dst=r_qb))
                kr, r_kr = krr.next()
                B.cp("act", kr, bks[2][0], [], [bks[2][1], r_kr])
                kb, r_kb = kbr.next()
                items.append(dict(src=kr, r_src=r_kr, nh=8, gain=kg_bc, r_gain=r_kg, dst=kb, r_dst=r_kb))
                vb, r_vb = vbr.next()
                B.cp("act", vb, bks[3][0], [], [bks[3][1], r_vb])
                yield
                if ty == 0:
                    B.dma("sp", xp_d[gt * 128:(gt + 1) * 128, :], xb, [r_xb], [])
                B.dma("sp", v_d[gt * 128:(gt + 1) * 128, :], vb, [r_vb], [])
                yield from chains_g(items, rings)
                yield
                if ty == 0:
                    head_transposes(qb, r_qb, 8, qT_d, gt, rings)
                head_transposes(kb, r_kb, 8, kT_d, gt, rings)

            pipeline(tile_gen, range(NT))

        def phase_odd_attn(li):
            B.phase_begin()
            mvs = load_mod_vecs(li, None, None, 5, 1, 1.0, ntypes=1)
            wout = A.alloc([8, D], BF16)
            r_wout = S.res()
            load_w_bf16("od_w_out", wout, r_wout, I["od_w_out"], 1024)
            wpool = A.alloc([4, 128], BF16)
            r_wpool = S.res()
            B.dma("pool", wpool, I["c_w_pool"].rearrange("g c d -> c g d"), [], [r_wpool])
            csc, r_csc = B.load_bc("sp", I["c_scale"][0:1, :], 512)
            band = A.alloc([4, 5, 128], BF16)
            r_band = S.res()
            B.dma("pool", band, I["k_band"][:, :, :, :], [], [r_band], max_dma_last_dim=2048)
            kTp = kT_d.rearrange("(g e) d t -> (e d) g t", e=2)
            qTp = qT_d.rearrange("(g e) d t -> (e d) g t", e=2)
            kTc = A.alloc([4, 256], BF16)
            r_kTc = S.res()
            B.dma("sp", kTc, kTp[:, :, NTL * 128:NT * 128], [], [r_kTc])
            Vc = A.alloc([2, 8, 65], BF16)
            r_Vc = S.res()
            B.memset("dve", Vc[:, :, :, 64:65], 1.0, [r_Vc])
            load_V(Vc, r_Vc, NTL, 2, 8)
            biasr = Ring(B, 1, [8, 7, 128], F32)
            for bb_, r_ in biasr.bufs:
                B.memset("pool", bb_[:, :, 5:7, :], 0.0, [r_])
            kTwr = Ring(B, 4, [4, 640], BF16)
            Vwr = Ring(B, 4, [5, 8, 65], BF16)
            for vb_, r_ in Vwr.bufs:
                B.memset("dve", vb_[:, :, :, 64:65], 1.0, [r_])
            qTr = Ring(B, 4, [4, 128], BF16)
            xpr = Ring(B, 3, [3, 512], BF16)
            ppr = Ring(B, 2, [4, 128], BF16)
            tAr = Ring(B, 3, [512], F32)
            tBr = Ring(B, 3, [384], F32)
            pAr = Ring(B, 14, [512], BF16)
            pBr = Ring(B, 14, [384], BF16)
            mixr = Ring(B, 8, [D], BF16)
            str_ = Ring(B, 4, [8], F32)
            rings = {"mixT": Ring(B, 2, [8, 128], BF16), "hin": Ring(B, 3, [D], F32), "et": Ring(B, 2, [D], F32),
                     "hout": Ring(B, 2, [D], F32)}
            state = {"case": None, "bias": None, "r_bias": None}
            B.nrr = 6

            def case_of(n):
                return 0 if n == 0 else 1 if n == 1 else 3 if n == NTL - 2 else 4 if n == NTL - 1 else 2

            def tile_gen(n):
                bg_tick()
                case = case_of(n)
                if case != state["case"]:
                    bias_, r_bias_ = biasr.next()
                    B.dma("sp", bias_[:, :, 0:5, :], I["k_dbias"][case], [], [r_bias_])
                    state["case"] = case
                    state["bias"] = bias_
                    state["r_bias"] = r_bias_
                bias = state["bias"]
                r_bias = state["r_bias"]
                kt0 = min(max(n - 2, 0), NTL - 5)
                kTw, r_kTw = kTwr.next()
                B.dma("sp", kTw, kTp[:, :, kt0 * 128:(kt0 + 5) * 128], [], [r_kTw])
                Vw, r_Vw = Vwr.next()
                load_V(Vw, r_Vw, kt0, 5, 8)
                qT, r_qT = qTr.next()
                B.dma("sp", qT, qTp[:, :, n * 128:(n + 1) * 128], [], [r_qT])
                xw, r_xw = xpr.next()
                jts = [j for j in (n - 1, n, n + 1) if 0 <= j < NTL]
                j0 = jts[0]
                B.dma("sp", xw[:, 0:len(jts), :], xp_d[j0 * 128:(j0 + len(jts)) * 128, :].rearrange("(w p) f -> p w f", p=128), [], [r_xw])
                yield
                mix, r_mix = mixr.next()
                bk, r_bk = B.bank()
                for g in range(4):
                    for ji, j in enumerate(jts):
                        if j == n - 1:
                            typ = 0
                        elif j == n + 1:
                            typ = 2
                        else:
                            typ = 3 if n == 0 else (4 if n == NTL - 1 else 1)
                        B.mm(bk[:, g * 128:(g + 1) * 128], xw[:, ji, g * 128:(g + 1) * 128], band[:, g, typ, :],
                             ji == 0, ji == len(jts) - 1, [r_xw, r_band], [r_bk])
                pp, r_pp = ppr.next()
                B.cp("act", pp, bk.rearrange("p (g t) -> p g t", g=4), [], [r_bk, r_pp])
                bk2, r_bk2 = B.bank()
                for g in range(4):
                    B.mm(bk2[:, g * 128:(g + 1) * 128], pp[:, g, :], wpool[:, g, :], True, True, [r_pp, r_wpool], [r_bk2])
                B.tt("dve", mix[:, 0:512], bk2, csc, ALU.mult, [r_csc], [r_bk2, r_mix])
                yield

                def scores(hq):
                    res_ = []
                    for hi in range(4):
                        h = hq * 4 + hi
                        bA, r_bA = B.bank()
                        bB, r_bB = B.bank()
                        pl = slice((h % 2) * 64, (h % 2) * 64 + 64)
                        g_ = h // 2
                        for s_ in range(4):
                            B.mm(bA[:, s_ * 128:(s_ + 1) * 128], kTw[pl, g_, s_ * 128:(s_ + 1) * 128], qT[pl, g_, :], True, True, [r_kTw, r_qT], [r_bA])
                        B.mm(bB[:, 0:128], kTw[pl, g_, 512:640], qT[pl, g_, :], True, True, [r_kTw, r_qT], [r_bB])
                        for s_ in range(2):
                            B.mm(bB[:, (1 + s_) * 128:(2 + s_) * 128], kTc[pl, g_, s_ * 128:(s_ + 1) * 128], qT[pl, g_, :], True, True, [r_kTc, r_qT], [r_bB])
                        tA, r_tA = tAr.next()
                        tB, r_tB = tBr.next()
                        B.stt("dve", tA, bA, 0.125, bias[:, h, 0:4, :].rearrange("p s q -> p (s q)"), ALU.mult, ALU.add, [r_bias], [r_bA, r_tA])
                        B.stt("dve", tB, bB[:, 0:384], 0.125, bias[:, h, 4:7, :].rearrange("p s q -> p (s q)"), ALU.mult, ALU.add, [r_bias], [r_bB, r_tB])
                        pA, r_pA = pAr.next()
                        pB, r_pB = pBr.next()
                        B.act(pA, tA, AF.Exp, [r_tA], [r_pA])
                        B.act(pB, tB, AF.Exp, [r_tB], [r_pB])
                        res_.append((h, pA, r_pA, pB, r_pB))
                    return res_

                def pvs(hq, res_):
                    ob, r_ob = B.bank_fixed(6 + hq)
                    for (ph, pA, r_pA, pB, r_pB) in res_:
                        osl = ob[:, (ph % 4) * 65:(ph % 4 + 1) * 65]
                        for s_ in range(7):
                            if s_ < 4:
                                lhs = pA[:, s_ * 128:(s_ + 1) * 128]
                                rp = r_pA
                            else:
                                lhs = pB[:, (s_ - 4) * 128:(s_ - 3) * 128]
                                rp = r_pB
                            if s_ < 5:
                                rhs = Vw[:, s_, ph, :]
                                rv = r_Vw
                            else:
                                rhs = Vc[:, s_ - 5, ph, :]
                                rv = r_Vc
                            B.mm(osl, lhs, rhs, s_ == 0, s_ == 6, [rp, rv], [r_ob])
                    ob3 = ob[:, 0:260].rearrange("p (h e) -> p h e", h=4)
                    sd, r_sd = str_.next()
                    B.recip(sd[:, 0:4], ob3[:, :, 64], [], [r_ob, r_sd])
                    B.tt("dve", mix[:, 512 + hq * 256:512 + (hq + 1) * 256].rearrange("p (h d) -> p h d", h=4), ob3[:, :, 0:64],
                         sd[:, 0:4].unsqueeze(2).to_broadcast([128, 4, 64]), ALU.mult, [r_sd], [r_ob, r_mix])

                r0 = scores(0)
                yield
                r1 = scores(1)
                pvs(0, r0)
                yield
                pvs(1, r1)
                yield
                yield from outproj_residual(mix, r_mix, wout, r_wout, mvs[0][2], mvs[0][3], n, rings)

            pipeline(tile_gen, range(NTL), drain_before=lambda n: case_of(n) != state["case"])
            B.nrr = 8

        plist = [
            lambda: phase_mod(0, 0, 6),
            lambda: phase_ffn(0, 0, NT, src0, hsrc),
            lambda: phase_mod(0, 6, 18),
            lambda: phase_even_prep(0),
            lambda: phase_even_attn(0),
            lambda: phase_ffn(0, 1, NT, hsrc, hsrc),
            lambda: phase_mod(1),
            lambda: phase_ffn(1, 0, NT, hsrc, hsrc),
            lambda: phase_odd_prep(1),
            lambda: phase_odd_attn(1),
            lambda: phase_ffn(1, 1, NTL, hsrc, osrc),
        ]
        if dbg_phases is not None:
            plist = plist[:dbg_phases]
        for p in plist:
            p()
        S.emit()
        build_program.stats = S.stats
    return nc


def _rope_table():
    t = np.arange(S_LAT)
    row = (t // 64).astype(np.float32)
    col = (t % 64).astype(np.float32)
    m = 16
    inv = (1.0 / (10000.0 ** (np.arange(m, dtype=np.float32) / m))).astype(np.float32)
    ar = row[:, None] * inv[None, :]
    ac = col[:, None] * inv[None, :]
    cos = np.concatenate([np.cos(ar), np.cos(ar), np.cos(ac), np.cos(ac)], axis=1)
    sin = np.concatenate([-np.sin(ar), np.sin(ar), -np.sin(ac), np.sin(ac)], axis=1)
    return np.stack([cos, sin], axis=1).astype(np.float32)


def _amask():
    pj = np.arange(128)[:, None]
    pi = np.arange(128)[None, :]
    prev = (pj >= pi).astype(np.float32)
    nxt = (pj <= pi).astype(np.float32)
    return np.stack([prev, nxt], axis=1)


def _band():
    out = np.zeros((128, 4, 5, 128), np.float32)
    for gi, w in enumerate((2, 4, 8, 16)):
        def mat(n, jn):
            tg = n * 128 + np.arange(128)
            lo = np.clip(tg - w // 2, 0, S_LAT)
            hi = np.clip(tg + w - w // 2, 0, S_LAT)
            cnt = (hi - lo).astype(np.float32)
            jg = jn * 128 + np.arange(128)
            m = ((jg[:, None] >= lo[None, :]) & (jg[:, None] < hi[None, :])).astype(np.float32) / cnt[None, :]
            m = m - (jg[:, None] == tg[None, :]).astype(np.float32)
            return m
        out[:, gi, 0] = mat(5, 4)
        out[:, gi, 1] = mat(5, 5)
        out[:, gi, 2] = mat(5, 6)
        out[:, gi, 3] = mat(0, 0)
        out[:, gi, 4] = mat(NTL - 1, NTL - 1)
    return out


def _dbias(rpb):
    out = np.full((5, 128, 8, 5, 128), NEGB, np.float32)
    for case, n in enumerate((0, 1, 5, NTL - 2, NTL - 1)):
        kt0 = min(max(n - 2, 0), NTL - 5)
        i = np.arange(128)
        r = 2 * n + i // 64
        c = i % 64
        r0 = np.clip(r - 4, 0, 56)
        q0 = np.clip(c - 8, 0, 48)
        for s in range(5):
            kt = kt0 + s
            j = np.arange(128)
            kr = 2 * kt + j // 64
            kc = j % 64
            valid = ((kr[:, None] >= r0[None, :]) & (kr[:, None] < r0[None, :] + 8) &
                     (kc[:, None] >= q0[None, :]) & (kc[:, None] < q0[None, :] + 16))
            ri = np.clip(kr[:, None] - r[None, :] + 7, 0, 14)
            ci = np.clip(kc[:, None] - c[None, :] + 15, 0, 30)
            g = rpb[:, ri, ci]
            g = np.where(valid[None], g, np.float32(NEGB))
            out[case, :, :, s, :] = np.transpose(g, (1, 0, 2))
    return out


_CACHE = {}


def kernel(x, c, ctx, c_ctx, ada_w, ada_b, norm_g, ffn_w_gu, ffn_w_down,
           ev_w_in, ev_w_out, a_q_gain, a_k_gain, a_sink, b_v_gain, b_ws, b_bias,
           od_w_in, od_w_out, c_w_pool, c_scale, d_q_gain, d_k_gain, d_rpb, _dbg_phases=None, _dbg=False):
    f = lambda a: np.ascontiguousarray(np.asarray(a, dtype=np.float32))
    key = (_dbg_phases, _dbg)
    if key not in _CACHE:
        _CACHE[key] = build_program(_dbg_phases, _dbg)
    nc = _CACHE[key]
    shared = {
        "c_ctx": f(c_ctx).reshape(1, D), "ada_w": f(ada_w), "ada_b": f(ada_b), "norm_g": f(norm_g),
        "ffn_w_gu": f(ffn_w_gu), "ffn_w_down": f(ffn_w_down),
        "ev_w_in": f(ev_w_in)[0], "ev_w_out": f(ev_w_out)[0],
        "a_q_gain": f(a_q_gain), "a_k_gain": f(a_k_gain), "a_sink": f(a_sink),
        "b_v_gain": f(b_v_gain), "b_ws": f(b_ws)[0], "b_bias": f(b_bias)[0],
        "od_w_in": f(od_w_in)[0], "od_w_out": f(od_w_out)[0],
        "c_w_pool": f(c_w_pool)[0], "c_scale": f(c_scale),
        "d_q_gain": f(d_q_gain), "d_k_gain": f(d_k_gain),
        "k_ident": np.eye(128, dtype=np.float32), "k_rope": _rope_table(), "k_amask": _amask(),
        "k_band": _band(), "k_dbias": _dbias(f(d_rpb)[0]),
    }
    x = f(x); c = f(c); ctx = f(ctx)
    in_maps = []
    for b in range(8):
        m = dict(shared)
        m["x"] = x[b]
        m["ctx"] = ctx[b]
        m["c"] = c[b].reshape(1, D)
        in_maps.append(m)
    res = run_bass_kernel_spmd(nc, in_maps, core_ids=list(range(8)))
    kernel.last = res
    return np.stack([r["out"] for r in res.results], axis=0)
```

```python
import contextlib
import numpy as np
import concourse.bass as bass
import concourse.mybir as mybir
from concourse.bass_utils import run_bass_kernel_spmd

F32 = mybir.dt.float32
BF16 = mybir.dt.bfloat16
AF = mybir.ActivationFunctionType
ALU = mybir.AluOpType
AX = mybir.AxisListType

D = 1024
S_LAT = 4096
S_CTX = 256
NTL = 32
NT = 34
DFF = 2816
NFF = 22
EPS = 1e-6
NEGB = -30000.0

ENGS = ("sp", "act", "pool", "dve", "pe")
NDMA_SEM = 8


class Res:
    __slots__ = ("name", "last_w", "readers", "gen")

    def __init__(self, name):
        self.name = name
        self.last_w = None
        self.readers = {}
        self.gen = 0

    def bump(self):
        self.gen += 1
        return Ref(self, self.gen)


class Ref:
    __slots__ = ("phys", "gen")

    def __init__(self, phys, gen):
        self.phys = phys
        self.gen = gen


def _norm_res(lst):
    out = []
    for r in lst:
        if isinstance(r, Ref):
            assert r.gen == r.phys.gen, f"stale buffer reference {r.phys.name}"
            r = r.phys
        out.append(r)
    return out


def pipeline(make_gen, items, drain_before=None):
    active = []
    for it in items:
        if drain_before is not None and drain_before(it):
            while active:
                nxt = []
                for g in active:
                    try:
                        next(g)
                        nxt.append(g)
                    except StopIteration:
                        pass
                active = nxt
        nxt = []
        for g in active:
            try:
                next(g)
                nxt.append(g)
            except StopIteration:
                pass
        active = nxt
        g = make_gen(it)
        try:
            next(g)
            active.append(g)
        except StopIteration:
            pass
    while active:
        nxt = []
        for g in active:
            try:
                next(g)
                nxt.append(g)
            except StopIteration:
                pass
        active = nxt


class Op:
    __slots__ = ("eng", "fn", "deps", "dma", "signal", "sem", "val", "prewait", "bg")

    def __init__(self, eng, fn, dma):
        self.eng = eng
        self.fn = fn
        self.dma = dma
        self.deps = []
        self.signal = False
        self.sem = None
        self.val = 0
        self.prewait = None
        self.bg = False


class Sched:
    def __init__(self, nc):
        self.nc = nc
        self.ops = {e: [] for e in ENGS}
        self.bar = {}
        self.nres = 0

    def res(self, name=None):
        self.nres += 1
        return Res(name or f"r{self.nres}")

    def op(self, eng, fn, reads=(), writes=(), dma=False):
        reads = _norm_res(reads)
        writes = _norm_res(writes)
        o = Op(eng, fn, dma)
        deps = {}
        for r in reads:
            if r.last_w is not None:
                deps[id(r.last_w)] = r.last_w
        for r in writes:
            if r.last_w is not None:
                deps[id(r.last_w)] = r.last_w
            for rd in r.readers.values():
                if isinstance(rd, list):
                    for x in rd:
                        deps[id(x)] = x
                else:
                    deps[id(rd)] = rd
        for r in reads:
            if dma:
                r.readers.setdefault(("dma", eng), []).append(o)
            else:
                r.readers[eng] = o
        for r in writes:
            r.last_w = o
            r.readers = {}
        b = self.bar.pop(eng, None)
        if b:
            for x in b:
                deps[id(x)] = x
        dl = []
        for d in deps.values():
            if d is o:
                continue
            if (not dma) and (not d.dma) and d.eng == "pe" and eng == "pe":
                continue
            dl.append(d)
        o.deps = dl
        self.ops[eng].append(o)
        return o

    def dma(self, q, out, in_, reads=(), writes=(), **kw):
        return self.op(q, lambda e: e.dma_start(out=out, in_=in_, **kw), reads, writes, dma=True)

    def barrier(self):
        tails = []
        for e in ENGS:
            ops = self.ops[e]
            for o in reversed(ops):
                if not o.dma:
                    tails.append(o)
                    break
            nd = sum(1 for o in ops if o.dma)
            seen_slots = set()
            idx = nd
            for o in reversed(ops):
                if not o.dma:
                    continue
                idx -= 1
                slot = idx % NDMA_SEM
                if slot in seen_slots or o.bg:
                    continue
                seen_slots.add(slot)
                tails.append(o)
                if len(seen_slots) >= NDMA_SEM:
                    break
        self.bar = {e: list(tails) for e in ENGS}

    def emit(self):
        nc = self.nc
        with contextlib.ExitStack() as st:
            csem = {e: st.enter_context(nc.semaphore(f"c_{e}")) for e in ENGS}
            dsem = {e: [st.enter_context(nc.semaphore(f"d_{e}{i}")) for i in range(NDMA_SEM)]
                    for e in ("sp", "act", "pool")}
            for e in ENGS:
                for o in self.ops[e]:
                    for d in o.deps:
                        d.signal = True
            self.stats = {}
            for e in ENGS:
                cnt = 0
                nd = 0
                for o in self.ops[e]:
                    if o.dma:
                        slot = nd % NDMA_SEM
                        o.sem = dsem[e][slot]
                        o.val = 16 * (nd // NDMA_SEM + 1)
                        if nd >= NDMA_SEM:
                            o.prewait = (dsem[e][slot], 16 * (nd // NDMA_SEM))
                        nd += 1
                    elif o.signal:
                        cnt += 1
                        o.sem = csem[e]
                        o.val = cnt
                self.stats[e] = (len(self.ops[e]), cnt, nd)
            block = st.enter_context(nc.Block())

            def run(eng_name, eng):
                seen = {}
                lastdma = {}
                for o in self.ops[eng_name]:
                    waits = []
                    if o.prewait is not None:
                        waits.append(o.prewait)
                    for d in o.deps:
                        waits.append((d.sem, d.val))
                    for sem, val in waits:
                        k = id(sem)
                        if seen.get(k, 0) >= val:
                            continue
                        seen[k] = val
                        eng.wait_ge(sem, val)
                    inst = o.fn(eng)
                    if o.dma:
                        inst.then_inc(o.sem, 16)
                        lastdma[id(o.sem)] = (o.sem, o.val)
                    elif o.signal:
                        inst.then_inc(o.sem, 1)
                for sem, val in lastdma.values():
                    if seen.get(id(sem), 0) < val:
                        eng.wait_ge(sem, val)

            @block.sync
            def _(e):
                run("sp", e)

            @block.scalar
            def _(e):
                run("act", e)

            @block.gpsimd
            def _(e):
                run("pool", e)

            @block.vector
            def _(e):
                run("dve", e)

            @block.tensor
            def _(e):
                run("pe", e)


class Arena:
    def __init__(self, nc, st, nbytes):
        self.nbytes = nbytes
        self.t = st.enter_context(nc.sbuf_tensor("arena", [128, nbytes // 2], BF16))
        self.off = 0
        self.base = 0

    def alloc(self, free, dtype, parts=128):
        free = list(free)
        n = int(np.prod(free))
        sz = n * (4 if dtype == F32 else 2)
        off = (self.off + 63) // 64 * 64
        assert off + sz <= self.nbytes, f"arena overflow {off + sz} > {self.nbytes}"
        self.off = off + sz
        ap = self.t[0:parts, off // 2: (off + sz) // 2]
        if dtype == F32:
            ap = ap.bitcast(F32)
        if len(free) == 2:
            ap = ap.rearrange("p (a b) -> p a b", a=free[0])
        elif len(free) == 3:
            ap = ap.rearrange("p (a b c) -> p a b c", a=free[0], b=free[1])
        elif len(free) == 4:
            ap = ap.rearrange("p (a b c d) -> p a b c d", a=free[0], b=free[1], c=free[2])
        return ap

    def mark_persistent(self):
        self.base = self.off

    def reset(self):
        self.off = self.base


class Ring:
    def __init__(self, B, n, free, dtype, parts=128):
        self.bufs = [(B.A.alloc(free, dtype, parts), B.S.res()) for _ in range(n)]
        self.i = 0

    def next(self):
        r = self.bufs[self.i % len(self.bufs)]
        self.i += 1
        return r[0], r[1].bump()


class Builder:
    def __init__(self, nc, st, dbg):
        self.nc = nc
        self.st = st
        self.S = Sched(nc)
        self.A = Arena(nc, st, 206 * 1024)
        self.banks = []
        for i in range(8):
            t = st.enter_context(nc.psum_tensor(f"bank{i}", [128, 512], F32))
            self.banks.append((t, self.S.res(f"bank{i}")))
        self.bi = 0
        self.nrr = 8
        self.dbg = dbg

    def bank(self):
        r = self.banks[self.bi % self.nrr]
        self.bi += 1
        return r[0][:], r[1].bump()

    def bank_fixed(self, idx):
        r = self.banks[idx]
        return r[0][:], r[1].bump()

    def dma(self, q, out, in_, reads=(), writes=(), **kw):
        return self.S.dma(q, out, in_, reads, writes, **kw)

    def mm(self, out, lhsT, rhs, start, stop, reads, writes):
        return self.S.op("pe", lambda e: e.matmul(out, lhsT=lhsT, rhs=rhs, start=start, stop=stop), reads, writes)

    def tr(self, out, in_, reads, writes):
        idn = self.ident
        return self.S.op("pe", lambda e: e.transpose(out, in_, idn), list(reads) + [self.r_ident], writes)

    def act(self, out, in_, func, reads, writes, scale=None, bias=None, accum=None):
        kw = {}
        if scale is not None:
            kw["scale"] = scale
        if bias is not None:
            kw["bias"] = bias
        if accum is not None:
            kw["accum_out"] = accum
        return self.S.op("act", lambda e: e.activation(out=out, in_=in_, func=func, **kw), reads, writes)

    def tt(self, eng, out, in0, in1, op, reads, writes):
        return self.S.op(eng, lambda e: e.tensor_tensor(out=out, in0=in0, in1=in1, op=op), reads, writes)

    def ts(self, eng, out, in0, s1, s2, op0, op1, reads, writes):
        if op1 is None:
            return self.S.op(eng, lambda e: e.tensor_scalar(out=out, in0=in0, scalar1=s1, scalar2=None, op0=op0), reads, writes)
        return self.S.op(eng, lambda e: e.tensor_scalar(out=out, in0=in0, scalar1=s1, scalar2=s2, op0=op0, op1=op1), reads, writes)

    def stt(self, eng, out, in0, scalar, in1, op0, op1, reads, writes):
        return self.S.op(eng, lambda e: e.scalar_tensor_tensor(out=out, in0=in0, scalar=scalar, in1=in1, op0=op0, op1=op1), reads, writes)

    def cp(self, eng, out, in_, reads, writes):
        if eng == "act":
            return self.S.op("act", lambda e: e.activation(out=out, in_=in_, func=AF.Copy), reads, writes)
        return self.S.op(eng, lambda e: e.tensor_copy(out=out, in_=in_), reads, writes)

    def recip(self, out, in_, reads, writes):
        return self.S.op("dve", lambda e: e.reciprocal(out=out, in_=in_), reads, writes)

    def memset(self, eng, out, val, writes):
        return self.S.op(eng, lambda e: e.memset(out, val), [], writes)

    def reduce_sum(self, out, in_, reads, writes):
        return self.S.op("dve", lambda e: e.tensor_reduce(out=out, in_=in_, axis=AX.X, op=ALU.add), reads, writes)

    def rstd(self, st, r_st, w, inv_n):
        self.ts("dve", st[:, w:2 * w], st[:, 0:w], inv_n, EPS, ALU.mult, ALU.add, [r_st], [r_st])
        self.act(st[:, w:2 * w], st[:, w:2 * w], AF.Sqrt, [r_st], [r_st])
        self.recip(st[:, 2 * w:3 * w], st[:, w:2 * w], [r_st], [r_st])
        return st[:, 2 * w:3 * w]

    def phase_begin(self):
        self.S.barrier()
        self.A.reset()

    def load_bc(self, q, src_1xn, n, name=None):
        t = self.A.alloc([n], F32)
        r = self.S.res(name)
        self.dma(q, t, src_1xn.partition_broadcast(128), [], [r])
        return t, r


def build_program(dbg_phases=None, dbg=False):
    nc = bass.Bass("TRN2", target_bir_lowering=False)
    I = {}

    def inp(name, shape, dt=F32):
        I[name] = nc.dram_tensor(name, list(shape), dt, kind="ExternalInput").ap()
        return I[name]

    inp("x", [S_LAT, D]); inp("ctx", [S_CTX, D]); inp("c", [1, D]); inp("c_ctx", [1, D])
    inp("ada_w", [2, D, 9 * D]); inp("ada_b", [2, 9 * D]); inp("norm_g", [2, 3, D])
    inp("ffn_w_gu", [2, 2, D, 2 * DFF]); inp("ffn_w_down", [2, 2, DFF, D])
    inp("ev_w_in", [D, 1792]); inp("ev_w_out", [D, D])
    inp("a_q_gain", [1, 64]); inp("a_k_gain", [1, 64]); inp("a_sink", [1, 8])
    inp("b_v_gain", [1, 512]); inp("b_ws", [8, 128, 128]); inp("b_bias", [8, 128])
    inp("od_w_in", [D, 2048]); inp("od_w_out", [D, D])
    inp("c_w_pool", [4, 128, 128]); inp("c_scale", [1, 512])
    inp("d_q_gain", [1, 64]); inp("d_k_gain", [1, 64])
    inp("k_ident", [128, 128]); inp("k_rope", [S_LAT, 2, 64]); inp("k_amask", [128, 2, 128])
    inp("k_band", [128, 4, 5, 128]); inp("k_dbias", [5, 128, 8, 5, 128])
    out = nc.dram_tensor("out", [S_LAT, D], F32, kind="ExternalOutput").ap()
    skind = "ExternalOutput" if dbg else "Internal"

    def scr(name, shape, dt):
        return nc.dram_tensor(name, list(shape), dt, kind=skind).ap()

    hA = scr("hA", [NT * 128, D], F32)
    mod_d = scr("mod_d", [2, 2, 9 * D], F32)
    qT_d = scr("qT_d", [8, 64, NT * 128], BF16)
    kT_d = scr("kT_d", [8, 64, NT * 128], BF16)
    v_d = scr("v_d", [NT * 128, 512], BF16)
    u_d = scr("u_d", [NT * 128, 512], F32)
    vv_d = scr("vv_d", [NT * 128, 512], BF16)
    xp_d = scr("xp_d", [S_LAT, 512], BF16)
    wgc_all = [nc.dram_tensor(f"wgc_d{f}", [NFF // 2, 128, 8 * 2 * 256], BF16, kind="Internal").ap() for f in range(4)]
    wdc_all = [nc.dram_tensor(f"wdc_d{f}", [128, NFF * D], BF16, kind="Internal").ap() for f in range(4)]
    adac_all = [nc.dram_tensor(f"adac_d{l_}", [18, 128, 8 * 512], BF16, kind="Internal").ap() for l_ in range(2)]

    st = contextlib.ExitStack()
    with st:
        B = Builder(nc, st, dbg)
        S, A = B.S, B.A
        B.ident = A.alloc([128], BF16)
        B.r_ident = S.res("ident")
        B.dma("pool", B.ident, I["k_ident"][:, :], [], [B.r_ident])
        B.identf = A.alloc([128], F32)
        B.dma("sp", B.identf, I["k_ident"][:, :], [], [B.r_ident])
        A.mark_persistent()

        bgq = []
        wres = {}

        def bg_add_ffn(f, li, which):
            wgu_ = I["ffn_w_gu"][li, which]
            for cp_ in range(NFF // 2):
                dst4 = wgc_all[f][cp_].rearrange("p (k g n) -> p k g n", k=8, g=2)
                for g_ in range(2):
                    r = S.res()
                    wres[("gu", f, cp_, g_)] = r
                    bgq.append((dst4[:, :, g_, :],
                                wgu_[:, g_ * DFF + cp_ * 256:g_ * DFF + (cp_ + 1) * 256].rearrange("(k p) n -> p k n", p=128), r))
            wd3 = wdc_all[f].rearrange("p (c n) -> p c n", c=NFF)
            for c4 in range(0, NFF, 2):
                r = S.res()
                wres[("wd", f, c4)] = r
                bgq.append((wd3[:, c4:c4 + 2, :],
                            I["ffn_w_down"][li, which, c4 * 128:(c4 + 2) * 128, :].rearrange("(c p) n -> p c n", p=128), r))

        def bg_add_ada(li, nb0=0, nb1=18):
            for nb in range(nb0, nb1):
                r = S.res()
                wres[("ada", li, nb)] = r
                bgq.append((adac_all[li][nb].rearrange("p (k n) -> p k n", k=8),
                            I["ada_w"][li, :, nb * 512:(nb + 1) * 512].rearrange("(k p) n -> p k n", p=128), r))

        def bg_need(pred):
            last = -1
            for i, (_, _, r) in enumerate(bgq):
                if pred(r):
                    last = i
            if last >= 0:
                bg_tick(last + 1)

        def bg_tick(n=1):
            for _ in range(n):
                if not bgq:
                    return
                dst, src, r = bgq.pop(0)
                o = B.dma("pool", dst, src, [], [r])
                o.bg = True

        wcache = {}

        def bg_add_w(name, src, ncols, split):
            t = nc.dram_tensor(f"wc_{name}", [128, 8 * ncols], BF16, kind="Internal").ap()
            rl = []
            w_ = ncols // split
            for i in range(split):
                r = S.res()
                rl.append(r)
                bgq.append((t.rearrange("p (k n) -> p k n", k=8)[:, :, i * w_:(i + 1) * w_],
                            src[:, i * w_:(i + 1) * w_].rearrange("(k p) n -> p k n", p=128), r))
            wcache[name] = (t, rl)

        def load_w_bf16(name, dst, r_dst, src, ncols):
            if name in wcache:
                t, rl = wcache[name]
                ids = {id(r) for r in rl}
                bg_need(lambda r: id(r) in ids)
                B.dma("sp", dst.rearrange("p k n -> p (k n)"), t, rl, [r_dst])
            else:
                step = 1024 if ncols > 1792 else ncols
                for c0 in range(0, ncols, step):
                    B.dma("pool", dst[:, :, c0:c0 + step], src[:, c0:c0 + step].rearrange("(k p) n -> p k n", p=128), [], [r_dst])

        bg_add_ada(0, 6, 18)
        bg_add_w("ev_w_in", I["ev_w_in"], 1792, 2)
        bg_add_w("ev_w_out", I["ev_w_out"], 1024, 1)
        bg_add_ffn(1, 0, 1)
        bg_add_ada(1)
        bg_add_ffn(2, 1, 0)
        bg_add_w("od_w_in", I["od_w_in"], 2048, 2)
        bg_add_w("od_w_out", I["od_w_out"], 1024, 1)
        bg_add_ffn(3, 1, 1)

        def src0(gt):
            if gt < NTL:
                return I["x"][gt * 128:(gt + 1) * 128, :]
            return I["ctx"][(gt - NTL) * 128:(gt - NTL + 1) * 128, :]

        def hsrc(gt):
            return hA[gt * 128:(gt + 1) * 128, :]

        def osrc(gt):
            return out[gt * 128:(gt + 1) * 128, :]

        phases = []

        def phase_mod(li, nb0=0, nb1=18):
            B.phase_begin()
            mine_ = {id(r) for k_, r in wres.items() if k_[0] == "ada" and k_[1] == li and nb0 <= k_[2] < nb1}
            bg_need(lambda r: id(r) in mine_)
            cc = A.alloc([2, 128], F32, parts=8)
            r_cc = S.res()
            B.dma("sp", cc[:, 0, :], I["c"][0, :].rearrange("(k p) -> k p", p=128), [], [r_cc])
            B.dma("sp", cc[:, 1, :], I["c_ctx"][0, :].rearrange("(k p) -> k p", p=128), [], [r_cc])
            ccb = A.alloc([2, 128], BF16, parts=8)
            r_ccb = S.res()
            B.act(ccb, cc, AF.Silu, [r_cc], [r_ccb])
            cs = A.alloc([8, 2], BF16)
            r_cs = S.res()
            bk, r_bk = B.bank()
            bkb = bk.bitcast(BF16)
            for j in range(2):
                B.S.op("pe", lambda e, j=j: e.transpose(bkb[:, j * 8:(j + 1) * 8], ccb[:, j, :], B.ident[0:8, 0:8]), [r_ccb, B.r_ident], [r_bk])
            B.cp("dve", cs, bkb[:, 0:16].rearrange("p (j k) -> p k j", j=2), [], [r_bk, r_cs])
            adab = A.alloc([9 * D], F32, parts=2)
            r_adab = S.res()
            B.dma("sp", adab, I["ada_b"][li:li + 1, :].partition_broadcast(2), [], [r_adab])
            msb = A.alloc([9 * D], F32, parts=2)
            r_msb = S.res()
            wr = Ring(B, 3, [8, 512], BF16)
            for nb in range(nb0, nb1):
                w, r_w = wr.next()
                if ("ada", li, nb) in wres:
                    B.dma("sp", w.rearrange("p k n -> p (k n)"), adac_all[li][nb], [wres[("ada", li, nb)]], [r_w])
                else:
                    B.dma("pool", w, I["ada_w"][li, :, nb * 512:(nb + 1) * 512].rearrange("(k p) n -> p k n", p=128), [], [r_w])
                bk, r_bk = B.bank()
                for k in range(8):
                    B.mm(bk[0:2, :], cs[:, k, :], w[:, k, :], k == 0, k == 7, [r_cs, r_w], [r_bk])
                B.tt("dve", msb[:, nb * 512:(nb + 1) * 512], bk[0:2, :], adab[:, nb * 512:(nb + 1) * 512], ALU.add,
                     [r_adab], [r_bk, r_msb])
            B.dma("sp", mod_d[li, :, nb0 * 512:nb1 * 512], msb[:, nb0 * 512:nb1 * 512], [r_msb], [])

        def load_mod_vecs(li, j_shift, j_scale, j_gate, gi, gate_mul, ntypes=2):
            outl = []
            gbc, r_g = (None, None)
            if j_scale is not None:
                gbc, r_g = B.load_bc("sp", I["norm_g"][li, gi:gi + 1, :], D)
            for ty in range(ntypes):
                r = S.res()
                sh = Gm = gt_ = None
                if j_shift is not None:
                    sh = A.alloc([D], F32)
                    B.dma("sp", sh, mod_d[li, ty:ty + 1, j_shift * D:(j_shift + 1) * D].partition_broadcast(128), [], [r])
                if j_scale is not None:
                    Gm = A.alloc([D], F32)
                    B.dma("sp", Gm, mod_d[li, ty:ty + 1, j_scale * D:(j_scale + 1) * D].partition_broadcast(128), [], [r])
                    B.stt("dve", Gm, Gm, 1.0, gbc, ALU.add, ALU.mult, [r_g, r], [r])
                if j_gate is not None:
                    gt_ = A.alloc([D], F32)
                    B.dma("sp", gt_, mod_d[li, ty:ty + 1, j_gate * D:(j_gate + 1) * D].partition_broadcast(128), [], [r])
                    if gate_mul != 1.0:
                        B.ts("dve", gt_, gt_, gate_mul, None, ALU.mult, None, [r], [r])
                outl.append((sh, Gm, gt_, r))
            return outl

        def norm_tile(hin, r_hin, mv, rings):
            sh, Gm, _, r_mv = mv
            sqj, r_sqj = rings["sqj"]
            st_, r_st = rings["st"].next()
            B.memset("dve", st_[:, 0:1], 0.0, [r_st])
            B.act(sqj, hin, AF.Square, [r_hin, r_st], [r_sqj, r_st], accum=st_[:, 0:1])
            rs = B.rstd(st_, r_st, 1, 1.0 / D)
            z1, r_z1 = rings["z1"].next()
            B.stt("dve", z1, hin, rs, Gm, ALU.mult, ALU.mult, [r_hin, r_st, r_mv], [r_z1])
            zt, r_zt = rings["ztok"].next()
            B.tt("pool", zt, z1, sh, ALU.add, [r_z1, r_mv], [r_zt])
            return zt, r_zt

        def norm_tile_g(hin, r_hin, mv, rings):
            sh, Gm, _, r_mv = mv
            sqj, r_sqj = rings["sqj"]
            st_, r_st = rings["st"].next()
            B.memset("dve", st_[:, 0:1], 0.0, [r_st])
            B.act(sqj, hin, AF.Square, [r_hin, r_st], [r_sqj, r_st], accum=st_[:, 0:1])
            yield
            B.ts("dve", st_[:, 1:2], st_[:, 0:1], 1.0 / D, EPS, ALU.mult, ALU.add, [r_st], [r_st])
            yield
            B.act(st_[:, 1:2], st_[:, 1:2], AF.Sqrt, [r_st], [r_st])
            yield
            B.recip(st_[:, 2:3], st_[:, 1:2], [r_st], [r_st])
            z1, r_z1 = rings["z1"].next()
            B.stt("dve", z1, hin, st_[:, 2:3], Gm, ALU.mult, ALU.mult, [r_hin, r_st, r_mv], [r_z1])
            yield
            zt, r_zt = rings["ztok"].next()
            B.tt("pool", zt, z1, sh, ALU.add, [r_z1, r_mv], [r_zt])
            return zt, r_zt

        def chains_g(items, rings):
            sts = []
            for it in items:
                w = it["nh"] * 64
                sq, r_sq = rings["sq"].next()
                B.act(sq[:, 0:w], it["src"][:, 0:w], AF.Square, [it["r_src"]], [r_sq])
                it["sq"], it["r_sq"] = sq, r_sq
            yield
            for it in items:
                nh = it["nh"]
                w = nh * 64
                st_, r_st = rings["st"].next()
                B.reduce_sum(st_[:, 0:nh], it["sq"][:, 0:w].rearrange("p (h d) -> p h d", h=nh), [it["r_sq"]], [r_st])
                B.ts("dve", st_[:, nh:2 * nh], st_[:, 0:nh], 1.0 / 64, EPS, ALU.mult, ALU.add, [r_st], [r_st])
                it["st"], it["r_st"] = st_, r_st
            yield
            for it in items:
                nh = it["nh"]
                B.act(it["st"][:, nh:2 * nh], it["st"][:, nh:2 * nh], AF.Sqrt, [it["r_st"]], [it["r_st"]])
            yield
            for it in items:
                nh = it["nh"]
                w = nh * 64
                st_ = it["st"]
                B.recip(st_[:, 2 * nh:3 * nh], st_[:, nh:2 * nh], [it["r_st"]], [it["r_st"]])
                x3 = it["src"][:, 0:w].rearrange("p (h d) -> p h d", h=nh)
                B.tt("dve", x3, x3, st_[:, 2 * nh:3 * nh].unsqueeze(2).to_broadcast([128, nh, 64]), ALU.mult,
                     [it["r_st"], it["r_src"]], [it["r_src"]])
            yield
            for it in items:
                nh = it["nh"]
                w = nh * 64
                if it.get("gain_full"):
                    B.tt("pool", it["dst"], it["src"][:, 0:w], it["gain"], ALU.mult, [it["r_gain"], it["r_src"]], [it["r_dst"]])
                elif it.get("dst") is not None:
                    B.tt("pool", it["dst"].rearrange("p (h d) -> p h d", h=nh), it["src"][:, 0:w].rearrange("p (h d) -> p h d", h=nh),
                         it["gain"].unsqueeze(1).to_broadcast([128, nh, 64]), ALU.mult, [it["r_gain"], it["r_src"]], [it["r_dst"]])
                else:
                    x3 = it["src"][:, 0:w].rearrange("p (h d) -> p h d", h=nh)
                    B.tt("pool", x3, x3, it["gain"].unsqueeze(1).to_broadcast([128, nh, 64]), ALU.mult,
                         [it["r_gain"], it["r_src"]], [it["r_src"]])

        def rope_g(items, ropt, r_ropt, rings):
            tmp = []
            for (qn, r_qn, nh, outb, r_out) in items:
                w = nh * 64
                a_, r_a = rings["ra"].next()
                q3 = qn[:, 0:w].rearrange("p (h d) -> p h d", h=nh)
                a3 = a_[:, 0:w].rearrange("p (h d) -> p h d", h=nh)
                B.tt("pool", a3, q3, ropt[:, 0, :].unsqueeze(1).to_broadcast([128, nh, 64]), ALU.mult, [r_qn, r_ropt], [r_a])
                b_, r_b = rings["rb"].next()
                q5 = qn[:, 0:w].rearrange("p (h a s d) -> p h a s d", h=nh, a=2, s=2)
                b5 = b_[:, 0:w].rearrange("p (h a s d) -> p h a s d", h=nh, a=2, s=2)
                s4 = ropt[:, 1, :].rearrange("p (a s d) -> p a s d", a=2, s=2)
                for ax in range(2):
                    for s_ in range(2):
                        B.tt("dve", b5[:, :, ax, s_, :], q5[:, :, ax, 1 - s_, :],
                             s4[:, ax, s_, :].unsqueeze(1).to_broadcast([128, nh, 16]), ALU.mult, [r_qn, r_ropt], [r_b])
                tmp.append((a_, r_a, b_, r_b))
            yield
            for (qn, r_qn, nh, outb, r_out), (a_, r_a, b_, r_b) in zip(items, tmp):
                w = nh * 64
                B.tt("dve", outb[:, 0:w], a_[:, 0:w], b_[:, 0:w], ALU.add, [r_a, r_b], [r_out])

        def transpose8(src, r_src, dst3, r_dst, eng="act"):
            bk, r_bk = B.bank()
            bkb = bk.bitcast(BF16)
            for k in range(8):
                B.tr(bkb[:, k * 128:(k + 1) * 128], src[:, k * 128:(k + 1) * 128], [r_src], [r_bk])
            B.cp(eng, dst3, bkb.rearrange("p (k t) -> p k t", k=8), [], [r_bk, r_dst])

        def phase_ffn(li, which, ntiles, srcf, dstf):
            B.phase_begin()
            f = li * 2 + which
            cached = ("wd", f, 0) in wres
            if cached:
                mine = {id(r) for k_, r in wres.items() if k_[0] in ("gu", "wd") and k_[1] == f}
                bg_need(lambda r: id(r) in mine)
            wgc_d = wgc_all[f]
            j0 = 0 if which == 0 else 6
            gi = 0 if which == 0 else 2
            mvs = load_mod_vecs(li, j0, j0 + 1, j0 + 2, gi, 0.5, ntypes=2 if ntiles > NTL else 1)
            wd = A.alloc([NFF, D], BF16)
            r_wd = S.res()

            def load_wd(c4):
                if cached:
                    B.dma("sp", wd[:, c4:c4 + 2, :], wdc_all[f].rearrange("p (c n) -> p c n", c=NFF)[:, c4:c4 + 2, :],
                          [wres[("wd", f, c4)]], [r_wd])
                else:
                    B.dma("pool", wd[:, c4:c4 + 2, :],
                          I["ffn_w_down"][li, which, c4 * 128:(c4 + 2) * 128, :].rearrange("(c p) n -> p c n", p=128), [], [r_wd])
            if ntiles == NT:
                groups = [list(range(0, 9)), list(range(9, 18)), list(range(18, 26)), list(range(26, 34))]
            else:
                groups = [list(range(g * 8, g * 8 + 8)) for g in range(4)]
            wc_res = [S.res() for _ in range(NFF // 2)]
            GM = max(len(g) for g in groups)
            zTs = [A.alloc([8, GM * 128], BF16) for _ in range(2)]
            zress = [[S.res() for _ in range(GM)] for _ in range(2)]
            actT = A.alloc([NFF, GM * 128], BF16)
            wgr = Ring(B, 2, [8, 2, 256], BF16)
            hinr = Ring(B, 2, [D], F32)
            rings = {"sqj": (A.alloc([D], BF16), S.res()), "st": Ring(B, 2, [3], F32),
                     "z1": Ring(B, 1, [D], F32), "ztok": Ring(B, 2, [D], BF16)}
            stmpr = Ring(B, 2, [512], F32)
            etmpr = Ring(B, 1, [D], F32)
            houtr = Ring(B, 1, [D], F32)
            wgu = I["ffn_w_gu"][li, which]

            def stage1_a(gidx, lt, gt):
                ty = 0 if gt < NTL else 1
                hin, r_hin = hinr.next()
                B.dma("sp", hin, srcf(gt), [], [r_hin])
                zt, r_zt = norm_tile(hin, r_hin, mvs[ty], rings)
                return (gidx, lt, zt, r_zt)

            def stage1_b(st1):
                gidx, lt, zt, r_zt = st1
                transpose8(zt, r_zt, zTs[gidx % 2][:, :, lt * 128:(lt + 1) * 128], zress[gidx % 2][lt])

            def stage1_tile(gidx, lt, gt):
                stage1_b(stage1_a(gidx, lt, gt))

            lag0 = None
            for lt, gt in enumerate(groups[0]):
                cur0 = stage1_a(0, lt, gt)
                if lag0 is not None:
                    stage1_b(lag0)
                lag0 = cur0
            stage1_b(lag0)
            for gidx, grp in enumerate(groups):
                zT = zTs[gidx % 2]
                zres = zress[gidx % 2]
                T = len(grp) * 128
                nblk = (T + 511) // 512
                bs = T // nblk
                blocks = [(i * bs, (i + 1) * bs if i < nblk - 1 else T) for i in range(nblk)]
                ares = [S.res() for _ in blocks]
                pending = list(enumerate(groups[gidx + 1])) if gidx + 1 < len(groups) else []
                npend = len(pending)
                nunits = NFF * nblk
                unit = 0
                emitted = 0
                lagged = None
                for cp_ in range(NFF // 2):
                    wg, r_wg = wgr.next()
                    if cached:
                        B.dma("sp", wg.rearrange("p k g n -> p (k g n)"), wgc_d[cp_],
                              [wres[("gu", f, cp_, 0)], wres[("gu", f, cp_, 1)]], [r_wg])
                    elif gidx == 0:
                        B.dma("pool", wg[:, :, 0, :], wgu[:, cp_ * 256:(cp_ + 1) * 256].rearrange("(k p) n -> p k n", p=128), [], [r_wg])
                        B.dma("pool", wg[:, :, 1, :], wgu[:, DFF + cp_ * 256:DFF + (cp_ + 1) * 256].rearrange("(k p) n -> p k n", p=128), [], [r_wg])
                        B.dma("pool", wgc_d[cp_], wg.rearrange("p k g n -> p (k g n)"), [r_wg], [wc_res[cp_]])
                    else:
                        B.dma("sp", wg.rearrange("p k g n -> p (k g n)"), wgc_d[cp_], [wc_res[cp_]], [r_wg])
                    if gidx == 0:
                        load_wd(cp_ * 2)
                    for ci in range(2):
                        c = cp_ * 2 + ci
                        for bi_, (a, b_) in enumerate(blocks):
                            n = b_ - a
                            zr = [zres[t] for t in range(a // 128, (b_ - 1) // 128 + 1)]
                            bg, r_bg = B.bank()
                            bu, r_bu = B.bank()
                            for k in range(8):
                                B.mm(bg[:, 0:n], wg[:, k, 0, ci * 128:(ci + 1) * 128], zT[:, k, a:b_], k == 0, k == 7, [r_wg] + zr, [r_bg])
                            for k in range(8):
                                B.mm(bu[:, 0:n], wg[:, k, 1, ci * 128:(ci + 1) * 128], zT[:, k, a:b_], k == 0, k == 7, [r_wg] + zr, [r_bu])
                            stp, r_stp = stmpr.next()
                            B.act(stp[:, 0:n], bg[:, 0:n], AF.Silu, [], [r_bg, r_stp])
                            B.tt("dve", actT[:, c, a:b_], stp[:, 0:n], bu[:, 0:n], ALU.mult, [r_stp], [r_bu, ares[bi_]])
                            unit += 1
                            if unit % 4 == 0 and (cached or gidx > 0):
                                bg_tick()
                            while emitted < npend and unit * npend >= (emitted + 1) * int(nunits * 0.7):
                                if lagged is not None:
                                    stage1_b(lagged)
                                lt2, gt2 = pending[emitted]
                                lagged = stage1_a(gidx + 1, lt2, gt2)
                                emitted += 1
                            if emitted == npend and lagged is not None and unit * npend >= (emitted + 1) * int(nunits * 0.7):
                                stage1_b(lagged)
                                lagged = None
                while emitted < npend:
                    if lagged is not None:
                        stage1_b(lagged)
                    lt2, gt2 = pending[emitted]
                    lagged = stage1_a(gidx + 1, lt2, gt2)
                    emitted += 1
                if lagged is not None:
                    stage1_b(lagged)
                    lagged = None
                for lt, gt in enumerate(grp):
                    ty = 0 if gt < NTL else 1
                    gate = mvs[ty][2]
                    r_mv = mvs[ty][3]
                    ar = [ares[i] for i, (a, b_) in enumerate(blocks) if a < (lt + 1) * 128 and b_ > lt * 128]
                    b0, r_b0 = B.bank()
                    b1, r_b1 = B.bank()
                    bb = [(b0, r_b0), (b1, r_b1)]
                    for c in range(NFF):
                        for hf in range(2):
                            B.mm(bb[hf][0], actT[:, c, lt * 128:(lt + 1) * 128], wd[:, c, hf * 512:(hf + 1) * 512],
                                 c == 0, c == NFF - 1, ar + [r_wd], [bb[hf][1]])
                    hin, r_hin = hinr.next()
                    B.dma("sp", hin, srcf(gt), [], [r_hin])
                    et, r_et = etmpr.next()
                    for hf in range(2):
                        B.tt("dve", et[:, hf * 512:(hf + 1) * 512], bb[hf][0], gate[:, hf * 512:(hf + 1) * 512], ALU.mult,
                             [r_mv], [bb[hf][1], r_et])
                    ho, r_ho = houtr.next()
                    B.tt("pool", ho, et, hin, ALU.add, [r_et, r_hin], [r_ho])
                    B.dma("pool", dstf(gt), ho, [r_ho], [])

        def head_norm(bank_ap, r_bank, nh, gain_bc, r_gain, rings, name):
            sq, r_sq = rings["sq"].next()
            w = nh * 64
            B.act(sq[:, 0:w], bank_ap, AF.Square, [], [r_bank, r_sq])
            st_, r_st = rings["st"].next()
            B.reduce_sum(st_[:, 0:nh], sq[:, 0:w].rearrange("p (h d) -> p h d", h=nh), [r_sq], [r_st])
            rs = B.rstd(st_, r_st, nh, 1.0 / 64)
            qn, r_qn = rings[name].next()
            qn3 = qn[:, 0:w].rearrange("p (h d) -> p h d", h=nh)
            B.tt("dve", qn3, bank_ap.rearrange("p (h d) -> p h d", h=nh), rs.unsqueeze(2).to_broadcast([128, nh, 64]), ALU.mult,
                 [r_st], [r_bank, r_qn])
            B.tt("pool", qn3, qn3, gain_bc.unsqueeze(1).to_broadcast([128, nh, 64]), ALU.mult, [r_gain, r_qn], [r_qn])
            return qn, r_qn

        def rope(qn, r_qn, nh, ropt, r_ropt, outb, r_out, rings):
            w = nh * 64
            a_, r_a = rings["ra"].next()
            b_, r_b = rings["rb"].next()
            q3 = qn[:, 0:w].rearrange("p (h d) -> p h d", h=nh)
            a3 = a_[:, 0:w].rearrange("p (h d) -> p h d", h=nh)
            B.tt("pool", a3, q3, ropt[:, 0, :].unsqueeze(1).to_broadcast([128, nh, 64]), ALU.mult, [r_qn, r_ropt], [r_a])
            q5 = qn[:, 0:w].rearrange("p (h a s d) -> p h a s d", h=nh, a=2, s=2)
            b5 = b_[:, 0:w].rearrange("p (h a s d) -> p h a s d", h=nh, a=2, s=2)
            s4 = ropt[:, 1, :].rearrange("p (a s d) -> p a s d", a=2, s=2)
            for ax in range(2):
                for s in range(2):
                    B.tt("dve", b5[:, :, ax, s, :], q5[:, :, ax, 1 - s, :],
                         s4[:, ax, s, :].unsqueeze(1).to_broadcast([128, nh, 16]), ALU.mult, [r_qn, r_ropt], [r_b])
            B.tt("dve", outb[:, 0:w], a_[:, 0:w], b_[:, 0:w], ALU.add, [r_a, r_b], [r_out])

        def head_transposes(src, r_src, nh, dst_dram, gt, rings):
            bk, r_bk = B.bank()
            bkb = bk.bitcast(BF16)
            for h in range(nh):
                B.tr(bkb[0:64, h * 128:(h + 1) * 128], src[:, h * 64:(h + 1) * 64], [r_src], [r_bk])
            ts_, r_ts = rings["hT"].next()
            B.cp("act", ts_[:, 0:nh, :], bkb[0:64, 0:nh * 128].rearrange("p (h t) -> p h t", h=nh), [], [r_bk, r_ts])
            B.dma("sp", dst_dram.rearrange("h d t -> d h t")[:, 0:nh, gt * 128:(gt + 1) * 128], ts_[:, 0:nh, :], [r_ts], [])

        def phase_even_prep(li):
            B.phase_begin()
            mvs = load_mod_vecs(li, 3, 4, None, 1, 1.0)
            win = A.alloc([8, 1792], BF16)
            r_win = S.res()
            load_w_bf16("ev_w_in", win, r_win, I["ev_w_in"], 1792)
            qg_bc, r_qg = B.load_bc("sp", I["a_q_gain"][0:1, :], 64)
            kg_bc, r_kg = B.load_bc("sp", I["a_k_gain"][0:1, :], 64)
            vg_bc, r_vg = B.load_bc("sp", I["b_v_gain"][0:1, :], 512)
            hinr = Ring(B, 6, [D], F32)
            rings = {"sqj": (A.alloc([D], BF16), S.res()), "st": Ring(B, 24, [30], F32),
                     "z1": Ring(B, 2, [D], F32), "ztok": Ring(B, 3, [D], BF16),
                     "sq": Ring(B, 6, [512], F32),
                     "ra": Ring(B, 3, [512], F32), "rb": Ring(B, 3, [512], F32), "hT": Ring(B, 3, [8, 128], BF16, parts=64)}
            zTr = Ring(B, 3, [8, 128], BF16)
            ropr = Ring(B, 14, [2, 64], F32)
            qrr = Ring(B, 6, [512], F32)
            krr = Ring(B, 6, [128], F32)
            gvr = Ring(B, 6, [512], F32)
            qbr = Ring(B, 9, [512], BF16)
            kbr = Ring(B, 9, [128], BF16)
            vbr = Ring(B, 4, [128], BF16)
            ur = Ring(B, 3, [512], F32)
            vvr = Ring(B, 8, [512], BF16)
            nsl = [(0, 512), (512, 768), (768, 1280), (1280, 1792)]

            def tile_gen(gt):
                bg_tick()
                ty = 0 if gt < NTL else 1
                hin, r_hin = hinr.next()
                B.dma("sp", hin, hsrc(gt), [], [r_hin])
                if ty == 0:
                    rt, r_rt = ropr.next()
                    B.dma("sp", rt, I["k_rope"][gt * 128:(gt + 1) * 128, :, :], [], [r_rt])
                yield
                zt, r_zt = yield from norm_tile_g(hin, r_hin, mvs[ty], rings)
                yield
                zT, r_zT = zTr.next()
                transpose8(zt, r_zt, zT, r_zT)
                yield
                bks = [B.bank() for _ in range(4)]
                for k in range(8):
                    for i, (n0, n1) in enumerate(nsl):
                        B.mm(bks[i][0][:, 0:n1 - n0], zT[:, k, :], win[:, k, n0:n1], k == 0, k == 7, [r_zT, r_win], [bks[i][1]])
                (bq, r_bq), (bkv, r_bkv), (bbu, r_bbu), (bbv, r_bbv) = bks
                yield
                qr, r_qr = qrr.next()
                B.cp("act", qr, bq, [], [r_bq, r_qr])
                kr, r_kr = krr.next()
                B.cp("act", kr, bkv[:, 0:128], [], [r_bkv, r_kr])
                vb, r_vb = vbr.next()
                B.cp("act", vb, bkv[:, 128:256], [], [r_bkv, r_vb])
                u_, r_u = ur.next()
                B.act(u_, bbu, AF.Gelu_apprx_tanh, [], [r_bbu, r_u])
                gv, r_gv = gvr.next()
                B.act(gv, bbv, AF.Gelu_apprx_tanh, [], [r_bbv, r_gv])
                yield
                B.dma("sp", v_d[gt * 128:(gt + 1) * 128, 0:128], vb, [r_vb], [])
                B.dma("sp", u_d[gt * 128:(gt + 1) * 128, :], u_, [r_u], [])
                vv, r_vv = vvr.next()
                items = [dict(src=qr, r_src=r_qr, nh=8, gain=qg_bc, r_gain=r_qg),
                         dict(src=kr, r_src=r_kr, nh=2, gain=kg_bc, r_gain=r_kg),
                         dict(src=gv, r_src=r_gv, nh=8, gain=vg_bc, r_gain=r_vg, gain_full=True, dst=vv, r_dst=r_vv)]
                qb, r_qb = qbr.next()
                kb, r_kb = kbr.next()
                if ty == 1:
                    items[0]["dst"], items[0]["r_dst"] = qb, r_qb
                    items[1]["dst"], items[1]["r_dst"] = kb, r_kb
                yield from chains_g(items, rings)
                yield
                B.dma("sp", vv_d[gt * 128:(gt + 1) * 128, :], vv, [r_vv], [])
                if ty == 0:
                    yield from rope_g([(qr, r_qr, 8, qb, r_qb), (kr, r_kr, 2, kb, r_kb)], rt, r_rt, rings)
                    yield
                head_transposes(qb, r_qb, 8, qT_d, gt, rings)
                head_transposes(kb, r_kb, 2, kT_d, gt, rings)

            pipeline(tile_gen, range(NT))

        def outproj_residual(mix, r_mix, wout, r_wout, gate, r_gate, gt, rings):
            hin, r_hin = rings["hin"].next()
            B.dma("sp", hin, hsrc(gt), [], [r_hin])
            mT, r_mT = rings["mixT"].next()
            transpose8(mix, r_mix, mT, r_mT)
            yield
            b0, r_b0 = B.bank()
            b1, r_b1 = B.bank()
            bb = [(b0, r_b0), (b1, r_b1)]
            for k in range(8):
                for hf in range(2):
                    B.mm(bb[hf][0], mT[:, k, :], wout[:, k, hf * 512:(hf + 1) * 512], k == 0, k == 7, [r_mT, r_wout], [bb[hf][1]])
            et, r_et = rings["et"].next()
            for hf in range(2):
                B.tt("dve", et[:, hf * 512:(hf + 1) * 512], bb[hf][0], gate[:, hf * 512:(hf + 1) * 512], ALU.mult,
                     [r_gate], [bb[hf][1], r_et])
            yield
            ho, r_ho = rings["hout"].next()
            B.tt("pool", ho, et, hin, ALU.add, [r_et, r_hin], [r_ho])
            yield
            B.dma("sp", hsrc(gt), ho, [r_ho], [])

        def load_V(dst, r_dst, kt0, nw, nk):
            for w in range(nw):
                B.dma("sp", dst[:, w, :, 0:64],
                      v_d[(kt0 + w) * 128:(kt0 + w + 1) * 128, 0:nk * 64].rearrange("p (k d) -> p k d", k=nk), [], [r_dst])

        def phase_even_attn(li):
            B.phase_begin()
            mvs = load_mod_vecs(li, None, None, 5, 1, 1.0)
            wout = A.alloc([8, D], BF16)
            r_wout = S.res()
            load_w_bf16("ev_w_out", wout, r_wout, I["ev_w_out"], 1024)
            wsn = A.alloc([8, 128], BF16)
            r_wsn = S.res()
            B.dma("pool", wsn, I["b_ws"].rearrange("g i j -> i g j"), [], [r_wsn])
            wsT = A.alloc([8, 128], BF16)
            r_wsT = S.res()
            bk, r_bk = B.bank()
            bkb = bk.bitcast(BF16)
            for g in range(8):
                B.tr(bkb[:, g * 128:(g + 1) * 128], wsn[:, g, :], [r_wsn], [r_bk])
            B.cp("act", wsT, bkb.rearrange("p (g t) -> p g t", g=8), [], [r_bk, r_wsT])
            bias_sb = A.alloc([128], F32, parts=8)
            r_bsb = S.res()
            B.dma("sp", bias_sb, I["b_bias"][:, :], [], [r_bsb])
            biasT = A.alloc([8], F32)
            r_biasT = S.res()
            bk, r_bk = B.bank()
            B.mm(bk[:, 0:8], bias_sb, B.identf[0:8, 0:8], True, True, [r_bsb, B.r_ident], [r_bk])
            B.cp("dve", biasT, bk[:, 0:8], [], [r_bk, r_biasT])
            esink, r_es = B.load_bc("sp", I["a_sink"][0:1, :], 8)
            B.act(esink, esink, AF.Exp, [r_es], [r_es])
            amask = A.alloc([2, 128], BF16)
            r_am = S.res()
            B.dma("pool", amask, I["k_amask"][:, :, :], [], [r_am])
            kTc = A.alloc([2, 256], BF16, parts=64)
            r_kTc = S.res()
            B.dma("sp", kTc, kT_d.rearrange("h d t -> d h t")[:, 0:2, NTL * 128:NT * 128], [], [r_kTc])
            Vc = A.alloc([2, 2, 65], BF16)
            r_Vc = S.res()
            B.memset("dve", Vc[:, :, :, 64:65], 1.0, [r_Vc])
            load_V(Vc, r_Vc, NTL, 2, 2)
            kTwr = Ring(B, 4, [2, 384], BF16, parts=64)
            Vwr = Ring(B, 5, [3, 2, 65], BF16)
            for vb_, r_ in Vwr.bufs:
                B.memset("dve", vb_[:, :, :, 64:65], 1.0, [r_])
            qTr = Ring(B, 4, [8, 128], BF16, parts=64)
            pTr = Ring(B, 16, [512], BF16)
            etr = Ring(B, 2, [512], BF16)
            mixr = Ring(B, 4, [D], BF16)
            str_ = Ring(B, 4, [8], F32)
            vvr = Ring(B, 5, [512], BF16)
            ur = Ring(B, 5, [512], F32)
            btr = Ring(B, 2, [512], F32)
            rings = {"mixT": Ring(B, 2, [8, 128], BF16), "hin": Ring(B, 3, [D], F32), "et": Ring(B, 2, [D], F32),
                     "hout": Ring(B, 2, [D], F32)}

            def tile_gen(n):
                bg_tick()
                ty = 0 if n < NTL else 1
                qT, r_qT = qTr.next()
                B.dma("sp", qT, qT_d.rearrange("h d t -> d h t")[:, :, n * 128:(n + 1) * 128], [], [r_qT])
                keys = []
                if ty == 0:
                    kt0 = min(max(n - 1, 0), NTL - 3)
                    kTw, r_kTw = kTwr.next()
                    B.dma("sp", kTw, kT_d.rearrange("h d t -> d h t")[:, 0:2, kt0 * 128:(kt0 + 3) * 128], [], [r_kTw])
                    Vw, r_Vw = Vwr.next()
                    load_V(Vw, r_Vw, kt0, 3, 2)
                    for kt, mk in ((n - 1, 0), (n, None), (n + 1, 1)):
                        if 0 <= kt < NTL:
                            s_ = kt - kt0
                            keys.append((kTw, s_, Vw, s_, mk, [r_kTw], [r_Vw]))
                for s_ in range(2):
                    keys.append((kTc, s_, Vc, s_, None, [r_kTc], [r_Vc]))
                vv, r_vv = vvr.next()
                B.dma("sp", vv, vv_d[n * 128:(n + 1) * 128, :], [], [r_vv])
                u_, r_u = ur.next()
                B.dma("sp", u_, u_d[n * 128:(n + 1) * 128, :], [], [r_u])
                yield

                def qk(kv):
                    pts = []
                    for (kTa, ks, Va, vs, mk, rk, rv) in keys:
                        bk, r_bk = B.bank()
                        B.mm(bk.rearrange("p (h q) -> p h q", h=4), kTa[:, kv, ks * 128:(ks + 1) * 128], qT[:, 4 * kv:4 * kv + 4, :],
                             True, True, rk + [r_qT], [r_bk])
                        pT, r_pT = pTr.next()
                        if mk is None:
                            B.act(pT, bk, AF.Exp, [], [r_bk, r_pT], scale=0.125)
                        else:
                            et, r_et = etr.next()
                            B.act(et, bk, AF.Exp, [], [r_bk, r_et], scale=0.125)
                            B.tt("dve", pT.rearrange("p (h q) -> p h q", h=4), et.rearrange("p (h q) -> p h q", h=4),
                                 amask[:, mk, :].unsqueeze(1).to_broadcast([128, 4, 128]), ALU.mult, [r_et, r_am], [r_pT])
                        pts.append((pT, r_pT, Va, vs, rv))
                    return pts

                def pv(pkv, pts, mix, r_mix):
                    ob, r_ob = B.bank()
                    for hh in range(4):
                        for ei, (pT, r_pT, Va, vs, rv) in enumerate(pts):
                            B.mm(ob[:, hh * 65:(hh + 1) * 65], pT[:, hh * 128:(hh + 1) * 128], Va[:, vs, pkv, :],
                                 ei == 0, ei == len(pts) - 1, [r_pT] + rv, [r_ob])
                    ob3 = ob[:, 0:260].rearrange("p (h e) -> p h e", h=4)
                    sd, r_sd = str_.next()
                    B.tt("dve", sd[:, 0:4], ob3[:, :, 64], esink[:, 4 * pkv:4 * pkv + 4], ALU.add, [r_es], [r_ob, r_sd])
                    B.recip(sd[:, 4:8], sd[:, 0:4], [r_sd], [r_sd])
                    B.tt("dve", mix[:, pkv * 256:(pkv + 1) * 256].rearrange("p (h d) -> p h d", h=4), ob3[:, :, 0:64],
                         sd[:, 4:8].unsqueeze(2).to_broadcast([128, 4, 64]), ALU.mult, [r_sd], [r_ob, r_mix])

                pts0 = qk(0)
                yield
                pts1 = qk(1)
                mix, r_mix = mixr.next()
                pv(0, pts0, mix, r_mix)
                yield
                pv(1, pts1, mix, r_mix)
                bk, r_bk = B.bank()
                for g in range(8):
                    B.mm(bk[:, g * 64:(g + 1) * 64], wsT[:, g, :], vv[:, g * 64:(g + 1) * 64], True, True, [r_wsT, r_vv], [r_bk])
                bt, r_bt = btr.next()
                B.tt("dve", bt.rearrange("p (g d) -> p g d", g=8), bk.rearrange("p (g d) -> p g d", g=8),
                     biasT.unsqueeze(2).to_broadcast([128, 8, 64]), ALU.add, [r_biasT], [r_bk, r_bt])
                B.tt("pool", mix[:, 512:1024], bt, u_, ALU.mult, [r_bt, r_u], [r_mix])
                yield
                yield from outproj_residual(mix, r_mix, wout, r_wout, mvs[ty][2], mvs[ty][3], n, rings)

            pipeline(tile_gen, range(NT))

        def phase_odd_prep(li):
            B.phase_begin()
            mvs = load_mod_vecs(li, 3, 4, None, 1, 1.0)
            win = A.alloc([8, 2048], BF16)
            r_win = S.res()
            load_w_bf16("od_w_in", win, r_win, I["od_w_in"], 2048)
            qg_bc, r_qg = B.load_bc("sp", I["d_q_gain"][0:1, :], 64)
            kg_bc, r_kg = B.load_bc("sp", I["d_k_gain"][0:1, :], 64)
            hinr = Ring(B, 6, [D], F32)
            rings = {"sqj": (A.alloc([D], BF16), S.res()), "st": Ring(B, 24, [30], F32),
                     "z1": Ring(B, 2, [D], F32), "ztok": Ring(B, 3, [D], BF16),
                     "sq": Ring(B, 6, [512], F32),
                     "hT": Ring(B, 3, [8, 128], BF16, parts=64)}
            zTr = Ring(B, 3, [8, 128], BF16)
            qrr = Ring(B, 7, [512], F32)
            krr = Ring(B, 7, [512], F32)
            qbr = Ring(B, 8, [512], BF16)
            kbr = Ring(B, 8, [512], BF16)
            vbr = Ring(B, 4, [512], BF16)
            xbr = Ring(B, 4, [512], BF16)

            def tile_gen(gt):
                bg_tick()
                ty = 0 if gt < NTL else 1
                hin, r_hin = hinr.next()
                B.dma("sp", hin, hsrc(gt), [], [r_hin])
                yield
                zt, r_zt = yield from norm_tile_g(hin, r_hin, mvs[ty], rings)
                yield
                zT, r_zT = zTr.next()
                transpose8(zt, r_zt, zT, r_zT)
                yield
                nbs = [0, 1, 2, 3] if ty == 0 else [2, 3]
                bks = {i: B.bank() for i in nbs}
                for k in range(8):
                    for i in nbs:
                        B.mm(bks[i][0], zT[:, k, :], win[:, k, i * 512:(i + 1) * 512], k == 0, k == 7, [r_zT, r_win], [bks[i][1]])
                yield
                items = []
                qb = r_qb = None
                if ty == 0:
                    xb, r_xb = xbr.next()
                    B.cp("act", xb, bks[0][0], [], [bks[0][1], r_xb])
                    qr, r_qr = qrr.next()
                    B.cp("act", qr, bks[1][0], [], [bks[1][1], r_qr])
                    qb, r_qb = qbr.next()
                    items.append(dict(src=qr, r_src=r_qr, nh=8, gain=qg_bc, r_gain=r_qg, dst=qb, r_dst=r_qb))
                kr, r_kr = krr.next()
                B.cp("act", kr, bks[2][0], [], [bks[2][1], r_kr])
                kb, r_kb = kbr.next()
                items.append(dict(src=kr, r_src=r_kr, nh=8, gain=kg_bc, r_gain=r_kg, dst=kb, r_dst=r_kb))
                vb, r_vb = vbr.next()
                B.cp("act", vb, bks[3][0], [], [bks[3][1], r_vb])
                yield
                if ty == 0:
                    B.dma("sp", xp_d[gt * 128:(gt + 1) * 128, :], xb, [r_xb], [])
                B.dma("sp", v_d[gt * 128:(gt + 1) * 128, :], vb, [r_vb], [])
                yield from chains_g(items, rings)
                yield
                if ty == 0:
                    head_transposes(qb, r_qb, 8, qT_d, gt, rings)
                head_transposes(kb, r_kb, 8, kT_d, gt, rings)

            pipeline(tile_gen, range(NT))

        def phase_odd_attn(li):
            B.phase_begin()
            mvs = load_mod_vecs(li, None, None, 5, 1, 1.0, ntypes=1)
            wout = A.alloc([8, D], BF16)
            r_wout = S.res()
            load_w_bf16("od_w_out", wout, r_wout, I["od_w_out"], 1024)
            wpool = A.alloc([4, 128], BF16)
            r_wpool = S.res()
            B.dma("pool", wpool, I["c_w_pool"].rearrange("g c d -> c g d"), [], [r_wpool])
            csc, r_csc = B.load_bc("sp", I["c_scale"][0:1, :], 512)
            band = A.alloc([4, 5, 128], BF16)
            r_band = S.res()
            B.dma("pool", band, I["k_band"][:, :, :, :], [], [r_band], max_dma_last_dim=2048)
            kTp = kT_d.rearrange("(g e) d t -> (e d) g t", e=2)
            qTp = qT_d.rearrange("(g e) d t -> (e d) g t", e=2)
            kTc = A.alloc([4, 256], BF16)
            r_kTc = S.res()
            B.dma("sp", kTc, kTp[:, :, NTL * 128:NT * 128], [], [r_kTc])
            Vc = A.alloc([2, 8, 65], BF16)
            r_Vc = S.res()
            B.memset("dve", Vc[:, :, :, 64:65], 1.0, [r_Vc])
            load_V(Vc, r_Vc, NTL, 2, 8)
            biasr = Ring(B, 1, [8, 7, 128], F32)
            for bb_, r_ in biasr.bufs:
                B.memset("pool", bb_[:, :, 5:7, :], 0.0, [r_])
            kTwr = Ring(B, 4, [4, 640], BF16)
            Vwr = Ring(B, 4, [5, 8, 65], BF16)
            for vb_, r_ in Vwr.bufs:
                B.memset("dve", vb_[:, :, :, 64:65], 1.0, [r_])
            qTr = Ring(B, 4, [4, 128], BF16)
            xpr = Ring(B, 3, [3, 512], BF16)
            ppr = Ring(B, 2, [4, 128], BF16)
            tAr = Ring(B, 3, [512], F32)
            tBr = Ring(B, 3, [384], F32)
            pAr = Ring(B, 14, [512], BF16)
            pBr = Ring(B, 14, [384], BF16)
            mixr = Ring(B, 8, [D], BF16)
            str_ = Ring(B, 4, [8], F32)
            rings = {"mixT": Ring(B, 2, [8, 128], BF16), "hin": Ring(B, 3, [D], F32), "et": Ring(B, 2, [D], F32),
                     "hout": Ring(B, 2, [D], F32)}
            state = {"case": None, "bias": None, "r_bias": None}
            B.nrr = 6

            def case_of(n):
                return 0 if n == 0 else 1 if n == 1 else 3 if n == NTL - 2 else 4 if n == NTL - 1 else 2

            def tile_gen(n):
                bg_tick()
                case = case_of(n)
                if case != state["case"]:
                    bias_, r_bias_ = biasr.next()
                    B.dma("sp", bias_[:, :, 0:5, :], I["k_dbias"][case], [], [r_bias_])
                    state["case"] = case
                    state["bias"] = bias_
                    state["r_bias"] = r_bias_
                bias = state["bias"]
                r_bias = state["r_bias"]
                kt0 = min(max(n - 2, 0), NTL - 5)
                kTw, r_kTw = kTwr.next()
                B.dma("sp", kTw, kTp[:, :, kt0 * 128:(kt0 + 5) * 128], [], [r_kTw])
                Vw, r_Vw = Vwr.next()
                load_V(Vw, r_Vw, kt0, 5, 8)
                qT, r_qT = qTr.next()
                B.dma("sp", qT, qTp[:, :, n * 128:(n + 1) * 128], [], [r_qT])
                xw, r_xw = xpr.next()
                jts = [j for j in (n - 1, n, n + 1) if 0 <= j < NTL]
                j0 = jts[0]
                B.dma("sp", xw[:, 0:len(jts), :], xp_d[j0 * 128:(j0 + len(jts)) * 128, :].rearrange("(w p) f -> p w f", p=128), [], [r_xw])
                yield
                mix, r_mix = mixr.next()
                bk, r_bk = B.bank()
                for g in range(4):
                    for ji, j in enumerate(jts):
                        if j == n - 1:
                            typ = 0
                        elif j == n + 1:
                            typ = 2
                        else:
                            typ = 3 if n == 0 else (4 if n == NTL - 1 else 1)
                        B.mm(bk[:, g * 128:(g + 1) * 128], xw[:, ji, g * 128:(g + 1) * 128], band[:, g, typ, :],
                             ji == 0, ji == len(jts) - 1, [r_xw, r_band], [r_bk])
                pp, r_pp = ppr.next()
                B.cp("act", pp, bk.rearrange("p (g t) -> p g t", g=4), [], [r_bk, r_pp])
                bk2, r_bk2 = B.bank()
                for g in range(4):
                    B.mm(bk2[:, g * 128:(g + 1) * 128], pp[:, g, :], wpool[:, g, :], True, True, [r_pp, r_wpool], [r_bk2])
                B.tt("dve", mix[:, 0:512], bk2, csc, ALU.mult, [r_csc], [r_bk2, r_mix])
                yield

                def scores(hq):
                    res_ = []
                    for hi in range(4):
                        h = hq * 4 + hi
                        bA, r_bA = B.bank()
                        bB, r_bB = B.bank()
                        pl = slice((h % 2) * 64, (h % 2) * 64 + 64)
                        g_ = h // 2
                        for s_ in range(4):
                            B.mm(bA[:, s_ * 128:(s_ + 1) * 128], kTw[pl, g_, s_ * 128:(s_ + 1) * 128], qT[pl, g_, :], True, True, [r_kTw, r_qT], [r_bA])
                        B.mm(bB[:, 0:128], kTw[pl, g_, 512:640], qT[pl, g_, :], True, True, [r_kTw, r_qT], [r_bB])
                        for s_ in range(2):
                            B.mm(bB[:, (1 + s_) * 128:(2 + s_) * 128], kTc[pl, g_, s_ * 128:(s_ + 1) * 128], qT[pl, g_, :], True, True, [r_kTc, r_qT], [r_bB])
                        tA, r_tA = tAr.next()
                        tB, r_tB = tBr.next()
                        B.stt("dve", tA, bA, 0.125, bias[:, h, 0:4, :].rearrange("p s q -> p (s q)"), ALU.mult, ALU.add, [r_bias], [r_bA, r_tA])
                        B.stt("dve", tB, bB[:, 0:384], 0.125, bias[:, h, 4:7, :].rearrange("p s q -> p (s q)"), ALU.mult, ALU.add, [r_bias], [r_bB, r_tB])
                        pA, r_pA = pAr.next()
                        pB, r_pB = pBr.next()
                        B.act(pA, tA, AF.Exp, [r_tA], [r_pA])
                        B.act(pB, tB, AF.Exp, [r_tB], [r_pB])
                        res_.append((h, pA, r_pA, pB, r_pB))
                    return res_

                def pvs(hq, res_):
                    ob, r_ob = B.bank_fixed(6 + hq)
                    for (ph, pA, r_pA, pB, r_pB) in res_:
                        osl = ob[:, (ph % 4) * 65:(ph % 4 + 1) * 65]
                        for s_ in range(7):
                            if s_ < 4:
                                lhs = pA[:, s_ * 128:(s_ + 1) * 128]
                                rp = r_pA
                            else:
                                lhs = pB[:, (s_ - 4) * 128:(s_ - 3) * 128]
                                rp = r_pB
                            if s_ < 5:
                                rhs = Vw[:, s_, ph, :]
                                rv = r_Vw
                            else:
                                rhs = Vc[:, s_ - 5, ph, :]
                                rv = r_Vc
                            B.mm(osl, lhs, rhs, s_ == 0, s_ == 6, [rp, rv], [r_ob])
                    ob3 = ob[:, 0:260].rearrange("p (h e) -> p h e", h=4)
                    sd, r_sd = str_.next()
                    B.recip(sd[:, 0:4], ob3[:, :, 64], [], [r_ob, r_sd])
                    B.tt("dve", mix[:, 512 + hq * 256:512 + (hq + 1) * 256].rearrange("p (h d) -> p h d", h=4), ob3[:, :, 0:64],
                         sd[:, 0:4].unsqueeze(2).to_broadcast([128, 4, 64]), ALU.mult, [r_sd], [r_ob, r_mix])

                r0 = scores(0)
                yield
                r1 = scores(1)
                pvs(0, r0)
                yield
                pvs(1, r1)
                yield
                yield from outproj_residual(mix, r_mix, wout, r_wout, mvs[0][2], mvs[0][3], n, rings)

            pipeline(tile_gen, range(NTL), drain_before=lambda n: case_of(n) != state["case"])
            B.nrr = 8

        plist = [
            lambda: phase_mod(0, 0, 6),
            lambda: phase_ffn(0, 0, NT, src0, hsrc),
            lambda: phase_mod(0, 6, 18),
            lambda: phase_even_prep(0),
            lambda: phase_even_attn(0),
            lambda: phase_ffn(0, 1, NT, hsrc, hsrc),
            lambda: phase_mod(1),
            lambda: phase_ffn(1, 0, NT, hsrc, hsrc),
            lambda: phase_odd_prep(1),
            lambda: phase_odd_attn(1),
            lambda: phase_ffn(1, 1, NTL, hsrc, osrc),
        ]
        if dbg_phases is not None:
            plist = plist[:dbg_phases]
        for p in plist:
            p()
        S.emit()
        build_program.stats = S.stats
    return nc


def _rope_table():
    t = np.arange(S_LAT)
    row = (t // 64).astype(np.float32)
    col = (t % 64).astype(np.float32)
    m = 16
    inv = (1.0 / (10000.0 ** (np.arange(m, dtype=np.float32) / m))).astype(np.float32)
    ar = row[:, None] * inv[None, :]
    ac = col[:, None] * inv[None, :]
    cos = np.concatenate([np.cos(ar), np.cos(ar), np.cos(ac), np.cos(ac)], axis=1)
    sin = np.concatenate([-np.sin(ar), np.sin(ar), -np.sin(ac), np.sin(ac)], axis=1)
    return np.stack([cos, sin], axis=1).astype(np.float32)


def _amask():
    pj = np.arange(128)[:, None]
    pi = np.arange(128)[None, :]
    prev = (pj >= pi).astype(np.float32)
    nxt = (pj <= pi).astype(np.float32)
    return np.stack([prev, nxt], axis=1)


def _band():
    out = np.zeros((128, 4, 5, 128), np.float32)
    for gi, w in enumerate((2, 4, 8, 16)):
        def mat(n, jn):
            tg = n * 128 + np.arange(128)
            lo = np.clip(tg - w // 2, 0, S_LAT)
            hi = np.clip(tg + w - w // 2, 0, S_LAT)
            cnt = (hi - lo).astype(np.float32)
            jg = jn * 128 + np.arange(128)
            m = ((jg[:, None] >= lo[None, :]) & (jg[:, None] < hi[None, :])).astype(np.float32) / cnt[None, :]
            m = m - (jg[:, None] == tg[None, :]).astype(np.float32)
            return m
        out[:, gi, 0] = mat(5, 4)
        out[:, gi, 1] = mat(5, 5)
        out[:, gi, 2] = mat(5, 6)
        out[:, gi, 3] = mat(0, 0)
        out[:, gi, 4] = mat(NTL - 1, NTL - 1)
    return out


def _dbias(rpb):
    out = np.full((5, 128, 8, 5, 128), NEGB, np.float32)
    for case, n in enumerate((0, 1, 5, NTL - 2, NTL - 1)):
        kt0 = min(max(n - 2, 0), NTL - 5)
        i = np.arange(128)
        r = 2 * n + i // 64
        c = i % 64
        r0 = np.clip(r - 4, 0, 56)
        q0 = np.clip(c - 8, 0, 48)
        for s in range(5):
            kt = kt0 + s
            j = np.arange(128)
            kr = 2 * kt + j // 64
            kc = j % 64
            valid = ((kr[:, None] >= r0[None, :]) & (kr[:, None] < r0[None, :] + 8) &
                     (kc[:, None] >= q0[None, :]) & (kc[:, None] < q0[None, :] + 16))
            ri = np.clip(kr[:, None] - r[None, :] + 7, 0, 14)
            ci = np.clip(kc[:, None] - c[None, :] + 15, 0, 30)
            g = rpb[:, ri, ci]
            g = np.where(valid[None], g, np.float32(NEGB))
            out[case, :, :, s, :] = np.transpose(g, (1, 0, 2))
    return out


_CACHE = {}


def kernel(x, c, ctx, c_ctx, ada_w, ada_b, norm_g, ffn_w_gu, ffn_w_down,
           ev_w_in, ev_w_out, a_q_gain, a_k_gain, a_sink, b_v_gain, b_ws, b_bias,
           od_w_in, od_w_out, c_w_pool, c_scale, d_q_gain, d_k_gain, d_rpb, _dbg_phases=None, _dbg=False):
    f = lambda a: np.ascontiguousarray(np.asarray(a, dtype=np.float32))
    key = (_dbg_phases, _dbg)
    if key not in _CACHE:
        _CACHE[key] = build_program(_dbg_phases, _dbg)
    nc = _CACHE[key]
    shared = {
        "c_ctx": f(c_ctx).reshape(1, D), "ada_w": f(ada_w), "ada_b": f(ada_b), "norm_g": f(norm_g),
        "ffn_w_gu": f(ffn_w_gu), "ffn_w_down": f(ffn_w_down),
        "ev_w_in": f(ev_w_in)[0], "ev_w_out": f(ev_w_out)[0],
        "a_q_gain": f(a_q_gain), "a_k_gain": f(a_k_gain), "a_sink": f(a_sink),
        "b_v_gain": f(b_v_gain), "b_ws": f(b_ws)[0], "b_bias": f(b_bias)[0],
        "od_w_in": f(od_w_in)[0], "od_w_out": f(od_w_out)[0],
        "c_w_pool": f(c_w_pool)[0], "c_scale": f(c_scale),
        "d_q_gain": f(d_q_gain), "d_k_gain": f(d_k_gain),
        "k_ident": np.eye(128, dtype=np.float32), "k_rope": _rope_table(), "k_amask": _amask(),
        "k_band": _band(), "k_dbias": _dbias(f(d_rpb)[0]),
    }
    x = f(x); c = f(c); ctx = f(ctx)
    in_maps = []
    for b in range(8):
        m = dict(shared)
        m["x"] = x[b]
        m["ctx"] = ctx[b]
        m["c"] = c[b].reshape(1, D)
        in_maps.append(m)
    res = run_bass_kernel_spmd(nc, in_maps, core_ids=list(range(8)))
    kernel.last = res
    return np.stack([r["out"] for r in res.results], axis=0)
```

```python
import contextlib
import numpy as np
import concourse.bass as bass
import concourse.mybir as mybir
from concourse.bass_utils import run_bass_kernel_spmd

F32 = mybir.dt.float32
BF16 = mybir.dt.bfloat16
AF = mybir.ActivationFunctionType
ALU = mybir.AluOpType
AX = mybir.AxisListType

D = 1024
S_LAT = 4096
S_CTX = 256
NTL = 32
NT = 34
DFF = 2816
NFF = 22
EPS = 1e-6
NEGB = -30000.0

ENGS = ("sp", "act", "pool", "dve", "pe")
NDMA_SEM = 8


class Res:
    __slots__ = ("name", "last_w", "readers", "gen")

    def __init__(self, name):
        self.name = name
        self.last_w = None
        self.readers = {}
        self.gen = 0

    def bump(self):
        self.gen += 1
        return Ref(self, self.gen)


class Ref:
    __slots__ = ("phys", "gen")

    def __init__(self, phys, gen):
        self.phys = phys
        self.gen = gen


def _norm_res(lst):
    out = []
    for r in lst:
        if isinstance(r, Ref):
            assert r.gen == r.phys.gen, f"stale buffer reference {r.phys.name}"
            r = r.phys
        out.append(r)
    return out


def pipeline(make_gen, items, drain_before=None):
    active = []
    for it in items:
        if drain_before is not None and drain_before(it):
            while active:
                nxt = []
                for g in active:
                    try:
                        next(g)
                        nxt.append(g)
                    except StopIteration:
                        pass
                active = nxt
        nxt = []
        for g in active:
            try:
                next(g)
                nxt.append(g)
            except StopIteration:
                pass
        active = nxt
        g = make_gen(it)
        try:
            next(g)
            active.append(g)
        except StopIteration:
            pass
    while active:
        nxt = []
        for g in active:
            try:
                next(g)
                nxt.append(g)
            except StopIteration:
                pass
        active = nxt


class Op:
    __slots__ = ("eng", "fn", "deps", "dma", "signal", "sem", "val", "prewait", "bg")

    def __init__(self, eng, fn, dma):
        self.eng = eng
        self.fn = fn
        self.dma = dma
        self.deps = []
        self.signal = False
        self.sem = None
        self.val = 0
        self.prewait = None
        self.bg = False


class Sched:
    def __init__(self, nc):
        self.nc = nc
        self.ops = {e: [] for e in ENGS}
        self.bar = {}
        self.nres = 0

    def res(self, name=None):
        self.nres += 1
        return Res(name or f"r{self.nres}")

    def op(self, eng, fn, reads=(), writes=(), dma=False):
        reads = _norm_res(reads)
        writes = _norm_res(writes)
        o = Op(eng, fn, dma)
        deps = {}
        for r in reads:
            if r.last_w is not None:
                deps[id(r.last_w)] = r.last_w
        for r in writes:
            if r.last_w is not None:
                deps[id(r.last_w)] = r.last_w
            for rd in r.readers.values():
                if isinstance(rd, list):
                    for x in rd:
                        deps[id(x)] = x
                else:
                    deps[id(rd)] = rd
        for r in reads:
            if dma:
                r.readers.setdefault(("dma", eng), []).append(o)
            else:
                r.readers[eng] = o
        for r in writes:
            r.last_w = o
            r.readers = {}
        b = self.bar.pop(eng, None)
        if b:
            for x in b:
                deps[id(x)] = x
        dl = []
        for d in deps.values():
            if d is o:
                continue
            if (not dma) and (not d.dma) and d.eng == "pe" and eng == "pe":
                continue
            dl.append(d)
        o.deps = dl
        self.ops[eng].append(o)
        return o

    def dma(self, q, out, in_, reads=(), writes=(), **kw):
        return self.op(q, lambda e: e.dma_start(out=out, in_=in_, **kw), reads, writes, dma=True)

    def barrier(self):
        tails = []
        for e in ENGS:
            ops = self.ops[e]
            for o in reversed(ops):
                if not o.dma:
                    tails.append(o)
                    break
            nd = sum(1 for o in ops if o.dma)
            seen_slots = set()
            idx = nd
            for o in reversed(ops):
                if not o.dma:
                    continue
                idx -= 1
                slot = idx % NDMA_SEM
                if slot in seen_slots or o.bg:
                    continue
                seen_slots.add(slot)
                tails.append(o)
                if len(seen_slots) >= NDMA_SEM:
                    break
        self.bar = {e: list(tails) for e in ENGS}

    def emit(self):
        nc = self.nc
        with contextlib.ExitStack() as st:
            csem = {e: st.enter_context(nc.semaphore(f"c_{e}")) for e in ENGS}
            dsem = {e: [st.enter_context(nc.semaphore(f"d_{e}{i}")) for i in range(NDMA_SEM)]
                    for e in ("sp", "act", "pool")}
            for e in ENGS:
                for o in self.ops[e]:
                    for d in o.deps:
                        d.signal = True
            self.stats = {}
            for e in ENGS:
                cnt = 0
                nd = 0
                for o in self.ops[e]:
                    if o.dma:
                        slot = nd % NDMA_SEM
                        o.sem = dsem[e][slot]
                        o.val = 16 * (nd // NDMA_SEM + 1)
                        if nd >= NDMA_SEM:
                            o.prewait = (dsem[e][slot], 16 * (nd // NDMA_SEM))
                        nd += 1
                    elif o.signal:
                        cnt += 1
                        o.sem = csem[e]
                        o.val = cnt
                self.stats[e] = (len(self.ops[e]), cnt, nd)
            block = st.enter_context(nc.Block())

            def run(eng_name, eng):
                seen = {}
                lastdma = {}
                for o in self.ops[eng_name]:
                    waits = []
                    if o.prewait is not None:
                        waits.append(o.prewait)
                    for d in o.deps:
                        waits.append((d.sem, d.val))
                    for sem, val in waits:
                        k = id(sem)
                        if seen.get(k, 0) >= val:
                            continue
                        seen[k] = val
                        eng.wait_ge(sem, val)
                    inst = o.fn(eng)
                    if o.dma:
                        inst.then_inc(o.sem, 16)
                        lastdma[id(o.sem)] = (o.sem, o.val)
                    elif o.signal:
                        inst.then_inc(o.sem, 1)
                for sem, val in lastdma.values():
                    if seen.get(id(sem), 0) < val:
                        eng.wait_ge(sem, val)

            @block.sync
            def _(e):
                run("sp", e)

            @block.scalar
            def _(e):
                run("act", e)

            @block.gpsimd
            def _(e):
                run("pool", e)

            @block.vector
            def _(e):
                run("dve", e)

            @block.tensor
            def _(e):
                run("pe", e)


class Arena:
    def __init__(self, nc, st, nbytes):
        self.nbytes = nbytes
        self.t = st.enter_context(nc.sbuf_tensor("arena", [128, nbytes // 2], BF16))
        self.off = 0
        self.base = 0

    def alloc(self, free, dtype, parts=128):
        free = list(free)
        n = int(np.prod(free))
        sz = n * (4 if dtype == F32 else 2)
        off = (self.off + 63) // 64 * 64
        assert off + sz <= self.nbytes, f"arena overflow {off + sz} > {self.nbytes}"
        self.off = off + sz
        ap = self.t[0:parts, off // 2: (off + sz) // 2]
        if dtype == F32:
            ap = ap.bitcast(F32)
        if len(free) == 2:
            ap = ap.rearrange("p (a b) -> p a b", a=free[0])
        elif len(free) == 3:
            ap = ap.rearrange("p (a b c) -> p a b c", a=free[0], b=free[1])
        elif len(free) == 4:
            ap = ap.rearrange("p (a b c d) -> p a b c d", a=free[0], b=free[1], c=free[2])
        return ap

    def mark_persistent(self):
        self.base = self.off

    def reset(self):
        self.off = self.base


class Ring:
    def __init__(self, B, n, free, dtype, parts=128):
        self.bufs = [(B.A.alloc(free, dtype, parts), B.S.res()) for _ in range(n)]
        self.i = 0

    def next(self):
        r = self.bufs[self.i % len(self.bufs)]
        self.i += 1
        return r[0], r[1].bump()


class Builder:
    def __init__(self, nc, st, dbg):
        self.nc = nc
        self.st = st
        self.S = Sched(nc)
        self.A = Arena(nc, st, 206 * 1024)
        self.banks = []
        for i in range(8):
            t = st.enter_context(nc.psum_tensor(f"bank{i}", [128, 512], F32))
            self.banks.append((t, self.S.res(f"bank{i}")))
        self.bi = 0
        self.nrr = 8
        self.dbg = dbg

    def bank(self):
        r = self.banks[self.bi % self.nrr]
        self.bi += 1
        return r[0][:], r[1].bump()

    def bank_fixed(self, idx):
        r = self.banks[idx]
        return r[0][:], r[1].bump()

    def dma(self, q, out, in_, reads=(), writes=(), **kw):
        return self.S.dma(q, out, in_, reads, writes, **kw)

    def mm(self, out, lhsT, rhs, start, stop, reads, writes):
        return self.S.op("pe", lambda e: e.matmul(out, lhsT=lhsT, rhs=rhs, start=start, stop=stop), reads, writes)

    def tr(self, out, in_, reads, writes):
        idn = self.ident
        return self.S.op("pe", lambda e: e.transpose(out, in_, idn), list(reads) + [self.r_ident], writes)

    def act(self, out, in_, func, reads, writes, scale=None, bias=None, accum=None):
        kw = {}
        if scale is not None:
            kw["scale"] = scale
        if bias is not None:
            kw["bias"] = bias
        if accum is not None:
            kw["accum_out"] = accum
        return self.S.op("act", lambda e: e.activation(out=out, in_=in_, func=func, **kw), reads, writes)

    def tt(self, eng, out, in0, in1, op, reads, writes):
        return self.S.op(eng, lambda e: e.tensor_tensor(out=out, in0=in0, in1=in1, op=op), reads, writes)

    def ts(self, eng, out, in0, s1, s2, op0, op1, reads, writes):
        if op1 is None:
            return self.S.op(eng, lambda e: e.tensor_scalar(out=out, in0=in0, scalar1=s1, scalar2=None, op0=op0), reads, writes)
        return self.S.op(eng, lambda e: e.tensor_scalar(out=out, in0=in0, scalar1=s1, scalar2=s2, op0=op0, op1=op1), reads, writes)

    def stt(self, eng, out, in0, scalar, in1, op0, op1, reads, writes):
        return self.S.op(eng, lambda e: e.scalar_tensor_tensor(out=out, in0=in0, scalar=scalar, in1=in1, op0=op0, op1=op1), reads, writes)

    def cp(self, eng, out, in_, reads, writes):
        if eng == "act":
            return self.S.op("act", lambda e: e.activation(out=out, in_=in_, func=AF.Copy), reads, writes)
        return self.S.op(eng, lambda e: e.tensor_copy(out=out, in_=in_), reads, writes)

    def recip(self, out, in_, reads, writes):
        return self.S.op("dve", lambda e: e.reciprocal(out=out, in_=in_), reads, writes)

    def memset(self, eng, out, val, writes):
        return self.S.op(eng, lambda e: e.memset(out, val), [], writes)

    def reduce_sum(self, out, in_, reads, writes):
        return self.S.op("dve", lambda e: e.tensor_reduce(out=out, in_=in_, axis=AX.X, op=ALU.add), reads, writes)

    def rstd(self, st, r_st, w, inv_n):
        self.ts("dve", st[:, w:2 * w], st[:, 0:w], inv_n, EPS, ALU.mult, ALU.add, [r_st], [r_st])
        self.act(st[:, w:2 * w], st[:, w:2 * w], AF.Sqrt, [r_st], [r_st])
        self.recip(st[:, 2 * w:3 * w], st[:, w:2 * w], [r_st], [r_st])
        return st[:, 2 * w:3 * w]

    def phase_begin(self):
        self.S.barrier()
        self.A.reset()

    def load_bc(self, q, src_1xn, n, name=None):
        t = self.A.alloc([n], F32)
        r = self.S.res(name)
        self.dma(q, t, src_1xn.partition_broadcast(128), [], [r])
        return t, r


def build_program(dbg_phases=None, dbg=False):
    nc = bass.Bass("TRN2", target_bir_lowering=False)
    I = {}

    def inp(name, shape, dt=F32):
        I[name] = nc.dram_tensor(name, list(shape), dt, kind="ExternalInput").ap()
        return I[name]

    inp("x", [S_LAT, D]); inp("ctx", [S_CTX, D]); inp("c", [1, D]); inp("c_ctx", [1, D])
    inp("ada_w", [2, D, 9 * D]); inp("ada_b", [2, 9 * D]); inp("norm_g", [2, 3, D])
    inp("ffn_w_gu", [2, 2, D, 2 * DFF]); inp("ffn_w_down", [2, 2, DFF, D])
    inp("ev_w_in", [D, 1792]); inp("ev_w_out", [D, D])
    inp("a_q_gain", [1, 64]); inp("a_k_gain", [1, 64]); inp("a_sink", [1, 8])
    inp("b_v_gain", [1, 512]); inp("b_ws", [8, 128, 128]); inp("b_bias", [8, 128])
    inp("od_w_in", [D, 2048]); inp("od_w_out", [D, D])
    inp("c_w_pool", [4, 128, 128]); inp("c_scale", [1, 512])
    inp("d_q_gain", [1, 64]); inp("d_k_gain", [1, 64])
    inp("k_ident", [128, 128]); inp("k_rope", [S_LAT, 2, 64]); inp("k_amask", [128, 2, 128])
    inp("k_band", [128, 4, 5, 128]); inp("k_dbias", [5, 128, 8, 5, 128])
    out = nc.dram_tensor("out", [S_LAT, D], F32, kind="ExternalOutput").ap()
    skind = "ExternalOutput" if dbg else "Internal"

    def scr(name, shape, dt):
        return nc.dram_tensor(name, list(shape), dt, kind=skind).ap()

    hA = scr("hA", [NT * 128, D], F32)
    mod_d = scr("mod_d", [2, 2, 9 * D], F32)
    qT_d = scr("qT_d", [8, 64, NT * 128], BF16)
    kT_d = scr("kT_d", [8, 64, NT * 128], BF16)
    v_d = scr("v_d", [NT * 128, 512], BF16)
    u_d = scr("u_d", [NT * 128, 512], F32)
    vv_d = scr("vv_d", [NT * 128, 512], BF16)
    xp_d = scr("xp_d", [S_LAT, 512], BF16)
    wgc_all = [nc.dram_tensor(f"wgc_d{f}", [NFF // 2, 128, 8 * 2 * 256], BF16, kind="Internal").ap() for f in range(4)]
    wdc_all = [nc.dram_tensor(f"wdc_d{f}", [128, NFF * D], BF16, kind="Internal").ap() for f in range(4)]
    adac_all = [nc.dram_tensor(f"adac_d{l_}", [18, 128, 8 * 512], BF16, kind="Internal").ap() for l_ in range(2)]

    st = contextlib.ExitStack()
    with st:
        B = Builder(nc, st, dbg)
        S, A = B.S, B.A
        B.ident = A.alloc([128], BF16)
        B.r_ident = S.res("ident")
        B.dma("pool", B.ident, I["k_ident"][:, :], [], [B.r_ident])
        B.identf = A.alloc([128], F32)
        B.dma("sp", B.identf, I["k_ident"][:, :], [], [B.r_ident])
        A.mark_persistent()

        bgq = []
        wres = {}

        def bg_add_ffn(f, li, which):
            wgu_ = I["ffn_w_gu"][li, which]
            for cp_ in range(NFF // 2):
                dst4 = wgc_all[f][cp_].rearrange("p (k g n) -> p k g n", k=8, g=2)
                for g_ in range(2):
                    r = S.res()
                    wres[("gu", f, cp_, g_)] = r
                    bgq.append((dst4[:, :, g_, :],
                                wgu_[:, g_ * DFF + cp_ * 256:g_ * DFF + (cp_ + 1) * 256].rearrange("(k p) n -> p k n", p=128), r))
            wd3 = wdc_all[f].rearrange("p (c n) -> p c n", c=NFF)
            for c4 in range(0, NFF, 2):
                r = S.res()
                wres[("wd", f, c4)] = r
                bgq.append((wd3[:, c4:c4 + 2, :],
                            I["ffn_w_down"][li, which, c4 * 128:(c4 + 2) * 128, :].rearrange("(c p) n -> p c n", p=128), r))

        def bg_add_ada(li, nb0=0, nb1=18):
            for nb in range(nb0, nb1):
                r = S.res()
                wres[("ada", li, nb)] = r
                bgq.append((adac_all[li][nb].rearrange("p (k n) -> p k n", k=8),
                            I["ada_w"][li, :, nb * 512:(nb + 1) * 512].rearrange("(k p) n -> p k n", p=128), r))

        def bg_need(pred):
            last = -1
            for i, (_, _, r) in enumerate(bgq):
                if pred(r):
                    last = i
            if last >= 0:
                bg_tick(last + 1)

        def bg_tick(n=1):
            for _ in range(n):
                if not bgq:
                    return
                dst, src, r = bgq.pop(0)
                o = B.dma("pool", dst, src, [], [r])
                o.bg = True

        wcache = {}

        def bg_add_w(name, src, ncols, split):
            t = nc.dram_tensor(f"wc_{name}", [128, 8 * ncols], BF16, kind="Internal").ap()
            rl = []
            w_ = ncols // split
            for i in range(split):
                r = S.res()
                rl.append(r)
                bgq.append((t.rearrange("p (k n) -> p k n", k=8)[:, :, i * w_:(i + 1) * w_],
                            src[:, i * w_:(i + 1) * w_].rearrange("(k p) n -> p k n", p=128), r))
            wcache[name] = (t, rl)

        def load_w_bf16(name, dst, r_dst, src, ncols):
            if name in wcache:
                t, rl = wcache[name]
                ids = {id(r) for r in rl}
                bg_need(lambda r: id(r) in ids)
                B.dma("sp", dst.rearrange("p k n -> p (k n)"), t, rl, [r_dst])
            else:
                step = 1024 if ncols > 1792 else ncols
                for c0 in range(0, ncols, step):
                    B.dma("pool", dst[:, :, c0:c0 + step], src[:, c0:c0 + step].rearrange("(k p) n -> p k n", p=128), [], [r_dst])

        bg_add_ada(0, 6, 18)
        bg_add_w("ev_w_in", I["ev_w_in"], 1792, 2)
        bg_add_w("ev_w_out", I["ev_w_out"], 1024, 1)
        bg_add_ada(1)
        bg_add_ffn(1, 0, 1)
        bg_add_ffn(2, 1, 0)
        bg_add_w("od_w_in", I["od_w_in"], 2048, 2)
        bg_add_w("od_w_out", I["od_w_out"], 1024, 1)
        bg_add_ffn(3, 1, 1)

        def src0(gt):
            if gt < NTL:
                return I["x"][gt * 128:(gt + 1) * 128, :]
            return I["ctx"][(gt - NTL) * 128:(gt - NTL + 1) * 128, :]

        def hsrc(gt):
            return hA[gt * 128:(gt + 1) * 128, :]

        def osrc(gt):
            return out[gt * 128:(gt + 1) * 128, :]

        phases = []

        def phase_mod(li, nb0=0, nb1=18, cont=False):
            if not cont:
                B.phase_begin()
            mine_ = {id(r) for k_, r in wres.items() if k_[0] == "ada" and k_[1] == li and nb0 <= k_[2] < nb1}
            bg_need(lambda r: id(r) in mine_)
            cc = A.alloc([2, 128], F32, parts=8)
            r_cc = S.res()
            B.dma("sp", cc[:, 0, :], I["c"][0, :].rearrange("(k p) -> k p", p=128), [], [r_cc])
            B.dma("sp", cc[:, 1, :], I["c_ctx"][0, :].rearrange("(k p) -> k p", p=128), [], [r_cc])
            ccb = A.alloc([2, 128], BF16, parts=8)
            r_ccb = S.res()
            B.act(ccb, cc, AF.Silu, [r_cc], [r_ccb])
            cs = A.alloc([8, 2], BF16)
            r_cs = S.res()
            bk, r_bk = B.bank()
            bkb = bk.bitcast(BF16)
            for j in range(2):
                B.S.op("pe", lambda e, j=j: e.transpose(bkb[:, j * 8:(j + 1) * 8], ccb[:, j, :], B.ident[0:8, 0:8]), [r_ccb, B.r_ident], [r_bk])
            B.cp("dve", cs, bkb[:, 0:16].rearrange("p (j k) -> p k j", j=2), [], [r_bk, r_cs])
            adab = A.alloc([9 * D], F32, parts=2)
            r_adab = S.res()
            B.dma("sp", adab, I["ada_b"][li:li + 1, :].partition_broadcast(2), [], [r_adab])
            msb = A.alloc([9 * D], F32, parts=2)
            r_msb = S.res()
            wr = Ring(B, 3, [8, 512], BF16)
            for nb in range(nb0, nb1):
                w, r_w = wr.next()
                if ("ada", li, nb) in wres:
                    B.dma("sp", w.rearrange("p k n -> p (k n)"), adac_all[li][nb], [wres[("ada", li, nb)]], [r_w])
                else:
                    B.dma("pool", w, I["ada_w"][li, :, nb * 512:(nb + 1) * 512].rearrange("(k p) n -> p k n", p=128), [], [r_w])
                bk, r_bk = B.bank()
                for k in range(8):
                    B.mm(bk[0:2, :], cs[:, k, :], w[:, k, :], k == 0, k == 7, [r_cs, r_w], [r_bk])
                B.tt("dve", msb[:, nb * 512:(nb + 1) * 512], bk[0:2, :], adab[:, nb * 512:(nb + 1) * 512], ALU.add,
                     [r_adab], [r_bk, r_msb])
            B.dma("sp", mod_d[li, :, nb0 * 512:nb1 * 512], msb[:, nb0 * 512:nb1 * 512], [r_msb], [])

        def load_mod_vecs(li, j_shift, j_scale, j_gate, gi, gate_mul, ntypes=2):
            outl = []
            gbc, r_g = (None, None)
            if j_scale is not None:
                gbc, r_g = B.load_bc("sp", I["norm_g"][li, gi:gi + 1, :], D)
            for ty in range(ntypes):
                r = S.res()
                sh = Gm = gt_ = None
                if j_shift is not None:
                    sh = A.alloc([D], F32)
                    B.dma("sp", sh, mod_d[li, ty:ty + 1, j_shift * D:(j_shift + 1) * D].partition_broadcast(128), [], [r])
                if j_scale is not None:
                    Gm = A.alloc([D], F32)
                    B.dma("sp", Gm, mod_d[li, ty:ty + 1, j_scale * D:(j_scale + 1) * D].partition_broadcast(128), [], [r])
                    B.stt("dve", Gm, Gm, 1.0, gbc, ALU.add, ALU.mult, [r_g, r], [r])
                if j_gate is not None:
                    gt_ = A.alloc([D], F32)
                    B.dma("sp", gt_, mod_d[li, ty:ty + 1, j_gate * D:(j_gate + 1) * D].partition_broadcast(128), [], [r])
                    if gate_mul != 1.0:
                        B.ts("dve", gt_, gt_, gate_mul, None, ALU.mult, None, [r], [r])
                outl.append((sh, Gm, gt_, r))
            return outl

        def norm_tile(hin, r_hin, mv, rings):
            sh, Gm, _, r_mv = mv
            sqj, r_sqj = rings["sqj"]
            st_, r_st = rings["st"].next()
            B.memset("dve", st_[:, 0:1], 0.0, [r_st])
            B.act(sqj, hin, AF.Square, [r_hin, r_st], [r_sqj, r_st], accum=st_[:, 0:1])
            rs = B.rstd(st_, r_st, 1, 1.0 / D)
            z1, r_z1 = rings["z1"].next()
            B.stt("dve", z1, hin, rs, Gm, ALU.mult, ALU.mult, [r_hin, r_st, r_mv], [r_z1])
            zt, r_zt = rings["ztok"].next()
            B.tt("pool", zt, z1, sh, ALU.add, [r_z1, r_mv], [r_zt])
            return zt, r_zt

        def norm_tile_g(hin, r_hin, mv, rings):
            sh, Gm, _, r_mv = mv
            sqj, r_sqj = rings["sqj"]
            st_, r_st = rings["st"].next()
            B.memset("dve", st_[:, 0:1], 0.0, [r_st])
            B.act(sqj, hin, AF.Square, [r_hin, r_st], [r_sqj, r_st], accum=st_[:, 0:1])
            yield
            B.ts("dve", st_[:, 1:2], st_[:, 0:1], 1.0 / D, EPS, ALU.mult, ALU.add, [r_st], [r_st])
            yield
            B.act(st_[:, 1:2], st_[:, 1:2], AF.Sqrt, [r_st], [r_st])
            yield
            B.recip(st_[:, 2:3], st_[:, 1:2], [r_st], [r_st])
            z1, r_z1 = rings["z1"].next()
            B.stt("dve", z1, hin, st_[:, 2:3], Gm, ALU.mult, ALU.mult, [r_hin, r_st, r_mv], [r_z1])
            yield
            zt, r_zt = rings["ztok"].next()
            B.tt("pool", zt, z1, sh, ALU.add, [r_z1, r_mv], [r_zt])
            return zt, r_zt

        def chains_g(items, rings):
            sts = []
            for it in items:
                w = it["nh"] * 64
                sq, r_sq = rings["sq"].next()
                B.act(sq[:, 0:w], it["src"][:, 0:w], AF.Square, [it["r_src"]], [r_sq])
                it["sq"], it["r_sq"] = sq, r_sq
            yield
            for it in items:
                nh = it["nh"]
                w = nh * 64
                st_, r_st = rings["st"].next()
                B.reduce_sum(st_[:, 0:nh], it["sq"][:, 0:w].rearrange("p (h d) -> p h d", h=nh), [it["r_sq"]], [r_st])
                B.ts("dve", st_[:, nh:2 * nh], st_[:, 0:nh], 1.0 / 64, EPS, ALU.mult, ALU.add, [r_st], [r_st])
                it["st"], it["r_st"] = st_, r_st
            yield
            for it in items:
                nh = it["nh"]
                B.act(it["st"][:, nh:2 * nh], it["st"][:, nh:2 * nh], AF.Sqrt, [it["r_st"]], [it["r_st"]])
            yield
            for it in items:
                nh = it["nh"]
                w = nh * 64
                st_ = it["st"]
                B.recip(st_[:, 2 * nh:3 * nh], st_[:, nh:2 * nh], [it["r_st"]], [it["r_st"]])
                x3 = it["src"][:, 0:w].rearrange("p (h d) -> p h d", h=nh)
                B.tt("dve", x3, x3, st_[:, 2 * nh:3 * nh].unsqueeze(2).to_broadcast([128, nh, 64]), ALU.mult,
                     [it["r_st"], it["r_src"]], [it["r_src"]])
            yield
            for it in items:
                nh = it["nh"]
                w = nh * 64
                if it.get("gain_full"):
                    B.tt("pool", it["dst"], it["src"][:, 0:w], it["gain"], ALU.mult, [it["r_gain"], it["r_src"]], [it["r_dst"]])
                elif it.get("dst") is not None:
                    B.tt("pool", it["dst"].rearrange("p (h d) -> p h d", h=nh), it["src"][:, 0:w].rearrange("p (h d) -> p h d", h=nh),
                         it["gain"].unsqueeze(1).to_broadcast([128, nh, 64]), ALU.mult, [it["r_gain"], it["r_src"]], [it["r_dst"]])
                else:
                    x3 = it["src"][:, 0:w].rearrange("p (h d) -> p h d", h=nh)
                    B.tt("pool", x3, x3, it["gain"].unsqueeze(1).to_broadcast([128, nh, 64]), ALU.mult,
                         [it["r_gain"], it["r_src"]], [it["r_src"]])

        def rope_g(items, ropt, r_ropt, rings):
            tmp = []
            for (qn, r_qn, nh, outb, r_out) in items:
                w = nh * 64
                a_, r_a = rings["ra"].next()
                q3 = qn[:, 0:w].rearrange("p (h d) -> p h d", h=nh)
                a3 = a_[:, 0:w].rearrange("p (h d) -> p h d", h=nh)
                B.tt("pool", a3, q3, ropt[:, 0, :].unsqueeze(1).to_broadcast([128, nh, 64]), ALU.mult, [r_qn, r_ropt], [r_a])
                b_, r_b = rings["rb"].next()
                q5 = qn[:, 0:w].rearrange("p (h a s d) -> p h a s d", h=nh, a=2, s=2)
                b5 = b_[:, 0:w].rearrange("p (h a s d) -> p h a s d", h=nh, a=2, s=2)
                s4 = ropt[:, 1, :].rearrange("p (a s d) -> p a s d", a=2, s=2)
                for ax in range(2):
                    for s_ in range(2):
                        B.tt("dve", b5[:, :, ax, s_, :], q5[:, :, ax, 1 - s_, :],
                             s4[:, ax, s_, :].unsqueeze(1).to_broadcast([128, nh, 16]), ALU.mult, [r_qn, r_ropt], [r_b])
                tmp.append((a_, r_a, b_, r_b))
            yield
            for (qn, r_qn, nh, outb, r_out), (a_, r_a, b_, r_b) in zip(items, tmp):
                w = nh * 64
                B.tt("dve", outb[:, 0:w], a_[:, 0:w], b_[:, 0:w], ALU.add, [r_a, r_b], [r_out])

        def transpose8(src, r_src, dst3, r_dst, eng="act"):
            bk, r_bk = B.bank()
            bkb = bk.bitcast(BF16)
            for k in range(8):
                B.tr(bkb[:, k * 128:(k + 1) * 128], src[:, k * 128:(k + 1) * 128], [r_src], [r_bk])
            B.cp(eng, dst3, bkb.rearrange("p (k t) -> p k t", k=8), [], [r_bk, r_dst])

        def phase_ffn(li, which, ntiles, srcf, dstf):
            B.phase_begin()
            f = li * 2 + which
            cached = ("wd", f, 0) in wres
            if cached:
                mine = {id(r) for k_, r in wres.items() if k_[0] in ("gu", "wd") and k_[1] == f}
                bg_need(lambda r: id(r) in mine)
            wgc_d = wgc_all[f]
            j0 = 0 if which == 0 else 6
            gi = 0 if which == 0 else 2
            mvs = load_mod_vecs(li, j0, j0 + 1, j0 + 2, gi, 0.5, ntypes=2 if ntiles > NTL else 1)
            wd = A.alloc([NFF, D], BF16)
            r_wd = S.res()

            def load_wd(c4):
                if cached:
                    B.dma("sp", wd[:, c4:c4 + 2, :], wdc_all[f].rearrange("p (c n) -> p c n", c=NFF)[:, c4:c4 + 2, :],
                          [wres[("wd", f, c4)]], [r_wd])
                else:
                    B.dma("pool", wd[:, c4:c4 + 2, :],
                          I["ffn_w_down"][li, which, c4 * 128:(c4 + 2) * 128, :].rearrange("(c p) n -> p c n", p=128), [], [r_wd])
            if ntiles == NT:
                groups = [list(range(0, 9)), list(range(9, 18)), list(range(18, 26)), list(range(26, 34))]
            else:
                groups = [list(range(g * 8, g * 8 + 8)) for g in range(4)]
            wc_res = [S.res() for _ in range(NFF // 2)]
            GM = max(len(g) for g in groups)
            zTs = [A.alloc([8, GM * 128], BF16) for _ in range(2)]
            zress = [[S.res() for _ in range(GM)] for _ in range(2)]
            actT = A.alloc([NFF, GM * 128], BF16)
            wgr = Ring(B, 2, [8, 2, 256], BF16)
            hinr = Ring(B, 2, [D], F32)
            rings = {"sqj": (A.alloc([D], BF16), S.res()), "st": Ring(B, 2, [3], F32),
                     "z1": Ring(B, 1, [D], F32), "ztok": Ring(B, 2, [D], BF16)}
            stmpr = Ring(B, 2, [512], F32)
            etmpr = Ring(B, 1, [D], F32)
            houtr = Ring(B, 1, [D], F32)
            wgu = I["ffn_w_gu"][li, which]

            def stage1_a(gidx, lt, gt):
                ty = 0 if gt < NTL else 1
                hin, r_hin = hinr.next()
                B.dma("sp", hin, srcf(gt), [], [r_hin])
                zt, r_zt = norm_tile(hin, r_hin, mvs[ty], rings)
                return (gidx, lt, zt, r_zt)

            def stage1_b(st1):
                gidx, lt, zt, r_zt = st1
                transpose8(zt, r_zt, zTs[gidx % 2][:, :, lt * 128:(lt + 1) * 128], zress[gidx % 2][lt])

            def stage1_tile(gidx, lt, gt):
                stage1_b(stage1_a(gidx, lt, gt))

            lag0 = None
            for lt, gt in enumerate(groups[0]):
                cur0 = stage1_a(0, lt, gt)
                if lag0 is not None:
                    stage1_b(lag0)
                lag0 = cur0
            stage1_b(lag0)
            for gidx, grp in enumerate(groups):
                zT = zTs[gidx % 2]
                zres = zress[gidx % 2]
                T = len(grp) * 128
                nblk = (T + 511) // 512
                bs = T // nblk
                blocks = [(i * bs, (i + 1) * bs if i < nblk - 1 else T) for i in range(nblk)]
                ares = [S.res() for _ in blocks]
                pending = list(enumerate(groups[gidx + 1])) if gidx + 1 < len(groups) else []
                npend = len(pending)
                nunits = NFF * nblk
                unit = 0
                emitted = 0
                lagged = None
                park = None
                for cp_ in range(NFF // 2):
                    wg, r_wg = wgr.next()
                    if cached:
                        B.dma("sp", wg.rearrange("p k g n -> p (k g n)"), wgc_d[cp_],
                              [wres[("gu", f, cp_, 0)], wres[("gu", f, cp_, 1)]], [r_wg])
                    elif gidx == 0:
                        B.dma("pool", wg[:, :, 0, :], wgu[:, cp_ * 256:(cp_ + 1) * 256].rearrange("(k p) n -> p k n", p=128), [], [r_wg])
                        B.dma("pool", wg[:, :, 1, :], wgu[:, DFF + cp_ * 256:DFF + (cp_ + 1) * 256].rearrange("(k p) n -> p k n", p=128), [], [r_wg])
                        if park is not None:
                            B.dma("pool", wgc_d[park[0]], park[1].rearrange("p k g n -> p (k g n)"), [park[2]], [wc_res[park[0]]])
                        park = (cp_, wg, r_wg)
                    else:
                        B.dma("sp", wg.rearrange("p k g n -> p (k g n)"), wgc_d[cp_], [wc_res[cp_]], [r_wg])
                    if gidx == 0:
                        load_wd(cp_ * 2)
                    for ci in range(2):
                        c = cp_ * 2 + ci
                        for bi_, (a, b_) in enumerate(blocks):
                            n = b_ - a
                            zr = [zres[t] for t in range(a // 128, (b_ - 1) // 128 + 1)]
                            bg, r_bg = B.bank()
                            bu, r_bu = B.bank()
                            for k in range(8):
                                B.mm(bg[:, 0:n], wg[:, k, 0, ci * 128:(ci + 1) * 128], zT[:, k, a:b_], k == 0, k == 7, [r_wg] + zr, [r_bg])
                            for k in range(8):
                                B.mm(bu[:, 0:n], wg[:, k, 1, ci * 128:(ci + 1) * 128], zT[:, k, a:b_], k == 0, k == 7, [r_wg] + zr, [r_bu])
                            stp, r_stp = stmpr.next()
                            B.act(stp[:, 0:n], bg[:, 0:n], AF.Silu, [], [r_bg, r_stp])
                            B.tt("dve", actT[:, c, a:b_], stp[:, 0:n], bu[:, 0:n], ALU.mult, [r_stp], [r_bu, ares[bi_]])
                            unit += 1
                            if unit % 4 == 0 and (cached or gidx > 0):
                                bg_tick()
                            while emitted < npend and unit * npend >= (emitted + 1) * int(nunits * 0.7):
                                if lagged is not None:
                                    stage1_b(lagged)
                                lt2, gt2 = pending[emitted]
                                lagged = stage1_a(gidx + 1, lt2, gt2)
                                emitted += 1
                            if emitted == npend and lagged is not None and unit * npend >= (emitted + 1) * int(nunits * 0.7):
                                stage1_b(lagged)
                                lagged = None
                if park is not None:
                    B.dma("pool", wgc_d[park[0]], park[1].rearrange("p k g n -> p (k g n)"), [park[2]], [wc_res[park[0]]])
                    park = None
                while emitted < npend:
                    if lagged is not None:
                        stage1_b(lagged)
                    lt2, gt2 = pending[emitted]
                    lagged = stage1_a(gidx + 1, lt2, gt2)
                    emitted += 1
                if lagged is not None:
                    stage1_b(lagged)
                    lagged = None
                for lt, gt in enumerate(grp):
                    ty = 0 if gt < NTL else 1
                    gate = mvs[ty][2]
                    r_mv = mvs[ty][3]
                    ar = [ares[i] for i, (a, b_) in enumerate(blocks) if a < (lt + 1) * 128 and b_ > lt * 128]
                    b0, r_b0 = B.bank()
                    b1, r_b1 = B.bank()
                    bb = [(b0, r_b0), (b1, r_b1)]
                    for c in range(NFF):
                        for hf in range(2):
                            B.mm(bb[hf][0], actT[:, c, lt * 128:(lt + 1) * 128], wd[:, c, hf * 512:(hf + 1) * 512],
                                 c == 0, c == NFF - 1, ar + [r_wd], [bb[hf][1]])
                    hin, r_hin = hinr.next()
                    B.dma("sp", hin, srcf(gt), [], [r_hin])
                    et, r_et = etmpr.next()
                    for hf in range(2):
                        B.tt("dve", et[:, hf * 512:(hf + 1) * 512], bb[hf][0], gate[:, hf * 512:(hf + 1) * 512], ALU.mult,
                             [r_mv], [bb[hf][1], r_et])
                    ho, r_ho = houtr.next()
                    B.tt("pool", ho, et, hin, ALU.add, [r_et, r_hin], [r_ho])
                    B.dma("pool", dstf(gt), ho, [r_ho], [])

        def head_norm(bank_ap, r_bank, nh, gain_bc, r_gain, rings, name):
            sq, r_sq = rings["sq"].next()
            w = nh * 64
            B.act(sq[:, 0:w], bank_ap, AF.Square, [], [r_bank, r_sq])
            st_, r_st = rings["st"].next()
            B.reduce_sum(st_[:, 0:nh], sq[:, 0:w].rearrange("p (h d) -> p h d", h=nh), [r_sq], [r_st])
            rs = B.rstd(st_, r_st, nh, 1.0 / 64)
            qn, r_qn = rings[name].next()
            qn3 = qn[:, 0:w].rearrange("p (h d) -> p h d", h=nh)
            B.tt("dve", qn3, bank_ap.rearrange("p (h d) -> p h d", h=nh), rs.unsqueeze(2).to_broadcast([128, nh, 64]), ALU.mult,
                 [r_st], [r_bank, r_qn])
            B.tt("pool", qn3, qn3, gain_bc.unsqueeze(1).to_broadcast([128, nh, 64]), ALU.mult, [r_gain, r_qn], [r_qn])
            return qn, r_qn

        def rope(qn, r_qn, nh, ropt, r_ropt, outb, r_out, rings):
            w = nh * 64
            a_, r_a = rings["ra"].next()
            b_, r_b = rings["rb"].next()
            q3 = qn[:, 0:w].rearrange("p (h d) -> p h d", h=nh)
            a3 = a_[:, 0:w].rearrange("p (h d) -> p h d", h=nh)
            B.tt("pool", a3, q3, ropt[:, 0, :].unsqueeze(1).to_broadcast([128, nh, 64]), ALU.mult, [r_qn, r_ropt], [r_a])
            q5 = qn[:, 0:w].rearrange("p (h a s d) -> p h a s d", h=nh, a=2, s=2)
            b5 = b_[:, 0:w].rearrange("p (h a s d) -> p h a s d", h=nh, a=2, s=2)
            s4 = ropt[:, 1, :].rearrange("p (a s d) -> p a s d", a=2, s=2)
            for ax in range(2):
                for s in range(2):
                    B.tt("dve", b5[:, :, ax, s, :], q5[:, :, ax, 1 - s, :],
                         s4[:, ax, s, :].unsqueeze(1).to_broadcast([128, nh, 16]), ALU.mult, [r_qn, r_ropt], [r_b])
            B.tt("dve", outb[:, 0:w], a_[:, 0:w], b_[:, 0:w], ALU.add, [r_a, r_b], [r_out])

        def head_transposes(src, r_src, nh, dst_dram, gt, rings):
            bk, r_bk = B.bank()
            bkb = bk.bitcast(BF16)
            for h in range(nh):
                B.tr(bkb[0:64, h * 128:(h + 1) * 128], src[:, h * 64:(h + 1) * 64], [r_src], [r_bk])
            ts_, r_ts = rings["hT"].next()
            B.cp("act", ts_[:, 0:nh, :], bkb[0:64, 0:nh * 128].rearrange("p (h t) -> p h t", h=nh), [], [r_bk, r_ts])
            B.dma("sp", dst_dram.rearrange("h d t -> d h t")[:, 0:nh, gt * 128:(gt + 1) * 128], ts_[:, 0:nh, :], [r_ts], [])

        def phase_even_prep(li):
            B.phase_begin()
            mvs = load_mod_vecs(li, 3, 4, None, 1, 1.0)
            win = A.alloc([8, 1792], BF16)
            r_win = S.res()
            load_w_bf16("ev_w_in", win, r_win, I["ev_w_in"], 1792)
            qg_bc, r_qg = B.load_bc("sp", I["a_q_gain"][0:1, :], 64)
            kg_bc, r_kg = B.load_bc("sp", I["a_k_gain"][0:1, :], 64)
            vg_bc, r_vg = B.load_bc("sp", I["b_v_gain"][0:1, :], 512)
            hinr = Ring(B, 6, [D], F32)
            rings = {"sqj": (A.alloc([D], BF16), S.res()), "st": Ring(B, 24, [30], F32),
                     "z1": Ring(B, 2, [D], F32), "ztok": Ring(B, 3, [D], BF16),
                     "sq": Ring(B, 6, [512], F32),
                     "ra": Ring(B, 3, [512], F32), "rb": Ring(B, 3, [512], F32), "hT": Ring(B, 3, [8, 128], BF16, parts=64)}
            zTr = Ring(B, 3, [8, 128], BF16)
            ropr = Ring(B, 14, [2, 64], F32)
            qrr = Ring(B, 6, [512], F32)
            krr = Ring(B, 6, [128], F32)
            gvr = Ring(B, 6, [512], F32)
            qbr = Ring(B, 9, [512], BF16)
            kbr = Ring(B, 9, [128], BF16)
            vbr = Ring(B, 4, [128], BF16)
            ur = Ring(B, 3, [512], F32)
            vvr = Ring(B, 8, [512], BF16)
            nsl = [(0, 512), (512, 768), (768, 1280), (1280, 1792)]

            def tile_gen(gt):
                bg_tick()
                ty = 0 if gt < NTL else 1
                hin, r_hin = hinr.next()
                B.dma("sp", hin, hsrc(gt), [], [r_hin])
                if ty == 0:
                    rt, r_rt = ropr.next()
                    B.dma("sp", rt, I["k_rope"][gt * 128:(gt + 1) * 128, :, :], [], [r_rt])
                yield
                zt, r_zt = yield from norm_tile_g(hin, r_hin, mvs[ty], rings)
                yield
                zT, r_zT = zTr.next()
                transpose8(zt, r_zt, zT, r_zT)
                yield
                bks = [B.bank() for _ in range(4)]
                for k in range(8):
                    for i, (n0, n1) in enumerate(nsl):
                        B.mm(bks[i][0][:, 0:n1 - n0], zT[:, k, :], win[:, k, n0:n1], k == 0, k == 7, [r_zT, r_win], [bks[i][1]])
                (bq, r_bq), (bkv, r_bkv), (bbu, r_bbu), (bbv, r_bbv) = bks
                yield
                qr, r_qr = qrr.next()
                B.cp("act", qr, bq, [], [r_bq, r_qr])
                kr, r_kr = krr.next()
                B.cp("act", kr, bkv[:, 0:128], [], [r_bkv, r_kr])
                vb, r_vb = vbr.next()
                B.cp("act", vb, bkv[:, 128:256], [], [r_bkv, r_vb])
                u_, r_u = ur.next()
                B.act(u_, bbu, AF.Gelu_apprx_tanh, [], [r_bbu, r_u])
                gv, r_gv = gvr.next()
                B.act(gv, bbv, AF.Gelu_apprx_tanh, [], [r_bbv, r_gv])
                yield
                B.dma("sp", v_d[gt * 128:(gt + 1) * 128, 0:128], vb, [r_vb], [])
                B.dma("sp", u_d[gt * 128:(gt + 1) * 128, :], u_, [r_u], [])
                vv, r_vv = vvr.next()
                items = [dict(src=qr, r_src=r_qr, nh=8, gain=qg_bc, r_gain=r_qg),
                         dict(src=kr, r_src=r_kr, nh=2, gain=kg_bc, r_gain=r_kg),
                         dict(src=gv, r_src=r_gv, nh=8, gain=vg_bc, r_gain=r_vg, gain_full=True, dst=vv, r_dst=r_vv)]
                qb, r_qb = qbr.next()
                kb, r_kb = kbr.next()
                if ty == 1:
                    items[0]["dst"], items[0]["r_dst"] = qb, r_qb
                    items[1]["dst"], items[1]["r_dst"] = kb, r_kb
                yield from chains_g(items, rings)
                yield
                B.dma("sp", vv_d[gt * 128:(gt + 1) * 128, :], vv, [r_vv], [])
                if ty == 0:
                    yield from rope_g([(qr, r_qr, 8, qb, r_qb), (kr, r_kr, 2, kb, r_kb)], rt, r_rt, rings)
                    yield
                head_transposes(qb, r_qb, 8, qT_d, gt, rings)
                head_transposes(kb, r_kb, 2, kT_d, gt, rings)

            pipeline(tile_gen, range(NT))

        def outproj_residual(mix, r_mix, wout, r_wout, gate, r_gate, gt, rings):
            hin, r_hin = rings["hin"].next()
            B.dma("sp", hin, hsrc(gt), [], [r_hin])
            mT, r_mT = rings["mixT"].next()
            transpose8(mix, r_mix, mT, r_mT)
            yield
            b0, r_b0 = B.bank()
            b1, r_b1 = B.bank()
            bb = [(b0, r_b0), (b1, r_b1)]
            for k in range(8):
                for hf in range(2):
                    B.mm(bb[hf][0], mT[:, k, :], wout[:, k, hf * 512:(hf + 1) * 512], k == 0, k == 7, [r_mT, r_wout], [bb[hf][1]])
            et, r_et = rings["et"].next()
            for hf in range(2):
                B.tt("dve", et[:, hf * 512:(hf + 1) * 512], bb[hf][0], gate[:, hf * 512:(hf + 1) * 512], ALU.mult,
                     [r_gate], [bb[hf][1], r_et])
            yield
            ho, r_ho = rings["hout"].next()
            B.tt("pool", ho, et, hin, ALU.add, [r_et, r_hin], [r_ho])
            yield
            B.dma("sp", hsrc(gt), ho, [r_ho], [])

        def load_V(dst, r_dst, kt0, nw, nk):
            for w in range(nw):
                B.dma("sp", dst[:, w, :, 0:64],
                      v_d[(kt0 + w) * 128:(kt0 + w + 1) * 128, 0:nk * 64].rearrange("p (k d) -> p k d", k=nk), [], [r_dst])

        def phase_even_attn(li):
            B.phase_begin()
            mvs = load_mod_vecs(li, None, None, 5, 1, 1.0)
            wout = A.alloc([8, D], BF16)
            r_wout = S.res()
            load_w_bf16("ev_w_out", wout, r_wout, I["ev_w_out"], 1024)
            wsn = A.alloc([8, 128], BF16)
            r_wsn = S.res()
            B.dma("pool", wsn, I["b_ws"].rearrange("g i j -> i g j"), [], [r_wsn])
            wsT = A.alloc([8, 128], BF16)
            r_wsT = S.res()
            bk, r_bk = B.bank()
            bkb = bk.bitcast(BF16)
            for g in range(8):
                B.tr(bkb[:, g * 128:(g + 1) * 128], wsn[:, g, :], [r_wsn], [r_bk])
            B.cp("act", wsT, bkb.rearrange("p (g t) -> p g t", g=8), [], [r_bk, r_wsT])
            bias_sb = A.alloc([128], F32, parts=8)
            r_bsb = S.res()
            B.dma("sp", bias_sb, I["b_bias"][:, :], [], [r_bsb])
            biasT = A.alloc([8], F32)
            r_biasT = S.res()
            bk, r_bk = B.bank()
            B.mm(bk[:, 0:8], bias_sb, B.identf[0:8, 0:8], True, True, [r_bsb, B.r_ident], [r_bk])
            B.cp("dve", biasT, bk[:, 0:8], [], [r_bk, r_biasT])
            esink, r_es = B.load_bc("sp", I["a_sink"][0:1, :], 8)
            B.act(esink, esink, AF.Exp, [r_es], [r_es])
            amask = A.alloc([2, 128], BF16)
            r_am = S.res()
            B.dma("pool", amask, I["k_amask"][:, :, :], [], [r_am])
            kTc = A.alloc([2, 256], BF16, parts=64)
            r_kTc = S.res()
            B.dma("sp", kTc, kT_d.rearrange("h d t -> d h t")[:, 0:2, NTL * 128:NT * 128], [], [r_kTc])
            Vc = A.alloc([2, 2, 65], BF16)
            r_Vc = S.res()
            B.memset("dve", Vc[:, :, :, 64:65], 1.0, [r_Vc])
            load_V(Vc, r_Vc, NTL, 2, 2)
            kTwr = Ring(B, 4, [2, 384], BF16, parts=64)
            Vwr = Ring(B, 5, [3, 2, 65], BF16)
            for vb_, r_ in Vwr.bufs:
                B.memset("dve", vb_[:, :, :, 64:65], 1.0, [r_])
            qTr = Ring(B, 4, [8, 128], BF16, parts=64)
            pTr = Ring(B, 16, [512], BF16)
            etr = Ring(B, 2, [512], BF16)
            mixr = Ring(B, 4, [D], BF16)
            str_ = Ring(B, 4, [8], F32)
            vvr = Ring(B, 5, [512], BF16)
            ur = Ring(B, 5, [512], F32)
            btr = Ring(B, 2, [512], F32)
            rings = {"mixT": Ring(B, 2, [8, 128], BF16), "hin": Ring(B, 3, [D], F32), "et": Ring(B, 2, [D], F32),
                     "hout": Ring(B, 2, [D], F32)}

            def tile_gen(n):
                bg_tick()
                ty = 0 if n < NTL else 1
                qT, r_qT = qTr.next()
                B.dma("sp", qT, qT_d.rearrange("h d t -> d h t")[:, :, n * 128:(n + 1) * 128], [], [r_qT])
                keys = []
                if ty == 0:
                    kt0 = min(max(n - 1, 0), NTL - 3)
                    kTw, r_kTw = kTwr.next()
                    B.dma("sp", kTw, kT_d.rearrange("h d t -> d h t")[:, 0:2, kt0 * 128:(kt0 + 3) * 128], [], [r_kTw])
                    Vw, r_Vw = Vwr.next()
                    load_V(Vw, r_Vw, kt0, 3, 2)
                    for kt, mk in ((n - 1, 0), (n, None), (n + 1, 1)):
                        if 0 <= kt < NTL:
                            s_ = kt - kt0
                            keys.append((kTw, s_, Vw, s_, mk, [r_kTw], [r_Vw]))
                for s_ in range(2):
                    keys.append((kTc, s_, Vc, s_, None, [r_kTc], [r_Vc]))
                vv, r_vv = vvr.next()
                B.dma("sp", vv, vv_d[n * 128:(n + 1) * 128, :], [], [r_vv])
                u_, r_u = ur.next()
                B.dma("sp", u_, u_d[n * 128:(n + 1) * 128, :], [], [r_u])
                yield

                def qk(kv):
                    pts = []
                    for (kTa, ks, Va, vs, mk, rk, rv) in keys:
                        bk, r_bk = B.bank()
                        B.mm(bk.rearrange("p (h q) -> p h q", h=4), kTa[:, kv, ks * 128:(ks + 1) * 128], qT[:, 4 * kv:4 * kv + 4, :],
                             True, True, rk + [r_qT], [r_bk])
                        pT, r_pT = pTr.next()
                        if mk is None:
                            B.act(pT, bk, AF.Exp, [], [r_bk, r_pT], scale=0.125)
                        else:
                            et, r_et = etr.next()
                            B.act(et, bk, AF.Exp, [], [r_bk, r_et], scale=0.125)
                            B.tt("dve", pT.rearrange("p (h q) -> p h q", h=4), et.rearrange("p (h q) -> p h q", h=4),
                                 amask[:, mk, :].unsqueeze(1).to_broadcast([128, 4, 128]), ALU.mult, [r_et, r_am], [r_pT])
                        pts.append((pT, r_pT, Va, vs, rv))
                    return pts

                def pv(pkv, pts, mix, r_mix):
                    ob, r_ob = B.bank()
                    for hh in range(4):
                        for ei, (pT, r_pT, Va, vs, rv) in enumerate(pts):
                            B.mm(ob[:, hh * 65:(hh + 1) * 65], pT[:, hh * 128:(hh + 1) * 128], Va[:, vs, pkv, :],
                                 ei == 0, ei == len(pts) - 1, [r_pT] + rv, [r_ob])
                    ob3 = ob[:, 0:260].rearrange("p (h e) -> p h e", h=4)
                    sd, r_sd = str_.next()
                    B.tt("dve", sd[:, 0:4], ob3[:, :, 64], esink[:, 4 * pkv:4 * pkv + 4], ALU.add, [r_es], [r_ob, r_sd])
                    B.recip(sd[:, 4:8], sd[:, 0:4], [r_sd], [r_sd])
                    B.tt("dve", mix[:, pkv * 256:(pkv + 1) * 256].rearrange("p (h d) -> p h d", h=4), ob3[:, :, 0:64],
                         sd[:, 4:8].unsqueeze(2).to_broadcast([128, 4, 64]), ALU.mult, [r_sd], [r_ob, r_mix])

                pts0 = qk(0)
                yield
                pts1 = qk(1)
                mix, r_mix = mixr.next()
                pv(0, pts0, mix, r_mix)
                yield
                pv(1, pts1, mix, r_mix)
                bk, r_bk = B.bank()
                for g in range(8):
                    B.mm(bk[:, g * 64:(g + 1) * 64], wsT[:, g, :], vv[:, g * 64:(g + 1) * 64], True, True, [r_wsT, r_vv], [r_bk])
                bt, r_bt = btr.next()
                B.tt("dve", bt.rearrange("p (g d) -> p g d", g=8), bk.rearrange("p (g d) -> p g d", g=8),
                     biasT.unsqueeze(2).to_broadcast([128, 8, 64]), ALU.add, [r_biasT], [r_bk, r_bt])
                B.tt("pool", mix[:, 512:1024], bt, u_, ALU.mult, [r_bt, r_u], [r_mix])
                yield
                yield from outproj_residual(mix, r_mix, wout, r_wout, mvs[ty][2], mvs[ty][3], n, rings)

            pipeline(tile_gen, range(NT))

        def phase_odd_prep(li):
            B.phase_begin()
            mvs = load_mod_vecs(li, 3, 4, None, 1, 1.0)
            win = A.alloc([8, 2048], BF16)
            r_win = S.res()
            load_w_bf16("od_w_in", win, r_win, I["od_w_in"], 2048)
            qg_bc, r_qg = B.load_bc("sp", I["d_q_gain"][0:1, :], 64)
            kg_bc, r_kg = B.load_bc("sp", I["d_k_gain"][0:1, :], 64)
            hinr = Ring(B, 6, [D], F32)
            rings = {"sqj": (A.alloc([D], BF16), S.res()), "st": Ring(B, 24, [30], F32),
                     "z1": Ring(B, 2, [D], F32), "ztok": Ring(B, 3, [D], BF16),
                     "sq": Ring(B, 6, [512], F32),
                     "hT": Ring(B, 3, [8, 128], BF16, parts=64)}
            zTr = Ring(B, 3, [8, 128], BF16)
            qrr = Ring(B, 7, [512], F32)
            krr = Ring(B, 7, [512], F32)
            qbr = Ring(B, 8, [512], BF16)
            kbr = Ring(B, 8, [512], BF16)
            vbr = Ring(B, 4, [512], BF16)
            xbr = Ring(B, 4, [512], BF16)

            def tile_gen(gt):
                bg_tick()
                ty = 0 if gt < NTL else 1
                hin, r_hin = hinr.next()
                B.dma("sp", hin, hsrc(gt), [], [r_hin])
                yield
                zt, r_zt = yield from norm_tile_g(hin, r_hin, mvs[ty], rings)
                yield
                zT, r_zT = zTr.next()
                transpose8(zt, r_zt, zT, r_zT)
                yield
                nbs = [0, 1, 2, 3] if ty == 0 else [2, 3]
                bks = {i: B.bank() for i in nbs}
                for k in range(8):
                    for i in nbs:
                        B.mm(bks[i][0], zT[:, k, :], win[:, k, i * 512:(i + 1) * 512], k == 0, k == 7, [r_zT, r_win], [bks[i][1]])
                yield
                items = []
                qb = r_qb = None
                if ty == 0:
                    xb, r_xb = xbr.next()
                    B.cp("act", xb, bks[0][0], [], [bks[0][1], r_xb])
                    qr, r_qr = qrr.next()
                    B.cp("act", qr, bks[1][0], [], [bks[1][1], r_qr])
                    qb, r_qb = qbr.next()
                    items.append(dict(src=qr, r_src=r_qr, nh=8, gain=qg_bc, r_gain=r_qg, dst=qb, r_dst=r_qb))
                kr, r_kr = krr.next()
                B.cp("act", kr, bks[2][0], [], [bks[2][1], r_kr])
                kb, r_kb = kbr.next()
                items.append(dict(src=kr, r_src=r_kr, nh=8, gain=kg_bc, r_gain=r_kg, dst=kb, r_dst=r_kb))
                vb, r_vb = vbr.next()
                B.cp("act", vb, bks[3][0], [], [bks[3][1], r_vb])
                yield
                if ty == 0:
                    B.dma("sp", xp_d[gt * 128:(gt + 1) * 128, :], xb, [r_xb], [])
                B.dma("sp", v_d[gt * 128:(gt + 1) * 128, :], vb, [r_vb], [])
                yield from chains_g(items, rings)
                yield
                if ty == 0:
                    head_transposes(qb, r_qb, 8, qT_d, gt, rings)
                head_transposes(kb, r_kb, 8, kT_d, gt, rings)

            pipeline(tile_gen, range(NT))

        def phase_odd_attn(li):
            B.phase_begin()
            mvs = load_mod_vecs(li, None, None, 5, 1, 1.0, ntypes=1)
            wout = A.alloc([8, D], BF16)
            r_wout = S.res()
            load_w_bf16("od_w_out", wout, r_wout, I["od_w_out"], 1024)
            wpool = A.alloc([4, 128], BF16)
            r_wpool = S.res()
            B.dma("pool", wpool, I["c_w_pool"].rearrange("g c d -> c g d"), [], [r_wpool])
            csc, r_csc = B.load_bc("sp", I["c_scale"][0:1, :], 512)
            band = A.alloc([4, 5, 128], BF16)
            r_band = S.res()
            B.dma("pool", band, I["k_band"][:, :, :, :], [], [r_band], max_dma_last_dim=2048)
            kTp = kT_d.rearrange("(g e) d t -> (e d) g t", e=2)
            qTp = qT_d.rearrange("(g e) d t -> (e d) g t", e=2)
            kTc = A.alloc([4, 256], BF16)
            r_kTc = S.res()
            B.dma("sp", kTc, kTp[:, :, NTL * 128:NT * 128], [], [r_kTc])
            Vc = A.alloc([2, 8, 65], BF16)
            r_Vc = S.res()
            B.memset("dve", Vc[:, :, :, 64:65], 1.0, [r_Vc])
            load_V(Vc, r_Vc, NTL, 2, 8)
            biasr = Ring(B, 1, [8, 7, 128], F32)
            for bb_, r_ in biasr.bufs:
                B.memset("pool", bb_[:, :, 5:7, :], 0.0, [r_])
            kTwr = Ring(B, 4, [4, 640], BF16)
            Vwr = Ring(B, 4, [5, 8, 65], BF16)
            for vb_, r_ in Vwr.bufs:
                B.memset("dve", vb_[:, :, :, 64:65], 1.0, [r_])
            qTr = Ring(B, 4, [4, 128], BF16)
            xpr = Ring(B, 3, [3, 512], BF16)
            ppr = Ring(B, 2, [4, 128], BF16)
            tAr = Ring(B, 3, [512], F32)
            tBr = Ring(B, 3, [384], F32)
            pAr = Ring(B, 14, [512], BF16)
            pBr = Ring(B, 14, [384], BF16)
            mixr = Ring(B, 8, [D], BF16)
            str_ = Ring(B, 4, [8], F32)
            rings = {"mixT": Ring(B, 2, [8, 128], BF16), "hin": Ring(B, 3, [D], F32), "et": Ring(B, 2, [D], F32),
                     "hout": Ring(B, 2, [D], F32)}
            state = {"case": None, "bias": None, "r_bias": None}
            B.nrr = 6

            def case_of(n):
                return 0 if n == 0 else 1 if n == 1 else 3 if n == NTL - 2 else 4 if n == NTL - 1 else 2

            def tile_gen(n):
                bg_tick()
                case = case_of(n)
                if case != state["case"]:
                    bias_, r_bias_ = biasr.next()
                    B.dma("sp", bias_[:, :, 0:5, :], I["k_dbias"][case], [], [r_bias_])
                    state["case"] = case
                    state["bias"] = bias_
                    state["r_bias"] = r_bias_
                bias = state["bias"]
                r_bias = state["r_bias"]
                kt0 = min(max(n - 2, 0), NTL - 5)
                kTw, r_kTw = kTwr.next()
                B.dma("sp", kTw, kTp[:, :, kt0 * 128:(kt0 + 5) * 128], [], [r_kTw])
                Vw, r_Vw = Vwr.next()
                load_V(Vw, r_Vw, kt0, 5, 8)
                qT, r_qT = qTr.next()
                B.dma("sp", qT, qTp[:, :, n * 128:(n + 1) * 128], [], [r_qT])
                xw, r_xw = xpr.next()
                jts = [j for j in (n - 1, n, n + 1) if 0 <= j < NTL]
                j0 = jts[0]
                B.dma("sp", xw[:, 0:len(jts), :], xp_d[j0 * 128:(j0 + len(jts)) * 128, :].rearrange("(w p) f -> p w f", p=128), [], [r_xw])
                yield
                mix, r_mix = mixr.next()
                bk, r_bk = B.bank()
                for g in range(4):
                    for ji, j in enumerate(jts):
                        if j == n - 1:
                            typ = 0
                        elif j == n + 1:
                            typ = 2
                        else:
                            typ = 3 if n == 0 else (4 if n == NTL - 1 else 1)
                        B.mm(bk[:, g * 128:(g + 1) * 128], xw[:, ji, g * 128:(g + 1) * 128], band[:, g, typ, :],
                             ji == 0, ji == len(jts) - 1, [r_xw, r_band], [r_bk])
                pp, r_pp = ppr.next()
                B.cp("act", pp, bk.rearrange("p (g t) -> p g t", g=4), [], [r_bk, r_pp])
                bk2, r_bk2 = B.bank()
                for g in range(4):
                    B.mm(bk2[:, g * 128:(g + 1) * 128], pp[:, g, :], wpool[:, g, :], True, True, [r_pp, r_wpool], [r_bk2])
                B.tt("dve", mix[:, 0:512], bk2, csc, ALU.mult, [r_csc], [r_bk2, r_mix])
                yield

                def scores(hq):
                    res_ = []
                    for hi in range(4):
                        h = hq * 4 + hi
                        bA, r_bA = B.bank()
                        bB, r_bB = B.bank()
                        pl = slice((h % 2) * 64, (h % 2) * 64 + 64)
                        g_ = h // 2
                        for s_ in range(4):
                            B.mm(bA[:, s_ * 128:(s_ + 1) * 128], kTw[pl, g_, s_ * 128:(s_ + 1) * 128], qT[pl, g_, :], True, True, [r_kTw, r_qT], [r_bA])
                        B.mm(bB[:, 0:128], kTw[pl, g_, 512:640], qT[pl, g_, :], True, True, [r_kTw, r_qT], [r_bB])
                        for s_ in range(2):
                            B.mm(bB[:, (1 + s_) * 128:(2 + s_) * 128], kTc[pl, g_, s_ * 128:(s_ + 1) * 128], qT[pl, g_, :], True, True, [r_kTc, r_qT], [r_bB])
                        tA, r_tA = tAr.next()
                        tB, r_tB = tBr.next()
                        B.stt("dve", tA, bA, 0.125, bias[:, h, 0:4, :].rearrange("p s q -> p (s q)"), ALU.mult, ALU.add, [r_bias], [r_bA, r_tA])
                        B.stt("dve", tB, bB[:, 0:384], 0.125, bias[:, h, 4:7, :].rearrange("p s q -> p (s q)"), ALU.mult, ALU.add, [r_bias], [r_bB, r_tB])
                        pA, r_pA = pAr.next()
                        pB, r_pB = pBr.next()
                        B.act(pA, tA, AF.Exp, [r_tA], [r_pA])
                        B.act(pB, tB, AF.Exp, [r_tB], [r_pB])
                        res_.append((h, pA, r_pA, pB, r_pB))
                    return res_

                def pvs(hq, res_):
                    ob, r_ob = B.bank_fixed(6 + hq)
                    for (ph, pA, r_pA, pB, r_pB) in res_:
                        osl = ob[:, (ph % 4) * 65:(ph % 4 + 1) * 65]
                        for s_ in range(7):
                            if s_ < 4:
                                lhs = pA[:, s_ * 128:(s_ + 1) * 128]
                                rp = r_pA
                            else:
                                lhs = pB[:, (s_ - 4) * 128:(s_ - 3) * 128]
                                rp = r_pB
                            if s_ < 5:
                                rhs = Vw[:, s_, ph, :]
                                rv = r_Vw
                            else:
                                rhs = Vc[:, s_ - 5, ph, :]
                                rv = r_Vc
                            B.mm(osl, lhs, rhs, s_ == 0, s_ == 6, [rp, rv], [r_ob])
                    ob3 = ob[:, 0:260].rearrange("p (h e) -> p h e", h=4)
                    sd, r_sd = str_.next()
                    B.recip(sd[:, 0:4], ob3[:, :, 64], [], [r_ob, r_sd])
                    B.tt("dve", mix[:, 512 + hq * 256:512 + (hq + 1) * 256].rearrange("p (h d) -> p h d", h=4), ob3[:, :, 0:64],
                         sd[:, 0:4].unsqueeze(2).to_broadcast([128, 4, 64]), ALU.mult, [r_sd], [r_ob, r_mix])

                r0 = scores(0)
                yield
                r1 = scores(1)
                pvs(0, r0)
                yield
                pvs(1, r1)
                yield
                yield from outproj_residual(mix, r_mix, wout, r_wout, mvs[0][2], mvs[0][3], n, rings)

            pipeline(tile_gen, range(NTL), drain_before=lambda n: case_of(n) != state["case"])
            B.nrr = 8

        plist = [
            lambda: phase_mod(0, 0, 6),
            lambda: phase_ffn(0, 0, NT, src0, hsrc),
            lambda: (phase_mod(0, 6, 18), phase_mod(1, 0, 18, cont=True)),
            lambda: phase_even_prep(0),
            lambda: phase_even_attn(0),
            lambda: phase_ffn(0, 1, NT, hsrc, hsrc),
            lambda: phase_ffn(1, 0, NT, hsrc, hsrc),
            lambda: phase_odd_prep(1),
            lambda: phase_odd_attn(1),
            lambda: phase_ffn(1, 1, NTL, hsrc, osrc),
        ]
        if dbg_phases is not None:
            plist = plist[:dbg_phases]
        for p in plist:
            p()
        S.emit()
        build_program.stats = S.stats
    return nc


def _rope_table():
    t = np.arange(S_LAT)
    row = (t // 64).astype(np.float32)
    col = (t % 64).astype(np.float32)
    m = 16
    inv = (1.0 / (10000.0 ** (np.arange(m, dtype=np.float32) / m))).astype(np.float32)
    ar = row[:, None] * inv[None, :]
    ac = col[:, None] * inv[None, :]
    cos = np.concatenate([np.cos(ar), np.cos(ar), np.cos(ac), np.cos(ac)], axis=1)
    sin = np.concatenate([-np.sin(ar), np.sin(ar), -np.sin(ac), np.sin(ac)], axis=1)
    return np.stack([cos, sin], axis=1).astype(np.float32)


def _amask():
    pj = np.arange(128)[:, None]
    pi = np.arange(128)[None, :]
    prev = (pj >= pi).astype(np.float32)
    nxt = (pj <= pi).astype(np.float32)
    return np.stack([prev, nxt], axis=1)


def _band():
    out = np.zeros((128, 4, 5, 128), np.float32)
    for gi, w in enumerate((2, 4, 8, 16)):
        def mat(n, jn):
            tg = n * 128 + np.arange(128)
            lo = np.clip(tg - w // 2, 0, S_LAT)
            hi = np.clip(tg + w - w // 2, 0, S_LAT)
            cnt = (hi - lo).astype(np.float32)
            jg = jn * 128 + np.arange(128)
            m = ((jg[:, None] >= lo[None, :]) & (jg[:, None] < hi[None, :])).astype(np.float32) / cnt[None, :]
            m = m - (jg[:, None] == tg[None, :]).astype(np.float32)
            return m
        out[:, gi, 0] = mat(5, 4)
        out[:, gi, 1] = mat(5, 5)
        out[:, gi, 2] = mat(5, 6)
        out[:, gi, 3] = mat(0, 0)
        out[:, gi, 4] = mat(NTL - 1, NTL - 1)
    return out


def _dbias(rpb):
    out = np.full((5, 128, 8, 5, 128), NEGB, np.float32)
    for case, n in enumerate((0, 1, 5, NTL - 2, NTL - 1)):
        kt0 = min(max(n - 2, 0), NTL - 5)
        i = np.arange(128)
        r = 2 * n + i // 64
        c = i % 64
        r0 = np.clip(r - 4, 0, 56)
        q0 = np.clip(c - 8, 0, 48)
        for s in range(5):
            kt = kt0 + s
            j = np.arange(128)
            kr = 2 * kt + j // 64
            kc = j % 64
            valid = ((kr[:, None] >= r0[None, :]) & (kr[:, None] < r0[None, :] + 8) &
                     (kc[:, None] >= q0[None, :]) & (kc[:, None] < q0[None, :] + 16))
            ri = np.clip(kr[:, None] - r[None, :] + 7, 0, 14)
            ci = np.clip(kc[:, None] - c[None, :] + 15, 0, 30)
            g = rpb[:, ri, ci]
            g = np.where(valid[None], g, np.float32(NEGB))
            out[case, :, :, s, :] = np.transpose(g, (1, 0, 2))
    return out


_CACHE = {}


def kernel(x, c, ctx, c_ctx, ada_w, ada_b, norm_g, ffn_w_gu, ffn_w_down,
           ev_w_in, ev_w_out, a_q_gain, a_k_gain, a_sink, b_v_gain, b_ws, b_bias,
           od_w_in, od_w_out, c_w_pool, c_scale, d_q_gain, d_k_gain, d_rpb, _dbg_phases=None, _dbg=False):
    f = lambda a: np.ascontiguousarray(np.asarray(a, dtype=np.float32))
    key = (_dbg_phases, _dbg)
    if key not in _CACHE:
        _CACHE[key] = build_program(_dbg_phases, _dbg)
    nc = _CACHE[key]
    shared = {
        "c_ctx": f(c_ctx).reshape(1, D), "ada_w": f(ada_w), "ada_b": f(ada_b), "norm_g": f(norm_g),
        "ffn_w_gu": f(ffn_w_gu), "ffn_w_down": f(ffn_w_down),
        "ev_w_in": f(ev_w_in)[0], "ev_w_out": f(ev_w_out)[0],
        "a_q_gain": f(a_q_gain), "a_k_gain": f(a_k_gain), "a_sink": f(a_sink),
        "b_v_gain": f(b_v_gain), "b_ws": f(b_ws)[0], "b_bias": f(b_bias)[0],
        "od_w_in": f(od_w_in)[0], "od_w_out": f(od_w_out)[0],
        "c_w_pool": f(c_w_pool)[0], "c_scale": f(c_scale),
        "d_q_gain": f(d_q_gain), "d_k_gain": f(d_k_gain),
        "k_ident": np.eye(128, dtype=np.float32), "k_rope": _rope_table(), "k_amask": _amask(),
        "k_band": _band(), "k_dbias": _dbias(f(d_rpb)[0]),
    }
    x = f(x); c = f(c); ctx = f(ctx)
    in_maps = []
    for b in range(8):
        m = dict(shared)
        m["x"] = x[b]
        m["ctx"] = ctx[b]
        m["c"] = c[b].reshape(1, D)
        in_maps.append(m)
    res = run_bass_kernel_spmd(nc, in_maps, core_ids=list(range(8)))
    kernel.last = res
    return np.stack([r["out"] for r in res.results], axis=0)
```

```python
import contextlib
import numpy as np
import concourse.bass as bass
import concourse.mybir as mybir
from concourse.bass_utils import run_bass_kernel_spmd

F32 = mybir.dt.float32
BF16 = mybir.dt.bfloat16
AF = mybir.ActivationFunctionType
ALU = mybir.AluOpType
AX = mybir.AxisListType

D = 1024
S_LAT = 4096
S_CTX = 256
NTL = 32
NT = 34
DFF = 2816
NFF = 22
EPS = 1e-6
NEGB = -30000.0

ENGS = ("sp", "act", "pool", "dve", "pe")
NDMA_SEM = 8


class Res:
    __slots__ = ("name", "last_w", "readers", "gen")

    def __init__(self, name):
        self.name = name
        self.last_w = None
        self.readers = {}
        self.gen = 0

    def bump(self):
        self.gen += 1
        return Ref(self, self.gen)


class Ref:
    __slots__ = ("phys", "gen")

    def __init__(self, phys, gen):
        self.phys = phys
        self.gen = gen


def _norm_res(lst):
    out = []
    for r in lst:
        if isinstance(r, Ref):
            assert r.gen == r.phys.gen, f"stale buffer reference {r.phys.name}"
            r = r.phys
        out.append(r)
    return out


def pipeline(make_gen, items, drain_before=None):
    active = []
    for it in items:
        if drain_before is not None and drain_before(it):
            while active:
                nxt = []
                for g in active:
                    try:
                        next(g)
                        nxt.append(g)
                    except StopIteration:
                        pass
                active = nxt
        nxt = []
        for g in active:
            try:
                next(g)
                nxt.append(g)
            except StopIteration:
                pass
        active = nxt
        g = make_gen(it)
        try:
            next(g)
            active.append(g)
        except StopIteration:
            pass
    while active:
        nxt = []
        for g in active:
            try:
                next(g)
                nxt.append(g)
            except StopIteration:
                pass
        active = nxt


class Op:
    __slots__ = ("eng", "fn", "deps", "dma", "signal", "sem", "val", "prewait", "bg")

    def __init__(self, eng, fn, dma):
        self.eng = eng
        self.fn = fn
        self.dma = dma
        self.deps = []
        self.signal = False
        self.sem = None
        self.val = 0
        self.prewait = None
        self.bg = False


class Sched:
    def __init__(self, nc):
        self.nc = nc
        self.ops = {e: [] for e in ENGS}
        self.bar = {}
        self.nres = 0

    def res(self, name=None):
        self.nres += 1
        return Res(name or f"r{self.nres}")

    def op(self, eng, fn, reads=(), writes=(), dma=False):
        reads = _norm_res(reads)
        writes = _norm_res(writes)
        o = Op(eng, fn, dma)
        deps = {}
        for r in reads:
            if r.last_w is not None:
                deps[id(r.last_w)] = r.last_w
        for r in writes:
            if r.last_w is not None:
                deps[id(r.last_w)] = r.last_w
            for rd in r.readers.values():
                if isinstance(rd, list):
                    for x in rd:
                        deps[id(x)] = x
                else:
                    deps[id(rd)] = rd
        for r in reads:
            if dma:
                r.readers.setdefault(("dma", eng), []).append(o)
            else:
                r.readers[eng] = o
        for r in writes:
            r.last_w = o
            r.readers = {}
        b = self.bar.pop(eng, None)
        if b:
            for x in b:
                deps[id(x)] = x
        dl = []
        for d in deps.values():
            if d is o:
                continue
            if (not dma) and (not d.dma) and d.eng == "pe" and eng == "pe":
                continue
            dl.append(d)
        o.deps = dl
        self.ops[eng].append(o)
        return o

    def dma(self, q, out, in_, reads=(), writes=(), **kw):
        return self.op(q, lambda e: e.dma_start(out=out, in_=in_, **kw), reads, writes, dma=True)

    def barrier(self):
        tails = []
        for e in ENGS:
            ops = self.ops[e]
            for o in reversed(ops):
                if not o.dma:
                    tails.append(o)
                    break
            nd = sum(1 for o in ops if o.dma)
            seen_slots = set()
            idx = nd
            for o in reversed(ops):
                if not o.dma:
                    continue
                idx -= 1
                slot = idx % NDMA_SEM
                if slot in seen_slots or o.bg:
                    continue
                seen_slots.add(slot)
                tails.append(o)
                if len(seen_slots) >= NDMA_SEM:
                    break
        self.bar = {e: list(tails) for e in ENGS}

    def emit(self):
        nc = self.nc
        with contextlib.ExitStack() as st:
            csem = {e: st.enter_context(nc.semaphore(f"c_{e}")) for e in ENGS}
            dsem = {e: [st.enter_context(nc.semaphore(f"d_{e}{i}")) for i in range(NDMA_SEM)]
                    for e in ("sp", "act", "pool")}
            for e in ENGS:
                for o in self.ops[e]:
                    for d in o.deps:
                        d.signal = True
            self.stats = {}
            for e in ENGS:
                cnt = 0
                nd = 0
                for o in self.ops[e]:
                    if o.dma:
                        slot = nd % NDMA_SEM
                        o.sem = dsem[e][slot]
                        o.val = 16 * (nd // NDMA_SEM + 1)
                        if nd >= NDMA_SEM:
                            o.prewait = (dsem[e][slot], 16 * (nd // NDMA_SEM))
                        nd += 1
                    elif o.signal:
                        cnt += 1
                        o.sem = csem[e]
                        o.val = cnt
                self.stats[e] = (len(self.ops[e]), cnt, nd)
            block = st.enter_context(nc.Block())

            def run(eng_name, eng):
                seen = {}
                lastdma = {}
                for o in self.ops[eng_name]:
                    waits = []
                    if o.prewait is not None:
                        waits.append(o.prewait)
                    for d in o.deps:
                        waits.append((d.sem, d.val))
                    for sem, val in waits:
                        k = id(sem)
                        if seen.get(k, 0) >= val:
                            continue
                        seen[k] = val
                        eng.wait_ge(sem, val)
                    inst = o.fn(eng)
                    if o.dma:
                        inst.then_inc(o.sem, 16)
                        lastdma[id(o.sem)] = (o.sem, o.val)
                    elif o.signal:
                        inst.then_inc(o.sem, 1)
                for sem, val in lastdma.values():
                    if seen.get(id(sem), 0) < val:
                        eng.wait_ge(sem, val)

            @block.sync
            def _(e):
                run("sp", e)

            @block.scalar
            def _(e):
                run("act", e)

            @block.gpsimd
            def _(e):
                run("pool", e)

            @block.vector
            def _(e):
                run("dve", e)

            @block.tensor
            def _(e):
                run("pe", e)


class Arena:
    def __init__(self, nc, st, nbytes):
        self.nbytes = nbytes
        self.t = st.enter_context(nc.sbuf_tensor("arena", [128, nbytes // 2], BF16))
        self.off = 0
        self.base = 0

    def alloc(self, free, dtype, parts=128):
        free = list(free)
        n = int(np.prod(free))
        sz = n * (4 if dtype == F32 else 2)
        off = (self.off + 63) // 64 * 64
        assert off + sz <= self.nbytes, f"arena overflow {off + sz} > {self.nbytes}"
        self.off = off + sz
        ap = self.t[0:parts, off // 2: (off + sz) // 2]
        if dtype == F32:
            ap = ap.bitcast(F32)
        if len(free) == 2:
            ap = ap.rearrange("p (a b) -> p a b", a=free[0])
        elif len(free) == 3:
            ap = ap.rearrange("p (a b c) -> p a b c", a=free[0], b=free[1])
        elif len(free) == 4:
            ap = ap.rearrange("p (a b c d) -> p a b c d", a=free[0], b=free[1], c=free[2])
        return ap

    def mark_persistent(self):
        self.base = self.off

    def reset(self):
        self.off = self.base


class Ring:
    def __init__(self, B, n, free, dtype, parts=128):
        self.bufs = [(B.A.alloc(free, dtype, parts), B.S.res()) for _ in range(n)]
        self.i = 0

    def next(self):
        r = self.bufs[self.i % len(self.bufs)]
        self.i += 1
        return r[0], r[1].bump()


class Builder:
    def __init__(self, nc, st, dbg):
        self.nc = nc
        self.st = st
        self.S = Sched(nc)
        self.A = Arena(nc, st, 206 * 1024)
        self.banks = []
        for i in range(8):
            t = st.enter_context(nc.psum_tensor(f"bank{i}", [128, 512], F32))
            self.banks.append((t, self.S.res(f"bank{i}")))
        self.bi = 0
        self.nrr = 8
        self.dbg = dbg

    def bank(self):
        r = self.banks[self.bi % self.nrr]
        self.bi += 1
        return r[0][:], r[1].bump()

    def bank_fixed(self, idx):
        r = self.banks[idx]
        return r[0][:], r[1].bump()

    def dma(self, q, out, in_, reads=(), writes=(), **kw):
        return self.S.dma(q, out, in_, reads, writes, **kw)

    def mm(self, out, lhsT, rhs, start, stop, reads, writes):
        return self.S.op("pe", lambda e: e.matmul(out, lhsT=lhsT, rhs=rhs, start=start, stop=stop), reads, writes)

    def tr(self, out, in_, reads, writes):
        idn = self.ident
        return self.S.op("pe", lambda e: e.transpose(out, in_, idn), list(reads) + [self.r_ident], writes)

    def act(self, out, in_, func, reads, writes, scale=None, bias=None, accum=None):
        kw = {}
        if scale is not None:
            kw["scale"] = scale
        if bias is not None:
            kw["bias"] = bias
        if accum is not None:
            kw["accum_out"] = accum
        return self.S.op("act", lambda e: e.activation(out=out, in_=in_, func=func, **kw), reads, writes)

    def tt(self, eng, out, in0, in1, op, reads, writes):
        return self.S.op(eng, lambda e: e.tensor_tensor(out=out, in0=in0, in1=in1, op=op), reads, writes)

    def ts(self, eng, out, in0, s1, s2, op0, op1, reads, writes):
        if op1 is None:
            return self.S.op(eng, lambda e: e.tensor_scalar(out=out, in0=in0, scalar1=s1, scalar2=None, op0=op0), reads, writes)
        return self.S.op(eng, lambda e: e.tensor_scalar(out=out, in0=in0, scalar1=s1, scalar2=s2, op0=op0, op1=op1), reads, writes)

    def stt(self, eng, out, in0, scalar, in1, op0, op1, reads, writes):
        return self.S.op(eng, lambda e: e.scalar_tensor_tensor(out=out, in0=in0, scalar=scalar, in1=in1, op0=op0, op1=op1), reads, writes)

    def cp(self, eng, out, in_, reads, writes):
        if eng == "act":
            return self.S.op("act", lambda e: e.activation(out=out, in_=in_, func=AF.Copy), reads, writes)
        return self.S.op(eng, lambda e: e.tensor_copy(out=out, in_=in_), reads, writes)

    def recip(self, out, in_, reads, writes):
        return self.S.op("dve", lambda e: e.reciprocal(out=out, in_=in_), reads, writes)

    def memset(self, eng, out, val, writes):
        return self.S.op(eng, lambda e: e.memset(out, val), [], writes)

    def reduce_sum(self, out, in_, reads, writes):
        return self.S.op("dve", lambda e: e.tensor_reduce(out=out, in_=in_, axis=AX.X, op=ALU.add), reads, writes)

    def rstd(self, st, r_st, w, inv_n):
        self.ts("dve", st[:, w:2 * w], st[:, 0:w], inv_n, EPS, ALU.mult, ALU.add, [r_st], [r_st])
        self.act(st[:, w:2 * w], st[:, w:2 * w], AF.Sqrt, [r_st], [r_st])
        self.recip(st[:, 2 * w:3 * w], st[:, w:2 * w], [r_st], [r_st])
        return st[:, 2 * w:3 * w]

    def phase_begin(self):
        self.S.barrier()
        self.A.reset()

    def load_bc(self, q, src_1xn, n, name=None):
        t = self.A.alloc([n], F32)
        r = self.S.res(name)
        self.dma(q, t, src_1xn.partition_broadcast(128), [], [r])
        return t, r


def build_program(dbg_phases=None, dbg=False):
    nc = bass.Bass("TRN2", target_bir_lowering=False)
    I = {}

    def inp(name, shape, dt=F32):
        I[name] = nc.dram_tensor(name, list(shape), dt, kind="ExternalInput").ap()
        return I[name]

    inp("x", [S_LAT, D]); inp("ctx", [S_CTX, D]); inp("c", [1, D]); inp("c_ctx", [1, D])
    inp("ada_w", [2, D, 9 * D]); inp("ada_b", [2, 9 * D]); inp("norm_g", [2, 3, D])
    inp("ffn_w_gu", [2, 2, D, 2 * DFF]); inp("ffn_w_down", [2, 2, DFF, D])
    inp("ev_w_in", [D, 1792]); inp("ev_w_out", [D, D])
    inp("a_q_gain", [1, 64]); inp("a_k_gain", [1, 64]); inp("a_sink", [1, 8])
    inp("b_v_gain", [1, 512]); inp("b_ws", [8, 128, 128]); inp("b_bias", [8, 128])
    inp("od_w_in", [D, 2048]); inp("od_w_out", [D, D])
    inp("c_w_pool", [4, 128, 128]); inp("c_scale", [1, 512])
    inp("d_q_gain", [1, 64]); inp("d_k_gain", [1, 64])
    inp("k_ident", [128, 128]); inp("k_rope", [S_LAT, 2, 64]); inp("k_amask", [128, 2, 128])
    inp("k_band", [128, 4, 5, 128]); inp("k_dbias", [5, 128, 8, 5, 128])
    out = nc.dram_tensor("out", [S_LAT, D], F32, kind="ExternalOutput").ap()
    skind = "ExternalOutput" if dbg else "Internal"

    def scr(name, shape, dt):
        return nc.dram_tensor(name, list(shape), dt, kind=skind).ap()

    hA = scr("hA", [NT * 128, D], F32)
    mod_d = scr("mod_d", [2, 2, 9 * D], F32)
    qT_d = scr("qT_d", [8, 64, NT * 128], BF16)
    kT_d = scr("kT_d", [8, 64, NT * 128], BF16)
    v_d = scr("v_d", [NT * 128, 512], BF16)
    u_d = scr("u_d", [NT * 128, 512], F32)
    vv_d = scr("vv_d", [NT * 128, 512], BF16)
    xp_d = scr("xp_d", [S_LAT, 512], BF16)
    wgc_all = [nc.dram_tensor(f"wgc_d{f}", [NFF // 2, 128, 8 * 2 * 256], BF16, kind="Internal").ap() for f in range(4)]
    wdc_all = [nc.dram_tensor(f"wdc_d{f}", [128, NFF * D], BF16, kind="Internal").ap() for f in range(4)]
    adac_all = [nc.dram_tensor(f"adac_d{l_}", [18, 128, 8 * 512], BF16, kind="Internal").ap() for l_ in range(2)]

    st = contextlib.ExitStack()
    with st:
        B = Builder(nc, st, dbg)
        S, A = B.S, B.A
        B.ident = A.alloc([128], BF16)
        B.r_ident = S.res("ident")
        B.dma("pool", B.ident, I["k_ident"][:, :], [], [B.r_ident])
        B.identf = A.alloc([128], F32)
        B.dma("sp", B.identf, I["k_ident"][:, :], [], [B.r_ident])
        A.mark_persistent()

        bgq = []
        wres = {}

        def bg_add_ffn(f, li, which):
            wgu_ = I["ffn_w_gu"][li, which]
            for cp_ in range(NFF // 2):
                dst4 = wgc_all[f][cp_].rearrange("p (k g n) -> p k g n", k=8, g=2)
                for g_ in range(2):
                    r = S.res()
                    wres[("gu", f, cp_, g_)] = r
                    bgq.append((dst4[:, :, g_, :],
                                wgu_[:, g_ * DFF + cp_ * 256:g_ * DFF + (cp_ + 1) * 256].rearrange("(k p) n -> p k n", p=128), r))
            wd3 = wdc_all[f].rearrange("p (c n) -> p c n", c=NFF)
            for c4 in range(0, NFF, 2):
                r = S.res()
                wres[("wd", f, c4)] = r
                bgq.append((wd3[:, c4:c4 + 2, :],
                            I["ffn_w_down"][li, which, c4 * 128:(c4 + 2) * 128, :].rearrange("(c p) n -> p c n", p=128), r))

        def bg_add_ada(li, nb0=0, nb1=18):
            for nb in range(nb0, nb1):
                r = S.res()
                wres[("ada", li, nb)] = r
                bgq.append((adac_all[li][nb].rearrange("p (k n) -> p k n", k=8),
                            I["ada_w"][li, :, nb * 512:(nb + 1) * 512].rearrange("(k p) n -> p k n", p=128), r))

        def bg_need(pred):
            last = -1
            for i, (_, _, r) in enumerate(bgq):
                if pred(r):
                    last = i
            if last >= 0:
                bg_tick(last + 1)

        def bg_tick(n=1):
            for _ in range(n):
                if not bgq:
                    return
                dst, src, r = bgq.pop(0)
                o = B.dma("pool", dst, src, [], [r])
                o.bg = True

        wcache = {}

        def bg_add_w(name, src, ncols, split):
            t = nc.dram_tensor(f"wc_{name}", [128, 8 * ncols], BF16, kind="Internal").ap()
            rl = []
            w_ = ncols // split
            for i in range(split):
                r = S.res()
                rl.append(r)
                bgq.append((t.rearrange("p (k n) -> p k n", k=8)[:, :, i * w_:(i + 1) * w_],
                            src[:, i * w_:(i + 1) * w_].rearrange("(k p) n -> p k n", p=128), r))
            wcache[name] = (t, rl)

        def load_w_bf16(name, dst, r_dst, src, ncols):
            if name in wcache:
                t, rl = wcache[name]
                ids = {id(r) for r in rl}
                bg_need(lambda r: id(r) in ids)
                B.dma("sp", dst.rearrange("p k n -> p (k n)"), t, rl, [r_dst])
            else:
                step = 1024 if ncols > 1792 else ncols
                for c0 in range(0, ncols, step):
                    B.dma("pool", dst[:, :, c0:c0 + step], src[:, c0:c0 + step].rearrange("(k p) n -> p k n", p=128), [], [r_dst])

        bg_add_ada(0, 6, 18)
        bg_add_w("ev_w_in", I["ev_w_in"], 1792, 2)
        bg_add_w("ev_w_out", I["ev_w_out"], 1024, 1)
        bg_add_ada(1)
        bg_add_ffn(1, 0, 1)
        bg_add_ffn(2, 1, 0)
        bg_add_w("od_w_in", I["od_w_in"], 2048, 2)
        bg_add_w("od_w_out", I["od_w_out"], 1024, 1)
        bg_add_ffn(3, 1, 1)

        def src0(gt):
            if gt < NTL:
                return I["x"][gt * 128:(gt + 1) * 128, :]
            return I["ctx"][(gt - NTL) * 128:(gt - NTL + 1) * 128, :]

        def hsrc(gt):
            return hA[gt * 128:(gt + 1) * 128, :]

        def osrc(gt):
            return out[gt * 128:(gt + 1) * 128, :]

        phases = []

        def phase_mod(specs):
            B.phase_begin()
            for (li, nb0, nb1) in specs:
                mine_ = {id(r) for k_, r in wres.items() if k_[0] == "ada" and k_[1] == li and nb0 <= k_[2] < nb1}
                bg_need(lambda r: id(r) in mine_)
            cc = A.alloc([2, 128], F32, parts=8)
            r_cc = S.res()
            B.dma("sp", cc[:, 0, :], I["c"][0, :].rearrange("(k p) -> k p", p=128), [], [r_cc])
            B.dma("sp", cc[:, 1, :], I["c_ctx"][0, :].rearrange("(k p) -> k p", p=128), [], [r_cc])
            ccb = A.alloc([2, 128], BF16, parts=8)
            r_ccb = S.res()
            B.act(ccb, cc, AF.Silu, [r_cc], [r_ccb])
            cs = A.alloc([8, 2], BF16)
            r_cs = S.res()
            bk, r_bk = B.bank()
            bkb = bk.bitcast(BF16)
            for j in range(2):
                B.S.op("pe", lambda e, j=j: e.transpose(bkb[:, j * 8:(j + 1) * 8], ccb[:, j, :], B.ident[0:8, 0:8]), [r_ccb, B.r_ident], [r_bk])
            B.cp("dve", cs, bkb[:, 0:16].rearrange("p (j k) -> p k j", j=2), [], [r_bk, r_cs])
            wr = Ring(B, 8, [8, 512], BF16)
            for (li, nb0, nb1) in specs:
                ncol = (nb1 - nb0) * 512
                adab = A.alloc([ncol], F32, parts=2)
                r_adab = S.res()
                B.dma("sp", adab, I["ada_b"][li:li + 1, nb0 * 512:nb1 * 512].partition_broadcast(2), [], [r_adab])
                msb = A.alloc([ncol], F32, parts=2)
                r_msb = S.res()
                for nb in range(nb0, nb1):
                    w, r_w = wr.next()
                    if ("ada", li, nb) in wres:
                        B.dma("sp", w.rearrange("p k n -> p (k n)"), adac_all[li][nb], [wres[("ada", li, nb)]], [r_w])
                    else:
                        B.dma("pool", w, I["ada_w"][li, :, nb * 512:(nb + 1) * 512].rearrange("(k p) n -> p k n", p=128), [], [r_w])
                    bk, r_bk = B.bank()
                    for k in range(8):
                        B.mm(bk[0:2, :], cs[:, k, :], w[:, k, :], k == 0, k == 7, [r_cs, r_w], [r_bk])
                    o0 = (nb - nb0) * 512
                    B.tt("dve", msb[:, o0:o0 + 512], bk[0:2, :], adab[:, o0:o0 + 512], ALU.add, [r_adab], [r_bk, r_msb])
                B.dma("sp", mod_d[li, :, nb0 * 512:nb1 * 512], msb, [r_msb], [])

        def load_mod_vecs(li, j_shift, j_scale, j_gate, gi, gate_mul, ntypes=2):
            outl = []
            gbc, r_g = (None, None)
            if j_scale is not None:
                gbc, r_g = B.load_bc("sp", I["norm_g"][li, gi:gi + 1, :], D)
            for ty in range(ntypes):
                r = S.res()
                sh = Gm = gt_ = None
                if j_shift is not None:
                    sh = A.alloc([D], F32)
                    B.dma("sp", sh, mod_d[li, ty:ty + 1, j_shift * D:(j_shift + 1) * D].partition_broadcast(128), [], [r])
                if j_scale is not None:
                    Gm = A.alloc([D], F32)
                    B.dma("sp", Gm, mod_d[li, ty:ty + 1, j_scale * D:(j_scale + 1) * D].partition_broadcast(128), [], [r])
                    B.stt("dve", Gm, Gm, 1.0, gbc, ALU.add, ALU.mult, [r_g, r], [r])
                if j_gate is not None:
                    gt_ = A.alloc([D], F32)
                    B.dma("sp", gt_, mod_d[li, ty:ty + 1, j_gate * D:(j_gate + 1) * D].partition_broadcast(128), [], [r])
                    if gate_mul != 1.0:
                        B.ts("dve", gt_, gt_, gate_mul, None, ALU.mult, None, [r], [r])
                outl.append((sh, Gm, gt_, r))
            return outl

        def norm_tile(hin, r_hin, mv, rings):
            sh, Gm, _, r_mv = mv
            sqj, r_sqj = rings["sqj"]
            st_, r_st = rings["st"].next()
            B.memset("dve", st_[:, 0:1], 0.0, [r_st])
            B.act(sqj, hin, AF.Square, [r_hin, r_st], [r_sqj, r_st], accum=st_[:, 0:1])
            rs = B.rstd(st_, r_st, 1, 1.0 / D)
            z1, r_z1 = rings["z1"].next()
            B.stt("dve", z1, hin, rs, Gm, ALU.mult, ALU.mult, [r_hin, r_st, r_mv], [r_z1])
            zt, r_zt = rings["ztok"].next()
            B.tt("pool", zt, z1, sh, ALU.add, [r_z1, r_mv], [r_zt])
            return zt, r_zt

        def norm_tile_g(hin, r_hin, mv, rings):
            sh, Gm, _, r_mv = mv
            sqj, r_sqj = rings["sqj"]
            st_, r_st = rings["st"].next()
            B.memset("dve", st_[:, 0:1], 0.0, [r_st])
            B.act(sqj, hin, AF.Square, [r_hin, r_st], [r_sqj, r_st], accum=st_[:, 0:1])
            yield
            B.ts("dve", st_[:, 1:2], st_[:, 0:1], 1.0 / D, EPS, ALU.mult, ALU.add, [r_st], [r_st])
            yield
            B.act(st_[:, 1:2], st_[:, 1:2], AF.Sqrt, [r_st], [r_st])
            yield
            B.recip(st_[:, 2:3], st_[:, 1:2], [r_st], [r_st])
            z1, r_z1 = rings["z1"].next()
            B.stt("dve", z1, hin, st_[:, 2:3], Gm, ALU.mult, ALU.mult, [r_hin, r_st, r_mv], [r_z1])
            yield
            zt, r_zt = rings["ztok"].next()
            B.tt("pool", zt, z1, sh, ALU.add, [r_z1, r_mv], [r_zt])
            return zt, r_zt

        def chains_g(items, rings):
            sts = []
            for it in items:
                w = it["nh"] * 64
                sq, r_sq = rings["sq"].next()
                B.act(sq[:, 0:w], it["src"][:, 0:w], AF.Square, [it["r_src"]], [r_sq])
                it["sq"], it["r_sq"] = sq, r_sq
            yield
            for it in items:
                nh = it["nh"]
                w = nh * 64
                st_, r_st = rings["st"].next()
                B.reduce_sum(st_[:, 0:nh], it["sq"][:, 0:w].rearrange("p (h d) -> p h d", h=nh), [it["r_sq"]], [r_st])
                B.ts("dve", st_[:, nh:2 * nh], st_[:, 0:nh], 1.0 / 64, EPS, ALU.mult, ALU.add, [r_st], [r_st])
                it["st"], it["r_st"] = st_, r_st
            yield
            for it in items:
                nh = it["nh"]
                B.act(it["st"][:, nh:2 * nh], it["st"][:, nh:2 * nh], AF.Sqrt, [it["r_st"]], [it["r_st"]])
            yield
            for it in items:
                nh = it["nh"]
                w = nh * 64
                st_ = it["st"]
                B.recip(st_[:, 2 * nh:3 * nh], st_[:, nh:2 * nh], [it["r_st"]], [it["r_st"]])
                x3 = it["src"][:, 0:w].rearrange("p (h d) -> p h d", h=nh)
                B.tt("dve", x3, x3, st_[:, 2 * nh:3 * nh].unsqueeze(2).to_broadcast([128, nh, 64]), ALU.mult,
                     [it["r_st"], it["r_src"]], [it["r_src"]])
            yield
            for it in items:
                nh = it["nh"]
                w = nh * 64
                if it.get("gain_full"):
                    B.tt("pool", it["dst"], it["src"][:, 0:w], it["gain"], ALU.mult, [it["r_gain"], it["r_src"]], [it["r_dst"]])
                elif it.get("dst") is not None:
                    B.tt("pool", it["dst"].rearrange("p (h d) -> p h d", h=nh), it["src"][:, 0:w].rearrange("p (h d) -> p h d", h=nh),
                         it["gain"].unsqueeze(1).to_broadcast([128, nh, 64]), ALU.mult, [it["r_gain"], it["r_src"]], [it["r_dst"]])
                else:
                    x3 = it["src"][:, 0:w].rearrange("p (h d) -> p h d", h=nh)
                    B.tt("pool", x3, x3, it["gain"].unsqueeze(1).to_broadcast([128, nh, 64]), ALU.mult,
                         [it["r_gain"], it["r_src"]], [it["r_src"]])

        def rope_g(items, ropt, r_ropt, rings):
            tmp = []
            for (qn, r_qn, nh, outb, r_out) in items:
                w = nh * 64
                a_, r_a = rings["ra"].next()
                q3 = qn[:, 0:w].rearrange("p (h d) -> p h d", h=nh)
                a3 = a_[:, 0:w].rearrange("p (h d) -> p h d", h=nh)
                B.tt("pool", a3, q3, ropt[:, 0, :].unsqueeze(1).to_broadcast([128, nh, 64]), ALU.mult, [r_qn, r_ropt], [r_a])
                b_, r_b = rings["rb"].next()
                q5 = qn[:, 0:w].rearrange("p (h a s d) -> p h a s d", h=nh, a=2, s=2)
                b5 = b_[:, 0:w].rearrange("p (h a s d) -> p h a s d", h=nh, a=2, s=2)
                s4 = ropt[:, 1, :].rearrange("p (a s d) -> p a s d", a=2, s=2)
                for ax in range(2):
                    for s_ in range(2):
                        B.tt("dve", b5[:, :, ax, s_, :], q5[:, :, ax, 1 - s_, :],
                             s4[:, ax, s_, :].unsqueeze(1).to_broadcast([128, nh, 16]), ALU.mult, [r_qn, r_ropt], [r_b])
                tmp.append((a_, r_a, b_, r_b))
            yield
            for (qn, r_qn, nh, outb, r_out), (a_, r_a, b_, r_b) in zip(items, tmp):
                w = nh * 64
                B.tt("dve", outb[:, 0:w], a_[:, 0:w], b_[:, 0:w], ALU.add, [r_a, r_b], [r_out])

        def transpose8(src, r_src, dst3, r_dst, eng="act"):
            bk, r_bk = B.bank()
            bkb = bk.bitcast(BF16)
            for k in range(8):
                B.tr(bkb[:, k * 128:(k + 1) * 128], src[:, k * 128:(k + 1) * 128], [r_src], [r_bk])
            B.cp(eng, dst3, bkb.rearrange("p (k t) -> p k t", k=8), [], [r_bk, r_dst])

        def phase_ffn(li, which, ntiles, srcf, dstf):
            B.phase_begin()
            f = li * 2 + which
            cached = ("wd", f, 0) in wres
            if cached:
                mine = {id(r) for k_, r in wres.items() if k_[0] in ("gu", "wd") and k_[1] == f}
                bg_need(lambda r: id(r) in mine)
            wgc_d = wgc_all[f]
            j0 = 0 if which == 0 else 6
            gi = 0 if which == 0 else 2
            mvs = load_mod_vecs(li, j0, j0 + 1, j0 + 2, gi, 0.5, ntypes=2 if ntiles > NTL else 1)
            wd = A.alloc([NFF, D], BF16)
            r_wd = S.res()

            def load_wd(c4):
                if cached:
                    B.dma("sp", wd[:, c4:c4 + 2, :], wdc_all[f].rearrange("p (c n) -> p c n", c=NFF)[:, c4:c4 + 2, :],
                          [wres[("wd", f, c4)]], [r_wd])
                else:
                    B.dma("pool", wd[:, c4:c4 + 2, :],
                          I["ffn_w_down"][li, which, c4 * 128:(c4 + 2) * 128, :].rearrange("(c p) n -> p c n", p=128), [], [r_wd])
            if ntiles == NT:
                groups = [list(range(0, 9)), list(range(9, 18)), list(range(18, 26)), list(range(26, 34))]
            else:
                groups = [list(range(g * 8, g * 8 + 8)) for g in range(4)]
            wc_res = [S.res() for _ in range(NFF // 2)]
            GM = max(len(g) for g in groups)
            zTs = [A.alloc([8, GM * 128], BF16) for _ in range(2)]
            zress = [[S.res() for _ in range(GM)] for _ in range(2)]
            actT = A.alloc([NFF, GM * 128], BF16)
            wgr = Ring(B, 2, [8, 2, 256], BF16)
            hinr = Ring(B, 2, [D], F32)
            rings = {"sqj": (A.alloc([D], BF16), S.res()), "st": Ring(B, 2, [3], F32),
                     "z1": Ring(B, 1, [D], F32), "ztok": Ring(B, 2, [D], BF16)}
            stmpr = Ring(B, 2, [512], F32)
            etmpr = Ring(B, 1, [D], F32)
            houtr = Ring(B, 1, [D], F32)
            wgu = I["ffn_w_gu"][li, which]

            def stage1_a(gidx, lt, gt):
                ty = 0 if gt < NTL else 1
                hin, r_hin = hinr.next()
                B.dma("sp", hin, srcf(gt), [], [r_hin])
                zt, r_zt = norm_tile(hin, r_hin, mvs[ty], rings)
                return (gidx, lt, zt, r_zt)

            def stage1_b(st1):
                gidx, lt, zt, r_zt = st1
                transpose8(zt, r_zt, zTs[gidx % 2][:, :, lt * 128:(lt + 1) * 128], zress[gidx % 2][lt])

            def stage1_tile(gidx, lt, gt):
                stage1_b(stage1_a(gidx, lt, gt))

            def issue_wg(gidx, cp_, wg, r_wg):
                if cached:
                    B.dma("sp", wg.rearrange("p k g n -> p (k g n)"), wgc_d[cp_],
                          [wres[("gu", f, cp_, 0)], wres[("gu", f, cp_, 1)]], [r_wg])
                elif gidx == 0:
                    B.dma("pool", wg[:, :, 0, :], wgu[:, cp_ * 256:(cp_ + 1) * 256].rearrange("(k p) n -> p k n", p=128), [], [r_wg])
                    B.dma("pool", wg[:, :, 1, :], wgu[:, DFF + cp_ * 256:DFF + (cp_ + 1) * 256].rearrange("(k p) n -> p k n", p=128), [], [r_wg])
                else:
                    B.dma("sp", wg.rearrange("p k g n -> p (k g n)"), wgc_d[cp_], [wc_res[cp_]], [r_wg])

            pre_wg = []
            for cp_ in range(2):
                wg, r_wg = wgr.next()
                issue_wg(0, cp_, wg, r_wg)
                pre_wg.append((wg, r_wg))
            lag0 = None
            for lt, gt in enumerate(groups[0]):
                cur0 = stage1_a(0, lt, gt)
                if lag0 is not None:
                    stage1_b(lag0)
                lag0 = cur0
            stage1_b(lag0)
            for gidx, grp in enumerate(groups):
                zT = zTs[gidx % 2]
                zres = zress[gidx % 2]
                T = len(grp) * 128
                nblk = (T + 511) // 512
                bs = T // nblk
                blocks = [(i * bs, (i + 1) * bs if i < nblk - 1 else T) for i in range(nblk)]
                ares = [S.res() for _ in blocks]
                pending = list(enumerate(groups[gidx + 1])) if gidx + 1 < len(groups) else []
                npend = len(pending)
                nunits = NFF * nblk
                unit = 0
                emitted = 0
                lagged = None
                park = None
                for cp_ in range(NFF // 2):
                    if gidx == 0 and cp_ < len(pre_wg):
                        wg, r_wg = pre_wg[cp_]
                    else:
                        wg, r_wg = wgr.next()
                        issue_wg(gidx, cp_, wg, r_wg)
                    if (not cached) and gidx == 0:
                        if park is not None:
                            B.dma("pool", wgc_d[park[0]], park[1].rearrange("p k g n -> p (k g n)"), [park[2]], [wc_res[park[0]]])
                        park = (cp_, wg, r_wg)
                    if gidx == 0:
                        load_wd(cp_ * 2)
                    for ci in range(2):
                        c = cp_ * 2 + ci
                        for bi_, (a, b_) in enumerate(blocks):
                            n = b_ - a
                            zr = [zres[t] for t in range(a // 128, (b_ - 1) // 128 + 1)]
                            bg, r_bg = B.bank()
                            bu, r_bu = B.bank()
                            for k in range(8):
                                B.mm(bg[:, 0:n], wg[:, k, 0, ci * 128:(ci + 1) * 128], zT[:, k, a:b_], k == 0, k == 7, [r_wg] + zr, [r_bg])
                            for k in range(8):
                                B.mm(bu[:, 0:n], wg[:, k, 1, ci * 128:(ci + 1) * 128], zT[:, k, a:b_], k == 0, k == 7, [r_wg] + zr, [r_bu])
                            stp, r_stp = stmpr.next()
                            B.act(stp[:, 0:n], bg[:, 0:n], AF.Silu, [], [r_bg, r_stp])
                            B.tt("dve", actT[:, c, a:b_], stp[:, 0:n], bu[:, 0:n], ALU.mult, [r_stp], [r_bu, ares[bi_]])
                            unit += 1
                            if unit % 4 == 0 and (cached or gidx > 0):
                                bg_tick()
                            while emitted < npend and unit * npend >= (emitted + 1) * int(nunits * 0.7):
                                if lagged is not None:
                                    stage1_b(lagged)
                                lt2, gt2 = pending[emitted]
                                lagged = stage1_a(gidx + 1, lt2, gt2)
                                emitted += 1
                            if emitted == npend and lagged is not None and unit * npend >= (emitted + 1) * int(nunits * 0.7):
                                stage1_b(lagged)
                                lagged = None
                if park is not None:
                    B.dma("pool", wgc_d[park[0]], park[1].rearrange("p k g n -> p (k g n)"), [park[2]], [wc_res[park[0]]])
                    park = None
                while emitted < npend:
                    if lagged is not None:
                        stage1_b(lagged)
                    lt2, gt2 = pending[emitted]
                    lagged = stage1_a(gidx + 1, lt2, gt2)
                    emitted += 1
                if lagged is not None:
                    stage1_b(lagged)
                    lagged = None
                for lt, gt in enumerate(grp):
                    ty = 0 if gt < NTL else 1
                    gate = mvs[ty][2]
                    r_mv = mvs[ty][3]
                    ar = [ares[i] for i, (a, b_) in enumerate(blocks) if a < (lt + 1) * 128 and b_ > lt * 128]
                    b0, r_b0 = B.bank()
                    b1, r_b1 = B.bank()
                    bb = [(b0, r_b0), (b1, r_b1)]
                    for c in range(NFF):
                        for hf in range(2):
                            B.mm(bb[hf][0], actT[:, c, lt * 128:(lt + 1) * 128], wd[:, c, hf * 512:(hf + 1) * 512],
                                 c == 0, c == NFF - 1, ar + [r_wd], [bb[hf][1]])
                    hin, r_hin = hinr.next()
                    B.dma("sp", hin, srcf(gt), [], [r_hin])
                    et, r_et = etmpr.next()
                    for hf in range(2):
                        B.tt("dve", et[:, hf * 512:(hf + 1) * 512], bb[hf][0], gate[:, hf * 512:(hf + 1) * 512], ALU.mult,
                             [r_mv], [bb[hf][1], r_et])
                    ho, r_ho = houtr.next()
                    B.tt("pool", ho, et, hin, ALU.add, [r_et, r_hin], [r_ho])
                    B.dma("pool", dstf(gt), ho, [r_ho], [])

        def head_norm(bank_ap, r_bank, nh, gain_bc, r_gain, rings, name):
            sq, r_sq = rings["sq"].next()
            w = nh * 64
            B.act(sq[:, 0:w], bank_ap, AF.Square, [], [r_bank, r_sq])
            st_, r_st = rings["st"].next()
            B.reduce_sum(st_[:, 0:nh], sq[:, 0:w].rearrange("p (h d) -> p h d", h=nh), [r_sq], [r_st])
            rs = B.rstd(st_, r_st, nh, 1.0 / 64)
            qn, r_qn = rings[name].next()
            qn3 = qn[:, 0:w].rearrange("p (h d) -> p h d", h=nh)
            B.tt("dve", qn3, bank_ap.rearrange("p (h d) -> p h d", h=nh), rs.unsqueeze(2).to_broadcast([128, nh, 64]), ALU.mult,
                 [r_st], [r_bank, r_qn])
            B.tt("pool", qn3, qn3, gain_bc.unsqueeze(1).to_broadcast([128, nh, 64]), ALU.mult, [r_gain, r_qn], [r_qn])
            return qn, r_qn

        def rope(qn, r_qn, nh, ropt, r_ropt, outb, r_out, rings):
            w = nh * 64
            a_, r_a = rings["ra"].next()
            b_, r_b = rings["rb"].next()
            q3 = qn[:, 0:w].rearrange("p (h d) -> p h d", h=nh)
            a3 = a_[:, 0:w].rearrange("p (h d) -> p h d", h=nh)
            B.tt("pool", a3, q3, ropt[:, 0, :].unsqueeze(1).to_broadcast([128, nh, 64]), ALU.mult, [r_qn, r_ropt], [r_a])
            q5 = qn[:, 0:w].rearrange("p (h a s d) -> p h a s d", h=nh, a=2, s=2)
            b5 = b_[:, 0:w].rearrange("p (h a s d) -> p h a s d", h=nh, a=2, s=2)
            s4 = ropt[:, 1, :].rearrange("p (a s d) -> p a s d", a=2, s=2)
            for ax in range(2):
                for s in range(2):
                    B.tt("dve", b5[:, :, ax, s, :], q5[:, :, ax, 1 - s, :],
                         s4[:, ax, s, :].unsqueeze(1).to_broadcast([128, nh, 16]), ALU.mult, [r_qn, r_ropt], [r_b])
            B.tt("dve", outb[:, 0:w], a_[:, 0:w], b_[:, 0:w], ALU.add, [r_a, r_b], [r_out])

        def head_transposes(src, r_src, nh, dst_dram, gt, rings):
            bk, r_bk = B.bank()
            bkb = bk.bitcast(BF16)
            for h in range(nh):
                B.tr(bkb[0:64, h * 128:(h + 1) * 128], src[:, h * 64:(h + 1) * 64], [r_src], [r_bk])
            ts_, r_ts = rings["hT"].next()
            B.cp("act", ts_[:, 0:nh, :], bkb[0:64, 0:nh * 128].rearrange("p (h t) -> p h t", h=nh), [], [r_bk, r_ts])
            B.dma("sp", dst_dram.rearrange("h d t -> d h t")[:, 0:nh, gt * 128:(gt + 1) * 128], ts_[:, 0:nh, :], [r_ts], [])

        def phase_even_prep(li):
            B.phase_begin()
            mvs = load_mod_vecs(li, 3, 4, None, 1, 1.0)
            win = A.alloc([8, 1792], BF16)
            r_win = S.res()
            load_w_bf16("ev_w_in", win, r_win, I["ev_w_in"], 1792)
            qg_bc, r_qg = B.load_bc("sp", I["a_q_gain"][0:1, :], 64)
            kg_bc, r_kg = B.load_bc("sp", I["a_k_gain"][0:1, :], 64)
            vg_bc, r_vg = B.load_bc("sp", I["b_v_gain"][0:1, :], 512)
            hinr = Ring(B, 6, [D], F32)
            rings = {"sqj": (A.alloc([D], BF16), S.res()), "st": Ring(B, 24, [30], F32),
                     "z1": Ring(B, 2, [D], F32), "ztok": Ring(B, 3, [D], BF16),
                     "sq": Ring(B, 6, [512], F32),
                     "ra": Ring(B, 3, [512], F32), "rb": Ring(B, 3, [512], F32), "hT": Ring(B, 3, [8, 128], BF16, parts=64)}
            zTr = Ring(B, 3, [8, 128], BF16)
            ropr = Ring(B, 14, [2, 64], F32)
            qrr = Ring(B, 6, [512], F32)
            krr = Ring(B, 6, [128], F32)
            gvr = Ring(B, 6, [512], F32)
            qbr = Ring(B, 9, [512], BF16)
            kbr = Ring(B, 9, [128], BF16)
            vbr = Ring(B, 4, [128], BF16)
            ur = Ring(B, 3, [512], F32)
            vvr = Ring(B, 8, [512], BF16)
            nsl = [(0, 512), (512, 768), (768, 1280), (1280, 1792)]

            def tile_gen(gt):
                bg_tick()
                ty = 0 if gt < NTL else 1
                hin, r_hin = hinr.next()
                B.dma("sp", hin, hsrc(gt), [], [r_hin])
                if ty == 0:
                    rt, r_rt = ropr.next()
                    B.dma("sp", rt, I["k_rope"][gt * 128:(gt + 1) * 128, :, :], [], [r_rt])
                yield
                zt, r_zt = yield from norm_tile_g(hin, r_hin, mvs[ty], rings)
                yield
                zT, r_zT = zTr.next()
                transpose8(zt, r_zt, zT, r_zT)
                yield
                bks = [B.bank() for _ in range(4)]
                for k in range(8):
                    for i, (n0, n1) in enumerate(nsl):
                        B.mm(bks[i][0][:, 0:n1 - n0], zT[:, k, :], win[:, k, n0:n1], k == 0, k == 7, [r_zT, r_win], [bks[i][1]])
                (bq, r_bq), (bkv, r_bkv), (bbu, r_bbu), (bbv, r_bbv) = bks
                yield
                qr, r_qr = qrr.next()
                B.cp("act", qr, bq, [], [r_bq, r_qr])
                kr, r_kr = krr.next()
                B.cp("act", kr, bkv[:, 0:128], [], [r_bkv, r_kr])
                vb, r_vb = vbr.next()
                B.cp("act", vb, bkv[:, 128:256], [], [r_bkv, r_vb])
                u_, r_u = ur.next()
                B.act(u_, bbu, AF.Gelu_apprx_tanh, [], [r_bbu, r_u])
                gv, r_gv = gvr.next()
                B.act(gv, bbv, AF.Gelu_apprx_tanh, [], [r_bbv, r_gv])
                yield
                B.dma("sp", v_d[gt * 128:(gt + 1) * 128, 0:128], vb, [r_vb], [])
                B.dma("sp", u_d[gt * 128:(gt + 1) * 128, :], u_, [r_u], [])
                vv, r_vv = vvr.next()
                items = [dict(src=qr, r_src=r_qr, nh=8, gain=qg_bc, r_gain=r_qg),
                         dict(src=kr, r_src=r_kr, nh=2, gain=kg_bc, r_gain=r_kg),
                         dict(src=gv, r_src=r_gv, nh=8, gain=vg_bc, r_gain=r_vg, gain_full=True, dst=vv, r_dst=r_vv)]
                qb, r_qb = qbr.next()
                kb, r_kb = kbr.next()
                if ty == 1:
                    items[0]["dst"], items[0]["r_dst"] = qb, r_qb
                    items[1]["dst"], items[1]["r_dst"] = kb, r_kb
                yield from chains_g(items, rings)
                yield
                B.dma("sp", vv_d[gt * 128:(gt + 1) * 128, :], vv, [r_vv], [])
                if ty == 0:
                    yield from rope_g([(qr, r_qr, 8, qb, r_qb), (kr, r_kr, 2, kb, r_kb)], rt, r_rt, rings)
                    yield
                head_transposes(qb, r_qb, 8, qT_d, gt, rings)
                head_transposes(kb, r_kb, 2, kT_d, gt, rings)

            pipeline(tile_gen, range(NT))

        def outproj_residual(mix, r_mix, wout, r_wout, gate, r_gate, gt, rings):
            hin, r_hin = rings["hin"].next()
            B.dma("sp", hin, hsrc(gt), [], [r_hin])
            mT, r_mT = rings["mixT"].next()
            transpose8(mix, r_mix, mT, r_mT)
            yield
            b0, r_b0 = B.bank()
            b1, r_b1 = B.bank()
            bb = [(b0, r_b0), (b1, r_b1)]
            for k in range(8):
                for hf in range(2):
                    B.mm(bb[hf][0], mT[:, k, :], wout[:, k, hf * 512:(hf + 1) * 512], k == 0, k == 7, [r_mT, r_wout], [bb[hf][1]])
            et, r_et = rings["et"].next()
            for hf in range(2):
                B.tt("dve", et[:, hf * 512:(hf + 1) * 512], bb[hf][0], gate[:, hf * 512:(hf + 1) * 512], ALU.mult,
                     [r_gate], [bb[hf][1], r_et])
            yield
            ho, r_ho = rings["hout"].next()
            B.tt("pool", ho, et, hin, ALU.add, [r_et, r_hin], [r_ho])
            yield
            B.dma("sp", hsrc(gt), ho, [r_ho], [])

        def load_V(dst, r_dst, kt0, nw, nk):
            for w in range(nw):
                B.dma("sp", dst[:, w, :, 0:64],
                      v_d[(kt0 + w) * 128:(kt0 + w + 1) * 128, 0:nk * 64].rearrange("p (k d) -> p k d", k=nk), [], [r_dst])

        def phase_even_attn(li):
            B.phase_begin()
            mvs = load_mod_vecs(li, None, None, 5, 1, 1.0)
            wout = A.alloc([8, D], BF16)
            r_wout = S.res()
            load_w_bf16("ev_w_out", wout, r_wout, I["ev_w_out"], 1024)
            wsn = A.alloc([8, 128], BF16)
            r_wsn = S.res()
            B.dma("pool", wsn, I["b_ws"].rearrange("g i j -> i g j"), [], [r_wsn])
            wsT = A.alloc([8, 128], BF16)
            r_wsT = S.res()
            bk, r_bk = B.bank()
            bkb = bk.bitcast(BF16)
            for g in range(8):
                B.tr(bkb[:, g * 128:(g + 1) * 128], wsn[:, g, :], [r_wsn], [r_bk])
            B.cp("act", wsT, bkb.rearrange("p (g t) -> p g t", g=8), [], [r_bk, r_wsT])
            bias_sb = A.alloc([128], F32, parts=8)
            r_bsb = S.res()
            B.dma("sp", bias_sb, I["b_bias"][:, :], [], [r_bsb])
            biasT = A.alloc([8], F32)
            r_biasT = S.res()
            bk, r_bk = B.bank()
            B.mm(bk[:, 0:8], bias_sb, B.identf[0:8, 0:8], True, True, [r_bsb, B.r_ident], [r_bk])
            B.cp("dve", biasT, bk[:, 0:8], [], [r_bk, r_biasT])
            esink, r_es = B.load_bc("sp", I["a_sink"][0:1, :], 8)
            B.act(esink, esink, AF.Exp, [r_es], [r_es])
            amask = A.alloc([2, 128], BF16)
            r_am = S.res()
            B.dma("pool", amask, I["k_amask"][:, :, :], [], [r_am])
            kTc = A.alloc([2, 256], BF16, parts=64)
            r_kTc = S.res()
            B.dma("sp", kTc, kT_d.rearrange("h d t -> d h t")[:, 0:2, NTL * 128:NT * 128], [], [r_kTc])
            Vc = A.alloc([2, 2, 65], BF16)
            r_Vc = S.res()
            B.memset("dve", Vc[:, :, :, 64:65], 1.0, [r_Vc])
            load_V(Vc, r_Vc, NTL, 2, 2)
            kTwr = Ring(B, 4, [2, 384], BF16, parts=64)
            Vwr = Ring(B, 5, [3, 2, 65], BF16)
            for vb_, r_ in Vwr.bufs:
                B.memset("dve", vb_[:, :, :, 64:65], 1.0, [r_])
            qTr = Ring(B, 4, [8, 128], BF16, parts=64)
            pTr = Ring(B, 16, [512], BF16)
            etr = Ring(B, 2, [512], BF16)
            mixr = Ring(B, 4, [D], BF16)
            str_ = Ring(B, 4, [8], F32)
            vvr = Ring(B, 5, [512], BF16)
            ur = Ring(B, 5, [512], F32)
            btr = Ring(B, 2, [512], F32)
            rings = {"mixT": Ring(B, 2, [8, 128], BF16), "hin": Ring(B, 3, [D], F32), "et": Ring(B, 2, [D], F32),
                     "hout": Ring(B, 2, [D], F32)}

            def tile_gen(n):
                bg_tick()
                ty = 0 if n < NTL else 1
                qT, r_qT = qTr.next()
                B.dma("sp", qT, qT_d.rearrange("h d t -> d h t")[:, :, n * 128:(n + 1) * 128], [], [r_qT])
                keys = []
                if ty == 0:
                    kt0 = min(max(n - 1, 0), NTL - 3)
                    kTw, r_kTw = kTwr.next()
                    B.dma("sp", kTw, kT_d.rearrange("h d t -> d h t")[:, 0:2, kt0 * 128:(kt0 + 3) * 128], [], [r_kTw])
                    Vw, r_Vw = Vwr.next()
                    load_V(Vw, r_Vw, kt0, 3, 2)
                    for kt, mk in ((n - 1, 0), (n, None), (n + 1, 1)):
                        if 0 <= kt < NTL:
                            s_ = kt - kt0
                            keys.append((kTw, s_, Vw, s_, mk, [r_kTw], [r_Vw]))
                for s_ in range(2):
                    keys.append((kTc, s_, Vc, s_, None, [r_kTc], [r_Vc]))
                vv, r_vv = vvr.next()
                B.dma("sp", vv, vv_d[n * 128:(n + 1) * 128, :], [], [r_vv])
                u_, r_u = ur.next()
                B.dma("sp", u_, u_d[n * 128:(n + 1) * 128, :], [], [r_u])
                yield

                def qk(kv):
                    pts = []
                    for (kTa, ks, Va, vs, mk, rk, rv) in keys:
                        bk, r_bk = B.bank()
                        B.mm(bk.rearrange("p (h q) -> p h q", h=4), kTa[:, kv, ks * 128:(ks + 1) * 128], qT[:, 4 * kv:4 * kv + 4, :],
                             True, True, rk + [r_qT], [r_bk])
                        pT, r_pT = pTr.next()
                        if mk is None:
                            B.act(pT, bk, AF.Exp, [], [r_bk, r_pT], scale=0.125)
                        else:
                            et, r_et = etr.next()
                            B.act(et, bk, AF.Exp, [], [r_bk, r_et], scale=0.125)
                            B.tt("dve", pT.rearrange("p (h q) -> p h q", h=4), et.rearrange("p (h q) -> p h q", h=4),
                                 amask[:, mk, :].unsqueeze(1).to_broadcast([128, 4, 128]), ALU.mult, [r_et, r_am], [r_pT])
                        pts.append((pT, r_pT, Va, vs, rv))
                    return pts

                def pv(pkv, pts, mix, r_mix):
                    ob, r_ob = B.bank()
                    for hh in range(4):
                        for ei, (pT, r_pT, Va, vs, rv) in enumerate(pts):
                            B.mm(ob[:, hh * 65:(hh + 1) * 65], pT[:, hh * 128:(hh + 1) * 128], Va[:, vs, pkv, :],
                                 ei == 0, ei == len(pts) - 1, [r_pT] + rv, [r_ob])
                    ob3 = ob[:, 0:260].rearrange("p (h e) -> p h e", h=4)
                    sd, r_sd = str_.next()
                    B.tt("dve", sd[:, 0:4], ob3[:, :, 64], esink[:, 4 * pkv:4 * pkv + 4], ALU.add, [r_es], [r_ob, r_sd])
                    B.recip(sd[:, 4:8], sd[:, 0:4], [r_sd], [r_sd])
                    B.tt("dve", mix[:, pkv * 256:(pkv + 1) * 256].rearrange("p (h d) -> p h d", h=4), ob3[:, :, 0:64],
                         sd[:, 4:8].unsqueeze(2).to_broadcast([128, 4, 64]), ALU.mult, [r_sd], [r_ob, r_mix])

                pts0 = qk(0)
                yield
                pts1 = qk(1)
                mix, r_mix = mixr.next()
                pv(0, pts0, mix, r_mix)
                yield
                pv(1, pts1, mix, r_mix)
                bk, r_bk = B.bank()
                for g in range(8):
                    B.mm(bk[:, g * 64:(g + 1) * 64], wsT[:, g, :], vv[:, g * 64:(g + 1) * 64], True, True, [r_wsT, r_vv], [r_bk])
                bt, r_bt = btr.next()
                B.tt("dve", bt.rearrange("p (g d) -> p g d", g=8), bk.rearrange("p (g d) -> p g d", g=8),
                     biasT.unsqueeze(2).to_broadcast([128, 8, 64]), ALU.add, [r_biasT], [r_bk, r_bt])
                B.tt("pool", mix[:, 512:1024], bt, u_, ALU.mult, [r_bt, r_u], [r_mix])
                yield
                yield from outproj_residual(mix, r_mix, wout, r_wout, mvs[ty][2], mvs[ty][3], n, rings)

            pipeline(tile_gen, range(NT))

        def phase_odd_prep(li):
            B.phase_begin()
            mvs = load_mod_vecs(li, 3, 4, None, 1, 1.0)
            win = A.alloc([8, 2048], BF16)
            r_win = S.res()
            load_w_bf16("od_w_in", win, r_win, I["od_w_in"], 2048)
            qg_bc, r_qg = B.load_bc("sp", I["d_q_gain"][0:1, :], 64)
            kg_bc, r_kg = B.load_bc("sp", I["d_k_gain"][0:1, :], 64)
            hinr = Ring(B, 6, [D], F32)
            rings = {"sqj": (A.alloc([D], BF16), S.res()), "st": Ring(B, 24, [30], F32),
                     "z1": Ring(B, 2, [D], F32), "ztok": Ring(B, 3, [D], BF16),
                     "sq": Ring(B, 6, [512], F32),
                     "hT": Ring(B, 3, [8, 128], BF16, parts=64)}
            zTr = Ring(B, 3, [8, 128], BF16)
            qrr = Ring(B, 7, [512], F32)
            krr = Ring(B, 7, [512], F32)
            qbr = Ring(B, 8, [512], BF16)
            kbr = Ring(B, 8, [512], BF16)
            vbr = Ring(B, 4, [512], BF16)
            xbr = Ring(B, 4, [512], BF16)

            def tile_gen(gt):
                bg_tick()
                ty = 0 if gt < NTL else 1
                hin, r_hin = hinr.next()
                B.dma("sp", hin, hsrc(gt), [], [r_hin])
                yield
                zt, r_zt = yield from norm_tile_g(hin, r_hin, mvs[ty], rings)
                yield
                zT, r_zT = zTr.next()
                transpose8(zt, r_zt, zT, r_zT)
                yield
                nbs = [0, 1, 2, 3] if ty == 0 else [2, 3]
                bks = {i: B.bank() for i in nbs}
                for k in range(8):
                    for i in nbs:
                        B.mm(bks[i][0], zT[:, k, :], win[:, k, i * 512:(i + 1) * 512], k == 0, k == 7, [r_zT, r_win], [bks[i][1]])
                yield
                items = []
                qb = r_qb = None
                if ty == 0:
                    xb, r_xb = xbr.next()
                    B.cp("act", xb, bks[0][0], [], [bks[0][1], r_xb])
                    qr, r_qr = qrr.next()
                    B.cp("act", qr, bks[1][0], [], [bks[1][1], r_qr])
                    qb, r_qb = qbr.next()
                    items.append(dict(src=qr, r_src=r_qr, nh=8, gain=qg_bc, r_gain=r_qg, dst=qb, r_dst=r_qb))
                kr, r_kr = krr.next()
                B.cp("act", kr, bks[2][0], [], [bks[2][1], r_kr])
                kb, r_kb = kbr.next()
                items.append(dict(src=kr, r_src=r_kr, nh=8, gain=kg_bc, r_gain=r_kg, dst=kb, r_dst=r_kb))
                vb, r_vb = vbr.next()
                B.cp("act", vb, bks[3][0], [], [bks[3][1], r_vb])
                yield
                if ty == 0:
                    B.dma("sp", xp_d[gt * 128:(gt + 1) * 128, :], xb, [r_xb], [])
                B.dma("sp", v_d[gt * 128:(gt + 1) * 128, :], vb, [r_vb], [])
                yield from chains_g(items, rings)
                yield
                if ty == 0:
                    head_transposes(qb, r_qb, 8, qT_d, gt, rings)
                head_transposes(kb, r_kb, 8, kT_d, gt, rings)

            pipeline(tile_gen, range(NT))

        def phase_odd_attn(li):
            B.phase_begin()
            mvs = load_mod_vecs(li, None, None, 5, 1, 1.0, ntypes=1)
            wout = A.alloc([8, D], BF16)
            r_wout = S.res()
            load_w_bf16("od_w_out", wout, r_wout, I["od_w_out"], 1024)
            wpool = A.alloc([4, 128], BF16)
            r_wpool = S.res()
            B.dma("pool", wpool, I["c_w_pool"].rearrange("g c d -> c g d"), [], [r_wpool])
            csc, r_csc = B.load_bc("sp", I["c_scale"][0:1, :], 512)
            band = A.alloc([4, 5, 128], BF16)
            r_band = S.res()
            B.dma("pool", band, I["k_band"][:, :, :, :], [], [r_band], max_dma_last_dim=2048)
            kTp = kT_d.rearrange("(g e) d t -> (e d) g t", e=2)
            qTp = qT_d.rearrange("(g e) d t -> (e d) g t", e=2)
            kTc = A.alloc([4, 256], BF16)
            r_kTc = S.res()
            B.dma("sp", kTc, kTp[:, :, NTL * 128:NT * 128], [], [r_kTc])
            Vc = A.alloc([2, 8, 65], BF16)
            r_Vc = S.res()
            B.memset("dve", Vc[:, :, :, 64:65], 1.0, [r_Vc])
            load_V(Vc, r_Vc, NTL, 2, 8)
            biasr = Ring(B, 1, [8, 7, 128], F32)
            for bb_, r_ in biasr.bufs:
                B.memset("pool", bb_[:, :, 5:7, :], 0.0, [r_])
            kTwr = Ring(B, 4, [4, 640], BF16)
            Vwr = Ring(B, 4, [5, 8, 65], BF16)
            for vb_, r_ in Vwr.bufs:
                B.memset("dve", vb_[:, :, :, 64:65], 1.0, [r_])
            qTr = Ring(B, 4, [4, 128], BF16)
            xpr = Ring(B, 3, [3, 512], BF16)
            ppr = Ring(B, 2, [4, 128], BF16)
            tAr = Ring(B, 3, [512], F32)
            tBr = Ring(B, 3, [384], F32)
            pAr = Ring(B, 14, [512], BF16)
            pBr = Ring(B, 14, [384], BF16)
            mixr = Ring(B, 8, [D], BF16)
            str_ = Ring(B, 4, [8], F32)
            rings = {"mixT": Ring(B, 2, [8, 128], BF16), "hin": Ring(B, 3, [D], F32), "et": Ring(B, 2, [D], F32),
                     "hout": Ring(B, 2, [D], F32)}
            state = {"case": None, "bias": None, "r_bias": None}
            B.nrr = 6

            def case_of(n):
                return 0 if n == 0 else 1 if n == 1 else 3 if n == NTL - 2 else 4 if n == NTL - 1 else 2

            def tile_gen(n):
                bg_tick()
                case = case_of(n)
                if case != state["case"]:
                    bias_, r_bias_ = biasr.next()
                    B.dma("sp", bias_[:, :, 0:5, :], I["k_dbias"][case], [], [r_bias_])
                    state["case"] = case
                    state["bias"] = bias_
                    state["r_bias"] = r_bias_
                bias = state["bias"]
                r_bias = state["r_bias"]
                kt0 = min(max(n - 2, 0), NTL - 5)
                kTw, r_kTw = kTwr.next()
                B.dma("sp", kTw, kTp[:, :, kt0 * 128:(kt0 + 5) * 128], [], [r_kTw])
                Vw, r_Vw = Vwr.next()
                load_V(Vw, r_Vw, kt0, 5, 8)
                qT, r_qT = qTr.next()
                B.dma("sp", qT, qTp[:, :, n * 128:(n + 1) * 128], [], [r_qT])
                xw, r_xw = xpr.next()
                jts = [j for j in (n - 1, n, n + 1) if 0 <= j < NTL]
                j0 = jts[0]
                B.dma("sp", xw[:, 0:len(jts), :], xp_d[j0 * 128:(j0 + len(jts)) * 128, :].rearrange("(w p) f -> p w f", p=128), [], [r_xw])
                yield
                mix, r_mix = mixr.next()
                bk, r_bk = B.bank()
                for g in range(4):
                    for ji, j in enumerate(jts):
                        if j == n - 1:
                            typ = 0
                        elif j == n + 1:
                            typ = 2
                        else:
                            typ = 3 if n == 0 else (4 if n == NTL - 1 else 1)
                        B.mm(bk[:, g * 128:(g + 1) * 128], xw[:, ji, g * 128:(g + 1) * 128], band[:, g, typ, :],
                             ji == 0, ji == len(jts) - 1, [r_xw, r_band], [r_bk])
                pp, r_pp = ppr.next()
                B.cp("act", pp, bk.rearrange("p (g t) -> p g t", g=4), [], [r_bk, r_pp])
                bk2, r_bk2 = B.bank()
                for g in range(4):
                    B.mm(bk2[:, g * 128:(g + 1) * 128], pp[:, g, :], wpool[:, g, :], True, True, [r_pp, r_wpool], [r_bk2])
                B.tt("dve", mix[:, 0:512], bk2, csc, ALU.mult, [r_csc], [r_bk2, r_mix])
                yield

                def scores(hq):
                    res_ = []
                    for hi in range(4):
                        h = hq * 4 + hi
                        bA, r_bA = B.bank()
                        bB, r_bB = B.bank()
                        pl = slice((h % 2) * 64, (h % 2) * 64 + 64)
                        g_ = h // 2
                        for s_ in range(4):
                            B.mm(bA[:, s_ * 128:(s_ + 1) * 128], kTw[pl, g_, s_ * 128:(s_ + 1) * 128], qT[pl, g_, :], True, True, [r_kTw, r_qT], [r_bA])
                        B.mm(bB[:, 0:128], kTw[pl, g_, 512:640], qT[pl, g_, :], True, True, [r_kTw, r_qT], [r_bB])
                        for s_ in range(2):
                            B.mm(bB[:, (1 + s_) * 128:(2 + s_) * 128], kTc[pl, g_, s_ * 128:(s_ + 1) * 128], qT[pl, g_, :], True, True, [r_kTc, r_qT], [r_bB])
                        tA, r_tA = tAr.next()
                        tB, r_tB = tBr.next()
                        B.stt("dve", tA, bA, 0.125, bias[:, h, 0:4, :].rearrange("p s q -> p (s q)"), ALU.mult, ALU.add, [r_bias], [r_bA, r_tA])
                        B.stt("dve", tB, bB[:, 0:384], 0.125, bias[:, h, 4:7, :].rearrange("p s q -> p (s q)"), ALU.mult, ALU.add, [r_bias], [r_bB, r_tB])
                        pA, r_pA = pAr.next()
                        pB, r_pB = pBr.next()
                        B.act(pA, tA, AF.Exp, [r_tA], [r_pA])
                        B.act(pB, tB, AF.Exp, [r_tB], [r_pB])
                        res_.append((h, pA, r_pA, pB, r_pB))
                    return res_

                def pvs(hq, res_):
                    ob, r_ob = B.bank_fixed(6 + hq)
                    for (ph, pA, r_pA, pB, r_pB) in res_:
                        osl = ob[:, (ph % 4) * 65:(ph % 4 + 1) * 65]
                        for s_ in range(7):
                            if s_ < 4:
                                lhs = pA[:, s_ * 128:(s_ + 1) * 128]
                                rp = r_pA
                            else:
                                lhs = pB[:, (s_ - 4) * 128:(s_ - 3) * 128]
                                rp = r_pB
                            if s_ < 5:
                                rhs = Vw[:, s_, ph, :]
                                rv = r_Vw
                            else:
                                rhs = Vc[:, s_ - 5, ph, :]
                                rv = r_Vc
                            B.mm(osl, lhs, rhs, s_ == 0, s_ == 6, [rp, rv], [r_ob])
                    ob3 = ob[:, 0:260].rearrange("p (h e) -> p h e", h=4)
                    sd, r_sd = str_.next()
                    B.recip(sd[:, 0:4], ob3[:, :, 64], [], [r_ob, r_sd])
                    B.tt("dve", mix[:, 512 + hq * 256:512 + (hq + 1) * 256].rearrange("p (h d) -> p h d", h=4), ob3[:, :, 0:64],
                         sd[:, 0:4].unsqueeze(2).to_broadcast([128, 4, 64]), ALU.mult, [r_sd], [r_ob, r_mix])

                r0 = scores(0)
                yield
                r1 = scores(1)
                pvs(0, r0)
                yield
                pvs(1, r1)
                yield
                yield from outproj_residual(mix, r_mix, wout, r_wout, mvs[0][2], mvs[0][3], n, rings)

            pipeline(tile_gen, range(NTL), drain_before=lambda n: case_of(n) != state["case"])
            B.nrr = 8

        plist = [
            lambda: phase_mod([(0, 0, 6)]),
            lambda: phase_ffn(0, 0, NT, src0, hsrc),
            lambda: phase_mod([(0, 6, 18), (1, 0, 18)]),
            lambda: phase_even_prep(0),
            lambda: phase_even_attn(0),
            lambda: phase_ffn(0, 1, NT, hsrc, hsrc),
            lambda: phase_ffn(1, 0, NT, hsrc, hsrc),
            lambda: phase_odd_prep(1),
            lambda: phase_odd_attn(1),
            lambda: phase_ffn(1, 1, NTL, hsrc, osrc),
        ]
        if dbg_phases is not None:
            plist = plist[:dbg_phases]
        for p in plist:
            p()
        S.emit()
        build_program.stats = S.stats
    return nc


def _rope_table():
    t = np.arange(S_LAT)
    row = (t // 64).astype(np.float32)
    col = (t % 64).astype(np.float32)
    m = 16
    inv = (1.0 / (10000.0 ** (np.arange(m, dtype=np.float32) / m))).astype(np.float32)
    ar = row[:, None] * inv[None, :]
    ac = col[:, None] * inv[None, :]
    cos = np.concatenate([np.cos(ar), np.cos(ar), np.cos(ac), np.cos(ac)], axis=1)
    sin = np.concatenate([-np.sin(ar), np.sin(ar), -np.sin(ac), np.sin(ac)], axis=1)
    return np.stack([cos, sin], axis=1).astype(np.float32)


def _amask():
    pj = np.arange(128)[:, None]
    pi = np.arange(128)[None, :]
    prev = (pj >= pi).astype(np.float32)
    nxt = (pj <= pi).astype(np.float32)
    return np.stack([prev, nxt], axis=1)


def _band():
    out = np.zeros((128, 4, 5, 128), np.float32)
    for gi, w in enumerate((2, 4, 8, 16)):
        def mat(n, jn):
            tg = n * 128 + np.arange(128)
            lo = np.clip(tg - w // 2, 0, S_LAT)
            hi = np.clip(tg + w - w // 2, 0, S_LAT)
            cnt = (hi - lo).astype(np.float32)
            jg = jn * 128 + np.arange(128)
            m = ((jg[:, None] >= lo[None, :]) & (jg[:, None] < hi[None, :])).astype(np.float32) / cnt[None, :]
            m = m - (jg[:, None] == tg[None, :]).astype(np.float32)
            return m
        out[:, gi, 0] = mat(5, 4)
        out[:, gi, 1] = mat(5, 5)
        out[:, gi, 2] = mat(5, 6)
        out[:, gi, 3] = mat(0, 0)
        out[:, gi, 4] = mat(NTL - 1, NTL - 1)
    return out


def _dbias(rpb):
    out = np.full((5, 128, 8, 5, 128), NEGB, np.float32)
    for case, n in enumerate((0, 1, 5, NTL - 2, NTL - 1)):
        kt0 = min(max(n - 2, 0), NTL - 5)
        i = np.arange(128)
        r = 2 * n + i // 64
        c = i % 64
        r0 = np.clip(r - 4, 0, 56)
        q0 = np.clip(c - 8, 0, 48)
        for s in range(5):
            kt = kt0 + s
            j = np.arange(128)
            kr = 2 * kt + j // 64
            kc = j % 64
            valid = ((kr[:, None] >= r0[None, :]) & (kr[:, None] < r0[None, :] + 8) &
                     (kc[:, None] >= q0[None, :]) & (kc[:, None] < q0[None, :] + 16))
            ri = np.clip(kr[:, None] - r[None, :] + 7, 0, 14)
            ci = np.clip(kc[:, None] - c[None, :] + 15, 0, 30)
            g = rpb[:, ri, ci]
            g = np.where(valid[None], g, np.float32(NEGB))
            out[case, :, :, s, :] = np.transpose(g, (1, 0, 2))
    return out


_CACHE = {}


def kernel(x, c, ctx, c_ctx, ada_w, ada_b, norm_g, ffn_w_gu, ffn_w_down,
           ev_w_in, ev_w_out, a_q_gain, a_k_gain, a_sink, b_v_gain, b_ws, b_bias,
           od_w_in, od_w_out, c_w_pool, c_scale, d_q_gain, d_k_gain, d_rpb, _dbg_phases=None, _dbg=False):
    f = lambda a: np.ascontiguousarray(np.asarray(a, dtype=np.float32))
    key = (_dbg_phases, _dbg)
    if key not in _CACHE:
        _CACHE[key] = build_program(_dbg_phases, _dbg)
    nc = _CACHE[key]
    shared = {
        "c_ctx": f(c_ctx).reshape(1, D), "ada_w": f(ada_w), "ada_b": f(ada_b), "norm_g": f(norm_g),
        "ffn_w_gu": f(ffn_w_gu), "ffn_w_down": f(ffn_w_down),
        "ev_w_in": f(ev_w_in)[0], "ev_w_out": f(ev_w_out)[0],
        "a_q_gain": f(a_q_gain), "a_k_gain": f(a_k_gain), "a_sink": f(a_sink),
        "b_v_gain": f(b_v_gain), "b_ws": f(b_ws)[0], "b_bias": f(b_bias)[0],
        "od_w_in": f(od_w_in)[0], "od_w_out": f(od_w_out)[0],
        "c_w_pool": f(c_w_pool)[0], "c_scale": f(c_scale),
        "d_q_gain": f(d_q_gain), "d_k_gain": f(d_k_gain),
        "k_ident": np.eye(128, dtype=np.float32), "k_rope": _rope_table(), "k_amask": _amask(),
        "k_band": _band(), "k_dbias": _dbias(f(d_rpb)[0]),
    }
    x = f(x); c = f(c); ctx = f(ctx)
    in_maps = []
    for b in range(8):
        m = dict(shared)
        m["x"] = x[b]
        m["ctx"] = ctx[b]
        m["c"] = c[b].reshape(1, D)
        in_maps.append(m)
    res = run_bass_kernel_spmd(nc, in_maps, core_ids=list(range(8)))
    kernel.last = res
    return np.stack([r["out"] for r in res.results], axis=0)
```
